# Optimizing a Trainium2 kernel written in Bass

```python
import math
import jax, jax.numpy as jnp
from jax import lax
import numpy as np

D_MODEL = 1024
BATCH = 2
SEQ = 8192
DEPTH = 4
DEC_BATCH = 8
DEC_SEQ = 4096
PAST_LEN = 128

MIX_WIDTH = D_MODEL
DIFF_WIDTH = MIX_WIDTH // 2
RET_WIDTH = MIX_WIDTH - DIFF_WIDTH
DIFF_QK_DIM = 64
DIFF_V_DIM = 2 * DIFF_QK_DIM
DIFF_HEADS = DIFF_WIDTH // DIFF_V_DIM
RET_HEAD_DIM = 64
RET_HEADS = RET_WIDTH // RET_HEAD_DIM
ROT_DIMS = DIFF_QK_DIM // 4
ROPE_THETA = 500000.0
RET_THETA = 10000.0
D_FF = 4 * D_MODEL
PLE_DIM = 256
CHUNK = 128
Q_BLOCK = 128
EPS = 1e-6
IN_WIDTH = 3 * DIFF_WIDTH + 4 * RET_WIDTH
SPLITS = (DIFF_WIDTH, 2 * DIFF_WIDTH, 3 * DIFF_WIDTH,
          3 * DIFF_WIDTH + RET_WIDTH, 3 * DIFF_WIDTH + 2 * RET_WIDTH,
          3 * DIFF_WIDTH + 3 * RET_WIDTH)

kernel_name = 'hybrid_diffattn_retention_encoder'


def rms_norm(x, w):
    xf = x.astype(jnp.float32)
    y = xf * lax.rsqrt(jnp.mean(xf * xf, axis=-1, keepdims=True) + EPS)
    return (y * w.astype(jnp.float32)).astype(x.dtype)


def rotary(x, rot_dims, theta):
    s = x.shape[-2]
    half = rot_dims // 2
    inv = 1.0 / (theta ** (jnp.arange(0, rot_dims, 2, dtype=jnp.float32) / rot_dims))
    ang = jnp.arange(s, dtype=jnp.float32)[:, None] * inv[None, :]
    cos, sin = jnp.cos(ang), jnp.sin(ang)
    xf = x.astype(jnp.float32)
    x1, x2, xp = xf[..., :half], xf[..., half:rot_dims], xf[..., rot_dims:]
    out = jnp.concatenate([x1 * cos - x2 * sin, x1 * sin + x2 * cos, xp], axis=-1)
    return out.astype(x.dtype)


def differential_attention(q, k, v, lam):
    b, h, _, s, d = q.shape
    q = q * (d ** -0.5)
    nq = s // Q_BLOCK
    qb = jnp.moveaxis(q.reshape(b, h, 2, nq, Q_BLOCK, d), 3, 0)

    def block(qi):
        sc = jnp.einsum('bhcqd,bhckd->bhcqk', qi, k).astype(jnp.float32)
        pr = jax.nn.softmax(sc, axis=-1)
        w = pr[:, :, 0] - lam * pr[:, :, 1]
        return jnp.einsum('bhqk,bhke->bhqe', w.astype(v.dtype), v)

    out = lax.map(block, qb)
    return jnp.moveaxis(out, 0, 2).reshape(b, h, s, v.shape[-1])


def retention_chunkwise(q, k, v, log_gamma, strict):
    q = q.astype(jnp.float32)
    k = k.astype(jnp.float32)
    v = v.astype(jnp.float32)
    b, h, s, dk = q.shape
    dv = v.shape[-1]
    n = s // CHUNK
    qc = q.reshape(b, h, n, CHUNK, dk)
    kc = k.reshape(b, h, n, CHUNK, dk)
    vc = v.reshape(b, h, n, CHUNK, dv)
    idx = jnp.arange(CHUNK, dtype=jnp.float32)
    rel = idx[:, None] - idx[None, :]
    mask = (rel > 0) if strict else (rel >= 0)
    lg = log_gamma[:, None, None]
    dmat = jnp.exp(jnp.where(mask[None], rel[None] * lg, -jnp.inf))
    scores = jnp.einsum('bhncd,bhnmd->bhncm', qc, kc) * dmat[None, :, None]
    inner = jnp.einsum('bhncm,bhnme->bhnce', scores, vc)
    lg1 = log_gamma[:, None]
    k_decay = jnp.exp((CHUNK - 1.0 - idx)[None, :] * lg1)
    kv = jnp.einsum('bhncd,bhnce->bhnde', kc * k_decay[None, :, None, :, None], vc)
    chunk_decay = jnp.exp(CHUNK * log_gamma)[None, :, None, None]

    def step(state, kv_i):
        return state * chunk_decay + kv_i, state

    init = jnp.zeros((b, h, dk, dv), jnp.float32)
    _, prev = lax.scan(step, init, jnp.moveaxis(kv, 2, 0))
    prev = jnp.moveaxis(prev, 0, 2)
    q_decay = jnp.exp((idx + 1.0)[None, :] * lg1)
    cross = jnp.einsum('bhncd,bhnde->bhnce', qc * q_decay[None, :, None, :, None], prev)
    return (inner + cross).reshape(b, h, s, dv)


def encoder_layer(x, p_i, layer_idx, ln1_w, w_in, q_norm_w, k_norm_w, lam_p, subln_w,
                  decay_logit, gn_w, w_out, ln2_w, w1, w2, wg, wp):
    b, s, _ = x.shape
    hn = rms_norm(x, ln1_w)
    proj = hn @ w_in
    dq, dk, dv, rq, rk, rv, rg = jnp.split(proj, SPLITS, axis=-1)

    dq = rms_norm(dq.reshape(b, s, DIFF_HEADS, 2, DIFF_QK_DIM), q_norm_w)
    dk = rms_norm(dk.reshape(b, s, DIFF_HEADS, 2, DIFF_QK_DIM), k_norm_w)
    dq = rotary(jnp.transpose(dq, (0, 2, 3, 1, 4)), ROT_DIMS, ROPE_THETA)
    dk = rotary(jnp.transpose(dk, (0, 2, 3, 1, 4)), ROT_DIMS, ROPE_THETA)
    dv = dv.reshape(b, s, DIFF_HEADS, DIFF_V_DIM).transpose(0, 2, 1, 3)
    lam_init = 0.8 - 0.6 * math.exp(-0.3 * layer_idx)
    lp = lam_p.astype(jnp.float32)
    lam = jnp.exp(jnp.sum(lp[0] * lp[1])) - jnp.exp(jnp.sum(lp[2] * lp[3])) + lam_init
    a = differential_attention(dq, dk, dv, lam)
    a = rms_norm(a, subln_w) * (1.0 - lam_init)
    a = a.transpose(0, 2, 1, 3).reshape(b, s, DIFF_WIDTH).astype(x.dtype)

    rq = rotary(rq.reshape(b, s, RET_HEADS, RET_HEAD_DIM).transpose(0, 2, 1, 3), RET_HEAD_DIM, RET_THETA)
    rk = rotary(rk.reshape(b, s, RET_HEADS, RET_HEAD_DIM).transpose(0, 2, 1, 3), RET_HEAD_DIM, RET_THETA)
    rk = rk * (RET_HEAD_DIM ** -0.5)
    rv = rv.reshape(b, s, RET_HEADS, RET_HEAD_DIM).transpose(0, 2, 1, 3)
    lg = jax.nn.log_sigmoid(decay_logit.astype(jnp.float32))
    fwd = retention_chunkwise(rq, rk, rv, lg[0], False)
    bwd = jnp.flip(retention_chunkwise(jnp.flip(rq, 2), jnp.flip(rk, 2), jnp.flip(rv, 2), lg[1], True), 2)
    r = rms_norm(fwd + bwd, gn_w).astype(x.dtype)
    r = r.transpose(0, 2, 1, 3).reshape(b, s, RET_WIDTH)
    r = jax.nn.silu(rg) * r

    x = x + jnp.concatenate([a, r], axis=-1) @ w_out

    h2 = rms_norm(x, ln2_w)
    x = x + jnp.square(jax.nn.relu(h2 @ w1)) @ w2

    x = x + jax.nn.sigmoid(x @ wg) * (p_i @ wp)
    return x


def run_trunk(x, p, ln1_w, w_in, diff_q_norm, diff_k_norm, diff_lambda, diff_subln,
              ret_decay_logit, ret_gn, w_out, ln2_w, w_mlp1, w_mlp2, w_ple_gate, w_ple_proj):
    for i in range(DEPTH):
        x = encoder_layer(x, p[i], i, ln1_w[i], w_in[i], diff_q_norm[i], diff_k_norm[i],
                          diff_lambda[i], diff_subln[i], ret_decay_logit[i], ret_gn[i],
                          w_out[i], ln2_w[i], w_mlp1[i], w_mlp2[i], w_ple_gate[i], w_ple_proj[i])
    return x


def setup_inputs(seed: int = 0) -> dict:
    key = jax.random.key(seed)
    ks = jax.random.split(key, 20)
    f32 = jnp.float32
    nrm = lambda k, shape, scale: jax.random.normal(k, shape, f32) * scale
    base_logit = jnp.log(2.0 ** (5.0 + jnp.arange(RET_HEADS, dtype=f32)) - 1.0)
    return {
        'x_prompt': nrm(ks[0], (BATCH, SEQ, D_MODEL), 1.0),
        'x_sample': nrm(ks[1], (DEC_BATCH, DEC_SEQ, D_MODEL), 1.0),
        'p_prompt': nrm(ks[2], (DEPTH, BATCH, SEQ, PLE_DIM), 1.0),
        'p_sample': nrm(ks[3], (DEPTH, DEC_BATCH, DEC_SEQ, PLE_DIM), 1.0),
        'ln1_w': 1.0 + nrm(ks[4], (DEPTH, D_MODEL), 0.02),
        'w_in': nrm(ks[5], (DEPTH, D_MODEL, IN_WIDTH), D_MODEL ** -0.5),
        'diff_q_norm': 1.0 + nrm(ks[6], (DEPTH, DIFF_QK_DIM), 0.02),
        'diff_k_norm': 1.0 + nrm(ks[7], (DEPTH, DIFF_QK_DIM), 0.02),
        'diff_lambda': nrm(ks[8], (DEPTH, 4, DIFF_QK_DIM), 0.1),
        'diff_subln': 1.0 + nrm(ks[9], (DEPTH, DIFF_V_DIM), 0.02),
        'ret_decay_logit': base_logit[None, None, :] + nrm(ks[10], (DEPTH, 2, RET_HEADS), 0.1),
        'ret_gn': 1.0 + nrm(ks[11], (DEPTH, RET_HEAD_DIM), 0.02),
        'w_out': nrm(ks[12], (DEPTH, MIX_WIDTH, D_MODEL), MIX_WIDTH ** -0.5),
        'ln2_w': 1.0 + nrm(ks[13], (DEPTH, D_MODEL), 0.02),
        'w_mlp1': nrm(ks[14], (DEPTH, D_MODEL, D_FF), D_MODEL ** -0.5),
        'w_mlp2': nrm(ks[15], (DEPTH, D_FF, D_MODEL), D_FF ** -0.5),
        'w_ple_gate': nrm(ks[16], (DEPTH, D_MODEL, D_MODEL), D_MODEL ** -0.5),
        'w_ple_proj': nrm(ks[17], (DEPTH, PLE_DIM, D_MODEL), PLE_DIM ** -0.5),
    }


def reference(x_prompt, x_sample, p_prompt, p_sample, ln1_w, w_in, diff_q_norm, diff_k_norm,
              diff_lambda, diff_subln, ret_decay_logit, ret_gn, w_out, ln2_w, w_mlp1, w_mlp2,
              w_ple_gate, w_ple_proj):
    y_prompt = run_trunk(x_prompt, p_prompt, ln1_w, w_in, diff_q_norm, diff_k_norm, diff_lambda,
                         diff_subln, ret_decay_logit, ret_gn, w_out, ln2_w, w_mlp1, w_mlp2,
                         w_ple_gate, w_ple_proj)
    y_sample = run_trunk(x_sample, p_sample, ln1_w, w_in, diff_q_norm, diff_k_norm, diff_lambda,
                         diff_subln, ret_decay_logit, ret_gn, w_out, ln2_w, w_mlp1, w_mlp2,
                         w_ple_gate, w_ple_proj)
    return (y_prompt, y_sample)
```

```python
import math
import types
import numpy as np
import concourse.bass as bass
import concourse.mybir as mybir
from concourse.bass_utils import run_bass_kernel_spmd
from contextlib import ExitStack

F32 = mybir.dt.float32
BF16 = mybir.dt.bfloat16
AF = mybir.ActivationFunctionType
ALU = mybir.AluOpType
AX = mybir.AxisListType

COMPUTE = ("pe", "act", "dve", "pool")
ALL_ENG = ("pe", "act", "dve", "pool", "sp")
SAME_ENG_SYNC = True

D = 1024
INW = 3584
DFF = 4096
PLE = 256
EPS = 1e-6
NR = 4


class Res:
    __slots__ = ("name", "w", "rs")

    def __init__(self, name=""):
        self.name = name
        self.w = None
        self.rs = []


class Op:
    __slots__ = ("eng", "fn", "deps", "mark", "val", "key", "is_mm", "inc", "cost", "idx", "fin", "succ", "npend", "ready")

    def __init__(self, eng, fn, key=None, is_mm=False, cost=None):
        self.eng = eng
        self.fn = fn
        self.deps = []
        self.mark = False
        self.val = 0
        self.key = key
        self.is_mm = is_mm
        self.inc = 16
        self.cost = cost
        self.idx = 0
        self.fin = 0.0
        self.succ = None
        self.npend = 0
        self.ready = 0.0


def _freeze(fn):
    if fn is None or fn.__closure__ is None:
        return fn
    cells = []
    for c in fn.__closure__:
        try:
            cells.append(types.CellType(c.cell_contents))
        except ValueError:
            cells.append(c)
    return types.FunctionType(fn.__code__, fn.__globals__, fn.__name__, fn.__defaults__, tuple(cells))


DEF_COST = {"pe": 0.27, "act": 0.45, "dve": 0.60, "pool": 0.90, "sp": 0.30}
XLAT = 0.3
DMA_LAT = 3.0
RESCHEDULE = True


class Sched:
    def __init__(self, nc):
        self.nc = nc
        self.ops = []

    def op(self, eng, fn, reads=(), writes=(), key=None, is_mm=False, cost=None):
        o = Op(eng, _freeze(fn), key, is_mm, cost)
        deps = o.deps
        for r in reads:
            if r.w is not None:
                deps.append(r.w)
        for w in writes:
            if w.w is not None:
                deps.append(w.w)
            deps.extend(w.rs)
        for r in reads:
            r.rs.append(o)
        for w in writes:
            w.w = o
            w.rs = []
        o.idx = len(self.ops)
        self.ops.append(o)
        return o

    def pe(self, fn, reads=(), writes=(), cost=None):
        return self.op("pe", fn, reads, writes, is_mm=True, cost=cost)

    def act(self, fn, reads=(), writes=(), cost=None):
        return self.op("act", fn, reads, writes, cost=cost)

    def dve(self, fn, reads=(), writes=(), cost=None):
        return self.op("dve", fn, reads, writes, cost=cost)

    def pool(self, fn, reads=(), writes=(), cost=None):
        return self.op("pool", fn, reads, writes, cost=cost)

    def dma(self, key, fn, reads=(), writes=(), eng="sp", inc=16, cost=None):
        o = self.op(eng, fn, reads, writes, key=key, cost=cost)
        o.inc = inc
        return o

    def barrier(self):
        o = Op(None, None)
        o.idx = len(self.ops)
        self.ops.append(o)

    @staticmethod
    def _skip(p, o):
        return p.key is None and p.eng == o.eng and (not SAME_ENG_SYNC or (p.is_mm and o.is_mm))

    def _schedule_segment(self, seg):
        import heapq
        inseg = set(id(o) for o in seg)
        for o in seg:
            o.succ = []
            o.npend = 0
            o.ready = 0.0
        for o in seg:
            for p in o.deps:
                if id(p) in inseg:
                    p.succ.append(o)
                    o.npend += 1
        free = {e: 0.0 for e in ALL_ENG}
        heaps = {e: [] for e in ALL_ENG}
        for o in seg:
            if o.npend == 0:
                heapq.heappush(heaps[o.eng], (0.0, o.idx, o))
        order = {e: [] for e in ALL_ENG}
        remaining = len(seg)
        while remaining:
            best = None
            for e in ALL_ENG:
                h = heaps[e]
                if not h:
                    continue
                t_free = free[e]
                cands = []
                while h and h[0][0] <= t_free:
                    cands.append(heapq.heappop(h))
                if cands:
                    c = min(cands, key=lambda x: x[1])
                    for x in cands:
                        if x is not c:
                            heapq.heappush(h, x)
                    heapq.heappush(h, c)
                    start = t_free
                    pick = c
                else:
                    pick = h[0]
                    start = pick[0]
                if best is None or start < best[0] or (start == best[0] and pick[1] < best[2][1]):
                    best = (start, e, pick)
            start, e, pick = best
            h = heaps[e]
            h.remove(pick)
            heapq.heapify(h)
            o = pick[2]
            cost = o.cost if o.cost is not None else DEF_COST[e]
            if o.key is not None:
                free[e] = start + DEF_COST["sp"]
                o.fin = start + (cost if o.cost is not None else DMA_LAT)
            else:
                free[e] = start + cost
                o.fin = free[e]
            order[e].append(o)
            remaining -= 1
            for q in o.succ:
                lat = 0.0 if (q.eng == o.eng and o.key is None) else XLAT
                r = o.fin + lat
                if r > q.ready:
                    q.ready = r
                q.npend -= 1
                if q.npend == 0:
                    heapq.heappush(heaps[q.eng], (q.ready, q.idx, q))
        return order

    def emit(self, stack):
        nc = self.nc
        segs = []
        cur = []
        for o in self.ops:
            if o.eng is None:
                if cur:
                    segs.append(cur)
                    cur = []
            else:
                cur.append(o)
        if cur:
            segs.append(cur)
        streams = {e: [] for e in ALL_ENG}
        for seg in segs:
            if RESCHEDULE:
                order = self._schedule_segment(seg)
            else:
                order = {e: [o for o in seg if o.eng == e] for e in ALL_ENG}
            lastops = {}
            for e in ALL_ENG:
                for o in order[e]:
                    lastops[o.key if o.key is not None else e] = o
            for e in ALL_ENG:
                streams[e].extend(order[e])
            deps = list(lastops.values())
            for e in ALL_ENG:
                b = Op(e, None)
                b.deps = list(deps)
                streams[e].append(b)
        for e in ALL_ENG:
            for o in streams[e]:
                for p in o.deps:
                    if p.key is None and not self._skip(p, o):
                        p.mark = True
        kcnt = {}
        for e in ALL_ENG:
            cnt = 0
            for o in streams[e]:
                if o.fn is None:
                    continue
                if o.key is not None:
                    kcnt[o.key] = kcnt.get(o.key, 0) + o.inc
                    o.val = kcnt[o.key]
                elif o.mark:
                    cnt += 1
                    o.val = cnt
        sems = {}
        for e in COMPUTE:
            sems[e] = stack.enter_context(nc.semaphore("s_" + e))
        for k in kcnt:
            sems[k] = stack.enter_context(nc.semaphore("d_" + k))
        self.nsem = len(sems)
        block = stack.enter_context(nc.Block())

        def run(engname, eng):
            seen = {}
            for o in streams[engname]:
                need = {}
                for p in o.deps:
                    if self._skip(p, o):
                        continue
                    sk = p.eng if p.key is None else p.key
                    if seen.get(sk, 0) >= p.val:
                        continue
                    if need.get(sk, 0) < p.val:
                        need[sk] = p.val
                for sk, v in need.items():
                    eng.wait_ge(sems[sk], v)
                    seen[sk] = v
                if o.fn is None:
                    continue
                ins = o.fn(eng)
                if o.key is not None:
                    ins.then_inc(sems[o.key], o.inc)
                elif o.mark:
                    ins.then_inc(sems[o.eng], 1)
            if engname == "sp":
                for k, v in kcnt.items():
                    if seen.get(k, 0) < v:
                        eng.wait_ge(sems[k], v)

        @block.tensor
        def _(e):
            run("pe", e)

        @block.scalar
        def _(e):
            run("act", e)

        @block.vector
        def _(e):
            run("dve", e)

        @block.gpsimd
        def _(e):
            run("pool", e)

        @block.sync
        def _(e):
            run("sp", e)


class Buf:
    __slots__ = ("t", "r")

    def __init__(self, t, name):
        self.t = t
        self.r = Res(name)


def build(TS, TP, DEPTH, DEBUG=False):
    T = TS + TP
    SKP = NR * TP
    NT = T // 128
    NST = T // 512
    assert TS % 512 == 0 and TP % 512 == 0
    nc = bass.Bass("TRN2", target_bir_lowering=False)

    def dram(name, shape, dtype, kind="Internal"):
        return nc.dram_tensor(name, shape, dtype, kind=kind).ap()

    x_in = dram("x", [T, D], F32, "ExternalInput")
    p_in = dram("p", [DEPTH, T, PLE], F32, "ExternalInput")
    pos_in = dram("pos", [T, 80], F32, "ExternalInput")
    cst_in = dram("cst", [128, 516], F32, "ExternalInput")
    rkt_in = dram("rkt", [128, 8], F32, "ExternalInput")
    ln1_in = dram("ln1_w", [DEPTH, D], F32, "ExternalInput")
    w_in_in = dram("w_in", [DEPTH, D, INW], F32, "ExternalInput")
    qn_in = dram("diff_q_norm", [DEPTH, 64], F32, "ExternalInput")
    kn_in = dram("diff_k_norm", [DEPTH, 64], F32, "ExternalInput")
    lam_in = dram("diff_lambda", [1, DEPTH * 256], F32, "ExternalInput")
    sub_in = dram("diff_subln", [DEPTH, 128], F32, "ExternalInput")
    dec_in = dram("ret_decay_logit", [1, DEPTH * 16], F32, "ExternalInput")
    gn_in = dram("ret_gn", [DEPTH, 64], F32, "ExternalInput")
    w_out_in = dram("w_out", [DEPTH, D, D], F32, "ExternalInput")
    ln2_in = dram("ln2_w", [DEPTH, D], F32, "ExternalInput")
    w1_in = dram("w_mlp1", [DEPTH, D, DFF], F32, "ExternalInput")
    w2_in = dram("w_mlp2", [DEPTH, DFF, D], F32, "ExternalInput")
    wg_in = dram("w_ple_gate", [DEPTH, D, D], F32, "ExternalInput")
    wp_in = dram("w_ple_proj", [DEPTH, PLE, D], F32, "ExternalInput")
    y_out = dram("y", [T, D], F32, "ExternalOutput")

    xres = dram("xres", [T, D], F32)
    dqT = dram("dqT", [512, T], BF16)
    dkT_s = dram("dkT_s", [512, TS], BF16)
    NSPL = max(1, (512 * TP * 2) // (1 << 20))
    HPS = 4 // NSPL
    TPS = TP // NSPL
    dkT_l = [dram(f"dkT_l{a}", [HPS * 128, TP], BF16) for a in range(NSPL)]
    dkT_g = [dram(f"dkT_g{a}", [NR * HPS * 128, TP], BF16) for a in range(NSPL)]
    dv_s = dram("dv_s", [TS, 512], BF16)
    dv_l = [dram(f"dv_l{a}", [TPS, 512], BF16) for a in range(NSPL)]
    dv_g = [dram(f"dv_g{a}", [NR * TPS, 512], BF16) for a in range(NSPL)]
    rqT = dram("rqT", [512, T], BF16)
    rkT = dram("rkT", [512, T], BF16)
    rqdT = dram("rqdT", [8 * 128, T], BF16)
    rkd = dram("rkd", [T, 1024], BF16)
    rv = dram("rv", [T, 512], BF16)
    rgs = dram("rgs", [T, 512], BF16)
    aT = dram("aT", [512, T], BF16)
    rT = dram("rT", [512, T], BF16)
    uT = dram("uT", [DFF, T], BF16)
    st_l = dram("st_l", [128, 512], F32)
    st_g = dram("st_g", [NR * 128, 512], F32)

    with ExitStack() as top:
        S = Sched(nc)

        uid = [0]

        def sb(st, name, shape, dt):
            uid[0] += 1
            return Buf(st.enter_context(nc.sbuf_tensor(f"sb{uid[0]}_{name}", shape, dt)), name)

        def psb(st, name, shape, dt):
            uid[0] += 1
            return Buf(st.enter_context(nc.psum_tensor(f"ps{uid[0]}_{name}", shape, dt)), name)

        ident = sb(top, "ident", [128, 128], BF16)
        identf = sb(top, "identf", [128, 128], F32)
        ones_b = sb(top, "ones_b", [128, 128], BF16)
        ones_f = sb(top, "ones_f", [128, 128], F32)
        cst = sb(top, "cst", [128, 516], F32)
        rkt = sb(top, "rkt", [128, 8], F32)
        lg = sb(top, "lg", [128, DEPTH * 16], F32)
        lgcol = sb(top, "lgcol", [128, DEPTH * 8], F32)
        lam = sb(top, "lam", [128, DEPTH], F32)
        neglam = sb(top, "neglam", [128, DEPTH], F32)
        epsc = sb(top, "epsc", [128, 1], F32)
        ln1b = sb(top, "ln1b", [128, D], F32)
        ln2b = sb(top, "ln2b", [128, D], F32)
        qnw = sb(top, "qnw", [128, 64], F32)
        knw = sb(top, "knw", [128, 64], F32)
        gnw = sb(top, "gnw", [128, 64], F32)
        subw = sb(top, "subw", [128, 1], F32)
        qdec = sb(top, "qdec", [128, 8, 2], F32)
        kdec = sb(top, "kdec", [128, 8, 2], F32)
        cdec = sb(top, "cdec", [128, 8], F32)
        DTe = sb(top, "DTe", [128, 4, 128], F32)
        DTo = sb(top, "DTo", [128, 4, 128], F32)
        coef = sb(top, "coef", [128, 4, 8], F32)

        relF = cst.t[:, 0:128]
        mskF = cst.t[:, 128:256]
        relB = cst.t[:, 256:384]
        mskB = cst.t[:, 384:512]
        idx_p1 = cst.t[:, 512:513]
        idx_128m = cst.t[:, 513:514]
        idx_127m = cst.t[:, 514:515]
        idx_p = cst.t[:, 515:516]

        S.dma("c_init", lambda e: e.dma_start(out=cst.t[:], in_=cst_in), writes=[cst.r])
        S.dma("c_init", lambda e: e.dma_start(out=rkt.t[:], in_=rkt_in), writes=[rkt.r])
        S.pool(lambda e: e.memset(identf.t[:], 0.0), writes=[identf.r])
        S.pool(lambda e: e.affine_select(out=identf.t[:], in_=identf.t[:], compare_op=ALU.not_equal, fill=1.0, base=0,
                                         pattern=[[-1, 128]], channel_multiplier=1), reads=[identf.r], writes=[identf.r])
        S.dve(lambda e: e.tensor_copy(out=ident.t[:], in_=identf.t[:]), reads=[identf.r], writes=[ident.r])
        S.dve(lambda e: e.memset(ones_b.t[:], 1.0), writes=[ones_b.r])
        S.dve(lambda e: e.memset(ones_f.t[:], 1.0), writes=[ones_f.r])
        S.dve(lambda e: e.memset(epsc.t[:], EPS), writes=[epsc.r])

        with ExitStack() as st0:
            NL = DEPTH * 16
            xx = sb(st0, "ls_x", [128, NL], F32)
            ax = sb(st0, "ls_ax", [128, NL], F32)
            uu = sb(st0, "ls_u", [128, NL], F32)
            ss_ = sb(st0, "ls_s", [128, NL], F32)
            s2 = sb(st0, "ls_s2", [128, NL], F32)
            pl = sb(st0, "ls_pl", [128, NL], F32)
            lp = sb(st0, "lp", [128, DEPTH * 256], F32)
            S.dma("c_init", lambda e: e.dma_start(out=xx.t[:], in_=dec_in.to_broadcast([128, NL])), writes=[xx.r])
            S.dma("c_init", lambda e: e.dma_start(out=lp.t[:], in_=lam_in.to_broadcast([128, DEPTH * 256])), writes=[lp.r])
            S.barrier()
            S.dve(lambda e: e.tensor_scalar(out=ax.t[:], in0=xx.t[:], scalar1=-1.0, scalar2=None, op0=ALU.mult), reads=[xx.r], writes=[ax.r])
            S.dve(lambda e: e.tensor_tensor(out=ax.t[:], in0=ax.t[:], in1=xx.t[:], op=ALU.max), reads=[xx.r, ax.r], writes=[ax.r])
            S.act(lambda e: e.activation(out=uu.t[:], in_=ax.t[:], func=AF.Exp, scale=-1.0), reads=[ax.r], writes=[uu.r])
            S.dve(lambda e: e.tensor_scalar(out=ss_.t[:], in0=uu.t[:], scalar1=2.0, scalar2=None, op0=ALU.add), reads=[uu.r], writes=[ss_.r])
            S.dve(lambda e: e.reciprocal(out=ss_.t[:], in_=ss_.t[:]), reads=[ss_.r], writes=[ss_.r])
            S.dve(lambda e: e.tensor_tensor(out=ss_.t[:], in0=ss_.t[:], in1=uu.t[:], op=ALU.mult), reads=[ss_.r, uu.r], writes=[ss_.r])
            S.dve(lambda e: e.tensor_tensor(out=s2.t[:], in0=ss_.t[:], in1=ss_.t[:], op=ALU.mult), reads=[ss_.r], writes=[s2.r])
            S.dve(lambda e: e.tensor_scalar(out=pl.t[:], in0=s2.t[:], scalar1=1.0 / 13, scalar2=1.0 / 11, op0=ALU.mult, op1=ALU.add), reads=[s2.r], writes=[pl.r])
            for cc in (1.0 / 9, 1.0 / 7, 1.0 / 5, 1.0 / 3, 1.0):
                S.dve(lambda e: e.tensor_tensor(out=pl.t[:], in0=pl.t[:], in1=s2.t[:], op=ALU.mult), reads=[pl.r, s2.r], writes=[pl.r])
                S.dve(lambda e, cc=cc: e.tensor_scalar(out=pl.t[:], in0=pl.t[:], scalar1=cc, scalar2=None, op0=ALU.add), reads=[pl.r], writes=[pl.r])
            S.dve(lambda e: e.tensor_tensor(out=pl.t[:], in0=pl.t[:], in1=ss_.t[:], op=ALU.mult), reads=[pl.r, ss_.r], writes=[pl.r])
            S.dve(lambda e: e.tensor_scalar(out=ax.t[:], in0=xx.t[:], scalar1=0.0, scalar2=None, op0=ALU.min), reads=[xx.r], writes=[ax.r])
            S.dve(lambda e: e.scalar_tensor_tensor(out=lg.t[:], in0=pl.t[:], scalar=-2.0, in1=ax.t[:], op0=ALU.mult, op1=ALU.add), reads=[pl.r, ax.r], writes=[lg.r])
            lg4 = lg.t[:].rearrange("p (l a h) -> p l a h", l=DEPTH, a=2)
            lgc3 = lgcol.t[:].rearrange("p (l h) -> p l h", l=DEPTH)
            S.dve(lambda e: e.tensor_copy(out=lgc3[0:64], in_=lg4[0:64, :, 0, :]), reads=[lg.r], writes=[lgcol.r])
            S.dve(lambda e: e.tensor_copy(out=lgc3[64:128], in_=lg4[64:128, :, 1, :]), reads=[lg.r], writes=[lgcol.r])
            pr = sb(st0, "lpr", [128, DEPTH * 2 * 64], F32)
            sm = sb(st0, "lsm", [128, DEPTH * 2], F32)
            lp5 = lp.t[:].rearrange("p (l a b d) -> p l a b d", l=DEPTH, a=2, b=2)
            pr4 = pr.t[:].rearrange("p (l a d) -> p l a d", l=DEPTH, a=2)
            S.dve(lambda e: e.tensor_tensor(out=pr4, in0=lp5[:, :, :, 0, :], in1=lp5[:, :, :, 1, :], op=ALU.mult), reads=[lp.r], writes=[pr.r])
            S.dve(lambda e: e.tensor_reduce(out=sm.t[:], in_=pr.t[:].rearrange("p (g d) -> p g d", d=64), axis=AX.X, op=ALU.add), reads=[pr.r], writes=[sm.r])
            S.act(lambda e: e.activation(out=sm.t[:], in_=sm.t[:], func=AF.Exp), reads=[sm.r], writes=[sm.r])
            sm3 = sm.t[:].rearrange("p (l a) -> p l a", a=2)
            S.dve(lambda e: e.tensor_tensor(out=lam.t[:], in0=sm3[:, :, 0], in1=sm3[:, :, 1], op=ALU.subtract), reads=[sm.r], writes=[lam.r])
            for l in range(DEPTH):
                li = 0.8 - 0.6 * math.exp(-0.3 * l)
                S.dve(lambda e, l=l, li=li: e.tensor_scalar(out=lam.t[:, l:l + 1], in0=lam.t[:, l:l + 1], scalar1=li, scalar2=None, op0=ALU.add), reads=[lam.r], writes=[lam.r])
            S.dve(lambda e: e.tensor_scalar(out=neglam.t[:], in0=lam.t[:], scalar1=-1.0, scalar2=None, op0=ALU.mult), reads=[lam.r], writes=[neglam.r])
            S.barrier()

        def rsqrt_act(out_ap, in_ap, scale, rbuf, wbuf):
            S.act(lambda e: e.activation(out=out_ap, in_=in_ap, func=AF.Ln, scale=scale, bias=epsc.t[0:out_ap.shape[0], 0:1]), reads=[rbuf.r, epsc.r], writes=[wbuf.r])
            S.act(lambda e: e.activation(out=out_ap, in_=out_ap, func=AF.Exp, scale=-0.5), reads=[wbuf.r], writes=[wbuf.r])

        class WGroups:
            def __init__(self, t, gn):
                self.t = t
                self.gn = gn
                self.res = {}

            def r(self, k, n0):
                return self.res[(k // 4, n0 // self.gn)]

        def load_w(st, name, src, K, N, key, gn=512, korder=False):
            kc = K // 128
            gn = min(gn, N, 2048)
            uid[0] += 1
            t = st.enter_context(nc.sbuf_tensor(f"sb{uid[0]}_{name}", [128, kc, N], BF16))
            w = WGroups(t, gn)
            srcv = src.rearrange("(c p) n -> p c n", p=128)
            chain = [Res(name + "_chA"), Res(name + "_chB")]
            groups = [(c0, n0) for n0 in range(0, N, gn) for c0 in range(0, kc, 4)]
            if korder:
                groups = [(c0, n0) for c0 in range(0, kc, 4) for n0 in range(0, N, gn)]
            for i, (c0, n0) in enumerate(groups):
                c1 = min(kc, c0 + 4)
                n1 = min(N, n0 + gn)
                rr_ = Res(f"{name}_{c0}_{n0}")
                w.res[(c0 // 4, n0 // gn)] = rr_
                S.dma(key + "AB"[i % 2], lambda e, c0=c0, c1=c1, n0=n0, n1=n1: e.dma_start(out=t[:, c0:c1, n0:n1], in_=srcv[:, c0:c1, n0:n1]),
                      writes=[rr_, chain[i % 2]], eng="pool", cost=8.0)
            return w

        for l in range(DEPTH):
            lam_init = 0.8 - 0.6 * math.exp(-0.3 * l)
            x_src = x_in if l == 0 else xres
            x_dst = y_out if l == DEPTH - 1 else xres

            S.dma("c_lay", lambda e, l=l: e.dma_start(out=ln1b.t[:], in_=ln1_in[l:l + 1, :].to_broadcast([128, D])), writes=[ln1b.r])
            S.dma("c_lay", lambda e, l=l: e.dma_start(out=ln2b.t[:], in_=ln2_in[l:l + 1, :].to_broadcast([128, D])), writes=[ln2b.r])
            S.dma("c_lay", lambda e, l=l: e.dma_start(out=qnw.t[:], in_=qn_in[l:l + 1, :].to_broadcast([128, 64])), writes=[qnw.r])
            S.dma("c_lay", lambda e, l=l: e.dma_start(out=knw.t[:], in_=kn_in[l:l + 1, :].to_broadcast([128, 64])), writes=[knw.r])
            S.dma("c_lay", lambda e, l=l: e.dma_start(out=gnw.t[:], in_=gn_in[l:l + 1, :].to_broadcast([128, 64])), writes=[gnw.r])
            S.dma("c_lay", lambda e, l=l: e.dma_start(out=subw.t[:], in_=sub_in[l:l + 1, :].rearrange("o e -> e o"), allow_slow_non_contiguous=True), writes=[subw.r])
            S.barrier()
            S.dve(lambda e: e.tensor_scalar(out=qnw.t[:], in0=qnw.t[:], scalar1=0.125, scalar2=None, op0=ALU.mult), reads=[qnw.r], writes=[qnw.r])
            S.dve(lambda e, li=lam_init: e.tensor_scalar(out=subw.t[:], in0=subw.t[:], scalar1=1.0 - li, scalar2=None, op0=ALU.mult), reads=[subw.r], writes=[subw.r])
            lgl = lg.t[:, l * 16:(l + 1) * 16]
            S.act(lambda e, lgl=lgl: e.activation(out=qdec.t[:, :, 0], in_=lgl[:, 0:8], func=AF.Exp, scale=idx_p1), reads=[lg.r, cst.r], writes=[qdec.r])
            S.act(lambda e, lgl=lgl: e.activation(out=qdec.t[:, :, 1], in_=lgl[:, 8:16], func=AF.Exp, scale=idx_128m), reads=[lg.r, cst.r], writes=[qdec.r])
            S.act(lambda e, lgl=lgl: e.activation(out=kdec.t[:, :, 0], in_=lgl[:, 0:8], func=AF.Exp, scale=idx_127m), reads=[lg.r, cst.r], writes=[kdec.r])
            S.act(lambda e, lgl=lgl: e.activation(out=kdec.t[:, :, 1], in_=lgl[:, 8:16], func=AF.Exp, scale=idx_p), reads=[lg.r, cst.r], writes=[kdec.r])
            S.dve(lambda e: e.tensor_scalar(out=kdec.t[:], in0=kdec.t[:], scalar1=0.125, scalar2=None, op0=ALU.mult), reads=[kdec.r], writes=[kdec.r])
            S.act(lambda e, l=l: e.activation(out=cdec.t[:], in_=lgcol.t[:, l * 8:(l + 1) * 8], func=AF.Exp, scale=128.0), reads=[lgcol.r], writes=[cdec.r])
            with ExitStack() as stc:
                tmpa = sb(stc, "dt_a", [128, 128], F32)
                tmpb = sb(stc, "dt_b", [128, 128], F32)
                for h in range(8):
                    dst = (DTe if h % 2 == 0 else DTo)
                    dsl = dst.t[:, h // 2, :]
                    S.act(lambda e, h=h: e.activation(out=tmpa.t[:], in_=relF, func=AF.Exp, scale=lgl[:, h:h + 1]), reads=[cst.r, lg.r], writes=[tmpa.r])
                    S.act(lambda e, h=h: e.activation(out=tmpb.t[:], in_=relB, func=AF.Exp, scale=lgl[:, 8 + h:9 + h]), reads=[cst.r, lg.r], writes=[tmpb.r])
                    S.dve(lambda e: e.tensor_tensor(out=tmpa.t[:], in0=tmpa.t[:], in1=mskF, op=ALU.mult), reads=[tmpa.r, cst.r], writes=[tmpa.r])
                    S.dve(lambda e: e.tensor_tensor(out=tmpb.t[:], in0=tmpb.t[:], in1=mskB, op=ALU.mult), reads=[tmpb.r, cst.r], writes=[tmpb.r])
                    S.dve(lambda e, dsl=dsl: e.tensor_tensor(out=dsl, in0=tmpa.t[:], in1=tmpb.t[:], op=ALU.add), reads=[tmpa.r, tmpb.r], writes=[dst.r])
                for r_ in range(NR):
                    S.act(lambda e, r_=r_, l=l: e.activation(out=coef.t[:, r_, :], in_=lgcol.t[:, l * 8:(l + 1) * 8], func=AF.Exp, scale=rkt.t[:, r_:r_ + 1]), reads=[lgcol.r, rkt.r], writes=[coef.r])
                    S.dve(lambda e, r_=r_: e.tensor_scalar(out=coef.t[:, r_, :], in0=coef.t[:, r_, :], scalar1=rkt.t[:, 4 + r_:5 + r_], scalar2=None, op0=ALU.mult), reads=[coef.r, rkt.r], writes=[coef.r])
                S.barrier()

            with ExitStack() as st:
                w_in = load_w(st, "w_in", w_in_in[l], D, INW, "w_in")
                P2 = range(2)
                xt = [sb(st, f"xt{i}", [128, D], F32) for i in P2]
                post = [sb(st, f"post{i}", [128, 80], F32) for i in P2]
                junk = sb(st, "junk", [128, D], BF16)
                ssx = [sb(st, f"ssx{i}", [128, 1], F32) for i in P2]
                hn = [sb(st, f"hn{i}", [128, D], BF16) for i in P2]
                hnT = [sb(st, f"hnT{i}", [128, 8, 128], BF16) for i in P2]
                qraw = [[sb(st, f"qraw{w}{i}", [128, 512], F32) for i in P2] for w in P2]
                qsq = [sb(st, f"qsq{w}", [128, 512], F32) for w in P2]
                ssg = [[sb(st, f"ssg{w}{i}", [128, 8], F32) for i in P2] for w in P2]
                qu = [sb(st, f"qu{w}", [128, 512], F32) for w in P2]
                w16 = [sb(st, f"w16{w}", [128, 8, 16], F32) for w in P2]
                rt = [[sb(st, f"rt{w}{i}", [128, 8, 8], F32) for i in range(4)] for w in P2]
                qb = [[sb(st, f"qb{w}{i}", [128, 512], BF16) for i in P2] for w in P2]
                rqf = [sb(st, f"rqf{w}", [128, 512], F32) for w in P2]
                rta = [sb(st, f"rta{w}", [128, 8, 32], F32) for w in P2]
                rtb = [sb(st, f"rtb{w}", [128, 8, 32], F32) for w in P2]
                rqb = [[sb(st, f"rqb{w}{i}", [128, 512], BF16) for i in P2] for w in P2]
                qd = [sb(st, f"qd{i}", [128, 1024], BF16) for i in P2]
                esg = [sb(st, f"esg{i}", [128, 512], F32) for i in P2]
                stg_qT = [sb(st, f"sg_qT{i}", [128, 4, 512], BF16) for i in P2]
                stg_kT = [sb(st, f"sg_kT{i}", [128, 4, 512], BF16) for i in P2]
                stg_rqT = [sb(st, f"sg_rqT{i}", [128, 4, 512], BF16) for i in P2]
                stg_rkT = [sb(st, f"sg_rkT{i}", [128, 4, 512], BF16) for i in P2]
                stg_qdT = [sb(st, f"sg_qdT{i}", [128, 8, 512], BF16) for i in P2]
                tv = [sb(st, f"tv{i}", [128, 512], BF16) for i in P2]
                trv = [sb(st, f"trv{i}", [128, 512], BF16) for i in P2]
                tgs = [sb(st, f"tgs{i}", [128, 512], BF16) for i in P2]
                tkd = [sb(st, f"tkd{i}", [128, 1024], BF16) for i in P2]
                pT = psb(st, "pT", [128, 8, 128], BF16)
                pT2 = [psb(st, f"pT2_{i}", [128, 8, 128], BF16) for i in P2]
                pj = [psb(st, f"pj{i}", [128, 512], F32) for i in range(5)]
                pjn = [0]

                def proj(cb, hnT_):
                    b = pj[pjn[0] % 5]
                    pjn[0] += 1
                    for k in range(8):
                        S.pe(lambda e, k=k, b=b, cb=cb: e.matmul(b.t[:], lhsT=hnT_.t[:, k, :], rhs=w_in.t[:, k, cb * 512:(cb + 1) * 512], start=(k == 0), stop=(k == 7)),
                             reads=[hnT_.r, w_in.r(k, cb * 512)], writes=[b.r])
                    return b

                def transpose_to(src, nblk, dst_stage, dst_ap_fn, pbuf):
                    for c in range(nblk):
                        S.pe(lambda e, c=c: e.transpose(out=pbuf.t[:, c, :], in_=src.t[:, c * 128:(c + 1) * 128], identity=ident.t[:]),
                             reads=[src.r, ident.r], writes=[pbuf.r])
                    S.act(lambda e: e.activation(out=dst_ap_fn(), in_=pbuf.t[:, 0:nblk, :], func=AF.Copy), reads=[pbuf.r], writes=[dst_stage.r])

                for t in range(NT):
                    sti = t // 4
                    j = t % 4
                    sl = sti % 2
                    par = t % 2
                    tok0 = t * 128
                    is_s = tok0 < TS
                    tl = tok0 if is_s else tok0 - TS
                    xb = xt[par]
                    pb = post[par]
                    hn_, hnT_, ssx_ = hn[par], hnT[par], ssx[par]
                    S.dma(f"a_x{par}", lambda e, xb=xb, tok0=tok0: e.dma_start(out=xb.t[:], in_=x_src[tok0:tok0 + 128, :]), writes=[xb.r])
                    S.dma(f"a_pos{par}", lambda e, pb=pb, tok0=tok0: e.dma_start(out=pb.t[:], in_=pos_in[tok0:tok0 + 128, :]), writes=[pb.r])
                    S.act(lambda e, xb=xb, ssx_=ssx_: e.activation(out=junk.t[:], in_=xb.t[:], func=AF.Square, accum_out=ssx_.t[:]), reads=[xb.r], writes=[junk.r, ssx_.r])
                    rsqrt_act(ssx_.t[:], ssx_.t[:], 1.0 / D, ssx_, ssx_)
                    S.dve(lambda e, xb=xb, hn_=hn_, ssx_=ssx_: e.scalar_tensor_tensor(out=hn_.t[:], in0=xb.t[:], scalar=ssx_.t[:, 0:1], in1=ln1b.t[:], op0=ALU.mult, op1=ALU.mult),
                          reads=[xb.r, ssx_.r, ln1b.r], writes=[hn_.r])
                    for c in range(8):
                        S.pe(lambda e, c=c, hn_=hn_: e.transpose(out=pT.t[:, c, :], in_=hn_.t[:, c * 128:(c + 1) * 128], identity=ident.t[:]), reads=[hn_.r, ident.r], writes=[pT.r])
                    S.act(lambda e, hnT_=hnT_: e.activation(out=hnT_.t[:], in_=pT.t[:], func=AF.Copy), reads=[pT.r], writes=[hnT_.r], cost=1.1)
                    cosd = pb.t[:, 0:8].unsqueeze(1).to_broadcast([128, 8, 8])
                    sind = pb.t[:, 8:16].unsqueeze(1).to_broadcast([128, 8, 8])
                    cosr = pb.t[:, 16:48].unsqueeze(1).to_broadcast([128, 8, 32])
                    sinr = pb.t[:, 48:80].unsqueeze(1).to_broadcast([128, 8, 32])
                    for which in range(2):
                        b = proj(which, hnT_)
                        nw = qnw if which == 0 else knw
                        stg = (stg_qT if which == 0 else stg_kT)[sl]
                        qraw_, qsq_, ssg_, qu_, w16_, rt_, qb_ = qraw[which][par], qsq[which], ssg[which][par], qu[which], w16[which], rt[which], qb[which][par]
                        S.act(lambda e, b=b, qraw_=qraw_: e.activation(out=qraw_.t[:], in_=b.t[:], func=AF.Copy), reads=[b.r], writes=[qraw_.r])
                        S.act(lambda e, b=b, qsq_=qsq_: e.activation(out=qsq_.t[:], in_=b.t[:], func=AF.Square), reads=[b.r], writes=[qsq_.r])
                        S.dve(lambda e, qsq_=qsq_, ssg_=ssg_: e.tensor_reduce(out=ssg_.t[:], in_=qsq_.t[:].rearrange("p (g d) -> p g d", d=64), axis=AX.X, op=ALU.add), reads=[qsq_.r], writes=[ssg_.r])
                        rsqrt_act(ssg_.t[:], ssg_.t[:], 1.0 / 64, ssg_, ssg_)
                        qr3 = qraw_.t[:].rearrange("p (g d) -> p g d", d=64)
                        qu3 = qu_.t[:].rearrange("p (g d) -> p g d", d=64)
                        qb3 = qb_.t[:].rearrange("p (g d) -> p g d", d=64)
                        S.dve(lambda e, qr3=qr3, qu3=qu3, ssg_=ssg_: e.tensor_tensor(out=qu3, in0=qr3, in1=ssg_.t[:].unsqueeze(2).to_broadcast([128, 8, 64]), op=ALU.mult), reads=[qraw_.r, ssg_.r], writes=[qu_.r])
                        S.dve(lambda e, qu3=qu3, qb3=qb3, nw=nw: e.tensor_tensor(out=qb3, in0=qu3, in1=nw.t[:].unsqueeze(1).to_broadcast([128, 8, 64]), op=ALU.mult), reads=[qu_.r, nw.r], writes=[qb_.r])
                        S.dve(lambda e, qu3=qu3, nw=nw, w16_=w16_: e.tensor_tensor(out=w16_.t[:], in0=qu3[:, :, 0:16], in1=nw.t[:, 0:16].unsqueeze(1).to_broadcast([128, 8, 16]), op=ALU.mult), reads=[qu_.r, nw.r], writes=[w16_.r], cost=0.15)
                        x1 = w16_.t[:, :, 0:8]
                        x2 = w16_.t[:, :, 8:16]
                        S.dve(lambda e, x1=x1, cosd=cosd, rt_=rt_: e.tensor_tensor(out=rt_[0].t[:], in0=x1, in1=cosd, op=ALU.mult), reads=[w16_.r, pb.r], writes=[rt_[0].r], cost=0.12)
                        S.dve(lambda e, x2=x2, sind=sind, rt_=rt_: e.tensor_tensor(out=rt_[1].t[:], in0=x2, in1=sind, op=ALU.mult), reads=[w16_.r, pb.r], writes=[rt_[1].r], cost=0.12)
                        S.dve(lambda e, x1=x1, sind=sind, rt_=rt_: e.tensor_tensor(out=rt_[2].t[:], in0=x1, in1=sind, op=ALU.mult), reads=[w16_.r, pb.r], writes=[rt_[2].r], cost=0.12)
                        S.dve(lambda e, x2=x2, cosd=cosd, rt_=rt_: e.tensor_tensor(out=rt_[3].t[:], in0=x2, in1=cosd, op=ALU.mult), reads=[w16_.r, pb.r], writes=[rt_[3].r], cost=0.12)
                        S.dve(lambda e, qb3=qb3, rt_=rt_: e.tensor_tensor(out=qb3[:, :, 0:8], in0=rt_[0].t[:], in1=rt_[1].t[:], op=ALU.subtract), reads=[rt_[0].r, rt_[1].r, qb_.r], writes=[qb_.r], cost=0.12)
                        S.dve(lambda e, qb3=qb3, rt_=rt_: e.tensor_tensor(out=qb3[:, :, 8:16], in0=rt_[2].t[:], in1=rt_[3].t[:], op=ALU.add), reads=[rt_[2].r, rt_[3].r, qb_.r], writes=[qb_.r], cost=0.12)
                        transpose_to(qb_, 4, stg, lambda stg=stg, j=j: stg.t[:, :, j * 128:(j + 1) * 128], pT2[0])
                    b = proj(2, hnT_)
                    tv_ = tv[par]
                    S.act(lambda e, b=b, tv_=tv_: e.activation(out=tv_.t[:], in_=b.t[:], func=AF.Copy), reads=[b.r], writes=[tv_.r])
                    if is_s:
                        S.dma(f"s_v{par}", lambda e, tv_=tv_, tl=tl: e.dma_start(out=dv_s[tl:tl + 128, :], in_=tv_.t[:]), reads=[tv_.r])
                    else:
                        S.dma(f"s_v{par}", lambda e, tv_=tv_, tl=tl: e.dma_start(out=dv_l[tl // TPS][tl % TPS:tl % TPS + 128, :], in_=tv_.t[:]), reads=[tv_.r])
                    for which in range(2):
                        b = proj(3 + which, hnT_)
                        rqf_, rta_, rtb_, rqb_ = rqf[which], rta[which], rtb[which], rqb[which][par]
                        b3 = b.t[:].rearrange("p (g d) -> p g d", d=64)
                        rq3 = rqf_.t[:].rearrange("p (g d) -> p g d", d=64)
                        S.dve(lambda e, b3=b3, cosr=cosr, rta_=rta_: e.tensor_tensor(out=rta_.t[:], in0=b3[:, :, 0:32], in1=cosr, op=ALU.mult), reads=[b.r, pb.r], writes=[rta_.r])
                        S.dve(lambda e, b3=b3, sinr=sinr, rtb_=rtb_: e.tensor_tensor(out=rtb_.t[:], in0=b3[:, :, 32:64], in1=sinr, op=ALU.mult), reads=[b.r, pb.r], writes=[rtb_.r])
                        S.dve(lambda e, rq3=rq3, rta_=rta_, rtb_=rtb_: e.tensor_tensor(out=rq3[:, :, 0:32], in0=rta_.t[:], in1=rtb_.t[:], op=ALU.subtract), reads=[rta_.r, rtb_.r], writes=[rqf_.r])
                        S.dve(lambda e, b3=b3, sinr=sinr, rta_=rta_: e.tensor_tensor(out=rta_.t[:], in0=b3[:, :, 0:32], in1=sinr, op=ALU.mult), reads=[b.r, pb.r], writes=[rta_.r])
                        S.dve(lambda e, b3=b3, cosr=cosr, rtb_=rtb_: e.tensor_tensor(out=rtb_.t[:], in0=b3[:, :, 32:64], in1=cosr, op=ALU.mult), reads=[b.r, pb.r], writes=[rtb_.r])
                        S.dve(lambda e, rq3=rq3, rta_=rta_, rtb_=rtb_: e.tensor_tensor(out=rq3[:, :, 32:64], in0=rta_.t[:], in1=rtb_.t[:], op=ALU.add), reads=[rta_.r, rtb_.r, rqf_.r], writes=[rqf_.r])
                        S.act(lambda e, rqb_=rqb_, rqf_=rqf_: e.activation(out=rqb_.t[:], in_=rqf_.t[:], func=AF.Copy), reads=[rqf_.r], writes=[rqb_.r], cost=0.6)
                        dec = qdec if which == 0 else kdec
                        rq4 = rqf_.t[:].rearrange("p (g d) -> p g d", d=64).unsqueeze(2).to_broadcast([128, 8, 2, 64])
                        dc4 = dec.t[:].unsqueeze(3).to_broadcast([128, 8, 2, 64])
                        if which == 0:
                            qd_ = qd[par]
                            S.pool(lambda e, rq4=rq4, dc4=dc4, qd_=qd_: e.tensor_tensor(out=qd_.t[:].rearrange("p (g a d) -> p g a d", g=8, a=2), in0=rq4, in1=dc4, op=ALU.mult),
                                  reads=[rqf_.r, dec.r], writes=[qd_.r], cost=2.5)
                            transpose_to(rqb_, 4, stg_rqT[sl], lambda sl=sl, j=j: stg_rqT[sl].t[:, :, j * 128:(j + 1) * 128], pT2[1])
                            transpose_to(qd_, 8, stg_qdT[sl], lambda sl=sl, j=j: stg_qdT[sl].t[:, :, j * 128:(j + 1) * 128], pT2[0])
                        else:
                            tkd_ = tkd[par]
                            S.pool(lambda e, rq4=rq4, dc4=dc4, tkd_=tkd_: e.tensor_tensor(out=tkd_.t[:].rearrange("p (g a d) -> p g a d", g=8, a=2), in0=rq4, in1=dc4, op=ALU.mult),
                                  reads=[rqf_.r, dec.r], writes=[tkd_.r], cost=2.5)
                            S.dma(f"s_kd{par}", lambda e, tkd_=tkd_, tok0=tok0: e.dma_start(out=rkd[tok0:tok0 + 128, :], in_=tkd_.t[:]), reads=[tkd_.r])
                            transpose_to(rqb_, 4, stg_rkT[sl], lambda sl=sl, j=j: stg_rkT[sl].t[:, :, j * 128:(j + 1) * 128], pT2[1])
                    b = proj(5, hnT_)
                    trv_ = trv[par]
                    S.act(lambda e, b=b, trv_=trv_: e.activation(out=trv_.t[:], in_=b.t[:], func=AF.Copy), reads=[b.r], writes=[trv_.r])
                    S.dma(f"s_rv{par}", lambda e, trv_=trv_, tok0=tok0: e.dma_start(out=rv[tok0:tok0 + 128, :], in_=trv_.t[:]), reads=[trv_.r])
                    b = proj(6, hnT_)
                    esg_, tgs_ = esg[par], tgs[par]
                    S.act(lambda e, b=b, esg_=esg_: e.activation(out=esg_.t[:], in_=b.t[:], func=AF.Exp, scale=-1.0), reads=[b.r], writes=[esg_.r])
                    S.act(lambda e, esg_=esg_: e.activation(out=esg_.t[:], in_=esg_.t[:], func=AF.Ln, bias=ones_f.t[:, 0:1]), reads=[esg_.r, ones_f.r], writes=[esg_.r], cost=0.6)
                    S.act(lambda e, esg_=esg_: e.activation(out=esg_.t[:], in_=esg_.t[:], func=AF.Exp, scale=-1.0), reads=[esg_.r], writes=[esg_.r], cost=0.6)
                    S.dve(lambda e, b=b, esg_=esg_, tgs_=tgs_: e.tensor_tensor(out=tgs_.t[:], in0=b.t[:], in1=esg_.t[:], op=ALU.mult), reads=[b.r, esg_.r], writes=[tgs_.r])
                    S.dma(f"s_gs{par}", lambda e, tgs_=tgs_, tok0=tok0: e.dma_start(out=rgs[tok0:tok0 + 128, :], in_=tgs_.t[:]), reads=[tgs_.r])
                    if j == 3:
                        c0 = sti * 512
                        cl = c0 if is_s else c0 - TS
                        S.dma(f"s_qT{sl}", lambda e, sl=sl, c0=c0: e.dma_start(out=dqT.rearrange("(h f) t -> f h t", f=128)[:, :, c0:c0 + 512], in_=stg_qT[sl].t[:]), reads=[stg_qT[sl].r])
                        if is_s:
                            S.dma(f"s_kT{sl}", lambda e, sl=sl, cl=cl: e.dma_start(out=dkT_s.rearrange("(h f) t -> f h t", f=128)[:, :, cl:cl + 512], in_=stg_kT[sl].t[:]), reads=[stg_kT[sl].r])
                        else:
                            for a in range(NSPL):
                                S.dma(f"s_kT{sl}", lambda e, sl=sl, cl=cl, a=a: e.dma_start(out=dkT_l[a].rearrange("(h f) t -> f h t", f=128)[:, :, cl:cl + 512], in_=stg_kT[sl].t[:, a * HPS:(a + 1) * HPS, :]), reads=[stg_kT[sl].r])
                        S.dma(f"s_rqT{sl}", lambda e, sl=sl, c0=c0: e.dma_start(out=rqT.rearrange("(h f) t -> f h t", f=128)[:, :, c0:c0 + 512], in_=stg_rqT[sl].t[:]), reads=[stg_rqT[sl].r])
                        S.dma(f"s_rkT{sl}", lambda e, sl=sl, c0=c0: e.dma_start(out=rkT.rearrange("(h f) t -> f h t", f=128)[:, :, c0:c0 + 512], in_=stg_rkT[sl].t[:]), reads=[stg_rkT[sl].r])
                        S.dma(f"s_qdT{sl}", lambda e, sl=sl, c0=c0: e.dma_start(out=rqdT.rearrange("(h f) t -> f h t", f=128)[:, :, c0:c0 + 512], in_=stg_qdT[sl].t[:]), reads=[stg_qdT[sl].r])
                S.barrier()

            RG = [[0, 1, 2, 3], [4, 5, 6, 7]]
            Rkg = Res("dkT_g")
            Rvg = Res("dv_g")
            for a in range(NSPL):
                S.dma("cc_k", lambda e, a=a: e.collective_compute("AllGather", ALU.bypass, replica_groups=RG, ins=[dkT_l[a]], outs=[dkT_g[a]]), writes=[Rkg], eng="pool", inc=1, cost=60.0)
                S.dma("cc_v", lambda e, a=a: e.collective_compute("AllGather", ALU.bypass, replica_groups=RG, ins=[dv_l[a]], outs=[dv_g[a]]), writes=[Rvg], eng="pool", inc=1, cost=60.0)

            with ExitStack() as st:
                SKMAX = max(TS, SKP)
                kTh = [sb(st, f"kTh{i}", [128, SKMAX], BF16) for i in range(2)]
                vh = [sb(st, f"vh{i}", [128, SKMAX // 128, 128], BF16) for i in range(2)]
                qA = [sb(st, f"qA{i}", [128, max(TS, TP)], BF16) for i in range(2)]
                qB = [sb(st, f"qB{i}", [128, max(TS, TP)], BF16) for i in range(2)]
                for i in range(2):
                    S.pool(lambda e, i=i: e.memset(qA[i].t[64:128, :], 0.0), writes=[qA[i].r])
                    S.pool(lambda e, i=i: e.memset(qB[i].t[0:64, :], 0.0), writes=[qB[i].r])
                NPX = 6
                pexp2 = [sb(st, f"pexp{i}", [128, 2, 512], BF16) for i in range(NPX)]
                s01 = [[sb(st, f"s01_{c}{i}", [128, 512], BF16) for i in range(2)] for c in range(2)]
                s23 = [[sb(st, f"s23_{c}{i}", [128, 512], BF16) for i in range(2)] for c in range(2)]
                s4 = [[sb(st, f"s4_{c}{i}", [128, 512], BF16) for i in range(2)] for c in range(2)]
                gcount = 0
                r0 = sb(st, "r0", [128, 512], F32)
                r1 = sb(st, "r1", [128, 512], F32)
                a0 = sb(st, "a0", [128, 512], F32)
                a1 = sb(st, "a1", [128, 512], F32)
                osq = sb(st, "osq", [128, 512], F32)
                rsn = sb(st, "rsn", [128, 512], F32)
                aout = [sb(st, f"aout{i}", [128, 512], BF16) for i in range(2)]
                sbk2 = [psb(st, f"sbk{i}", [128, 2, 512], F32) for i in range(2)]
                O = [psb(st, f"Oacc{i}", [128, 512], F32) for i in range(2)]
                L = [psb(st, f"Lacc{i}", [128, 512], F32) for i in range(2)]
                heads = [(job, h) for job in range(2) for h in range(4)]

                def load_head(hi):
                    job, h = heads[hi]
                    hb = hi % 2
                    kb_, vb_ = kTh[hb], vh[hb]
                    Tq = TS if job == 0 else TP
                    qoff = 0 if job == 0 else TS
                    if job == 0:
                        S.dma(f"b_k{hb}", lambda e, kb_=kb_, h=h: e.dma_start(out=kb_.t[:, 0:TS], in_=dkT_s[h * 128:(h + 1) * 128, :]), writes=[kb_.r])
                        S.dma(f"b_v{hb}", lambda e, vb_=vb_, h=h: e.dma_start(out=vb_.t[:, 0:TS // 128, :], in_=dv_s[:, h * 128:(h + 1) * 128].rearrange("(k p) e -> p k e", p=128)), writes=[vb_.r])
                    else:
                        ha = h // HPS
                        hl = h % HPS
                        S.dma(f"b_k{hb}", lambda e, kb_=kb_, ha=ha, hl=hl: e.dma_start(out=kb_.t[:, 0:SKP].rearrange("p (r t) -> p r t", r=NR), in_=dkT_g[ha].rearrange("(r f) t -> f r t", f=HPS * 128)[hl * 128:(hl + 1) * 128, :, :]), reads=[Rkg], writes=[kb_.r])
                        for a in range(NSPL):
                            for r_ in range(NR):
                                S.dma(f"b_v{hb}", lambda e, vb_=vb_, h=h, a=a, r_=r_: e.dma_start(
                                    out=vb_.t[:, (r_ * TP + a * TPS) // 128:(r_ * TP + (a + 1) * TPS) // 128, :],
                                    in_=dv_g[a][r_ * TPS:(r_ + 1) * TPS, h * 128:(h + 1) * 128].rearrange("(k p) e -> p k e", p=128)), reads=[Rvg], writes=[vb_.r])
                    S.dma(f"b_q{hb}", lambda e, hb=hb, h=h, qoff=qoff, Tq=Tq: e.dma_start(out=qA[hb].t[0:64, 0:Tq], in_=dqT[h * 128:h * 128 + 64, qoff:qoff + Tq]), writes=[qA[hb].r])
                    S.dma(f"b_r{hb}", lambda e, hb=hb, h=h, qoff=qoff, Tq=Tq: e.dma_start(out=qB[hb].t[64:128, 0:Tq], in_=dqT[h * 128 + 64:h * 128 + 128, qoff:qoff + Tq]), writes=[qB[hb].r])

                units = []
                for hi, (job, h) in enumerate(heads):
                    Tq = TS if job == 0 else TP
                    Sk = TS if job == 0 else SKP
                    for qc in range(Tq // 512):
                        for kb in range(Sk // 128):
                            units.append((hi, qc, kb, Sk // 128))

                def emit_qk(u):
                    hi, qc, kb, nkb = units[u]
                    hb = hi % 2
                    kb_ = kTh[hb]
                    sbuf_ = sbk2[u % 2]
                    for c in range(2):
                        qb_ = (qA if c == 0 else qB)[hb]
                        S.pe(lambda e, sbuf_=sbuf_, c=c, kb=kb, qc=qc, kb_=kb_, qb_=qb_: e.matmul(sbuf_.t[:, c, :], lhsT=kb_.t[:, kb * 128:(kb + 1) * 128],
                                                                                             rhs=qb_.t[:, qc * 512:(qc + 1) * 512], start=True, stop=True),
                             reads=[kb_.r, qb_.r], writes=[sbuf_.r])

                ocount = 0
                load_head(0)
                emit_qk(0)
                emit_qk(1)
                for u in range(len(units)):
                    hi, qc, kb, nkb = units[u]
                    job, h = heads[hi]
                    hb = hi % 2
                    vb_ = vh[hb]
                    qoff = 0 if job == 0 else TS
                    if qc == 0 and kb == 0 and hi + 1 < len(heads):
                        load_head(hi + 1)
                    pe2 = pexp2[u % NPX]
                    s2_ = sbk2[u % 2]
                    S.act(lambda e, pe2=pe2, s2_=s2_: e.activation(out=pe2.t[:], in_=s2_.t[:], func=AF.Exp), reads=[s2_.r], writes=[pe2.r], cost=1.05)
                    if u + 2 < len(units):
                        if units[u + 2][0] != hi and units[u + 2][1] == 0 and units[u + 2][2] == 0 and units[u + 2][0] + 1 < len(heads):
                            pass
                        emit_qk(u + 2)
                    gp = gcount % 2
                    for c in range(2):
                        pe_ = pexp2[u % NPX]
                        S.pe(lambda e, pe_=pe_, c=c, kb=kb, vb_=vb_, nkb=nkb: e.matmul(O[c].t[:], lhsT=vb_.t[:, kb, :], rhs=pe_.t[:, c, :], start=(kb == 0), stop=(kb == nkb - 1)),
                             reads=[vb_.r, pe_.r], writes=[O[c].r])
                        if kb % 2 == 1:
                            pp_ = pexp2[(u - 1) % NPX]
                            dst = (s01 if kb % 4 == 1 else s23)[c][gp]
                            S.dve(lambda e, dst=dst, pp_=pp_, pe_=pe_, c=c: e.tensor_tensor(out=dst.t[:], in0=pp_.t[:, c, :], in1=pe_.t[:, c, :], op=ALU.add), reads=[pp_.r, pe_.r], writes=[dst.r], cost=0.3)
                        if kb % 4 == 3:
                            a_, b_, d_ = s01[c][gp], s23[c][gp], s4[c][gp]
                            S.dve(lambda e, a_=a_, b_=b_, d_=d_: e.tensor_tensor(out=d_.t[:], in0=a_.t[:], in1=b_.t[:], op=ALU.add), reads=[a_.r, b_.r], writes=[d_.r], cost=0.3)
                            S.pe(lambda e, d_=d_, c=c, kb=kb, nkb=nkb: e.matmul(L[c].t[:], lhsT=ones_b.t[:], rhs=d_.t[:], start=(kb == 3), stop=(kb == nkb - 1)),
                                 reads=[ones_b.r, d_.r], writes=[L[c].r])
                    if kb % 4 == 3:
                        gcount += 1
                    if kb != nkb - 1:
                        continue
                    S.act(lambda e: e.activation(out=a0.t[:], in_=O[0].t[:], func=AF.Copy), reads=[O[0].r], writes=[a0.r])
                    S.dve(lambda e: e.tensor_copy(out=a1.t[:], in_=O[1].t[:]), reads=[O[1].r], writes=[a1.r])
                    S.dve(lambda e: e.reciprocal(out=r0.t[:], in_=L[0].t[:]), reads=[L[0].r], writes=[r0.r])
                    S.dve(lambda e: e.reciprocal(out=r1.t[:], in_=L[1].t[:]), reads=[L[1].r], writes=[r1.r])
                    S.dve(lambda e: e.tensor_tensor(out=a0.t[:], in0=a0.t[:], in1=r0.t[:], op=ALU.mult), reads=[a0.r, r0.r], writes=[a0.r])
                    S.dve(lambda e: e.tensor_tensor(out=a1.t[:], in0=a1.t[:], in1=r1.t[:], op=ALU.mult), reads=[a1.r, r1.r], writes=[a1.r])
                    S.dve(lambda e, l=l: e.scalar_tensor_tensor(out=a0.t[:], in0=a1.t[:], scalar=neglam.t[:, l:l + 1], in1=a0.t[:], op0=ALU.mult, op1=ALU.add),
                          reads=[a1.r, a0.r, neglam.r], writes=[a0.r])
                    S.act(lambda e: e.activation(out=osq.t[:], in_=a0.t[:], func=AF.Square), reads=[a0.r], writes=[osq.r])
                    sB = L[0]
                    S.pe(lambda e, sB=sB: e.matmul(sB.t[:], lhsT=ones_f.t[:], rhs=osq.t[:], start=True, stop=True), reads=[ones_f.r, osq.r], writes=[sB.r])
                    rsqrt_act(rsn.t[:], sB.t[:], 1.0 / 128, sB, rsn)
                    ao = aout[ocount % 2]
                    S.dve(lambda e, ao=ao: e.scalar_tensor_tensor(out=ao.t[:], in0=a0.t[:], scalar=subw.t[:, 0:1], in1=rsn.t[:], op0=ALU.mult, op1=ALU.mult),
                          reads=[a0.r, subw.r, rsn.r], writes=[ao.r])
                    tcol = qoff + qc * 512
                    S.dma(f"b_o{ocount % 2}", lambda e, ao=ao, h=h, tcol=tcol: e.dma_start(out=aT[h * 128:(h + 1) * 128, tcol:tcol + 512], in_=ao.t[:]), reads=[ao.r], eng="act")
                    ocount += 1
                S.barrier()

            with ExitStack() as st:
                SallJ = [sb(st, "Sall0", [128, TS // 128, 512], BF16), sb(st, "Sall1", [128, TP // 128, 512], BF16)]
                sttJ = [sb(st, f"stt{i}", [128, 512], F32) for i in range(2)]
                sttmpJ = [sb(st, f"sttmp{i}", [128, 512], F32) for i in range(2)]
                tg = sb(st, "tg", [128, NR, 512], F32)
                kdl = [sb(st, f"kdl{i}", [128, 4, 1024], BF16) for i in range(4)]
                rvl = [sb(st, f"rvl{i}", [128, 4, 512], BF16) for i in range(4)]
                okT = [sb(st, f"okT{i}", [128, 4, 512], BF16) for i in range(2)]
                oqT = [sb(st, f"oqT{i}", [128, 4, 512], BF16) for i in range(2)]
                oqd = [sb(st, f"oqd{i}", [128, 8, 512], BF16) for i in range(2)]
                orv = [sb(st, f"orv{i}", [128, 4, 512], BF16) for i in range(2)]
                ogs = [sb(st, f"ogs{i}", [128, 4, 512], BF16) for i in range(2)]
                PT = [sb(st, f"PT{i}", [128, 8, 128], BF16) for i in range(2)]
                rsq = sb(st, "rsq", [128, 512], F32)
                rss = sb(st, "rss", [128, 8], F32)
                rn = sb(st, "rn", [128, 512], F32)
                rr = sb(st, "rr", [128, 512], BF16)
                rTst = [sb(st, f"rTst{i}", [128, 4, 512], BF16) for i in range(2)]
                pkvJ = [psb(st, f"pkv{i}", [128, 512], F32) for i in range(2)]
                psc = [psb(st, f"psc{i}", [128, 4, 128], F32) for i in range(2)]
                po = [psb(st, f"pro{i}", [128, 512], F32) for i in range(2)]
                ptr = psb(st, "ptr", [128, 8, 128], BF16)
                Rtg = Res("st_g")
                ldn = [0]

                def sweep(job, use_init):
                    stt, sttmp, Sall = sttJ[job], sttmpJ[job], SallJ[job]
                    Tq = TS if job == 0 else TP
                    qoff = 0 if job == 0 else TS
                    n = Tq // 128
                    nsc = n // 4
                    if use_init:
                        S.dma("r_tg", lambda e: e.dma_start(out=tg.t[:], in_=st_g.rearrange("(r p) c -> p r c", p=128)), reads=[Rtg], writes=[tg.r])
                        for r_ in range(NR):
                            cb = coef.t[:, r_, :].unsqueeze(2).to_broadcast([128, 8, 64])
                            tg3 = tg.t[:, r_, :].rearrange("p (h e) -> p h e", e=64)
                            if r_ == 0:
                                S.dve(lambda e, cb=cb, tg3=tg3: e.tensor_tensor(out=stt.t[:].rearrange("p (h e) -> p h e", e=64), in0=tg3, in1=cb, op=ALU.mult), reads=[tg.r, coef.r], writes=[stt.r])
                            else:
                                S.dve(lambda e, cb=cb, tg3=tg3: e.tensor_tensor(out=sttmp.t[:].rearrange("p (h e) -> p h e", e=64), in0=tg3, in1=cb, op=ALU.mult), reads=[tg.r, coef.r], writes=[sttmp.r])
                                S.dve(lambda e: e.tensor_tensor(out=stt.t[:], in0=stt.t[:], in1=sttmp.t[:], op=ALU.add), reads=[stt.r, sttmp.r], writes=[stt.r])
                    else:
                        S.dve(lambda e: e.memset(stt.t[:], 0.0), writes=[stt.r])
                    cur = {}
                    for t in range(n):
                        tf = t
                        tb = n - 1 - t
                        bufs = {}
                        for nm, ti in (("f", tf), ("b", tb)):
                            sc = ti // 4
                            if (nm, sc) not in cur:
                                slot = job * 2 + (0 if nm == "f" else 1)
                                c0 = qoff + sc * 512
                                S.dma(f"r_kd{slot}", lambda e, slot=slot, c0=c0: e.dma_start(out=kdl[slot].t[:], in_=rkd[c0:c0 + 512, :].rearrange("(j p) c -> p j c", p=128)), writes=[kdl[slot].r])
                                S.dma(f"r_rv{slot}", lambda e, slot=slot, c0=c0: e.dma_start(out=rvl[slot].t[:], in_=rv[c0:c0 + 512, :].rearrange("(j p) c -> p j c", p=128)), writes=[rvl[slot].r])
                                cur = {k: v for k, v in cur.items() if k[0] != nm}
                                cur[(nm, sc)] = slot
                            bufs[nm] = (cur[(nm, sc)], ti % 4)
                        S.act(lambda e, tf=tf: e.activation(out=Sall.t[0:64, tf, :], in_=stt.t[0:64, :], func=AF.Copy), reads=[stt.r], writes=[Sall.r])
                        S.act(lambda e, tb=tb: e.activation(out=Sall.t[64:128, tb, :], in_=stt.t[64:128, :], func=AF.Copy), reads=[stt.r], writes=[Sall.r])
                        pk = pkvJ[job]
                        (sf, jf), (sb_, jb) = bufs["f"], bufs["b"]
                        for h in range(8):
                            kf = kdl[sf].t[:, jf, :].rearrange("p (g a d) -> p g a d", g=8, a=2)
                            kbk = kdl[sb_].t[:, jb, :].rearrange("p (g a d) -> p g a d", g=8, a=2)
                            S.pe(lambda e, pk=pk, h=h, kf=kf, sf=sf, jf=jf: e.matmul(pk.t[0:64, h * 64:(h + 1) * 64], lhsT=kf[:, h, 0, :], rhs=rvl[sf].t[:, jf, h * 64:(h + 1) * 64], start=True, stop=True),
                                 reads=[kdl[sf].r, rvl[sf].r], writes=[pk.r])
                            S.pe(lambda e, pk=pk, h=h, kbk=kbk, sb_=sb_, jb=jb: e.matmul(pk.t[64:128, h * 64:(h + 1) * 64], lhsT=kbk[:, h, 1, :], rhs=rvl[sb_].t[:, jb, h * 64:(h + 1) * 64], start=True, stop=True),
                                 reads=[kdl[sb_].r, rvl[sb_].r], writes=[pk.r])
                        S.dve(lambda e: e.tensor_tensor(out=sttmp.t[:].rearrange("p (h e) -> p h e", e=64), in0=stt.t[:].rearrange("p (h e) -> p h e", e=64),
                                                        in1=cdec.t[:].unsqueeze(2).to_broadcast([128, 8, 64]), op=ALU.mult), reads=[stt.r, cdec.r], writes=[sttmp.r])
                        S.dve(lambda e, pk=pk: e.tensor_tensor(out=stt.t[:], in0=pk.t[:], in1=sttmp.t[:], op=ALU.add), reads=[pk.r, sttmp.r], writes=[stt.r])

                def outputs(job):
                    Sall = SallJ[job]
                    Tq = TS if job == 0 else TP
                    qoff = 0 if job == 0 else TS
                    n = Tq // 128
                    for sc in range(n // 4):
                        sl = sc % 2
                        c0 = qoff + sc * 512
                        S.dma(f"o_kT{sl}", lambda e, sl=sl, c0=c0: e.dma_start(out=okT[sl].t[:], in_=rkT.rearrange("(b p) t -> p b t", p=128)[:, :, c0:c0 + 512]), writes=[okT[sl].r])
                        S.dma(f"o_qT{sl}", lambda e, sl=sl, c0=c0: e.dma_start(out=oqT[sl].t[:], in_=rqT.rearrange("(b p) t -> p b t", p=128)[:, :, c0:c0 + 512]), writes=[oqT[sl].r])
                        S.dma(f"o_qd{sl}", lambda e, sl=sl, c0=c0: e.dma_start(out=oqd[sl].t[:], in_=rqdT.rearrange("(b p) t -> p b t", p=128)[:, :, c0:c0 + 512]), writes=[oqd[sl].r])
                        S.dma(f"o_rv{sl}", lambda e, sl=sl, c0=c0: e.dma_start(out=orv[sl].t[:], in_=rv[c0:c0 + 512, :].rearrange("(j p) c -> p j c", p=128)), writes=[orv[sl].r])
                        S.dma(f"o_gs{sl}", lambda e, sl=sl, c0=c0: e.dma_start(out=ogs[sl].t[:], in_=rgs[c0:c0 + 512, :].rearrange("(j p) c -> p j c", p=128)), writes=[ogs[sl].r])
                        for j in range(4):
                            i = sc * 4 + j
                            ptb = PT[j % 2]
                            for h in range(8):
                                pb_ = psc[h % 2]
                                hp = (h % 2) * 64
                                S.pe(lambda e, pb_=pb_, h=h, hp=hp, sl=sl, j=j: e.matmul(pb_.t[:, h // 2, :], lhsT=okT[sl].t[hp:hp + 64, h // 2, j * 128:(j + 1) * 128],
                                                                                   rhs=oqT[sl].t[hp:hp + 64, h // 2, j * 128:(j + 1) * 128], start=True, stop=True),
                                     reads=[okT[sl].r, oqT[sl].r], writes=[pb_.r])
                            pt4 = ptb.t[:].rearrange("p (b a) n -> p b a n", a=2)
                            S.dve(lambda e, pt4=pt4: e.tensor_tensor(out=pt4[:, :, 0, :], in0=psc[0].t[:], in1=DTe.t[:], op=ALU.mult), reads=[psc[0].r, DTe.r], writes=[ptb.r])
                            S.dve(lambda e, pt4=pt4: e.tensor_tensor(out=pt4[:, :, 1, :], in0=psc[1].t[:], in1=DTo.t[:], op=ALU.mult), reads=[psc[1].r, DTo.r, ptb.r], writes=[ptb.r])
                            pob = po[j % 2]
                            for h in range(8):
                                S.pe(lambda e, pob=pob, h=h, ptb=ptb, sl=sl, j=j: e.matmul(pob.t[:, h * 64:(h + 1) * 64], lhsT=ptb.t[:, h, :], rhs=orv[sl].t[:, j, h * 64:(h + 1) * 64], start=True, stop=False),
                                     reads=[ptb.r, orv[sl].r], writes=[pob.r])
                                S.pe(lambda e, pob=pob, h=h, sl=sl, j=j, i=i: e.matmul(pob.t[:, h * 64:(h + 1) * 64], lhsT=oqd[sl].t[:, h, j * 128:(j + 1) * 128], rhs=Sall.t[:, i, h * 64:(h + 1) * 64], start=False, stop=True),
                                     reads=[oqd[sl].r, Sall.r], writes=[pob.r])
                            S.act(lambda e, pob=pob: e.activation(out=rsq.t[:], in_=pob.t[:], func=AF.Square), reads=[pob.r], writes=[rsq.r])
                            S.dve(lambda e: e.tensor_reduce(out=rss.t[:], in_=rsq.t[:].rearrange("p (g d) -> p g d", d=64), axis=AX.X, op=ALU.add), reads=[rsq.r], writes=[rss.r])
                            rsqrt_act(rss.t[:], rss.t[:], 1.0 / 64, rss, rss)
                            S.dve(lambda e, pob=pob: e.tensor_tensor(out=rn.t[:].rearrange("p (g d) -> p g d", d=64), in0=pob.t[:].rearrange("p (g d) -> p g d", d=64),
                                                                    in1=rss.t[:].unsqueeze(2).to_broadcast([128, 8, 64]), op=ALU.mult), reads=[pob.r, rss.r], writes=[rn.r])
                            S.dve(lambda e: e.tensor_tensor(out=rn.t[:].rearrange("p (g d) -> p g d", d=64), in0=rn.t[:].rearrange("p (g d) -> p g d", d=64),
                                                            in1=gnw.t[:].unsqueeze(1).to_broadcast([128, 8, 64]), op=ALU.mult), reads=[rn.r, gnw.r], writes=[rn.r])
                            S.dve(lambda e, sl=sl, j=j: e.tensor_tensor(out=rr.t[:], in0=rn.t[:], in1=ogs[sl].t[:, j, :], op=ALU.mult), reads=[rn.r, ogs[sl].r], writes=[rr.r])
                            for c in range(4):
                                S.pe(lambda e, c=c: e.transpose(out=ptr.t[:, c, :], in_=rr.t[:, c * 128:(c + 1) * 128], identity=ident.t[:]), reads=[rr.r, ident.r], writes=[ptr.r])
                            S.act(lambda e, sl=sl, j=j: e.activation(out=rTst[sl].t[:, :, j * 128:(j + 1) * 128], in_=ptr.t[:, 0:4, :], func=AF.Copy), reads=[ptr.r], writes=[rTst[sl].r])
                        S.dma(f"o_rT{sl}", lambda e, sl=sl, c0=c0: e.dma_start(out=rT.rearrange("(b p) t -> p b t", p=128)[:, :, c0:c0 + 512], in_=rTst[sl].t[:]), reads=[rTst[sl].r])

                sweep(1, False)
                Rstl = Res("st_l")
                S.dma("r_stl", lambda e: e.dma_start(out=st_l, in_=sttJ[1].t[:]), reads=[sttJ[1].r], writes=[Rstl])
                S.dma("cc_s", lambda e: e.collective_compute("AllGather", ALU.bypass, replica_groups=RG, ins=[st_l], outs=[st_g]), reads=[Rstl], writes=[Rtg], eng="pool", inc=1, cost=250.0)
                sweep(0, False)
                outputs(0)
                sweep(1, True)
                outputs(1)
                S.barrier()

            with ExitStack() as st:
                w_out = load_w(st, "w_out", w_out_in[l], D, D, "w_out")
                w1 = load_w(st, "w1", w1_in[l], D, DFF, "w1")
                arT = [sb(st, f"arT{i}", [128, 8, 512], BF16) for i in range(2)]
                xs_ = [sb(st, f"xs{i}", [128, 4, D], F32) for i in range(2)]
                junk = sb(st, "junk2", [128, D], BF16)
                ss2 = sb(st, "ss2", [128, 1], F32)
                h2l = [sb(st, f"h2_{i}", [128, D], BF16) for i in range(2)]
                h2T = [sb(st, f"h2T{i}", [128, 8, 512], BF16) for i in range(1)] * 2
                rl = [sb(st, f"rl{i}", [128, 512], F32) for i in range(2)]
                uTs = [sb(st, f"uTs{i}", [128, 32, 512], BF16) for i in range(1)] * 2
                pw = [psb(st, f"pw{i}", [128, 512], F32) for i in range(2)]
                ph = psb(st, "ph", [128, 8, 128], BF16)
                pu = [psb(st, f"pu{i}", [128, 512], F32) for i in range(4)]
                for s in range(NST):
                    sl = s % 2
                    c0 = s * 512
                    S.dma(f"c_a{sl}", lambda e, sl=sl, c0=c0: e.dma_start(out=arT[sl].t[:, 0:4, :], in_=aT.rearrange("(b p) t -> p b t", p=128)[:, :, c0:c0 + 512]), writes=[arT[sl].r])
                    S.dma(f"c_r{sl}", lambda e, sl=sl, c0=c0: e.dma_start(out=arT[sl].t[:, 4:8, :], in_=rT.rearrange("(b p) t -> p b t", p=128)[:, :, c0:c0 + 512]), writes=[arT[sl].r])
                    S.dma(f"c_x{sl}", lambda e, sl=sl, c0=c0: e.dma_start(out=xs_[sl].t[:], in_=x_src[c0:c0 + 512, :].rearrange("(j p) c -> p j c", p=128)), writes=[xs_[sl].r])
                    for j in range(4):
                        for cb in range(2):
                            pb_ = pw[cb]
                            for k in range(8):
                                S.pe(lambda e, pb_=pb_, k=k, cb=cb, sl=sl, j=j: e.matmul(pb_.t[:], lhsT=arT[sl].t[:, k, j * 128:(j + 1) * 128], rhs=w_out.t[:, k, cb * 512:(cb + 1) * 512], start=(k == 0), stop=(k == 7)),
                                     reads=[arT[sl].r, w_out.r(k, cb * 512)], writes=[pb_.r])
                            S.dve(lambda e, pb_=pb_, cb=cb, sl=sl, j=j: e.tensor_tensor(out=xs_[sl].t[:, j, cb * 512:(cb + 1) * 512], in0=pb_.t[:], in1=xs_[sl].t[:, j, cb * 512:(cb + 1) * 512], op=ALU.add),
                                  reads=[pb_.r, xs_[sl].r], writes=[xs_[sl].r])
                        S.act(lambda e, sl=sl, j=j: e.activation(out=junk.t[:], in_=xs_[sl].t[:, j, :], func=AF.Square, accum_out=ss2.t[:]), reads=[xs_[sl].r], writes=[junk.r, ss2.r])
                        rsqrt_act(ss2.t[:], ss2.t[:], 1.0 / D, ss2, ss2)
                        h2 = h2l[j % 2]
                        S.dve(lambda e, sl=sl, j=j, h2=h2: e.scalar_tensor_tensor(out=h2.t[:], in0=xs_[sl].t[:, j, :], scalar=ss2.t[:, 0:1], in1=ln2b.t[:], op0=ALU.mult, op1=ALU.mult),
                              reads=[xs_[sl].r, ss2.r, ln2b.r], writes=[h2.r])
                        for c in range(8):
                            S.pe(lambda e, c=c, h2=h2: e.transpose(out=ph.t[:, c, :], in_=h2.t[:, c * 128:(c + 1) * 128], identity=ident.t[:]), reads=[h2.r, ident.r], writes=[ph.r])
                        S.act(lambda e, sl=sl, j=j: e.activation(out=h2T[sl].t[:, :, j * 128:(j + 1) * 128], in_=ph.t[:], func=AF.Copy), reads=[ph.r], writes=[h2T[sl].r])
                    S.dma(f"c_xo{sl}", lambda e, sl=sl, c0=c0: e.dma_start(out=xres[c0:c0 + 512, :].rearrange("(j p) c -> p j c", p=128), in_=xs_[sl].t[:]), reads=[xs_[sl].r], eng="act")
                    for fc in range(32):
                        pb_ = pu[fc % 4]
                        for k in range(8):
                            S.pe(lambda e, pb_=pb_, k=k, fc=fc, sl=sl: e.matmul(pb_.t[:], lhsT=w1.t[:, k, fc * 128:(fc + 1) * 128], rhs=h2T[sl].t[:, k, :], start=(k == 0), stop=(k == 7)),
                                 reads=[w1.r(k, fc * 128), h2T[sl].r], writes=[pb_.r])
                        rb = rl[fc % 2]
                        S.act(lambda e, pb_=pb_, rb=rb: e.activation(out=rb.t[:], in_=pb_.t[:], func=AF.Relu), reads=[pb_.r], writes=[rb.r])
                        S.dve(lambda e, rb=rb, fc=fc, sl=sl: e.tensor_tensor(out=uTs[sl].t[:, fc, :], in0=rb.t[:], in1=rb.t[:], op=ALU.mult), reads=[rb.r], writes=[uTs[sl].r])
                    S.dma(f"c_u{sl}", lambda e, sl=sl, c0=c0: e.dma_start(out=uT.rearrange("(c p) t -> p c t", p=128)[:, :, c0:c0 + 512], in_=uTs[sl].t[:]), reads=[uTs[sl].r], eng="act")
                S.barrier()

            with ExitStack() as st:
                w2 = load_w(st, "w2", w2_in[l], DFF, D, "w2", gn=1024, korder=True)
                wg = load_w(st, "wg", wg_in[l], D, D, "wg")
                wp = load_w(st, "wp", wp_in[l], PLE, D, "wp")
                uTl = [sb(st, f"uTl{i}", [128, 32, 512], BF16) for i in range(2)]
                xs_ = [sb(st, f"xc{i}", [128, 4, D], F32) for i in range(1)] * 2
                pl_ = [sb(st, f"pl{i}", [128, 4, PLE], F32) for i in range(1)] * 2
                x2bl = [sb(st, f"x2b{i}", [128, D], BF16) for i in range(2)]
                x2Tl = [sb(st, f"x2T{i}", [128, 8, 128], BF16) for i in range(2)]
                pbfl = [sb(st, f"pbf{i}", [128, PLE], BF16) for i in range(2)]
                ppTl = [sb(st, f"ppT{i}", [128, 2, 128], BF16) for i in range(2)]
                sg = [sb(st, f"sg{i}", [128, 512], F32) for i in range(2)]
                pm = [psb(st, f"pm{i}", [128, 512], F32) for i in range(2)]
                pg = [psb(st, f"pg{i}", [128, 512], F32) for i in range(2)]
                pq = [psb(st, f"pq{i}", [128, 512], F32) for i in range(2)]
                px = psb(st, "px", [128, 8, 128], BF16)
                pp2 = psb(st, "pp2", [128, 8, 128], BF16)
                for s in range(NST):
                    sl = s % 2
                    c0 = s * 512
                    S.dma(f"d_u{sl}", lambda e, sl=sl, c0=c0: e.dma_start(out=uTl[sl].t[:], in_=uT.rearrange("(c p) t -> p c t", p=128)[:, :, c0:c0 + 512]), writes=[uTl[sl].r])
                    S.dma(f"d_x{sl}", lambda e, sl=sl, c0=c0: e.dma_start(out=xs_[sl].t[:], in_=xres[c0:c0 + 512, :].rearrange("(j p) c -> p j c", p=128)), writes=[xs_[sl].r])
                    S.dma(f"d_p{sl}", lambda e, sl=sl, c0=c0, l=l: e.dma_start(out=pl_[sl].t[:], in_=p_in[l, c0:c0 + 512, :].rearrange("(j p) c -> p j c", p=128)), writes=[pl_[sl].r])
                    for j in range(4):
                        for cb in range(2):
                            pb_ = pm[cb]
                            for fc in range(32):
                                S.pe(lambda e, pb_=pb_, fc=fc, cb=cb, sl=sl, j=j: e.matmul(pb_.t[:], lhsT=uTl[sl].t[:, fc, j * 128:(j + 1) * 128], rhs=w2.t[:, fc, cb * 512:(cb + 1) * 512], start=(fc == 0), stop=(fc == 31)),
                                     reads=[uTl[sl].r, w2.r(fc, cb * 512)], writes=[pb_.r])
                            S.dve(lambda e, pb_=pb_, cb=cb, sl=sl, j=j: e.tensor_tensor(out=xs_[sl].t[:, j, cb * 512:(cb + 1) * 512], in0=pb_.t[:], in1=xs_[sl].t[:, j, cb * 512:(cb + 1) * 512], op=ALU.add),
                                  reads=[pb_.r, xs_[sl].r], writes=[xs_[sl].r])
                        x2b, x2T, pbf, ppT = x2bl[j % 2], x2Tl[j % 2], pbfl[j % 2], ppTl[j % 2]
                        S.act(lambda e, sl=sl, j=j, x2b=x2b: e.activation(out=x2b.t[:], in_=xs_[sl].t[:, j, :], func=AF.Copy), reads=[xs_[sl].r], writes=[x2b.r], cost=1.1)
                        for c in range(8):
                            S.pe(lambda e, c=c, x2b=x2b: e.transpose(out=px.t[:, c, :], in_=x2b.t[:, c * 128:(c + 1) * 128], identity=ident.t[:]), reads=[x2b.r, ident.r], writes=[px.r])
                        S.act(lambda e, x2T=x2T: e.activation(out=x2T.t[:], in_=px.t[:], func=AF.Copy), reads=[px.r], writes=[x2T.r], cost=1.1)
                        S.pool(lambda e, sl=sl, j=j, pbf=pbf: e.tensor_copy(out=pbf.t[:], in_=pl_[sl].t[:, j, :]), reads=[pl_[sl].r], writes=[pbf.r])
                        for c in range(2):
                            S.pe(lambda e, c=c, pbf=pbf: e.transpose(out=pp2.t[:, c, :], in_=pbf.t[:, c * 128:(c + 1) * 128], identity=ident.t[:]), reads=[pbf.r, ident.r], writes=[pp2.r])
                        S.act(lambda e, ppT=ppT: e.activation(out=ppT.t[:], in_=pp2.t[:, 0:2, :], func=AF.Copy), reads=[pp2.r], writes=[ppT.r])
                        for cb in range(2):
                            g_ = pg[cb]
                            q_ = pq[cb]
                            for k in range(8):
                                S.pe(lambda e, g_=g_, k=k, cb=cb: e.matmul(g_.t[:], lhsT=x2T.t[:, k, :], rhs=wg.t[:, k, cb * 512:(cb + 1) * 512], start=(k == 0), stop=(k == 7)),
                                     reads=[x2T.r, wg.r(k, cb * 512)], writes=[g_.r])
                            for k in range(2):
                                S.pe(lambda e, q_=q_, k=k, cb=cb: e.matmul(q_.t[:], lhsT=ppT.t[:, k, :], rhs=wp.t[:, k, cb * 512:(cb + 1) * 512], start=(k == 0), stop=(k == 1)),
                                     reads=[ppT.r, wp.r(k, cb * 512)], writes=[q_.r])
                            sgb = sg[cb]
                            S.act(lambda e, g_=g_, sgb=sgb: e.activation(out=sgb.t[:], in_=g_.t[:], func=AF.Exp, scale=-1.0), reads=[g_.r], writes=[sgb.r])
                            S.dve(lambda e, sgb=sgb: e.tensor_scalar(out=sgb.t[:], in0=sgb.t[:], scalar1=1.0, scalar2=None, op0=ALU.add), reads=[sgb.r], writes=[sgb.r])
                            S.dve(lambda e, sgb=sgb: e.reciprocal(out=sgb.t[:], in_=sgb.t[:]), reads=[sgb.r], writes=[sgb.r])
                            S.dve(lambda e, sgb=sgb, q_=q_: e.tensor_tensor(out=sgb.t[:], in0=q_.t[:], in1=sgb.t[:], op=ALU.mult), reads=[q_.r, sgb.r], writes=[sgb.r])
                            S.dve(lambda e, sgb=sgb, cb=cb, sl=sl, j=j: e.tensor_tensor(out=xs_[sl].t[:, j, cb * 512:(cb + 1) * 512], in0=sgb.t[:], in1=xs_[sl].t[:, j, cb * 512:(cb + 1) * 512], op=ALU.add),
                                  reads=[sgb.r, xs_[sl].r], writes=[xs_[sl].r])
                    S.dma(f"d_xo{sl}", lambda e, sl=sl, c0=c0: e.dma_start(out=x_dst[c0:c0 + 512, :].rearrange("(j p) c -> p j c", p=128), in_=xs_[sl].t[:]), reads=[xs_[sl].r], eng="act")
                S.barrier()

        if DEBUG:
            for nm, src in (("rgs", rgs), ("rv", rv), ("dv_s", dv_s), ("dqT", dqT), ("aT", aT), ("rT", rT), ("xres", xres), ("uT", uT), ("rqT", rqT), ("rkd", rkd), ("rqdT", rqdT), ("rkT", rkT), ("dkT_s", dkT_s)):
                dbg = dram("dbg_" + nm, list(src.shape), src.dtype, "ExternalOutput")
                S.dma("dbg", lambda e, dbg=dbg, src=src: e.dma_start(out=dbg, in_=src))
        S.emit(top)
        nc._n_ops = len(S.ops)
        nc._n_sem = S.nsem
    return nc


ROPE_THETA = 500000.0
RET_THETA = 10000.0


def host_tables(TS, TP, rank):
    posv = np.concatenate([np.arange(TS), rank * TP + np.arange(TP)]).astype(np.float32)
    inv_d = (np.float32(1.0) / (np.float32(ROPE_THETA) ** (np.arange(0, 16, 2, dtype=np.float32) / np.float32(16)))).astype(np.float32)
    inv_r = (np.float32(1.0) / (np.float32(RET_THETA) ** (np.arange(0, 64, 2, dtype=np.float32) / np.float32(64)))).astype(np.float32)
    ang_d = (posv[:, None] * inv_d[None, :]).astype(np.float32)
    ang_r = (posv[:, None] * inv_r[None, :]).astype(np.float32)
    pos = np.concatenate([np.cos(ang_d), np.sin(ang_d), np.cos(ang_r), np.sin(ang_r)], axis=1).astype(np.float32)
    m = np.arange(128)[:, None].astype(np.float32)
    n = np.arange(128)[None, :].astype(np.float32)
    relF = np.maximum(n - m, 0.0)
    mskF = (n >= m).astype(np.float32) * 0.125
    relB = np.maximum(m - n, 0.0)
    mskB = (m > n).astype(np.float32) * 0.125
    p = np.arange(128, dtype=np.float32)[:, None]
    cst = np.concatenate([relF, mskF, relB, mskB, p + 1, 128 - p, 127 - p, p], axis=1).astype(np.float32)
    rkt = np.zeros((128, 8), np.float32)
    for r in range(NR):
        if r < rank:
            rkt[0:64, r] = TP * (rank - 1 - r)
            rkt[0:64, 4 + r] = 1.0
        if r > rank:
            rkt[64:128, r] = TP * (r - rank - 1)
            rkt[64:128, 4 + r] = 1.0
    return pos, cst, rkt


_NC_CACHE = {}


def run_cores(inputs, TS, TP, DEPTH):
    key = (TS, TP, DEPTH)
    if key not in _NC_CACHE:
        _NC_CACHE[key] = build(TS, TP, DEPTH)
    nc = _NC_CACHE[key]
    f = lambda a: np.ascontiguousarray(np.asarray(a, dtype=np.float32))
    xp = f(inputs["x_prompt"])
    xs = f(inputs["x_sample"])
    pp = f(inputs["p_prompt"])
    ps = f(inputs["p_sample"])
    shared = {
        "ln1_w": f(inputs["ln1_w"]), "w_in": f(inputs["w_in"]), "diff_q_norm": f(inputs["diff_q_norm"]),
        "diff_k_norm": f(inputs["diff_k_norm"]), "diff_lambda": f(inputs["diff_lambda"]).reshape(1, DEPTH * 256),
        "diff_subln": f(inputs["diff_subln"]), "ret_decay_logit": f(inputs["ret_decay_logit"]).reshape(1, DEPTH * 16),
        "ret_gn": f(inputs["ret_gn"]), "w_out": f(inputs["w_out"]), "ln2_w": f(inputs["ln2_w"]),
        "w_mlp1": f(inputs["w_mlp1"]), "w_mlp2": f(inputs["w_mlp2"]), "w_ple_gate": f(inputs["w_ple_gate"]),
        "w_ple_proj": f(inputs["w_ple_proj"]),
    }
    in_maps = []
    for c in range(8):
        g, r = c // NR, c % NR
        pos, cst, rkt = host_tables(TS, TP, r)
        m = dict(shared)
        m["x"] = np.ascontiguousarray(np.concatenate([xs[c], xp[g, r * TP:(r + 1) * TP]], axis=0))
        m["p"] = np.ascontiguousarray(np.concatenate([ps[:, c], pp[:, g, r * TP:(r + 1) * TP]], axis=1))
        m["pos"] = pos
        m["cst"] = cst
        m["rkt"] = rkt
        in_maps.append(m)
    res = run_bass_kernel_spmd(nc, in_maps, core_ids=list(range(8)))
    y_s = np.stack([np.asarray(res.results[c]["y"][:TS]) for c in range(8)], axis=0)
    y_p = np.stack([np.concatenate([np.asarray(res.results[g * NR + r]["y"][TS:]) for r in range(NR)], axis=0) for g in range(2)], axis=0)
    return y_p.astype(np.float32), y_s.astype(np.float32)


def kernel(**inputs):
    return run_cores(inputs, 4096, 2048, 4)
```

```python
import math
import types
import numpy as np
import concourse.bass as bass
import concourse.mybir as mybir
from concourse.bass_utils import run_bass_kernel_spmd
from contextlib import ExitStack

F32 = mybir.dt.float32
BF16 = mybir.dt.bfloat16
AF = mybir.ActivationFunctionType
ALU = mybir.AluOpType
AX = mybir.AxisListType

COMPUTE = ("pe", "act", "dve", "pool")
ALL_ENG = ("pe", "act", "dve", "pool", "sp")
SAME_ENG_SYNC = True

D = 1024
INW = 3584
DFF = 4096
PLE = 256
EPS = 1e-6
NR = 4


class Res:
    __slots__ = ("name", "w", "rs")

    def __init__(self, name=""):
        self.name = name
        self.w = None
        self.rs = []


class Op:
    __slots__ = ("eng", "fn", "deps", "mark", "val", "key", "is_mm", "inc", "cost", "idx", "fin", "succ", "npend", "ready")

    def __init__(self, eng, fn, key=None, is_mm=False, cost=None):
        self.eng = eng
        self.fn = fn
        self.deps = []
        self.mark = False
        self.val = 0
        self.key = key
        self.is_mm = is_mm
        self.inc = 16
        self.cost = cost
        self.idx = 0
        self.fin = 0.0
        self.succ = None
        self.npend = 0
        self.ready = 0.0


def _freeze(fn):
    if fn is None or fn.__closure__ is None:
        return fn
    cells = []
    for c in fn.__closure__:
        try:
            cells.append(types.CellType(c.cell_contents))
        except ValueError:
            cells.append(c)
    return types.FunctionType(fn.__code__, fn.__globals__, fn.__name__, fn.__defaults__, tuple(cells))


DEF_COST = {"pe": 0.27, "act": 0.45, "dve": 0.60, "pool": 0.90, "sp": 0.30}
SLAT = 0.25
XLAT = 0.3
DMA_LAT = 3.0
RESCHEDULE = True


class Sched:
    def __init__(self, nc):
        self.nc = nc
        self.ops = []

    def op(self, eng, fn, reads=(), writes=(), key=None, is_mm=False, cost=None):
        o = Op(eng, _freeze(fn), key, is_mm, cost)
        deps = o.deps
        for r in reads:
            if r.w is not None:
                deps.append(r.w)
        for w in writes:
            if w.w is not None:
                deps.append(w.w)
            deps.extend(w.rs)
        for r in reads:
            r.rs.append(o)
        for w in writes:
            w.w = o
            w.rs = []
        o.idx = len(self.ops)
        self.ops.append(o)
        return o

    def pe(self, fn, reads=(), writes=(), cost=None):
        return self.op("pe", fn, reads, writes, is_mm=True, cost=cost)

    def act(self, fn, reads=(), writes=(), cost=None):
        return self.op("act", fn, reads, writes, cost=cost)

    def dve(self, fn, reads=(), writes=(), cost=None):
        return self.op("dve", fn, reads, writes, cost=cost)

    def pool(self, fn, reads=(), writes=(), cost=None):
        return self.op("pool", fn, reads, writes, cost=cost)

    def dma(self, key, fn, reads=(), writes=(), eng="sp", inc=16, cost=None):
        o = self.op(eng, fn, reads, writes, key=key, cost=cost)
        o.inc = inc
        return o

    def barrier(self):
        o = Op(None, None)
        o.idx = len(self.ops)
        self.ops.append(o)

    @staticmethod
    def _skip(p, o):
        return p.key is None and p.eng == o.eng and (not SAME_ENG_SYNC or (p.is_mm and o.is_mm))

    def _schedule_segment(self, seg):
        import heapq
        inseg = set(id(o) for o in seg)
        for o in seg:
            o.succ = []
            o.npend = 0
            o.ready = 0.0
        for o in seg:
            for p in o.deps:
                if id(p) in inseg:
                    p.succ.append(o)
                    o.npend += 1
        free = {e: 0.0 for e in ALL_ENG}
        heaps = {e: [] for e in ALL_ENG}
        for o in seg:
            if o.npend == 0:
                heapq.heappush(heaps[o.eng], (0.0, o.idx, o))
        order = {e: [] for e in ALL_ENG}
        remaining = len(seg)
        while remaining:
            best = None
            for e in ALL_ENG:
                h = heaps[e]
                if not h:
                    continue
                t_free = free[e]
                cands = []
                while h and h[0][0] <= t_free:
                    cands.append(heapq.heappop(h))
                if cands:
                    c = min(cands, key=lambda x: x[1])
                    for x in cands:
                        if x is not c:
                            heapq.heappush(h, x)
                    heapq.heappush(h, c)
                    start = t_free
                    pick = c
                else:
                    pick = h[0]
                    start = pick[0]
                if best is None or start < best[0] or (start == best[0] and pick[1] < best[2][1]):
                    best = (start, e, pick)
            start, e, pick = best
            h = heaps[e]
            h.remove(pick)
            heapq.heapify(h)
            o = pick[2]
            cost = o.cost if o.cost is not None else DEF_COST[e]
            if o.key is not None:
                free[e] = start + DEF_COST["sp"]
                o.fin = start + (cost if o.cost is not None else DMA_LAT)
            else:
                free[e] = start + cost
                o.fin = free[e]
            order[e].append(o)
            remaining -= 1
            for q in o.succ:
                if q.eng == o.eng and o.key is None:
                    lat = 0.0 if (o.is_mm and q.is_mm) else SLAT
                else:
                    lat = XLAT
                r = o.fin + lat
                if r > q.ready:
                    q.ready = r
                q.npend -= 1
                if q.npend == 0:
                    heapq.heappush(heaps[q.eng], (q.ready, q.idx, q))
        return order

    def emit(self, stack):
        nc = self.nc
        segs = []
        cur = []
        for o in self.ops:
            if o.eng is None:
                if cur:
                    segs.append(cur)
                    cur = []
            else:
                cur.append(o)
        if cur:
            segs.append(cur)
        streams = {e: [] for e in ALL_ENG}
        for seg in segs:
            if RESCHEDULE:
                order = self._schedule_segment(seg)
            else:
                order = {e: [o for o in seg if o.eng == e] for e in ALL_ENG}
            lastops = {}
            for e in ALL_ENG:
                for o in order[e]:
                    lastops[o.key if o.key is not None else e] = o
            for e in ALL_ENG:
                streams[e].extend(order[e])
            deps = list(lastops.values())
            for e in ALL_ENG:
                b = Op(e, None)
                b.deps = list(deps)
                streams[e].append(b)
        for e in ALL_ENG:
            for o in streams[e]:
                for p in o.deps:
                    if p.key is None and not self._skip(p, o):
                        p.mark = True
        kcnt = {}
        for e in ALL_ENG:
            cnt = 0
            for o in streams[e]:
                if o.fn is None:
                    continue
                if o.key is not None:
                    kcnt[o.key] = kcnt.get(o.key, 0) + o.inc
                    o.val = kcnt[o.key]
                elif o.mark:
                    cnt += 1
                    o.val = cnt
        sems = {}
        for e in COMPUTE:
            sems[e] = stack.enter_context(nc.semaphore("s_" + e))
        for k in kcnt:
            sems[k] = stack.enter_context(nc.semaphore("d_" + k))
        self.nsem = len(sems)
        block = stack.enter_context(nc.Block())

        def run(engname, eng):
            seen = {}
            for o in streams[engname]:
                need = {}
                for p in o.deps:
                    if self._skip(p, o):
                        continue
                    sk = p.eng if p.key is None else p.key
                    if seen.get(sk, 0) >= p.val:
                        continue
                    if need.get(sk, 0) < p.val:
                        need[sk] = p.val
                for sk, v in need.items():
                    eng.wait_ge(sems[sk], v)
                    seen[sk] = v
                if o.fn is None:
                    continue
                ins = o.fn(eng)
                if o.key is not None:
                    ins.then_inc(sems[o.key], o.inc)
                elif o.mark:
                    ins.then_inc(sems[o.eng], 1)
            if engname == "sp":
                for k, v in kcnt.items():
                    if seen.get(k, 0) < v:
                        eng.wait_ge(sems[k], v)

        @block.tensor
        def _(e):
            run("pe", e)

        @block.scalar
        def _(e):
            run("act", e)

        @block.vector
        def _(e):
            run("dve", e)

        @block.gpsimd
        def _(e):
            run("pool", e)

        @block.sync
        def _(e):
            run("sp", e)


class Buf:
    __slots__ = ("t", "r")

    def __init__(self, t, name):
        self.t = t
        self.r = Res(name)


def build(TS, TP, DEPTH, DEBUG=False):
    T = TS + TP
    SKP = NR * TP
    NT = T // 128
    NST = T // 512
    assert TS % 512 == 0 and TP % 512 == 0
    nc = bass.Bass("TRN2", target_bir_lowering=False)

    def dram(name, shape, dtype, kind="Internal"):
        return nc.dram_tensor(name, shape, dtype, kind=kind).ap()

    x_in = dram("x", [T, D], F32, "ExternalInput")
    p_in = dram("p", [DEPTH, T, PLE], F32, "ExternalInput")
    pos_in = dram("pos", [T, 80], F32, "ExternalInput")
    cst_in = dram("cst", [128, 516], F32, "ExternalInput")
    rkt_in = dram("rkt", [128, 8], F32, "ExternalInput")
    ln1_in = dram("ln1_w", [DEPTH, D], F32, "ExternalInput")
    w_in_in = dram("w_in", [DEPTH, D, INW], F32, "ExternalInput")
    qn_in = dram("diff_q_norm", [DEPTH, 64], F32, "ExternalInput")
    kn_in = dram("diff_k_norm", [DEPTH, 64], F32, "ExternalInput")
    lam_in = dram("diff_lambda", [1, DEPTH * 256], F32, "ExternalInput")
    sub_in = dram("diff_subln", [DEPTH, 128], F32, "ExternalInput")
    dec_in = dram("ret_decay_logit", [1, DEPTH * 16], F32, "ExternalInput")
    gn_in = dram("ret_gn", [DEPTH, 64], F32, "ExternalInput")
    w_out_in = dram("w_out", [DEPTH, D, D], F32, "ExternalInput")
    ln2_in = dram("ln2_w", [DEPTH, D], F32, "ExternalInput")
    w1_in = dram("w_mlp1", [DEPTH, D, DFF], F32, "ExternalInput")
    w2_in = dram("w_mlp2", [DEPTH, DFF, D], F32, "ExternalInput")
    wg_in = dram("w_ple_gate", [DEPTH, D, D], F32, "ExternalInput")
    wp_in = dram("w_ple_proj", [DEPTH, PLE, D], F32, "ExternalInput")
    y_out = dram("y", [T, D], F32, "ExternalOutput")

    xres = dram("xres", [T, D], F32)
    dqT = dram("dqT", [512, T], BF16)
    dkT_s = dram("dkT_s", [512, TS], BF16)
    NSPL = max(1, (512 * TP * 2) // (1 << 20))
    HPS = 4 // NSPL
    TPS = TP // NSPL
    dkT_l = [dram(f"dkT_l{a}", [HPS * 128, TP], BF16) for a in range(NSPL)]
    dkT_g = [dram(f"dkT_g{a}", [NR * HPS * 128, TP], BF16) for a in range(NSPL)]
    dv_s = dram("dv_s", [TS, 512], BF16)
    dv_l = [dram(f"dv_l{a}", [TPS, 512], BF16) for a in range(NSPL)]
    dv_g = [dram(f"dv_g{a}", [NR * TPS, 512], BF16) for a in range(NSPL)]
    rqT = dram("rqT", [512, T], BF16)
    rkT = dram("rkT", [512, T], BF16)
    rqdT = dram("rqdT", [8 * 128, T], BF16)
    rkd = dram("rkd", [T, 1024], BF16)
    rv = dram("rv", [T, 512], BF16)
    rgs = dram("rgs", [T, 512], BF16)
    aT = dram("aT", [512, T], BF16)
    rT = dram("rT", [512, T], BF16)
    uT = dram("uT", [DFF, T], BF16)
    st_l = dram("st_l", [128, 512], F32)
    st_g = dram("st_g", [NR * 128, 512], F32)

    with ExitStack() as top:
        S = Sched(nc)

        uid = [0]

        def sb(st, name, shape, dt):
            uid[0] += 1
            return Buf(st.enter_context(nc.sbuf_tensor(f"sb{uid[0]}_{name}", shape, dt)), name)

        def psb(st, name, shape, dt):
            uid[0] += 1
            return Buf(st.enter_context(nc.psum_tensor(f"ps{uid[0]}_{name}", shape, dt)), name)

        ident = sb(top, "ident", [128, 128], BF16)
        identf = sb(top, "identf", [128, 128], F32)
        ones_b = sb(top, "ones_b", [128, 128], BF16)
        ones_f = sb(top, "ones_f", [128, 128], F32)
        cst = sb(top, "cst", [128, 516], F32)
        rkt = sb(top, "rkt", [128, 8], F32)
        lg = sb(top, "lg", [128, DEPTH * 16], F32)
        lgcol = sb(top, "lgcol", [128, DEPTH * 8], F32)
        lam = sb(top, "lam", [128, DEPTH], F32)
        neglam = sb(top, "neglam", [128, DEPTH], F32)
        epsc = sb(top, "epsc", [128, 1], F32)
        ln1b = sb(top, "ln1b", [128, D], F32)
        ln2b = sb(top, "ln2b", [128, D], F32)
        qnw = sb(top, "qnw", [128, 64], F32)
        knw = sb(top, "knw", [128, 64], F32)
        gnw = sb(top, "gnw", [128, 64], F32)
        subw = sb(top, "subw", [128, 1], F32)
        qdec = sb(top, "qdec", [128, 8, 2], F32)
        kdec = sb(top, "kdec", [128, 8, 2], F32)
        cdec = sb(top, "cdec", [128, 8], F32)
        DTe = sb(top, "DTe", [128, 4, 128], F32)
        DTo = sb(top, "DTo", [128, 4, 128], F32)
        coef = sb(top, "coef", [128, 4, 8], F32)

        relF = cst.t[:, 0:128]
        mskF = cst.t[:, 128:256]
        relB = cst.t[:, 256:384]
        mskB = cst.t[:, 384:512]
        idx_p1 = cst.t[:, 512:513]
        idx_128m = cst.t[:, 513:514]
        idx_127m = cst.t[:, 514:515]
        idx_p = cst.t[:, 515:516]

        S.dma("c_init", lambda e: e.dma_start(out=cst.t[:], in_=cst_in), writes=[cst.r])
        S.dma("c_init", lambda e: e.dma_start(out=rkt.t[:], in_=rkt_in), writes=[rkt.r])
        S.pool(lambda e: e.memset(identf.t[:], 0.0), writes=[identf.r])
        S.pool(lambda e: e.affine_select(out=identf.t[:], in_=identf.t[:], compare_op=ALU.not_equal, fill=1.0, base=0,
                                         pattern=[[-1, 128]], channel_multiplier=1), reads=[identf.r], writes=[identf.r])
        S.dve(lambda e: e.tensor_copy(out=ident.t[:], in_=identf.t[:]), reads=[identf.r], writes=[ident.r])
        S.dve(lambda e: e.memset(ones_b.t[:], 1.0), writes=[ones_b.r])
        S.dve(lambda e: e.memset(ones_f.t[:], 1.0), writes=[ones_f.r])
        S.dve(lambda e: e.memset(epsc.t[:], EPS), writes=[epsc.r])

        with ExitStack() as st0:
            NL = DEPTH * 16
            xx = sb(st0, "ls_x", [128, NL], F32)
            ax = sb(st0, "ls_ax", [128, NL], F32)
            uu = sb(st0, "ls_u", [128, NL], F32)
            ss_ = sb(st0, "ls_s", [128, NL], F32)
            s2 = sb(st0, "ls_s2", [128, NL], F32)
            pl = sb(st0, "ls_pl", [128, NL], F32)
            lp = sb(st0, "lp", [128, DEPTH * 256], F32)
            S.dma("c_init", lambda e: e.dma_start(out=xx.t[:], in_=dec_in.to_broadcast([128, NL])), writes=[xx.r])
            S.dma("c_init", lambda e: e.dma_start(out=lp.t[:], in_=lam_in.to_broadcast([128, DEPTH * 256])), writes=[lp.r])
            S.barrier()
            S.dve(lambda e: e.tensor_scalar(out=ax.t[:], in0=xx.t[:], scalar1=-1.0, scalar2=None, op0=ALU.mult), reads=[xx.r], writes=[ax.r])
            S.dve(lambda e: e.tensor_tensor(out=ax.t[:], in0=ax.t[:], in1=xx.t[:], op=ALU.max), reads=[xx.r, ax.r], writes=[ax.r])
            S.act(lambda e: e.activation(out=uu.t[:], in_=ax.t[:], func=AF.Exp, scale=-1.0), reads=[ax.r], writes=[uu.r])
            S.dve(lambda e: e.tensor_scalar(out=ss_.t[:], in0=uu.t[:], scalar1=2.0, scalar2=None, op0=ALU.add), reads=[uu.r], writes=[ss_.r])
            S.dve(lambda e: e.reciprocal(out=ss_.t[:], in_=ss_.t[:]), reads=[ss_.r], writes=[ss_.r])
            S.dve(lambda e: e.tensor_tensor(out=ss_.t[:], in0=ss_.t[:], in1=uu.t[:], op=ALU.mult), reads=[ss_.r, uu.r], writes=[ss_.r])
            S.dve(lambda e: e.tensor_tensor(out=s2.t[:], in0=ss_.t[:], in1=ss_.t[:], op=ALU.mult), reads=[ss_.r], writes=[s2.r])
            S.dve(lambda e: e.tensor_scalar(out=pl.t[:], in0=s2.t[:], scalar1=1.0 / 13, scalar2=1.0 / 11, op0=ALU.mult, op1=ALU.add), reads=[s2.r], writes=[pl.r])
            for cc in (1.0 / 9, 1.0 / 7, 1.0 / 5, 1.0 / 3, 1.0):
                S.dve(lambda e: e.tensor_tensor(out=pl.t[:], in0=pl.t[:], in1=s2.t[:], op=ALU.mult), reads=[pl.r, s2.r], writes=[pl.r])
                S.dve(lambda e, cc=cc: e.tensor_scalar(out=pl.t[:], in0=pl.t[:], scalar1=cc, scalar2=None, op0=ALU.add), reads=[pl.r], writes=[pl.r])
            S.dve(lambda e: e.tensor_tensor(out=pl.t[:], in0=pl.t[:], in1=ss_.t[:], op=ALU.mult), reads=[pl.r, ss_.r], writes=[pl.r])
            S.dve(lambda e: e.tensor_scalar(out=ax.t[:], in0=xx.t[:], scalar1=0.0, scalar2=None, op0=ALU.min), reads=[xx.r], writes=[ax.r])
            S.dve(lambda e: e.scalar_tensor_tensor(out=lg.t[:], in0=pl.t[:], scalar=-2.0, in1=ax.t[:], op0=ALU.mult, op1=ALU.add), reads=[pl.r, ax.r], writes=[lg.r])
            lg4 = lg.t[:].rearrange("p (l a h) -> p l a h", l=DEPTH, a=2)
            lgc3 = lgcol.t[:].rearrange("p (l h) -> p l h", l=DEPTH)
            S.dve(lambda e: e.tensor_copy(out=lgc3[0:64], in_=lg4[0:64, :, 0, :]), reads=[lg.r], writes=[lgcol.r])
            S.dve(lambda e: e.tensor_copy(out=lgc3[64:128], in_=lg4[64:128, :, 1, :]), reads=[lg.r], writes=[lgcol.r])
            pr = sb(st0, "lpr", [128, DEPTH * 2 * 64], F32)
            sm = sb(st0, "lsm", [128, DEPTH * 2], F32)
            lp5 = lp.t[:].rearrange("p (l a b d) -> p l a b d", l=DEPTH, a=2, b=2)
            pr4 = pr.t[:].rearrange("p (l a d) -> p l a d", l=DEPTH, a=2)
            S.dve(lambda e: e.tensor_tensor(out=pr4, in0=lp5[:, :, :, 0, :], in1=lp5[:, :, :, 1, :], op=ALU.mult), reads=[lp.r], writes=[pr.r])
            S.dve(lambda e: e.tensor_reduce(out=sm.t[:], in_=pr.t[:].rearrange("p (g d) -> p g d", d=64), axis=AX.X, op=ALU.add), reads=[pr.r], writes=[sm.r])
            S.act(lambda e: e.activation(out=sm.t[:], in_=sm.t[:], func=AF.Exp), reads=[sm.r], writes=[sm.r])
            sm3 = sm.t[:].rearrange("p (l a) -> p l a", a=2)
            S.dve(lambda e: e.tensor_tensor(out=lam.t[:], in0=sm3[:, :, 0], in1=sm3[:, :, 1], op=ALU.subtract), reads=[sm.r], writes=[lam.r])
            for l in range(DEPTH):
                li = 0.8 - 0.6 * math.exp(-0.3 * l)
                S.dve(lambda e, l=l, li=li: e.tensor_scalar(out=lam.t[:, l:l + 1], in0=lam.t[:, l:l + 1], scalar1=li, scalar2=None, op0=ALU.add), reads=[lam.r], writes=[lam.r])
            S.dve(lambda e: e.tensor_scalar(out=neglam.t[:], in0=lam.t[:], scalar1=-1.0, scalar2=None, op0=ALU.mult), reads=[lam.r], writes=[neglam.r])
            S.barrier()

        def rsqrt_act(out_ap, in_ap, scale, rbuf, wbuf):
            S.act(lambda e: e.activation(out=out_ap, in_=in_ap, func=AF.Ln, scale=scale, bias=epsc.t[0:out_ap.shape[0], 0:1]), reads=[rbuf.r, epsc.r], writes=[wbuf.r])
            S.act(lambda e: e.activation(out=out_ap, in_=out_ap, func=AF.Exp, scale=-0.5), reads=[wbuf.r], writes=[wbuf.r])

        class WGroups:
            def __init__(self, t, gn):
                self.t = t
                self.gn = gn
                self.res = {}

            def r(self, k, n0):
                return self.res[(k // 4, n0 // self.gn)]

        def load_w(st, name, src, K, N, key, gn=512, korder=False):
            kc = K // 128
            gn = min(gn, N, 2048)
            uid[0] += 1
            t = st.enter_context(nc.sbuf_tensor(f"sb{uid[0]}_{name}", [128, kc, N], BF16))
            w = WGroups(t, gn)
            srcv = src.rearrange("(c p) n -> p c n", p=128)
            chain = [Res(name + "_chA"), Res(name + "_chB")]
            groups = [(c0, n0) for n0 in range(0, N, gn) for c0 in range(0, kc, 4)]
            if korder:
                groups = [(c0, n0) for c0 in range(0, kc, 4) for n0 in range(0, N, gn)]
            for i, (c0, n0) in enumerate(groups):
                c1 = min(kc, c0 + 4)
                n1 = min(N, n0 + gn)
                rr_ = Res(f"{name}_{c0}_{n0}")
                w.res[(c0 // 4, n0 // gn)] = rr_
                S.dma(key + "AB"[i % 2], lambda e, c0=c0, c1=c1, n0=n0, n1=n1: e.dma_start(out=t[:, c0:c1, n0:n1], in_=srcv[:, c0:c1, n0:n1]),
                      writes=[rr_, chain[i % 2]], eng="pool", cost=8.0)
            return w

        for l in range(DEPTH):
            lam_init = 0.8 - 0.6 * math.exp(-0.3 * l)
            x_src = x_in if l == 0 else xres
            x_dst = y_out if l == DEPTH - 1 else xres

            S.dma("c_lay", lambda e, l=l: e.dma_start(out=ln1b.t[:], in_=ln1_in[l:l + 1, :].to_broadcast([128, D])), writes=[ln1b.r])
            S.dma("c_lay", lambda e, l=l: e.dma_start(out=ln2b.t[:], in_=ln2_in[l:l + 1, :].to_broadcast([128, D])), writes=[ln2b.r])
            S.dma("c_lay", lambda e, l=l: e.dma_start(out=qnw.t[:], in_=qn_in[l:l + 1, :].to_broadcast([128, 64])), writes=[qnw.r])
            S.dma("c_lay", lambda e, l=l: e.dma_start(out=knw.t[:], in_=kn_in[l:l + 1, :].to_broadcast([128, 64])), writes=[knw.r])
            S.dma("c_lay", lambda e, l=l: e.dma_start(out=gnw.t[:], in_=gn_in[l:l + 1, :].to_broadcast([128, 64])), writes=[gnw.r])
            S.dma("c_lay", lambda e, l=l: e.dma_start(out=subw.t[:], in_=sub_in[l:l + 1, :].rearrange("o e -> e o"), allow_slow_non_contiguous=True), writes=[subw.r])
            S.barrier()
            S.dve(lambda e: e.tensor_scalar(out=qnw.t[:], in0=qnw.t[:], scalar1=0.125, scalar2=None, op0=ALU.mult), reads=[qnw.r], writes=[qnw.r])
            S.dve(lambda e, li=lam_init: e.tensor_scalar(out=subw.t[:], in0=subw.t[:], scalar1=1.0 - li, scalar2=None, op0=ALU.mult), reads=[subw.r], writes=[subw.r])
            lgl = lg.t[:, l * 16:(l + 1) * 16]
            S.act(lambda e, lgl=lgl: e.activation(out=qdec.t[:, :, 0], in_=lgl[:, 0:8], func=AF.Exp, scale=idx_p1), reads=[lg.r, cst.r], writes=[qdec.r])
            S.act(lambda e, lgl=lgl: e.activation(out=qdec.t[:, :, 1], in_=lgl[:, 8:16], func=AF.Exp, scale=idx_128m), reads=[lg.r, cst.r], writes=[qdec.r])
            S.act(lambda e, lgl=lgl: e.activation(out=kdec.t[:, :, 0], in_=lgl[:, 0:8], func=AF.Exp, scale=idx_127m), reads=[lg.r, cst.r], writes=[kdec.r])
            S.act(lambda e, lgl=lgl: e.activation(out=kdec.t[:, :, 1], in_=lgl[:, 8:16], func=AF.Exp, scale=idx_p), reads=[lg.r, cst.r], writes=[kdec.r])
            S.dve(lambda e: e.tensor_scalar(out=kdec.t[:], in0=kdec.t[:], scalar1=0.125, scalar2=None, op0=ALU.mult), reads=[kdec.r], writes=[kdec.r])
            S.act(lambda e, l=l: e.activation(out=cdec.t[:], in_=lgcol.t[:, l * 8:(l + 1) * 8], func=AF.Exp, scale=128.0), reads=[lgcol.r], writes=[cdec.r])
            with ExitStack() as stc:
                tmpa = sb(stc, "dt_a", [128, 128], F32)
                tmpb = sb(stc, "dt_b", [128, 128], F32)
                for h in range(8):
                    dst = (DTe if h % 2 == 0 else DTo)
                    dsl = dst.t[:, h // 2, :]
                    S.act(lambda e, h=h: e.activation(out=tmpa.t[:], in_=relF, func=AF.Exp, scale=lgl[:, h:h + 1]), reads=[cst.r, lg.r], writes=[tmpa.r])
                    S.act(lambda e, h=h: e.activation(out=tmpb.t[:], in_=relB, func=AF.Exp, scale=lgl[:, 8 + h:9 + h]), reads=[cst.r, lg.r], writes=[tmpb.r])
                    S.dve(lambda e: e.tensor_tensor(out=tmpa.t[:], in0=tmpa.t[:], in1=mskF, op=ALU.mult), reads=[tmpa.r, cst.r], writes=[tmpa.r])
                    S.dve(lambda e: e.tensor_tensor(out=tmpb.t[:], in0=tmpb.t[:], in1=mskB, op=ALU.mult), reads=[tmpb.r, cst.r], writes=[tmpb.r])
                    S.dve(lambda e, dsl=dsl: e.tensor_tensor(out=dsl, in0=tmpa.t[:], in1=tmpb.t[:], op=ALU.add), reads=[tmpa.r, tmpb.r], writes=[dst.r])
                for r_ in range(NR):
                    S.act(lambda e, r_=r_, l=l: e.activation(out=coef.t[:, r_, :], in_=lgcol.t[:, l * 8:(l + 1) * 8], func=AF.Exp, scale=rkt.t[:, r_:r_ + 1]), reads=[lgcol.r, rkt.r], writes=[coef.r])
                    S.dve(lambda e, r_=r_: e.tensor_scalar(out=coef.t[:, r_, :], in0=coef.t[:, r_, :], scalar1=rkt.t[:, 4 + r_:5 + r_], scalar2=None, op0=ALU.mult), reads=[coef.r, rkt.r], writes=[coef.r])
                S.barrier()

            with ExitStack() as st:
                w_in = load_w(st, "w_in", w_in_in[l], D, INW, "w_in")
                P2 = range(2)
                xt = [sb(st, f"xt{i}", [128, D], F32) for i in P2]
                post = [sb(st, f"post{i}", [128, 80], F32) for i in P2]
                junk = sb(st, "junk", [128, D], BF16)
                ssx = [sb(st, f"ssx{i}", [128, 1], F32) for i in P2]
                hn = [sb(st, f"hn{i}", [128, D], BF16) for i in P2]
                hnT = [sb(st, f"hnT{i}", [128, 8, 128], BF16) for i in P2]
                qraw = [[sb(st, f"qraw{w}{i}", [128, 512], F32) for i in P2] for w in P2]
                qsq = [sb(st, f"qsq{w}", [128, 512], F32) for w in P2]
                ssg = [[sb(st, f"ssg{w}{i}", [128, 8], F32) for i in P2] for w in P2]
                qu = [sb(st, f"qu{w}", [128, 512], F32) for w in P2]
                w16 = [sb(st, f"w16{w}", [128, 8, 16], F32) for w in P2]
                rt = [[sb(st, f"rt{w}{i}", [128, 8, 8], F32) for i in range(4)] for w in P2]
                qb = [[sb(st, f"qb{w}{i}", [128, 512], BF16) for i in P2] for w in P2]
                rqf = [sb(st, f"rqf{w}", [128, 512], F32) for w in P2]
                rta = [sb(st, f"rta{w}", [128, 8, 32], F32) for w in P2]
                rtb = [sb(st, f"rtb{w}", [128, 8, 32], F32) for w in P2]
                rqb = [[sb(st, f"rqb{w}{i}", [128, 512], BF16) for i in P2] for w in P2]
                qd = [sb(st, f"qd{i}", [128, 1024], BF16) for i in P2]
                esg = [sb(st, f"esg{i}", [128, 512], F32) for i in P2]
                stg_qT = [sb(st, f"sg_qT{i}", [128, 4, 512], BF16) for i in P2]
                stg_kT = [sb(st, f"sg_kT{i}", [128, 4, 512], BF16) for i in P2]
                stg_rqT = [sb(st, f"sg_rqT{i}", [128, 4, 512], BF16) for i in P2]
                stg_rkT = [sb(st, f"sg_rkT{i}", [128, 4, 512], BF16) for i in P2]
                stg_qdT = [sb(st, f"sg_qdT{i}", [128, 8, 512], BF16) for i in P2]
                tv = [sb(st, f"tv{i}", [128, 512], BF16) for i in P2]
                trv = [sb(st, f"trv{i}", [128, 512], BF16) for i in P2]
                tgs = [sb(st, f"tgs{i}", [128, 512], BF16) for i in P2]
                tkd = [sb(st, f"tkd{i}", [128, 1024], BF16) for i in P2]
                pT = psb(st, "pT", [128, 8, 128], BF16)
                pT2 = [psb(st, f"pT2_{i}", [128, 8, 128], BF16) for i in P2]
                pj = [psb(st, f"pj{i}", [128, 512], F32) for i in range(5)]
                pjn = [0]

                def proj(cb, hnT_):
                    b = pj[pjn[0] % 5]
                    pjn[0] += 1
                    for k in range(8):
                        S.pe(lambda e, k=k, b=b, cb=cb: e.matmul(b.t[:], lhsT=hnT_.t[:, k, :], rhs=w_in.t[:, k, cb * 512:(cb + 1) * 512], start=(k == 0), stop=(k == 7)),
                             reads=[hnT_.r, w_in.r(k, cb * 512)], writes=[b.r])
                    return b

                def transpose_to(src, nblk, dst_stage, dst_ap_fn, pbuf):
                    for c in range(nblk):
                        S.pe(lambda e, c=c: e.transpose(out=pbuf.t[:, c, :], in_=src.t[:, c * 128:(c + 1) * 128], identity=ident.t[:]),
                             reads=[src.r, ident.r], writes=[pbuf.r])
                    S.act(lambda e: e.activation(out=dst_ap_fn(), in_=pbuf.t[:, 0:nblk, :], func=AF.Copy), reads=[pbuf.r], writes=[dst_stage.r])

                for t in range(NT):
                    sti = t // 4
                    j = t % 4
                    sl = sti % 2
                    par = t % 2
                    tok0 = t * 128
                    is_s = tok0 < TS
                    tl = tok0 if is_s else tok0 - TS
                    xb = xt[par]
                    pb = post[par]
                    hn_, hnT_, ssx_ = hn[par], hnT[par], ssx[par]
                    S.dma(f"a_x{par}", lambda e, xb=xb, tok0=tok0: e.dma_start(out=xb.t[:], in_=x_src[tok0:tok0 + 128, :]), writes=[xb.r])
                    S.dma(f"a_pos{par}", lambda e, pb=pb, tok0=tok0: e.dma_start(out=pb.t[:], in_=pos_in[tok0:tok0 + 128, :]), writes=[pb.r])
                    S.act(lambda e, xb=xb, ssx_=ssx_: e.activation(out=junk.t[:], in_=xb.t[:], func=AF.Square, accum_out=ssx_.t[:]), reads=[xb.r], writes=[junk.r, ssx_.r])
                    rsqrt_act(ssx_.t[:], ssx_.t[:], 1.0 / D, ssx_, ssx_)
                    S.dve(lambda e, xb=xb, hn_=hn_, ssx_=ssx_: e.scalar_tensor_tensor(out=hn_.t[:], in0=xb.t[:], scalar=ssx_.t[:, 0:1], in1=ln1b.t[:], op0=ALU.mult, op1=ALU.mult),
                          reads=[xb.r, ssx_.r, ln1b.r], writes=[hn_.r])
                    for c in range(8):
                        S.pe(lambda e, c=c, hn_=hn_: e.transpose(out=pT.t[:, c, :], in_=hn_.t[:, c * 128:(c + 1) * 128], identity=ident.t[:]), reads=[hn_.r, ident.r], writes=[pT.r])
                    S.act(lambda e, hnT_=hnT_: e.activation(out=hnT_.t[:], in_=pT.t[:], func=AF.Copy), reads=[pT.r], writes=[hnT_.r], cost=1.1)
                    cosd = pb.t[:, 0:8].unsqueeze(1).to_broadcast([128, 8, 8])
                    sind = pb.t[:, 8:16].unsqueeze(1).to_broadcast([128, 8, 8])
                    cosr = pb.t[:, 16:48].unsqueeze(1).to_broadcast([128, 8, 32])
                    sinr = pb.t[:, 48:80].unsqueeze(1).to_broadcast([128, 8, 32])
                    for which in range(2):
                        b = proj(which, hnT_)
                        nw = qnw if which == 0 else knw
                        stg = (stg_qT if which == 0 else stg_kT)[sl]
                        qraw_, qsq_, ssg_, qu_, w16_, rt_, qb_ = qraw[which][par], qsq[which], ssg[which][par], qu[which], w16[which], rt[which], qb[which][par]
                        S.act(lambda e, b=b, qraw_=qraw_: e.activation(out=qraw_.t[:], in_=b.t[:], func=AF.Copy), reads=[b.r], writes=[qraw_.r])
                        S.act(lambda e, b=b, qsq_=qsq_: e.activation(out=qsq_.t[:], in_=b.t[:], func=AF.Square), reads=[b.r], writes=[qsq_.r])
                        S.dve(lambda e, qsq_=qsq_, ssg_=ssg_: e.tensor_reduce(out=ssg_.t[:], in_=qsq_.t[:].rearrange("p (g d) -> p g d", d=64), axis=AX.X, op=ALU.add), reads=[qsq_.r], writes=[ssg_.r])
                        rsqrt_act(ssg_.t[:], ssg_.t[:], 1.0 / 64, ssg_, ssg_)
                        qr3 = qraw_.t[:].rearrange("p (g d) -> p g d", d=64)
                        qu3 = qu_.t[:].rearrange("p (g d) -> p g d", d=64)
                        qb3 = qb_.t[:].rearrange("p (g d) -> p g d", d=64)
                        S.dve(lambda e, qr3=qr3, qu3=qu3, ssg_=ssg_: e.tensor_tensor(out=qu3, in0=qr3, in1=ssg_.t[:].unsqueeze(2).to_broadcast([128, 8, 64]), op=ALU.mult), reads=[qraw_.r, ssg_.r], writes=[qu_.r])
                        S.dve(lambda e, qu3=qu3, qb3=qb3, nw=nw: e.tensor_tensor(out=qb3, in0=qu3, in1=nw.t[:].unsqueeze(1).to_broadcast([128, 8, 64]), op=ALU.mult), reads=[qu_.r, nw.r], writes=[qb_.r])
                        S.dve(lambda e, qu3=qu3, nw=nw, w16_=w16_: e.tensor_tensor(out=w16_.t[:], in0=qu3[:, :, 0:16], in1=nw.t[:, 0:16].unsqueeze(1).to_broadcast([128, 8, 16]), op=ALU.mult), reads=[qu_.r, nw.r], writes=[w16_.r], cost=0.15)
                        x1 = w16_.t[:, :, 0:8]
                        x2 = w16_.t[:, :, 8:16]
                        S.dve(lambda e, x1=x1, cosd=cosd, rt_=rt_: e.tensor_tensor(out=rt_[0].t[:], in0=x1, in1=cosd, op=ALU.mult), reads=[w16_.r, pb.r], writes=[rt_[0].r], cost=0.12)
                        S.dve(lambda e, x2=x2, sind=sind, rt_=rt_: e.tensor_tensor(out=rt_[1].t[:], in0=x2, in1=sind, op=ALU.mult), reads=[w16_.r, pb.r], writes=[rt_[1].r], cost=0.12)
                        S.dve(lambda e, x1=x1, sind=sind, rt_=rt_: e.tensor_tensor(out=rt_[2].t[:], in0=x1, in1=sind, op=ALU.mult), reads=[w16_.r, pb.r], writes=[rt_[2].r], cost=0.12)
                        S.dve(lambda e, x2=x2, cosd=cosd, rt_=rt_: e.tensor_tensor(out=rt_[3].t[:], in0=x2, in1=cosd, op=ALU.mult), reads=[w16_.r, pb.r], writes=[rt_[3].r], cost=0.12)
                        S.dve(lambda e, qb3=qb3, rt_=rt_: e.tensor_tensor(out=qb3[:, :, 0:8], in0=rt_[0].t[:], in1=rt_[1].t[:], op=ALU.subtract), reads=[rt_[0].r, rt_[1].r, qb_.r], writes=[qb_.r], cost=0.12)
                        S.dve(lambda e, qb3=qb3, rt_=rt_: e.tensor_tensor(out=qb3[:, :, 8:16], in0=rt_[2].t[:], in1=rt_[3].t[:], op=ALU.add), reads=[rt_[2].r, rt_[3].r, qb_.r], writes=[qb_.r], cost=0.12)
                        transpose_to(qb_, 4, stg, lambda stg=stg, j=j: stg.t[:, :, j * 128:(j + 1) * 128], pT2[0])
                    b = proj(2, hnT_)
                    tv_ = tv[par]
                    S.act(lambda e, b=b, tv_=tv_: e.activation(out=tv_.t[:], in_=b.t[:], func=AF.Copy), reads=[b.r], writes=[tv_.r])
                    if is_s:
                        S.dma(f"s_v{par}", lambda e, tv_=tv_, tl=tl: e.dma_start(out=dv_s[tl:tl + 128, :], in_=tv_.t[:]), reads=[tv_.r])
                    else:
                        S.dma(f"s_v{par}", lambda e, tv_=tv_, tl=tl: e.dma_start(out=dv_l[tl // TPS][tl % TPS:tl % TPS + 128, :], in_=tv_.t[:]), reads=[tv_.r])
                    for which in range(2):
                        b = proj(3 + which, hnT_)
                        rqf_, rta_, rtb_, rqb_ = rqf[which], rta[which], rtb[which], rqb[which][par]
                        b3 = b.t[:].rearrange("p (g d) -> p g d", d=64)
                        rq3 = rqf_.t[:].rearrange("p (g d) -> p g d", d=64)
                        S.dve(lambda e, b3=b3, cosr=cosr, rta_=rta_: e.tensor_tensor(out=rta_.t[:], in0=b3[:, :, 0:32], in1=cosr, op=ALU.mult), reads=[b.r, pb.r], writes=[rta_.r])
                        S.dve(lambda e, b3=b3, sinr=sinr, rtb_=rtb_: e.tensor_tensor(out=rtb_.t[:], in0=b3[:, :, 32:64], in1=sinr, op=ALU.mult), reads=[b.r, pb.r], writes=[rtb_.r])
                        S.dve(lambda e, rq3=rq3, rta_=rta_, rtb_=rtb_: e.tensor_tensor(out=rq3[:, :, 0:32], in0=rta_.t[:], in1=rtb_.t[:], op=ALU.subtract), reads=[rta_.r, rtb_.r], writes=[rqf_.r])
                        S.dve(lambda e, b3=b3, sinr=sinr, rta_=rta_: e.tensor_tensor(out=rta_.t[:], in0=b3[:, :, 0:32], in1=sinr, op=ALU.mult), reads=[b.r, pb.r], writes=[rta_.r])
                        S.dve(lambda e, b3=b3, cosr=cosr, rtb_=rtb_: e.tensor_tensor(out=rtb_.t[:], in0=b3[:, :, 32:64], in1=cosr, op=ALU.mult), reads=[b.r, pb.r], writes=[rtb_.r])
                        S.dve(lambda e, rq3=rq3, rta_=rta_, rtb_=rtb_: e.tensor_tensor(out=rq3[:, :, 32:64], in0=rta_.t[:], in1=rtb_.t[:], op=ALU.add), reads=[rta_.r, rtb_.r, rqf_.r], writes=[rqf_.r])
                        S.act(lambda e, rqb_=rqb_, rqf_=rqf_: e.activation(out=rqb_.t[:], in_=rqf_.t[:], func=AF.Copy), reads=[rqf_.r], writes=[rqb_.r], cost=0.6)
                        dec = qdec if which == 0 else kdec
                        rq4 = rqf_.t[:].rearrange("p (g d) -> p g d", d=64).unsqueeze(2).to_broadcast([128, 8, 2, 64])
                        dc4 = dec.t[:].unsqueeze(3).to_broadcast([128, 8, 2, 64])
                        if which == 0:
                            qd_ = qd[par]
                            S.pool(lambda e, rq4=rq4, dc4=dc4, qd_=qd_: e.tensor_tensor(out=qd_.t[:].rearrange("p (g a d) -> p g a d", g=8, a=2), in0=rq4, in1=dc4, op=ALU.mult),
                                  reads=[rqf_.r, dec.r], writes=[qd_.r], cost=2.5)
                            transpose_to(rqb_, 4, stg_rqT[sl], lambda sl=sl, j=j: stg_rqT[sl].t[:, :, j * 128:(j + 1) * 128], pT2[1])
                            transpose_to(qd_, 8, stg_qdT[sl], lambda sl=sl, j=j: stg_qdT[sl].t[:, :, j * 128:(j + 1) * 128], pT2[0])
                        else:
                            tkd_ = tkd[par]
                            S.pool(lambda e, rq4=rq4, dc4=dc4, tkd_=tkd_: e.tensor_tensor(out=tkd_.t[:].rearrange("p (g a d) -> p g a d", g=8, a=2), in0=rq4, in1=dc4, op=ALU.mult),
                                  reads=[rqf_.r, dec.r], writes=[tkd_.r], cost=2.5)
                            S.dma(f"s_kd{par}", lambda e, tkd_=tkd_, tok0=tok0: e.dma_start(out=rkd[tok0:tok0 + 128, :], in_=tkd_.t[:]), reads=[tkd_.r])
                            transpose_to(rqb_, 4, stg_rkT[sl], lambda sl=sl, j=j: stg_rkT[sl].t[:, :, j * 128:(j + 1) * 128], pT2[1])
                    b = proj(5, hnT_)
                    trv_ = trv[par]
                    S.act(lambda e, b=b, trv_=trv_: e.activation(out=trv_.t[:], in_=b.t[:], func=AF.Copy), reads=[b.r], writes=[trv_.r])
                    S.dma(f"s_rv{par}", lambda e, trv_=trv_, tok0=tok0: e.dma_start(out=rv[tok0:tok0 + 128, :], in_=trv_.t[:]), reads=[trv_.r])
                    b = proj(6, hnT_)
                    esg_, tgs_ = esg[par], tgs[par]
                    S.act(lambda e, b=b, esg_=esg_: e.activation(out=esg_.t[:], in_=b.t[:], func=AF.Exp, scale=-1.0), reads=[b.r], writes=[esg_.r])
                    S.act(lambda e, esg_=esg_: e.activation(out=esg_.t[:], in_=esg_.t[:], func=AF.Ln, bias=ones_f.t[:, 0:1]), reads=[esg_.r, ones_f.r], writes=[esg_.r], cost=0.6)
                    S.act(lambda e, esg_=esg_: e.activation(out=esg_.t[:], in_=esg_.t[:], func=AF.Exp, scale=-1.0), reads=[esg_.r], writes=[esg_.r], cost=0.6)
                    S.dve(lambda e, b=b, esg_=esg_, tgs_=tgs_: e.tensor_tensor(out=tgs_.t[:], in0=b.t[:], in1=esg_.t[:], op=ALU.mult), reads=[b.r, esg_.r], writes=[tgs_.r])
                    S.dma(f"s_gs{par}", lambda e, tgs_=tgs_, tok0=tok0: e.dma_start(out=rgs[tok0:tok0 + 128, :], in_=tgs_.t[:]), reads=[tgs_.r])
                    if j == 3:
                        c0 = sti * 512
                        cl = c0 if is_s else c0 - TS
                        S.dma(f"s_qT{sl}", lambda e, sl=sl, c0=c0: e.dma_start(out=dqT.rearrange("(h f) t -> f h t", f=128)[:, :, c0:c0 + 512], in_=stg_qT[sl].t[:]), reads=[stg_qT[sl].r])
                        if is_s:
                            S.dma(f"s_kT{sl}", lambda e, sl=sl, cl=cl: e.dma_start(out=dkT_s.rearrange("(h f) t -> f h t", f=128)[:, :, cl:cl + 512], in_=stg_kT[sl].t[:]), reads=[stg_kT[sl].r])
                        else:
                            for a in range(NSPL):
                                S.dma(f"s_kT{sl}", lambda e, sl=sl, cl=cl, a=a: e.dma_start(out=dkT_l[a].rearrange("(h f) t -> f h t", f=128)[:, :, cl:cl + 512], in_=stg_kT[sl].t[:, a * HPS:(a + 1) * HPS, :]), reads=[stg_kT[sl].r])
                        S.dma(f"s_rqT{sl}", lambda e, sl=sl, c0=c0: e.dma_start(out=rqT.rearrange("(h f) t -> f h t", f=128)[:, :, c0:c0 + 512], in_=stg_rqT[sl].t[:]), reads=[stg_rqT[sl].r])
                        S.dma(f"s_rkT{sl}", lambda e, sl=sl, c0=c0: e.dma_start(out=rkT.rearrange("(h f) t -> f h t", f=128)[:, :, c0:c0 + 512], in_=stg_rkT[sl].t[:]), reads=[stg_rkT[sl].r])
                        S.dma(f"s_qdT{sl}", lambda e, sl=sl, c0=c0: e.dma_start(out=rqdT.rearrange("(h f) t -> f h t", f=128)[:, :, c0:c0 + 512], in_=stg_qdT[sl].t[:]), reads=[stg_qdT[sl].r])
                S.barrier()

            RG = [[0, 1, 2, 3], [4, 5, 6, 7]]
            Rkg = Res("dkT_g")
            Rvg = Res("dv_g")
            for a in range(NSPL):
                S.dma("cc_k", lambda e, a=a: e.collective_compute("AllGather", ALU.bypass, replica_groups=RG, ins=[dkT_l[a]], outs=[dkT_g[a]]), writes=[Rkg], eng="pool", inc=1, cost=60.0)
                S.dma("cc_v", lambda e, a=a: e.collective_compute("AllGather", ALU.bypass, replica_groups=RG, ins=[dv_l[a]], outs=[dv_g[a]]), writes=[Rvg], eng="pool", inc=1, cost=60.0)

            with ExitStack() as st:
                SKMAX = max(TS, SKP)
                kTh = [sb(st, f"kTh{i}", [128, SKMAX], BF16) for i in range(2)]
                vh = [sb(st, f"vh{i}", [128, SKMAX // 128, 128], BF16) for i in range(2)]
                qA = [sb(st, f"qA{i}", [128, max(TS, TP)], BF16) for i in range(2)]
                qB = [sb(st, f"qB{i}", [128, max(TS, TP)], BF16) for i in range(2)]
                for i in range(2):
                    S.pool(lambda e, i=i: e.memset(qA[i].t[64:128, :], 0.0), writes=[qA[i].r])
                    S.pool(lambda e, i=i: e.memset(qB[i].t[0:64, :], 0.0), writes=[qB[i].r])
                NPX = 6
                pexp2 = [sb(st, f"pexp{i}", [128, 2, 512], BF16) for i in range(NPX)]
                s01 = [[sb(st, f"s01_{c}{i}", [128, 512], BF16) for i in range(2)] for c in range(2)]
                s23 = [[sb(st, f"s23_{c}{i}", [128, 512], BF16) for i in range(2)] for c in range(2)]
                s4 = [[sb(st, f"s4_{c}{i}", [128, 512], BF16) for i in range(2)] for c in range(2)]
                gcount = 0
                r0 = sb(st, "r0", [128, 512], F32)
                r1 = sb(st, "r1", [128, 512], F32)
                a0 = sb(st, "a0", [128, 512], F32)
                a1 = sb(st, "a1", [128, 512], F32)
                osq = sb(st, "osq", [128, 512], F32)
                rsn = sb(st, "rsn", [128, 512], F32)
                aout = [sb(st, f"aout{i}", [128, 512], BF16) for i in range(2)]
                sbk2 = [psb(st, f"sbk{i}", [128, 2, 512], F32) for i in range(2)]
                O = [psb(st, f"Oacc{i}", [128, 512], F32) for i in range(2)]
                L = [psb(st, f"Lacc{i}", [128, 512], F32) for i in range(2)]
                heads = [(job, h) for job in range(2) for h in range(4)]

                def load_head(hi):
                    job, h = heads[hi]
                    hb = hi % 2
                    kb_, vb_ = kTh[hb], vh[hb]
                    Tq = TS if job == 0 else TP
                    qoff = 0 if job == 0 else TS
                    if job == 0:
                        S.dma(f"b_k{hb}", lambda e, kb_=kb_, h=h: e.dma_start(out=kb_.t[:, 0:TS], in_=dkT_s[h * 128:(h + 1) * 128, :]), writes=[kb_.r])
                        S.dma(f"b_v{hb}", lambda e, vb_=vb_, h=h: e.dma_start(out=vb_.t[:, 0:TS // 128, :], in_=dv_s[:, h * 128:(h + 1) * 128].rearrange("(k p) e -> p k e", p=128)), writes=[vb_.r])
                    else:
                        ha = h // HPS
                        hl = h % HPS
                        S.dma(f"b_k{hb}", lambda e, kb_=kb_, ha=ha, hl=hl: e.dma_start(out=kb_.t[:, 0:SKP].rearrange("p (r t) -> p r t", r=NR), in_=dkT_g[ha].rearrange("(r f) t -> f r t", f=HPS * 128)[hl * 128:(hl + 1) * 128, :, :]), reads=[Rkg], writes=[kb_.r])
                        for a in range(NSPL):
                            for r_ in range(NR):
                                S.dma(f"b_v{hb}", lambda e, vb_=vb_, h=h, a=a, r_=r_: e.dma_start(
                                    out=vb_.t[:, (r_ * TP + a * TPS) // 128:(r_ * TP + (a + 1) * TPS) // 128, :],
                                    in_=dv_g[a][r_ * TPS:(r_ + 1) * TPS, h * 128:(h + 1) * 128].rearrange("(k p) e -> p k e", p=128)), reads=[Rvg], writes=[vb_.r])
                    S.dma(f"b_q{hb}", lambda e, hb=hb, h=h, qoff=qoff, Tq=Tq: e.dma_start(out=qA[hb].t[0:64, 0:Tq], in_=dqT[h * 128:h * 128 + 64, qoff:qoff + Tq]), writes=[qA[hb].r])
                    S.dma(f"b_r{hb}", lambda e, hb=hb, h=h, qoff=qoff, Tq=Tq: e.dma_start(out=qB[hb].t[64:128, 0:Tq], in_=dqT[h * 128 + 64:h * 128 + 128, qoff:qoff + Tq]), writes=[qB[hb].r])

                units = []
                for hi, (job, h) in enumerate(heads):
                    Tq = TS if job == 0 else TP
                    Sk = TS if job == 0 else SKP
                    for qc in range(Tq // 512):
                        for kb in range(Sk // 128):
                            units.append((hi, qc, kb, Sk // 128))

                def emit_qk(u):
                    hi, qc, kb, nkb = units[u]
                    hb = hi % 2
                    kb_ = kTh[hb]
                    sbuf_ = sbk2[u % 2]
                    for c in range(2):
                        qb_ = (qA if c == 0 else qB)[hb]
                        S.pe(lambda e, sbuf_=sbuf_, c=c, kb=kb, qc=qc, kb_=kb_, qb_=qb_: e.matmul(sbuf_.t[:, c, :], lhsT=kb_.t[:, kb * 128:(kb + 1) * 128],
                                                                                             rhs=qb_.t[:, qc * 512:(qc + 1) * 512], start=True, stop=True),
                             reads=[kb_.r, qb_.r], writes=[sbuf_.r])

                ocount = 0
                load_head(0)
                emit_qk(0)
                emit_qk(1)
                for u in range(len(units)):
                    hi, qc, kb, nkb = units[u]
                    job, h = heads[hi]
                    hb = hi % 2
                    vb_ = vh[hb]
                    qoff = 0 if job == 0 else TS
                    if qc == 0 and kb == 0 and hi + 1 < len(heads):
                        load_head(hi + 1)
                    pe2 = pexp2[u % NPX]
                    s2_ = sbk2[u % 2]
                    S.act(lambda e, pe2=pe2, s2_=s2_: e.activation(out=pe2.t[:], in_=s2_.t[:], func=AF.Exp), reads=[s2_.r], writes=[pe2.r], cost=1.05)
                    if u + 2 < len(units):
                        if units[u + 2][0] != hi and units[u + 2][1] == 0 and units[u + 2][2] == 0 and units[u + 2][0] + 1 < len(heads):
                            pass
                        emit_qk(u + 2)
                    gp = gcount % 2
                    for c in range(2):
                        pe_ = pexp2[u % NPX]
                        S.pe(lambda e, pe_=pe_, c=c, kb=kb, vb_=vb_, nkb=nkb: e.matmul(O[c].t[:], lhsT=vb_.t[:, kb, :], rhs=pe_.t[:, c, :], start=(kb == 0), stop=(kb == nkb - 1)),
                             reads=[vb_.r, pe_.r], writes=[O[c].r])
                        if kb % 2 == 1:
                            pp_ = pexp2[(u - 1) % NPX]
                            dst = (s01 if kb % 4 == 1 else s23)[c][gp]
                            S.dve(lambda e, dst=dst, pp_=pp_, pe_=pe_, c=c: e.tensor_tensor(out=dst.t[:], in0=pp_.t[:, c, :], in1=pe_.t[:, c, :], op=ALU.add), reads=[pp_.r, pe_.r], writes=[dst.r], cost=0.3)
                        if kb % 4 == 3:
                            a_, b_, d_ = s01[c][gp], s23[c][gp], s4[c][gp]
                            S.dve(lambda e, a_=a_, b_=b_, d_=d_: e.tensor_tensor(out=d_.t[:], in0=a_.t[:], in1=b_.t[:], op=ALU.add), reads=[a_.r, b_.r], writes=[d_.r], cost=0.3)
                            S.pe(lambda e, d_=d_, c=c, kb=kb, nkb=nkb: e.matmul(L[c].t[:], lhsT=ones_b.t[:], rhs=d_.t[:], start=(kb == 3), stop=(kb == nkb - 1)),
                                 reads=[ones_b.r, d_.r], writes=[L[c].r])
                    if kb % 4 == 3:
                        gcount += 1
                    if kb != nkb - 1:
                        continue
                    S.act(lambda e: e.activation(out=a0.t[:], in_=O[0].t[:], func=AF.Copy), reads=[O[0].r], writes=[a0.r])
                    S.dve(lambda e: e.tensor_copy(out=a1.t[:], in_=O[1].t[:]), reads=[O[1].r], writes=[a1.r])
                    S.dve(lambda e: e.reciprocal(out=r0.t[:], in_=L[0].t[:]), reads=[L[0].r], writes=[r0.r])
                    S.dve(lambda e: e.reciprocal(out=r1.t[:], in_=L[1].t[:]), reads=[L[1].r], writes=[r1.r])
                    S.dve(lambda e: e.tensor_tensor(out=a0.t[:], in0=a0.t[:], in1=r0.t[:], op=ALU.mult), reads=[a0.r, r0.r], writes=[a0.r])
                    S.dve(lambda e: e.tensor_tensor(out=a1.t[:], in0=a1.t[:], in1=r1.t[:], op=ALU.mult), reads=[a1.r, r1.r], writes=[a1.r])
                    S.dve(lambda e, l=l: e.scalar_tensor_tensor(out=a0.t[:], in0=a1.t[:], scalar=neglam.t[:, l:l + 1], in1=a0.t[:], op0=ALU.mult, op1=ALU.add),
                          reads=[a1.r, a0.r, neglam.r], writes=[a0.r])
                    S.act(lambda e: e.activation(out=osq.t[:], in_=a0.t[:], func=AF.Square), reads=[a0.r], writes=[osq.r])
                    sB = L[0]
                    S.pe(lambda e, sB=sB: e.matmul(sB.t[:], lhsT=ones_f.t[:], rhs=osq.t[:], start=True, stop=True), reads=[ones_f.r, osq.r], writes=[sB.r])
                    rsqrt_act(rsn.t[:], sB.t[:], 1.0 / 128, sB, rsn)
                    ao = aout[ocount % 2]
                    S.dve(lambda e, ao=ao: e.scalar_tensor_tensor(out=ao.t[:], in0=a0.t[:], scalar=subw.t[:, 0:1], in1=rsn.t[:], op0=ALU.mult, op1=ALU.mult),
                          reads=[a0.r, subw.r, rsn.r], writes=[ao.r])
                    tcol = qoff + qc * 512
                    S.dma(f"b_o{ocount % 2}", lambda e, ao=ao, h=h, tcol=tcol: e.dma_start(out=aT[h * 128:(h + 1) * 128, tcol:tcol + 512], in_=ao.t[:]), reads=[ao.r], eng="act")
                    ocount += 1
                S.barrier()

            with ExitStack() as st:
                SallJ = [sb(st, "Sall0", [128, TS // 128, 512], BF16), sb(st, "Sall1", [128, TP // 128, 512], BF16)]
                sttJ = [sb(st, f"stt{i}", [128, 512], F32) for i in range(2)]
                sttmpJ = [sb(st, f"sttmp{i}", [128, 512], F32) for i in range(2)]
                tg = sb(st, "tg", [128, NR, 512], F32)
                kdl = [sb(st, f"kdl{i}", [128, 4, 1024], BF16) for i in range(4)]
                rvl = [sb(st, f"rvl{i}", [128, 4, 512], BF16) for i in range(4)]
                okT = [sb(st, f"okT{i}", [128, 4, 512], BF16) for i in range(2)]
                oqT = [sb(st, f"oqT{i}", [128, 4, 512], BF16) for i in range(2)]
                oqd = [sb(st, f"oqd{i}", [128, 8, 512], BF16) for i in range(2)]
                orv = [sb(st, f"orv{i}", [128, 4, 512], BF16) for i in range(2)]
                ogs = [sb(st, f"ogs{i}", [128, 4, 512], BF16) for i in range(2)]
                PT = [sb(st, f"PT{i}", [128, 8, 128], BF16) for i in range(2)]
                rsq = sb(st, "rsq", [128, 512], F32)
                rss = sb(st, "rss", [128, 8], F32)
                rn = sb(st, "rn", [128, 512], F32)
                rr = sb(st, "rr", [128, 512], BF16)
                rTst = [sb(st, f"rTst{i}", [128, 4, 512], BF16) for i in range(2)]
                pkvJ = [psb(st, f"pkv{i}", [128, 512], F32) for i in range(2)]
                psc = [psb(st, f"psc{i}", [128, 4, 128], F32) for i in range(2)]
                po = [psb(st, f"pro{i}", [128, 512], F32) for i in range(2)]
                ptr = psb(st, "ptr", [128, 8, 128], BF16)
                Rtg = Res("st_g")
                ldn = [0]

                def sweep(job, use_init):
                    stt, sttmp, Sall = sttJ[job], sttmpJ[job], SallJ[job]
                    Tq = TS if job == 0 else TP
                    qoff = 0 if job == 0 else TS
                    n = Tq // 128
                    nsc = n // 4
                    if use_init:
                        S.dma("r_tg", lambda e: e.dma_start(out=tg.t[:], in_=st_g.rearrange("(r p) c -> p r c", p=128)), reads=[Rtg], writes=[tg.r])
                        for r_ in range(NR):
                            cb = coef.t[:, r_, :].unsqueeze(2).to_broadcast([128, 8, 64])
                            tg3 = tg.t[:, r_, :].rearrange("p (h e) -> p h e", e=64)
                            if r_ == 0:
                                S.dve(lambda e, cb=cb, tg3=tg3: e.tensor_tensor(out=stt.t[:].rearrange("p (h e) -> p h e", e=64), in0=tg3, in1=cb, op=ALU.mult), reads=[tg.r, coef.r], writes=[stt.r])
                            else:
                                S.dve(lambda e, cb=cb, tg3=tg3: e.tensor_tensor(out=sttmp.t[:].rearrange("p (h e) -> p h e", e=64), in0=tg3, in1=cb, op=ALU.mult), reads=[tg.r, coef.r], writes=[sttmp.r])
                                S.dve(lambda e: e.tensor_tensor(out=stt.t[:], in0=stt.t[:], in1=sttmp.t[:], op=ALU.add), reads=[stt.r, sttmp.r], writes=[stt.r])
                    else:
                        S.dve(lambda e: e.memset(stt.t[:], 0.0), writes=[stt.r])
                    cur = {}
                    for t in range(n):
                        tf = t
                        tb = n - 1 - t
                        bufs = {}
                        for nm, ti in (("f", tf), ("b", tb)):
                            sc = ti // 4
                            if (nm, sc) not in cur:
                                slot = job * 2 + (0 if nm == "f" else 1)
                                c0 = qoff + sc * 512
                                S.dma(f"r_kd{slot}", lambda e, slot=slot, c0=c0: e.dma_start(out=kdl[slot].t[:], in_=rkd[c0:c0 + 512, :].rearrange("(j p) c -> p j c", p=128)), writes=[kdl[slot].r])
                                S.dma(f"r_rv{slot}", lambda e, slot=slot, c0=c0: e.dma_start(out=rvl[slot].t[:], in_=rv[c0:c0 + 512, :].rearrange("(j p) c -> p j c", p=128)), writes=[rvl[slot].r])
                                cur = {k: v for k, v in cur.items() if k[0] != nm}
                                cur[(nm, sc)] = slot
                            bufs[nm] = (cur[(nm, sc)], ti % 4)
                        S.act(lambda e, tf=tf: e.activation(out=Sall.t[0:64, tf, :], in_=stt.t[0:64, :], func=AF.Copy), reads=[stt.r], writes=[Sall.r])
                        S.act(lambda e, tb=tb: e.activation(out=Sall.t[64:128, tb, :], in_=stt.t[64:128, :], func=AF.Copy), reads=[stt.r], writes=[Sall.r])
                        pk = pkvJ[job]
                        (sf, jf), (sb_, jb) = bufs["f"], bufs["b"]
                        for h in range(8):
                            kf = kdl[sf].t[:, jf, :].rearrange("p (g a d) -> p g a d", g=8, a=2)
                            kbk = kdl[sb_].t[:, jb, :].rearrange("p (g a d) -> p g a d", g=8, a=2)
                            S.pe(lambda e, pk=pk, h=h, kf=kf, sf=sf, jf=jf: e.matmul(pk.t[0:64, h * 64:(h + 1) * 64], lhsT=kf[:, h, 0, :], rhs=rvl[sf].t[:, jf, h * 64:(h + 1) * 64], start=True, stop=True),
                                 reads=[kdl[sf].r, rvl[sf].r], writes=[pk.r])
                            S.pe(lambda e, pk=pk, h=h, kbk=kbk, sb_=sb_, jb=jb: e.matmul(pk.t[64:128, h * 64:(h + 1) * 64], lhsT=kbk[:, h, 1, :], rhs=rvl[sb_].t[:, jb, h * 64:(h + 1) * 64], start=True, stop=True),
                                 reads=[kdl[sb_].r, rvl[sb_].r], writes=[pk.r])
                        S.dve(lambda e: e.tensor_tensor(out=sttmp.t[:].rearrange("p (h e) -> p h e", e=64), in0=stt.t[:].rearrange("p (h e) -> p h e", e=64),
                                                        in1=cdec.t[:].unsqueeze(2).to_broadcast([128, 8, 64]), op=ALU.mult), reads=[stt.r, cdec.r], writes=[sttmp.r])
                        S.dve(lambda e, pk=pk: e.tensor_tensor(out=stt.t[:], in0=pk.t[:], in1=sttmp.t[:], op=ALU.add), reads=[pk.r, sttmp.r], writes=[stt.r])

                def outputs(job):
                    Sall = SallJ[job]
                    Tq = TS if job == 0 else TP
                    qoff = 0 if job == 0 else TS
                    n = Tq // 128
                    for sc in range(n // 4):
                        sl = sc % 2
                        c0 = qoff + sc * 512
                        S.dma(f"o_kT{sl}", lambda e, sl=sl, c0=c0: e.dma_start(out=okT[sl].t[:], in_=rkT.rearrange("(b p) t -> p b t", p=128)[:, :, c0:c0 + 512]), writes=[okT[sl].r])
                        S.dma(f"o_qT{sl}", lambda e, sl=sl, c0=c0: e.dma_start(out=oqT[sl].t[:], in_=rqT.rearrange("(b p) t -> p b t", p=128)[:, :, c0:c0 + 512]), writes=[oqT[sl].r])
                        S.dma(f"o_qd{sl}", lambda e, sl=sl, c0=c0: e.dma_start(out=oqd[sl].t[:], in_=rqdT.rearrange("(b p) t -> p b t", p=128)[:, :, c0:c0 + 512]), writes=[oqd[sl].r])
                        S.dma(f"o_rv{sl}", lambda e, sl=sl, c0=c0: e.dma_start(out=orv[sl].t[:], in_=rv[c0:c0 + 512, :].rearrange("(j p) c -> p j c", p=128)), writes=[orv[sl].r])
                        S.dma(f"o_gs{sl}", lambda e, sl=sl, c0=c0: e.dma_start(out=ogs[sl].t[:], in_=rgs[c0:c0 + 512, :].rearrange("(j p) c -> p j c", p=128)), writes=[ogs[sl].r])
                        for j in range(4):
                            i = sc * 4 + j
                            ptb = PT[j % 2]
                            for h in range(8):
                                pb_ = psc[h % 2]
                                hp = (h % 2) * 64
                                S.pe(lambda e, pb_=pb_, h=h, hp=hp, sl=sl, j=j: e.matmul(pb_.t[:, h // 2, :], lhsT=okT[sl].t[hp:hp + 64, h // 2, j * 128:(j + 1) * 128],
                                                                                   rhs=oqT[sl].t[hp:hp + 64, h // 2, j * 128:(j + 1) * 128], start=True, stop=True),
                                     reads=[okT[sl].r, oqT[sl].r], writes=[pb_.r])
                            pt4 = ptb.t[:].rearrange("p (b a) n -> p b a n", a=2)
                            S.dve(lambda e, pt4=pt4: e.tensor_tensor(out=pt4[:, :, 0, :], in0=psc[0].t[:], in1=DTe.t[:], op=ALU.mult), reads=[psc[0].r, DTe.r], writes=[ptb.r])
                            S.dve(lambda e, pt4=pt4: e.tensor_tensor(out=pt4[:, :, 1, :], in0=psc[1].t[:], in1=DTo.t[:], op=ALU.mult), reads=[psc[1].r, DTo.r, ptb.r], writes=[ptb.r])
                            pob = po[j % 2]
                            for h in range(8):
                                S.pe(lambda e, pob=pob, h=h, ptb=ptb, sl=sl, j=j: e.matmul(pob.t[:, h * 64:(h + 1) * 64], lhsT=ptb.t[:, h, :], rhs=orv[sl].t[:, j, h * 64:(h + 1) * 64], start=True, stop=False),
                                     reads=[ptb.r, orv[sl].r], writes=[pob.r])
                                S.pe(lambda e, pob=pob, h=h, sl=sl, j=j, i=i: e.matmul(pob.t[:, h * 64:(h + 1) * 64], lhsT=oqd[sl].t[:, h, j * 128:(j + 1) * 128], rhs=Sall.t[:, i, h * 64:(h + 1) * 64], start=False, stop=True),
                                     reads=[oqd[sl].r, Sall.r], writes=[pob.r])
                            S.act(lambda e, pob=pob: e.activation(out=rsq.t[:], in_=pob.t[:], func=AF.Square), reads=[pob.r], writes=[rsq.r])
                            S.dve(lambda e: e.tensor_reduce(out=rss.t[:], in_=rsq.t[:].rearrange("p (g d) -> p g d", d=64), axis=AX.X, op=ALU.add), reads=[rsq.r], writes=[rss.r])
                            rsqrt_act(rss.t[:], rss.t[:], 1.0 / 64, rss, rss)
                            S.dve(lambda e, pob=pob: e.tensor_tensor(out=rn.t[:].rearrange("p (g d) -> p g d", d=64), in0=pob.t[:].rearrange("p (g d) -> p g d", d=64),
                                                                    in1=rss.t[:].unsqueeze(2).to_broadcast([128, 8, 64]), op=ALU.mult), reads=[pob.r, rss.r], writes=[rn.r])
                            S.dve(lambda e: e.tensor_tensor(out=rn.t[:].rearrange("p (g d) -> p g d", d=64), in0=rn.t[:].rearrange("p (g d) -> p g d", d=64),
                                                            in1=gnw.t[:].unsqueeze(1).to_broadcast([128, 8, 64]), op=ALU.mult), reads=[rn.r, gnw.r], writes=[rn.r])
                            S.dve(lambda e, sl=sl, j=j: e.tensor_tensor(out=rr.t[:], in0=rn.t[:], in1=ogs[sl].t[:, j, :], op=ALU.mult), reads=[rn.r, ogs[sl].r], writes=[rr.r])
                            for c in range(4):
                                S.pe(lambda e, c=c: e.transpose(out=ptr.t[:, c, :], in_=rr.t[:, c * 128:(c + 1) * 128], identity=ident.t[:]), reads=[rr.r, ident.r], writes=[ptr.r])
                            S.act(lambda e, sl=sl, j=j: e.activation(out=rTst[sl].t[:, :, j * 128:(j + 1) * 128], in_=ptr.t[:, 0:4, :], func=AF.Copy), reads=[ptr.r], writes=[rTst[sl].r])
                        S.dma(f"o_rT{sl}", lambda e, sl=sl, c0=c0: e.dma_start(out=rT.rearrange("(b p) t -> p b t", p=128)[:, :, c0:c0 + 512], in_=rTst[sl].t[:]), reads=[rTst[sl].r])

                sweep(1, False)
                Rstl = Res("st_l")
                S.dma("r_stl", lambda e: e.dma_start(out=st_l, in_=sttJ[1].t[:]), reads=[sttJ[1].r], writes=[Rstl])
                S.dma("cc_s", lambda e: e.collective_compute("AllGather", ALU.bypass, replica_groups=RG, ins=[st_l], outs=[st_g]), reads=[Rstl], writes=[Rtg], eng="pool", inc=1, cost=250.0)
                sweep(0, False)
                outputs(0)
                sweep(1, True)
                outputs(1)
                S.barrier()

            with ExitStack() as st:
                w_out = load_w(st, "w_out", w_out_in[l], D, D, "w_out")
                w1 = load_w(st, "w1", w1_in[l], D, DFF, "w1")
                arT = [sb(st, f"arT{i}", [128, 8, 512], BF16) for i in range(2)]
                xs_ = [sb(st, f"xs{i}", [128, 4, D], F32) for i in range(2)]
                junk = sb(st, "junk2", [128, D], BF16)
                ss2 = sb(st, "ss2", [128, 1], F32)
                h2l = [sb(st, f"h2_{i}", [128, D], BF16) for i in range(2)]
                h2T = [sb(st, f"h2T{i}", [128, 8, 512], BF16) for i in range(1)] * 2
                rl = [sb(st, f"rl{i}", [128, 512], F32) for i in range(2)]
                uTs = [sb(st, f"uTs{i}", [128, 32, 512], BF16) for i in range(1)] * 2
                pw = [psb(st, f"pw{i}", [128, 512], F32) for i in range(2)]
                ph = psb(st, "ph", [128, 8, 128], BF16)
                pu = [psb(st, f"pu{i}", [128, 512], F32) for i in range(4)]
                for s in range(NST):
                    sl = s % 2
                    c0 = s * 512
                    S.dma(f"c_a{sl}", lambda e, sl=sl, c0=c0: e.dma_start(out=arT[sl].t[:, 0:4, :], in_=aT.rearrange("(b p) t -> p b t", p=128)[:, :, c0:c0 + 512]), writes=[arT[sl].r])
                    S.dma(f"c_r{sl}", lambda e, sl=sl, c0=c0: e.dma_start(out=arT[sl].t[:, 4:8, :], in_=rT.rearrange("(b p) t -> p b t", p=128)[:, :, c0:c0 + 512]), writes=[arT[sl].r])
                    S.dma(f"c_x{sl}", lambda e, sl=sl, c0=c0: e.dma_start(out=xs_[sl].t[:], in_=x_src[c0:c0 + 512, :].rearrange("(j p) c -> p j c", p=128)), writes=[xs_[sl].r])
                    for j in range(4):
                        for cb in range(2):
                            pb_ = pw[cb]
                            for k in range(8):
                                S.pe(lambda e, pb_=pb_, k=k, cb=cb, sl=sl, j=j: e.matmul(pb_.t[:], lhsT=arT[sl].t[:, k, j * 128:(j + 1) * 128], rhs=w_out.t[:, k, cb * 512:(cb + 1) * 512], start=(k == 0), stop=(k == 7)),
                                     reads=[arT[sl].r, w_out.r(k, cb * 512)], writes=[pb_.r])
                            S.dve(lambda e, pb_=pb_, cb=cb, sl=sl, j=j: e.tensor_tensor(out=xs_[sl].t[:, j, cb * 512:(cb + 1) * 512], in0=pb_.t[:], in1=xs_[sl].t[:, j, cb * 512:(cb + 1) * 512], op=ALU.add),
                                  reads=[pb_.r, xs_[sl].r], writes=[xs_[sl].r])
                        S.act(lambda e, sl=sl, j=j: e.activation(out=junk.t[:], in_=xs_[sl].t[:, j, :], func=AF.Square, accum_out=ss2.t[:]), reads=[xs_[sl].r], writes=[junk.r, ss2.r])
                        rsqrt_act(ss2.t[:], ss2.t[:], 1.0 / D, ss2, ss2)
                        h2 = h2l[j % 2]
                        S.dve(lambda e, sl=sl, j=j, h2=h2: e.scalar_tensor_tensor(out=h2.t[:], in0=xs_[sl].t[:, j, :], scalar=ss2.t[:, 0:1], in1=ln2b.t[:], op0=ALU.mult, op1=ALU.mult),
                              reads=[xs_[sl].r, ss2.r, ln2b.r], writes=[h2.r])
                        for c in range(8):
                            S.pe(lambda e, c=c, h2=h2: e.transpose(out=ph.t[:, c, :], in_=h2.t[:, c * 128:(c + 1) * 128], identity=ident.t[:]), reads=[h2.r, ident.r], writes=[ph.r])
                        S.act(lambda e, sl=sl, j=j: e.activation(out=h2T[sl].t[:, :, j * 128:(j + 1) * 128], in_=ph.t[:], func=AF.Copy), reads=[ph.r], writes=[h2T[sl].r])
                    S.dma(f"c_xo{sl}", lambda e, sl=sl, c0=c0: e.dma_start(out=xres[c0:c0 + 512, :].rearrange("(j p) c -> p j c", p=128), in_=xs_[sl].t[:]), reads=[xs_[sl].r], eng="act")
                    for fc in range(32):
                        pb_ = pu[fc % 4]
                        for k in range(8):
                            S.pe(lambda e, pb_=pb_, k=k, fc=fc, sl=sl: e.matmul(pb_.t[:], lhsT=w1.t[:, k, fc * 128:(fc + 1) * 128], rhs=h2T[sl].t[:, k, :], start=(k == 0), stop=(k == 7)),
                                 reads=[w1.r(k, fc * 128), h2T[sl].r], writes=[pb_.r])
                        rb = rl[fc % 2]
                        S.act(lambda e, pb_=pb_, rb=rb: e.activation(out=rb.t[:], in_=pb_.t[:], func=AF.Relu), reads=[pb_.r], writes=[rb.r])
                        S.dve(lambda e, rb=rb, fc=fc, sl=sl: e.tensor_tensor(out=uTs[sl].t[:, fc, :], in0=rb.t[:], in1=rb.t[:], op=ALU.mult), reads=[rb.r], writes=[uTs[sl].r])
                    S.dma(f"c_u{sl}", lambda e, sl=sl, c0=c0: e.dma_start(out=uT.rearrange("(c p) t -> p c t", p=128)[:, :, c0:c0 + 512], in_=uTs[sl].t[:]), reads=[uTs[sl].r], eng="act")
                S.barrier()

            with ExitStack() as st:
                w2 = load_w(st, "w2", w2_in[l], DFF, D, "w2", gn=1024, korder=True)
                wg = load_w(st, "wg", wg_in[l], D, D, "wg")
                wp = load_w(st, "wp", wp_in[l], PLE, D, "wp")
                uTl = [sb(st, f"uTl{i}", [128, 32, 512], BF16) for i in range(2)]
                xs_ = [sb(st, f"xc{i}", [128, 4, D], F32) for i in range(1)] * 2
                pl_ = [sb(st, f"pl{i}", [128, 4, PLE], F32) for i in range(1)] * 2
                x2bl = [sb(st, f"x2b{i}", [128, D], BF16) for i in range(2)]
                x2Tl = [sb(st, f"x2T{i}", [128, 8, 128], BF16) for i in range(2)]
                pbfl = [sb(st, f"pbf{i}", [128, PLE], BF16) for i in range(2)]
                ppTl = [sb(st, f"ppT{i}", [128, 2, 128], BF16) for i in range(2)]
                sg = [sb(st, f"sg{i}", [128, 512], F32) for i in range(2)]
                pm = [psb(st, f"pm{i}", [128, 512], F32) for i in range(2)]
                pg = [psb(st, f"pg{i}", [128, 512], F32) for i in range(2)]
                pq = [psb(st, f"pq{i}", [128, 512], F32) for i in range(2)]
                px = psb(st, "px", [128, 8, 128], BF16)
                pp2 = psb(st, "pp2", [128, 8, 128], BF16)
                for s in range(NST):
                    sl = s % 2
                    c0 = s * 512
                    S.dma(f"d_u{sl}", lambda e, sl=sl, c0=c0: e.dma_start(out=uTl[sl].t[:], in_=uT.rearrange("(c p) t -> p c t", p=128)[:, :, c0:c0 + 512]), writes=[uTl[sl].r])
                    S.dma(f"d_x{sl}", lambda e, sl=sl, c0=c0: e.dma_start(out=xs_[sl].t[:], in_=xres[c0:c0 + 512, :].rearrange("(j p) c -> p j c", p=128)), writes=[xs_[sl].r])
                    S.dma(f"d_p{sl}", lambda e, sl=sl, c0=c0, l=l: e.dma_start(out=pl_[sl].t[:], in_=p_in[l, c0:c0 + 512, :].rearrange("(j p) c -> p j c", p=128)), writes=[pl_[sl].r])
                    for j in range(4):
                        for cb in range(2):
                            pb_ = pm[cb]
                            for fc in range(32):
                                S.pe(lambda e, pb_=pb_, fc=fc, cb=cb, sl=sl, j=j: e.matmul(pb_.t[:], lhsT=uTl[sl].t[:, fc, j * 128:(j + 1) * 128], rhs=w2.t[:, fc, cb * 512:(cb + 1) * 512], start=(fc == 0), stop=(fc == 31)),
                                     reads=[uTl[sl].r, w2.r(fc, cb * 512)], writes=[pb_.r])
                            S.dve(lambda e, pb_=pb_, cb=cb, sl=sl, j=j: e.tensor_tensor(out=xs_[sl].t[:, j, cb * 512:(cb + 1) * 512], in0=pb_.t[:], in1=xs_[sl].t[:, j, cb * 512:(cb + 1) * 512], op=ALU.add),
                                  reads=[pb_.r, xs_[sl].r], writes=[xs_[sl].r])
                        x2b, x2T, pbf, ppT = x2bl[j % 2], x2Tl[j % 2], pbfl[j % 2], ppTl[j % 2]
                        S.act(lambda e, sl=sl, j=j, x2b=x2b: e.activation(out=x2b.t[:], in_=xs_[sl].t[:, j, :], func=AF.Copy), reads=[xs_[sl].r], writes=[x2b.r], cost=1.1)
                        for c in range(8):
                            S.pe(lambda e, c=c, x2b=x2b: e.transpose(out=px.t[:, c, :], in_=x2b.t[:, c * 128:(c + 1) * 128], identity=ident.t[:]), reads=[x2b.r, ident.r], writes=[px.r])
                        S.act(lambda e, x2T=x2T: e.activation(out=x2T.t[:], in_=px.t[:], func=AF.Copy), reads=[px.r], writes=[x2T.r], cost=1.1)
                        S.pool(lambda e, sl=sl, j=j, pbf=pbf: e.tensor_copy(out=pbf.t[:], in_=pl_[sl].t[:, j, :]), reads=[pl_[sl].r], writes=[pbf.r])
                        for c in range(2):
                            S.pe(lambda e, c=c, pbf=pbf: e.transpose(out=pp2.t[:, c, :], in_=pbf.t[:, c * 128:(c + 1) * 128], identity=ident.t[:]), reads=[pbf.r, ident.r], writes=[pp2.r])
                        S.act(lambda e, ppT=ppT: e.activation(out=ppT.t[:], in_=pp2.t[:, 0:2, :], func=AF.Copy), reads=[pp2.r], writes=[ppT.r])
                        for cb in range(2):
                            g_ = pg[cb]
                            q_ = pq[cb]
                            for k in range(8):
                                S.pe(lambda e, g_=g_, k=k, cb=cb: e.matmul(g_.t[:], lhsT=x2T.t[:, k, :], rhs=wg.t[:, k, cb * 512:(cb + 1) * 512], start=(k == 0), stop=(k == 7)),
                                     reads=[x2T.r, wg.r(k, cb * 512)], writes=[g_.r])
                            for k in range(2):
                                S.pe(lambda e, q_=q_, k=k, cb=cb: e.matmul(q_.t[:], lhsT=ppT.t[:, k, :], rhs=wp.t[:, k, cb * 512:(cb + 1) * 512], start=(k == 0), stop=(k == 1)),
                                     reads=[ppT.r, wp.r(k, cb * 512)], writes=[q_.r])
                            sgb = sg[cb]
                            S.act(lambda e, g_=g_, sgb=sgb: e.activation(out=sgb.t[:], in_=g_.t[:], func=AF.Exp, scale=-1.0), reads=[g_.r], writes=[sgb.r])
                            S.dve(lambda e, sgb=sgb: e.tensor_scalar(out=sgb.t[:], in0=sgb.t[:], scalar1=1.0, scalar2=None, op0=ALU.add), reads=[sgb.r], writes=[sgb.r])
                            S.dve(lambda e, sgb=sgb: e.reciprocal(out=sgb.t[:], in_=sgb.t[:]), reads=[sgb.r], writes=[sgb.r])
                            S.dve(lambda e, sgb=sgb, q_=q_: e.tensor_tensor(out=sgb.t[:], in0=q_.t[:], in1=sgb.t[:], op=ALU.mult), reads=[q_.r, sgb.r], writes=[sgb.r])
                            S.dve(lambda e, sgb=sgb, cb=cb, sl=sl, j=j: e.tensor_tensor(out=xs_[sl].t[:, j, cb * 512:(cb + 1) * 512], in0=sgb.t[:], in1=xs_[sl].t[:, j, cb * 512:(cb + 1) * 512], op=ALU.add),
                                  reads=[sgb.r, xs_[sl].r], writes=[xs_[sl].r])
                    S.dma(f"d_xo{sl}", lambda e, sl=sl, c0=c0: e.dma_start(out=x_dst[c0:c0 + 512, :].rearrange("(j p) c -> p j c", p=128), in_=xs_[sl].t[:]), reads=[xs_[sl].r], eng="act")
                S.barrier()

        if DEBUG:
            for nm, src in (("rgs", rgs), ("rv", rv), ("dv_s", dv_s), ("dqT", dqT), ("aT", aT), ("rT", rT), ("xres", xres), ("uT", uT), ("rqT", rqT), ("rkd", rkd), ("rqdT", rqdT), ("rkT", rkT), ("dkT_s", dkT_s)):
                dbg = dram("dbg_" + nm, list(src.shape), src.dtype, "ExternalOutput")
                S.dma("dbg", lambda e, dbg=dbg, src=src: e.dma_start(out=dbg, in_=src))
        S.emit(top)
        nc._n_ops = len(S.ops)
        nc._n_sem = S.nsem
    return nc


ROPE_THETA = 500000.0
RET_THETA = 10000.0


def host_tables(TS, TP, rank):
    posv = np.concatenate([np.arange(TS), rank * TP + np.arange(TP)]).astype(np.float32)
    inv_d = (np.float32(1.0) / (np.float32(ROPE_THETA) ** (np.arange(0, 16, 2, dtype=np.float32) / np.float32(16)))).astype(np.float32)
    inv_r = (np.float32(1.0) / (np.float32(RET_THETA) ** (np.arange(0, 64, 2, dtype=np.float32) / np.float32(64)))).astype(np.float32)
    ang_d = (posv[:, None] * inv_d[None, :]).astype(np.float32)
    ang_r = (posv[:, None] * inv_r[None, :]).astype(np.float32)
    pos = np.concatenate([np.cos(ang_d), np.sin(ang_d), np.cos(ang_r), np.sin(ang_r)], axis=1).astype(np.float32)
    m = np.arange(128)[:, None].astype(np.float32)
    n = np.arange(128)[None, :].astype(np.float32)
    relF = np.maximum(n - m, 0.0)
    mskF = (n >= m).astype(np.float32) * 0.125
    relB = np.maximum(m - n, 0.0)
    mskB = (m > n).astype(np.float32) * 0.125
    p = np.arange(128, dtype=np.float32)[:, None]
    cst = np.concatenate([relF, mskF, relB, mskB, p + 1, 128 - p, 127 - p, p], axis=1).astype(np.float32)
    rkt = np.zeros((128, 8), np.float32)
    for r in range(NR):
        if r < rank:
            rkt[0:64, r] = TP * (rank - 1 - r)
            rkt[0:64, 4 + r] = 1.0
        if r > rank:
            rkt[64:128, r] = TP * (r - rank - 1)
            rkt[64:128, 4 + r] = 1.0
    return pos, cst, rkt


_NC_CACHE = {}


def run_cores(inputs, TS, TP, DEPTH):
    key = (TS, TP, DEPTH)
    if key not in _NC_CACHE:
        _NC_CACHE[key] = build(TS, TP, DEPTH)
    nc = _NC_CACHE[key]
    f = lambda a: np.ascontiguousarray(np.asarray(a, dtype=np.float32))
    xp = f(inputs["x_prompt"])
    xs = f(inputs["x_sample"])
    pp = f(inputs["p_prompt"])
    ps = f(inputs["p_sample"])
    shared = {
        "ln1_w": f(inputs["ln1_w"]), "w_in": f(inputs["w_in"]), "diff_q_norm": f(inputs["diff_q_norm"]),
        "diff_k_norm": f(inputs["diff_k_norm"]), "diff_lambda": f(inputs["diff_lambda"]).reshape(1, DEPTH * 256),
        "diff_subln": f(inputs["diff_subln"]), "ret_decay_logit": f(inputs["ret_decay_logit"]).reshape(1, DEPTH * 16),
        "ret_gn": f(inputs["ret_gn"]), "w_out": f(inputs["w_out"]), "ln2_w": f(inputs["ln2_w"]),
        "w_mlp1": f(inputs["w_mlp1"]), "w_mlp2": f(inputs["w_mlp2"]), "w_ple_gate": f(inputs["w_ple_gate"]),
        "w_ple_proj": f(inputs["w_ple_proj"]),
    }
    in_maps = []
    for c in range(8):
        g, r = c // NR, c % NR
        pos, cst, rkt = host_tables(TS, TP, r)
        m = dict(shared)
        m["x"] = np.ascontiguousarray(np.concatenate([xs[c], xp[g, r * TP:(r + 1) * TP]], axis=0))
        m["p"] = np.ascontiguousarray(np.concatenate([ps[:, c], pp[:, g, r * TP:(r + 1) * TP]], axis=1))
        m["pos"] = pos
        m["cst"] = cst
        m["rkt"] = rkt
        in_maps.append(m)
    res = run_bass_kernel_spmd(nc, in_maps, core_ids=list(range(8)))
    y_s = np.stack([np.asarray(res.results[c]["y"][:TS]) for c in range(8)], axis=0)
    y_p = np.stack([np.concatenate([np.asarray(res.results[g * NR + r]["y"][TS:]) for r in range(NR)], axis=0) for g in range(2)], axis=0)
    return y_p.astype(np.float32), y_s.astype(np.float32)


def kernel(**inputs):
    return run_cores(inputs, 4096, 2048, 4)
```

```python
import math
import types
import numpy as np
import concourse.bass as bass
import concourse.mybir as mybir
from concourse.bass_utils import run_bass_kernel_spmd
from contextlib import ExitStack

F32 = mybir.dt.float32
BF16 = mybir.dt.bfloat16
AF = mybir.ActivationFunctionType
ALU = mybir.AluOpType
AX = mybir.AxisListType

COMPUTE = ("pe", "act", "dve", "pool")
ALL_ENG = ("pe", "act", "dve", "pool", "sp")
SAME_ENG_SYNC = True

D = 1024
INW = 3584
DFF = 4096
PLE = 256
EPS = 1e-6
NR = 4


class Res:
    __slots__ = ("name", "w", "rs")

    def __init__(self, name=""):
        self.name = name
        self.w = None
        self.rs = []


class Op:
    __slots__ = ("eng", "fn", "deps", "mark", "val", "key", "is_mm", "inc", "cost", "idx", "fin", "succ", "npend", "ready")

    def __init__(self, eng, fn, key=None, is_mm=False, cost=None):
        self.eng = eng
        self.fn = fn
        self.deps = []
        self.mark = False
        self.val = 0
        self.key = key
        self.is_mm = is_mm
        self.inc = 16
        self.cost = cost
        self.idx = 0
        self.fin = 0.0
        self.succ = None
        self.npend = 0
        self.ready = 0.0


def _freeze(fn):
    if fn is None or fn.__closure__ is None:
        return fn
    cells = []
    for c in fn.__closure__:
        try:
            cells.append(types.CellType(c.cell_contents))
        except ValueError:
            cells.append(c)
    return types.FunctionType(fn.__code__, fn.__globals__, fn.__name__, fn.__defaults__, tuple(cells))


DEF_COST = {"pe": 0.27, "act": 0.45, "dve": 0.60, "pool": 0.90, "sp": 0.30}
SLAT = 0.5
XLAT = 0.45
DMA_LAT = 3.0
RESCHEDULE = True


class Sched:
    def __init__(self, nc):
        self.nc = nc
        self.ops = []

    def op(self, eng, fn, reads=(), writes=(), key=None, is_mm=False, cost=None):
        o = Op(eng, _freeze(fn), key, is_mm, cost)
        deps = o.deps
        for r in reads:
            if r.w is not None:
                deps.append(r.w)
        for w in writes:
            if w.w is not None:
                deps.append(w.w)
            deps.extend(w.rs)
        for r in reads:
            r.rs.append(o)
        for w in writes:
            w.w = o
            w.rs = []
        o.idx = len(self.ops)
        self.ops.append(o)
        return o

    def pe(self, fn, reads=(), writes=(), cost=None):
        return self.op("pe", fn, reads, writes, is_mm=True, cost=cost)

    def act(self, fn, reads=(), writes=(), cost=None):
        return self.op("act", fn, reads, writes, cost=cost)

    def dve(self, fn, reads=(), writes=(), cost=None):
        return self.op("dve", fn, reads, writes, cost=cost)

    def pool(self, fn, reads=(), writes=(), cost=None):
        return self.op("pool", fn, reads, writes, cost=cost)

    def dma(self, key, fn, reads=(), writes=(), eng="sp", inc=16, cost=None):
        o = self.op(eng, fn, reads, writes, key=key, cost=cost)
        o.inc = inc
        return o

    def barrier(self):
        o = Op(None, None)
        o.idx = len(self.ops)
        self.ops.append(o)

    @staticmethod
    def _skip(p, o):
        return p.key is None and p.eng == o.eng and (not SAME_ENG_SYNC or (p.is_mm and o.is_mm))

    def _schedule_segment(self, seg):
        import heapq
        inseg = set(id(o) for o in seg)
        for o in seg:
            o.succ = []
            o.npend = 0
            o.ready = 0.0
        for o in seg:
            for p in o.deps:
                if id(p) in inseg:
                    p.succ.append(o)
                    o.npend += 1
        free = {e: 0.0 for e in ALL_ENG}
        heaps = {e: [] for e in ALL_ENG}
        for o in seg:
            if o.npend == 0:
                heapq.heappush(heaps[o.eng], (0.0, o.idx, o))
        order = {e: [] for e in ALL_ENG}
        remaining = len(seg)
        while remaining:
            best = None
            for e in ALL_ENG:
                h = heaps[e]
                if not h:
                    continue
                t_free = free[e]
                cands = []
                while h and h[0][0] <= t_free:
                    cands.append(heapq.heappop(h))
                if cands:
                    c = min(cands, key=lambda x: x[1])
                    for x in cands:
                        if x is not c:
                            heapq.heappush(h, x)
                    heapq.heappush(h, c)
                    start = t_free
                    pick = c
                else:
                    pick = h[0]
                    start = pick[0]
                if best is None or start < best[0] or (start == best[0] and pick[1] < best[2][1]):
                    best = (start, e, pick)
            start, e, pick = best
            h = heaps[e]
            h.remove(pick)
            heapq.heapify(h)
            o = pick[2]
            cost = o.cost if o.cost is not None else DEF_COST[e]
            if o.key is not None:
                free[e] = start + DEF_COST["sp"]
                o.fin = start + (cost if o.cost is not None else DMA_LAT)
            else:
                free[e] = start + cost
                o.fin = free[e]
            order[e].append(o)
            remaining -= 1
            for q in o.succ:
                if q.eng == o.eng and o.key is None:
                    lat = 0.0 if (o.is_mm and q.is_mm) else SLAT
                else:
                    lat = XLAT
                r = o.fin + lat
                if r > q.ready:
                    q.ready = r
                q.npend -= 1
                if q.npend == 0:
                    heapq.heappush(heaps[q.eng], (q.ready, q.idx, q))
        return order

    def emit(self, stack):
        nc = self.nc
        segs = []
        cur = []
        for o in self.ops:
            if o.eng is None:
                if cur:
                    segs.append(cur)
                    cur = []
            else:
                cur.append(o)
        if cur:
            segs.append(cur)
        streams = {e: [] for e in ALL_ENG}
        for seg in segs:
            if RESCHEDULE:
                order = self._schedule_segment(seg)
            else:
                order = {e: [o for o in seg if o.eng == e] for e in ALL_ENG}
            lastops = {}
            for e in ALL_ENG:
                for o in order[e]:
                    lastops[o.key if o.key is not None else e] = o
            for e in ALL_ENG:
                streams[e].extend(order[e])
            deps = list(lastops.values())
            for e in ALL_ENG:
                b = Op(e, None)
                b.deps = list(deps)
                streams[e].append(b)
        for e in ALL_ENG:
            for o in streams[e]:
                for p in o.deps:
                    if p.key is None and not self._skip(p, o):
                        p.mark = True
        kcnt = {}
        for e in ALL_ENG:
            cnt = 0
            for o in streams[e]:
                if o.fn is None:
                    continue
                if o.key is not None:
                    kcnt[o.key] = kcnt.get(o.key, 0) + o.inc
                    o.val = kcnt[o.key]
                elif o.mark:
                    cnt += 1
                    o.val = cnt
        sems = {}
        for e in COMPUTE:
            sems[e] = stack.enter_context(nc.semaphore("s_" + e))
        for k in kcnt:
            sems[k] = stack.enter_context(nc.semaphore("d_" + k))
        self.nsem = len(sems)
        block = stack.enter_context(nc.Block())

        def run(engname, eng):
            seen = {}
            for o in streams[engname]:
                need = {}
                for p in o.deps:
                    if self._skip(p, o):
                        continue
                    sk = p.eng if p.key is None else p.key
                    if seen.get(sk, 0) >= p.val:
                        continue
                    if need.get(sk, 0) < p.val:
                        need[sk] = p.val
                for sk, v in need.items():
                    eng.wait_ge(sems[sk], v)
                    seen[sk] = v
                if o.fn is None:
                    continue
                ins = o.fn(eng)
                if o.key is not None:
                    ins.then_inc(sems[o.key], o.inc)
                elif o.mark:
                    ins.then_inc(sems[o.eng], 1)
            if engname == "sp":
                for k, v in kcnt.items():
                    if seen.get(k, 0) < v:
                        eng.wait_ge(sems[k], v)

        @block.tensor
        def _(e):
            run("pe", e)

        @block.scalar
        def _(e):
            run("act", e)

        @block.vector
        def _(e):
            run("dve", e)

        @block.gpsimd
        def _(e):
            run("pool", e)

        @block.sync
        def _(e):
            run("sp", e)


class Buf:
    __slots__ = ("t", "r")

    def __init__(self, t, name):
        self.t = t
        self.r = Res(name)


def build(TS, TP, DEPTH, DEBUG=False):
    T = TS + TP
    SKP = NR * TP
    NT = T // 128
    NST = T // 512
    assert TS % 512 == 0 and TP % 512 == 0
    nc = bass.Bass("TRN2", target_bir_lowering=False)

    def dram(name, shape, dtype, kind="Internal"):
        return nc.dram_tensor(name, shape, dtype, kind=kind).ap()

    x_in = dram("x", [T, D], F32, "ExternalInput")
    p_in = dram("p", [DEPTH, T, PLE], F32, "ExternalInput")
    pos_in = dram("pos", [T, 80], F32, "ExternalInput")
    cst_in = dram("cst", [128, 516], F32, "ExternalInput")
    rkt_in = dram("rkt", [128, 8], F32, "ExternalInput")
    ln1_in = dram("ln1_w", [DEPTH, D], F32, "ExternalInput")
    w_in_in = dram("w_in", [DEPTH, D, INW], F32, "ExternalInput")
    qn_in = dram("diff_q_norm", [DEPTH, 64], F32, "ExternalInput")
    kn_in = dram("diff_k_norm", [DEPTH, 64], F32, "ExternalInput")
    lam_in = dram("diff_lambda", [1, DEPTH * 256], F32, "ExternalInput")
    sub_in = dram("diff_subln", [DEPTH, 128], F32, "ExternalInput")
    dec_in = dram("ret_decay_logit", [1, DEPTH * 16], F32, "ExternalInput")
    gn_in = dram("ret_gn", [DEPTH, 64], F32, "ExternalInput")
    w_out_in = dram("w_out", [DEPTH, D, D], F32, "ExternalInput")
    ln2_in = dram("ln2_w", [DEPTH, D], F32, "ExternalInput")
    w1_in = dram("w_mlp1", [DEPTH, D, DFF], F32, "ExternalInput")
    w2_in = dram("w_mlp2", [DEPTH, DFF, D], F32, "ExternalInput")
    wg_in = dram("w_ple_gate", [DEPTH, D, D], F32, "ExternalInput")
    wp_in = dram("w_ple_proj", [DEPTH, PLE, D], F32, "ExternalInput")
    y_out = dram("y", [T, D], F32, "ExternalOutput")

    xres = dram("xres", [T, D], F32)
    dqT = dram("dqT", [512, T], BF16)
    dkT_s = dram("dkT_s", [512, TS], BF16)
    NSPL = max(1, (512 * TP * 2) // (1 << 20))
    HPS = 4 // NSPL
    TPS = TP // NSPL
    dkT_l = [dram(f"dkT_l{a}", [HPS * 128, TP], BF16) for a in range(NSPL)]
    dkT_g = [dram(f"dkT_g{a}", [NR * HPS * 128, TP], BF16) for a in range(NSPL)]
    dv_s = dram("dv_s", [TS, 512], BF16)
    dv_l = [dram(f"dv_l{a}", [TPS, 512], BF16) for a in range(NSPL)]
    dv_g = [dram(f"dv_g{a}", [NR * TPS, 512], BF16) for a in range(NSPL)]
    rqT = dram("rqT", [512, T], BF16)
    rkT = dram("rkT", [512, T], BF16)
    rqdT = dram("rqdT", [8 * 128, T], BF16)
    rkd = dram("rkd", [T, 1024], BF16)
    rv = dram("rv", [T, 512], BF16)
    rgs = dram("rgs", [T, 512], BF16)
    aT = dram("aT", [512, T], BF16)
    rT = dram("rT", [512, T], BF16)
    uT = dram("uT", [DFF, T], BF16)
    st_l = dram("st_l", [128, 512], F32)
    st_g = dram("st_g", [NR * 128, 512], F32)

    with ExitStack() as top:
        S = Sched(nc)

        uid = [0]

        def sb(st, name, shape, dt):
            uid[0] += 1
            return Buf(st.enter_context(nc.sbuf_tensor(f"sb{uid[0]}_{name}", shape, dt)), name)

        def psb(st, name, shape, dt):
            uid[0] += 1
            return Buf(st.enter_context(nc.psum_tensor(f"ps{uid[0]}_{name}", shape, dt)), name)

        ident = sb(top, "ident", [128, 128], BF16)
        identf = sb(top, "identf", [128, 128], F32)
        ones_b = sb(top, "ones_b", [128, 128], BF16)
        ones_f = sb(top, "ones_f", [128, 128], F32)
        cst = sb(top, "cst", [128, 516], F32)
        rkt = sb(top, "rkt", [128, 8], F32)
        lg = sb(top, "lg", [128, DEPTH * 16], F32)
        lgcol = sb(top, "lgcol", [128, DEPTH * 8], F32)
        lam = sb(top, "lam", [128, DEPTH], F32)
        neglam = sb(top, "neglam", [128, DEPTH], F32)
        epsc = sb(top, "epsc", [128, 1], F32)
        ln1b = sb(top, "ln1b", [128, D], F32)
        ln2b = sb(top, "ln2b", [128, D], F32)
        qnw = sb(top, "qnw", [128, 64], F32)
        knw = sb(top, "knw", [128, 64], F32)
        gnw = sb(top, "gnw", [128, 64], F32)
        subw = sb(top, "subw", [128, 1], F32)
        qdec = sb(top, "qdec", [128, 8, 2], F32)
        kdec = sb(top, "kdec", [128, 8, 2], F32)
        cdec = sb(top, "cdec", [128, 8], F32)
        DTe = sb(top, "DTe", [128, 4, 128], F32)
        DTo = sb(top, "DTo", [128, 4, 128], F32)
        coef = sb(top, "coef", [128, 4, 8], F32)

        relF = cst.t[:, 0:128]
        mskF = cst.t[:, 128:256]
        relB = cst.t[:, 256:384]
        mskB = cst.t[:, 384:512]
        idx_p1 = cst.t[:, 512:513]
        idx_128m = cst.t[:, 513:514]
        idx_127m = cst.t[:, 514:515]
        idx_p = cst.t[:, 515:516]

        S.dma("c_init", lambda e: e.dma_start(out=cst.t[:], in_=cst_in), writes=[cst.r])
        S.dma("c_init", lambda e: e.dma_start(out=rkt.t[:], in_=rkt_in), writes=[rkt.r])
        S.pool(lambda e: e.memset(identf.t[:], 0.0), writes=[identf.r])
        S.pool(lambda e: e.affine_select(out=identf.t[:], in_=identf.t[:], compare_op=ALU.not_equal, fill=1.0, base=0,
                                         pattern=[[-1, 128]], channel_multiplier=1), reads=[identf.r], writes=[identf.r])
        S.dve(lambda e: e.tensor_copy(out=ident.t[:], in_=identf.t[:]), reads=[identf.r], writes=[ident.r])
        S.dve(lambda e: e.memset(ones_b.t[:], 1.0), writes=[ones_b.r])
        S.dve(lambda e: e.memset(ones_f.t[:], 1.0), writes=[ones_f.r])
        S.dve(lambda e: e.memset(epsc.t[:], EPS), writes=[epsc.r])

        with ExitStack() as st0:
            NL = DEPTH * 16
            xx = sb(st0, "ls_x", [128, NL], F32)
            ax = sb(st0, "ls_ax", [128, NL], F32)
            uu = sb(st0, "ls_u", [128, NL], F32)
            ss_ = sb(st0, "ls_s", [128, NL], F32)
            s2 = sb(st0, "ls_s2", [128, NL], F32)
            pl = sb(st0, "ls_pl", [128, NL], F32)
            lp = sb(st0, "lp", [128, DEPTH * 256], F32)
            S.dma("c_init", lambda e: e.dma_start(out=xx.t[:], in_=dec_in.to_broadcast([128, NL])), writes=[xx.r])
            S.dma("c_init", lambda e: e.dma_start(out=lp.t[:], in_=lam_in.to_broadcast([128, DEPTH * 256])), writes=[lp.r])
            S.barrier()
            S.dve(lambda e: e.tensor_scalar(out=ax.t[:], in0=xx.t[:], scalar1=-1.0, scalar2=None, op0=ALU.mult), reads=[xx.r], writes=[ax.r])
            S.dve(lambda e: e.tensor_tensor(out=ax.t[:], in0=ax.t[:], in1=xx.t[:], op=ALU.max), reads=[xx.r, ax.r], writes=[ax.r])
            S.act(lambda e: e.activation(out=uu.t[:], in_=ax.t[:], func=AF.Exp, scale=-1.0), reads=[ax.r], writes=[uu.r])
            S.dve(lambda e: e.tensor_scalar(out=ss_.t[:], in0=uu.t[:], scalar1=2.0, scalar2=None, op0=ALU.add), reads=[uu.r], writes=[ss_.r])
            S.dve(lambda e: e.reciprocal(out=ss_.t[:], in_=ss_.t[:]), reads=[ss_.r], writes=[ss_.r])
            S.dve(lambda e: e.tensor_tensor(out=ss_.t[:], in0=ss_.t[:], in1=uu.t[:], op=ALU.mult), reads=[ss_.r, uu.r], writes=[ss_.r])
            S.dve(lambda e: e.tensor_tensor(out=s2.t[:], in0=ss_.t[:], in1=ss_.t[:], op=ALU.mult), reads=[ss_.r], writes=[s2.r])
            S.dve(lambda e: e.tensor_scalar(out=pl.t[:], in0=s2.t[:], scalar1=1.0 / 13, scalar2=1.0 / 11, op0=ALU.mult, op1=ALU.add), reads=[s2.r], writes=[pl.r])
            for cc in (1.0 / 9, 1.0 / 7, 1.0 / 5, 1.0 / 3, 1.0):
                S.dve(lambda e: e.tensor_tensor(out=pl.t[:], in0=pl.t[:], in1=s2.t[:], op=ALU.mult), reads=[pl.r, s2.r], writes=[pl.r])
                S.dve(lambda e, cc=cc: e.tensor_scalar(out=pl.t[:], in0=pl.t[:], scalar1=cc, scalar2=None, op0=ALU.add), reads=[pl.r], writes=[pl.r])
            S.dve(lambda e: e.tensor_tensor(out=pl.t[:], in0=pl.t[:], in1=ss_.t[:], op=ALU.mult), reads=[pl.r, ss_.r], writes=[pl.r])
            S.dve(lambda e: e.tensor_scalar(out=ax.t[:], in0=xx.t[:], scalar1=0.0, scalar2=None, op0=ALU.min), reads=[xx.r], writes=[ax.r])
            S.dve(lambda e: e.scalar_tensor_tensor(out=lg.t[:], in0=pl.t[:], scalar=-2.0, in1=ax.t[:], op0=ALU.mult, op1=ALU.add), reads=[pl.r, ax.r], writes=[lg.r])
            lg4 = lg.t[:].rearrange("p (l a h) -> p l a h", l=DEPTH, a=2)
            lgc3 = lgcol.t[:].rearrange("p (l h) -> p l h", l=DEPTH)
            S.dve(lambda e: e.tensor_copy(out=lgc3[0:64], in_=lg4[0:64, :, 0, :]), reads=[lg.r], writes=[lgcol.r])
            S.dve(lambda e: e.tensor_copy(out=lgc3[64:128], in_=lg4[64:128, :, 1, :]), reads=[lg.r], writes=[lgcol.r])
            pr = sb(st0, "lpr", [128, DEPTH * 2 * 64], F32)
            sm = sb(st0, "lsm", [128, DEPTH * 2], F32)
            lp5 = lp.t[:].rearrange("p (l a b d) -> p l a b d", l=DEPTH, a=2, b=2)
            pr4 = pr.t[:].rearrange("p (l a d) -> p l a d", l=DEPTH, a=2)
            S.dve(lambda e: e.tensor_tensor(out=pr4, in0=lp5[:, :, :, 0, :], in1=lp5[:, :, :, 1, :], op=ALU.mult), reads=[lp.r], writes=[pr.r])
            S.dve(lambda e: e.tensor_reduce(out=sm.t[:], in_=pr.t[:].rearrange("p (g d) -> p g d", d=64), axis=AX.X, op=ALU.add), reads=[pr.r], writes=[sm.r])
            S.act(lambda e: e.activation(out=sm.t[:], in_=sm.t[:], func=AF.Exp), reads=[sm.r], writes=[sm.r])
            sm3 = sm.t[:].rearrange("p (l a) -> p l a", a=2)
            S.dve(lambda e: e.tensor_tensor(out=lam.t[:], in0=sm3[:, :, 0], in1=sm3[:, :, 1], op=ALU.subtract), reads=[sm.r], writes=[lam.r])
            for l in range(DEPTH):
                li = 0.8 - 0.6 * math.exp(-0.3 * l)
                S.dve(lambda e, l=l, li=li: e.tensor_scalar(out=lam.t[:, l:l + 1], in0=lam.t[:, l:l + 1], scalar1=li, scalar2=None, op0=ALU.add), reads=[lam.r], writes=[lam.r])
            S.dve(lambda e: e.tensor_scalar(out=neglam.t[:], in0=lam.t[:], scalar1=-1.0, scalar2=None, op0=ALU.mult), reads=[lam.r], writes=[neglam.r])
            S.barrier()

        def rsqrt_act(out_ap, in_ap, scale, rbuf, wbuf):
            S.act(lambda e: e.activation(out=out_ap, in_=in_ap, func=AF.Ln, scale=scale, bias=epsc.t[0:out_ap.shape[0], 0:1]), reads=[rbuf.r, epsc.r], writes=[wbuf.r])
            S.act(lambda e: e.activation(out=out_ap, in_=out_ap, func=AF.Exp, scale=-0.5), reads=[wbuf.r], writes=[wbuf.r])

        class WGroups:
            def __init__(self, t, gn):
                self.t = t
                self.gn = gn
                self.res = {}

            def r(self, k, n0):
                return self.res[(k // 4, n0 // self.gn)]

        def load_w(st, name, src, K, N, key, gn=512, korder=False):
            kc = K // 128
            gn = min(gn, N, 2048)
            uid[0] += 1
            t = st.enter_context(nc.sbuf_tensor(f"sb{uid[0]}_{name}", [128, kc, N], BF16))
            w = WGroups(t, gn)
            srcv = src.rearrange("(c p) n -> p c n", p=128)
            chain = [Res(name + "_chA"), Res(name + "_chB")]
            groups = [(c0, n0) for n0 in range(0, N, gn) for c0 in range(0, kc, 4)]
            if korder:
                groups = [(c0, n0) for c0 in range(0, kc, 4) for n0 in range(0, N, gn)]
            for i, (c0, n0) in enumerate(groups):
                c1 = min(kc, c0 + 4)
                n1 = min(N, n0 + gn)
                rr_ = Res(f"{name}_{c0}_{n0}")
                w.res[(c0 // 4, n0 // gn)] = rr_
                S.dma(key + "AB"[i % 2], lambda e, c0=c0, c1=c1, n0=n0, n1=n1: e.dma_start(out=t[:, c0:c1, n0:n1], in_=srcv[:, c0:c1, n0:n1]),
                      writes=[rr_, chain[i % 2]], eng="pool", cost=8.0)
            return w

        for l in range(DEPTH):
            lam_init = 0.8 - 0.6 * math.exp(-0.3 * l)
            x_src = x_in if l == 0 else xres
            x_dst = y_out if l == DEPTH - 1 else xres

            S.dma("c_lay", lambda e, l=l: e.dma_start(out=ln1b.t[:], in_=ln1_in[l:l + 1, :].to_broadcast([128, D])), writes=[ln1b.r])
            S.dma("c_lay", lambda e, l=l: e.dma_start(out=ln2b.t[:], in_=ln2_in[l:l + 1, :].to_broadcast([128, D])), writes=[ln2b.r])
            S.dma("c_lay", lambda e, l=l: e.dma_start(out=qnw.t[:], in_=qn_in[l:l + 1, :].to_broadcast([128, 64])), writes=[qnw.r])
            S.dma("c_lay", lambda e, l=l: e.dma_start(out=knw.t[:], in_=kn_in[l:l + 1, :].to_broadcast([128, 64])), writes=[knw.r])
            S.dma("c_lay", lambda e, l=l: e.dma_start(out=gnw.t[:], in_=gn_in[l:l + 1, :].to_broadcast([128, 64])), writes=[gnw.r])
            S.dma("c_lay", lambda e, l=l: e.dma_start(out=subw.t[:], in_=sub_in[l:l + 1, :].rearrange("o e -> e o"), allow_slow_non_contiguous=True), writes=[subw.r])
            S.barrier()
            S.dve(lambda e: e.tensor_scalar(out=qnw.t[:], in0=qnw.t[:], scalar1=0.125, scalar2=None, op0=ALU.mult), reads=[qnw.r], writes=[qnw.r])
            S.dve(lambda e, li=lam_init: e.tensor_scalar(out=subw.t[:], in0=subw.t[:], scalar1=1.0 - li, scalar2=None, op0=ALU.mult), reads=[subw.r], writes=[subw.r])
            lgl = lg.t[:, l * 16:(l + 1) * 16]
            S.act(lambda e, lgl=lgl: e.activation(out=qdec.t[:, :, 0], in_=lgl[:, 0:8], func=AF.Exp, scale=idx_p1), reads=[lg.r, cst.r], writes=[qdec.r])
            S.act(lambda e, lgl=lgl: e.activation(out=qdec.t[:, :, 1], in_=lgl[:, 8:16], func=AF.Exp, scale=idx_128m), reads=[lg.r, cst.r], writes=[qdec.r])
            S.act(lambda e, lgl=lgl: e.activation(out=kdec.t[:, :, 0], in_=lgl[:, 0:8], func=AF.Exp, scale=idx_127m), reads=[lg.r, cst.r], writes=[kdec.r])
            S.act(lambda e, lgl=lgl: e.activation(out=kdec.t[:, :, 1], in_=lgl[:, 8:16], func=AF.Exp, scale=idx_p), reads=[lg.r, cst.r], writes=[kdec.r])
            S.dve(lambda e: e.tensor_scalar(out=kdec.t[:], in0=kdec.t[:], scalar1=0.125, scalar2=None, op0=ALU.mult), reads=[kdec.r], writes=[kdec.r])
            S.act(lambda e, l=l: e.activation(out=cdec.t[:], in_=lgcol.t[:, l * 8:(l + 1) * 8], func=AF.Exp, scale=128.0), reads=[lgcol.r], writes=[cdec.r])
            with ExitStack() as stc:
                tmpa = sb(stc, "dt_a", [128, 128], F32)
                tmpb = sb(stc, "dt_b", [128, 128], F32)
                for h in range(8):
                    dst = (DTe if h % 2 == 0 else DTo)
                    dsl = dst.t[:, h // 2, :]
                    S.act(lambda e, h=h: e.activation(out=tmpa.t[:], in_=relF, func=AF.Exp, scale=lgl[:, h:h + 1]), reads=[cst.r, lg.r], writes=[tmpa.r])
                    S.act(lambda e, h=h: e.activation(out=tmpb.t[:], in_=relB, func=AF.Exp, scale=lgl[:, 8 + h:9 + h]), reads=[cst.r, lg.r], writes=[tmpb.r])
                    S.dve(lambda e: e.tensor_tensor(out=tmpa.t[:], in0=tmpa.t[:], in1=mskF, op=ALU.mult), reads=[tmpa.r, cst.r], writes=[tmpa.r])
                    S.dve(lambda e: e.tensor_tensor(out=tmpb.t[:], in0=tmpb.t[:], in1=mskB, op=ALU.mult), reads=[tmpb.r, cst.r], writes=[tmpb.r])
                    S.dve(lambda e, dsl=dsl: e.tensor_tensor(out=dsl, in0=tmpa.t[:], in1=tmpb.t[:], op=ALU.add), reads=[tmpa.r, tmpb.r], writes=[dst.r])
                for r_ in range(NR):
                    S.act(lambda e, r_=r_, l=l: e.activation(out=coef.t[:, r_, :], in_=lgcol.t[:, l * 8:(l + 1) * 8], func=AF.Exp, scale=rkt.t[:, r_:r_ + 1]), reads=[lgcol.r, rkt.r], writes=[coef.r])
                    S.dve(lambda e, r_=r_: e.tensor_scalar(out=coef.t[:, r_, :], in0=coef.t[:, r_, :], scalar1=rkt.t[:, 4 + r_:5 + r_], scalar2=None, op0=ALU.mult), reads=[coef.r, rkt.r], writes=[coef.r])
                S.barrier()

            with ExitStack() as st:
                w_in = load_w(st, "w_in", w_in_in[l], D, INW, "w_in")
                P2 = range(2)
                xt = [sb(st, f"xt{i}", [128, D], F32) for i in P2]
                post = [sb(st, f"post{i}", [128, 80], F32) for i in P2]
                junk = sb(st, "junk", [128, D], BF16)
                ssx = [sb(st, f"ssx{i}", [128, 1], F32) for i in P2]
                hn = [sb(st, f"hn{i}", [128, D], BF16) for i in P2]
                hnT = [sb(st, f"hnT{i}", [128, 8, 128], BF16) for i in P2]
                qraw = [[sb(st, f"qraw{w}{i}", [128, 512], F32) for i in P2] for w in P2]
                qsq = [sb(st, f"qsq{w}", [128, 512], F32) for w in P2]
                ssg = [[sb(st, f"ssg{w}{i}", [128, 8], F32) for i in P2] for w in P2]
                qu = [sb(st, f"qu{w}", [128, 512], F32) for w in P2]
                w16 = [sb(st, f"w16{w}", [128, 8, 16], F32) for w in P2]
                rt = [[sb(st, f"rt{w}{i}", [128, 8, 8], F32) for i in range(4)] for w in P2]
                qb = [[sb(st, f"qb{w}{i}", [128, 512], BF16) for i in P2] for w in P2]
                rqf = [sb(st, f"rqf{w}", [128, 512], F32) for w in P2]
                rta = [sb(st, f"rta{w}", [128, 8, 32], F32) for w in P2]
                rtb = [sb(st, f"rtb{w}", [128, 8, 32], F32) for w in P2]
                rqb = [[sb(st, f"rqb{w}{i}", [128, 512], BF16) for i in P2] for w in P2]
                qd = [sb(st, f"qd{i}", [128, 1024], BF16) for i in P2]
                esg = [sb(st, f"esg{i}", [128, 512], F32) for i in P2]
                stg_qT = [sb(st, f"sg_qT{i}", [128, 4, 512], BF16) for i in P2]
                stg_kT = [sb(st, f"sg_kT{i}", [128, 4, 512], BF16) for i in P2]
                stg_rqT = [sb(st, f"sg_rqT{i}", [128, 4, 512], BF16) for i in P2]
                stg_rkT = [sb(st, f"sg_rkT{i}", [128, 4, 512], BF16) for i in P2]
                stg_qdT = [sb(st, f"sg_qdT{i}", [128, 8, 512], BF16) for i in P2]
                tv = [sb(st, f"tv{i}", [128, 512], BF16) for i in P2]
                trv = [sb(st, f"trv{i}", [128, 512], BF16) for i in P2]
                tgs = [sb(st, f"tgs{i}", [128, 512], BF16) for i in P2]
                tkd = [sb(st, f"tkd{i}", [128, 1024], BF16) for i in P2]
                pT = psb(st, "pT", [128, 8, 128], BF16)
                pT2 = [psb(st, f"pT2_{i}", [128, 8, 128], BF16) for i in P2]
                pj = [psb(st, f"pj{i}", [128, 512], F32) for i in range(5)]
                pjn = [0]

                def proj(cb, hnT_):
                    b = pj[pjn[0] % 5]
                    pjn[0] += 1
                    for k in range(8):
                        S.pe(lambda e, k=k, b=b, cb=cb: e.matmul(b.t[:], lhsT=hnT_.t[:, k, :], rhs=w_in.t[:, k, cb * 512:(cb + 1) * 512], start=(k == 0), stop=(k == 7)),
                             reads=[hnT_.r, w_in.r(k, cb * 512)], writes=[b.r])
                    return b

                def transpose_to(src, nblk, dst_stage, dst_ap_fn, pbuf):
                    for c in range(nblk):
                        S.pe(lambda e, c=c: e.transpose(out=pbuf.t[:, c, :], in_=src.t[:, c * 128:(c + 1) * 128], identity=ident.t[:]),
                             reads=[src.r, ident.r], writes=[pbuf.r])
                    S.act(lambda e: e.activation(out=dst_ap_fn(), in_=pbuf.t[:, 0:nblk, :], func=AF.Copy), reads=[pbuf.r], writes=[dst_stage.r])

                for t in range(NT):
                    sti = t // 4
                    j = t % 4
                    sl = sti % 2
                    par = t % 2
                    tok0 = t * 128
                    is_s = tok0 < TS
                    tl = tok0 if is_s else tok0 - TS
                    xb = xt[par]
                    pb = post[par]
                    hn_, hnT_, ssx_ = hn[par], hnT[par], ssx[par]
                    S.dma(f"a_x{par}", lambda e, xb=xb, tok0=tok0: e.dma_start(out=xb.t[:], in_=x_src[tok0:tok0 + 128, :]), writes=[xb.r])
                    S.dma(f"a_pos{par}", lambda e, pb=pb, tok0=tok0: e.dma_start(out=pb.t[:], in_=pos_in[tok0:tok0 + 128, :]), writes=[pb.r])
                    S.act(lambda e, xb=xb, ssx_=ssx_: e.activation(out=junk.t[:], in_=xb.t[:], func=AF.Square, accum_out=ssx_.t[:]), reads=[xb.r], writes=[junk.r, ssx_.r])
                    rsqrt_act(ssx_.t[:], ssx_.t[:], 1.0 / D, ssx_, ssx_)
                    S.dve(lambda e, xb=xb, hn_=hn_, ssx_=ssx_: e.scalar_tensor_tensor(out=hn_.t[:], in0=xb.t[:], scalar=ssx_.t[:, 0:1], in1=ln1b.t[:], op0=ALU.mult, op1=ALU.mult),
                          reads=[xb.r, ssx_.r, ln1b.r], writes=[hn_.r])
                    for c in range(8):
                        S.pe(lambda e, c=c, hn_=hn_: e.transpose(out=pT.t[:, c, :], in_=hn_.t[:, c * 128:(c + 1) * 128], identity=ident.t[:]), reads=[hn_.r, ident.r], writes=[pT.r])
                    S.act(lambda e, hnT_=hnT_: e.activation(out=hnT_.t[:], in_=pT.t[:], func=AF.Copy), reads=[pT.r], writes=[hnT_.r], cost=1.1)
                    cosd = pb.t[:, 0:8].unsqueeze(1).to_broadcast([128, 8, 8])
                    sind = pb.t[:, 8:16].unsqueeze(1).to_broadcast([128, 8, 8])
                    cosr = pb.t[:, 16:48].unsqueeze(1).to_broadcast([128, 8, 32])
                    sinr = pb.t[:, 48:80].unsqueeze(1).to_broadcast([128, 8, 32])
                    for which in range(2):
                        b = proj(which, hnT_)
                        nw = qnw if which == 0 else knw
                        stg = (stg_qT if which == 0 else stg_kT)[sl]
                        qraw_, qsq_, ssg_, qu_, w16_, rt_, qb_ = qraw[which][par], qsq[which], ssg[which][par], qu[which], w16[which], rt[which], qb[which][par]
                        S.act(lambda e, b=b, qraw_=qraw_: e.activation(out=qraw_.t[:], in_=b.t[:], func=AF.Copy), reads=[b.r], writes=[qraw_.r])
                        S.act(lambda e, b=b, qsq_=qsq_: e.activation(out=qsq_.t[:], in_=b.t[:], func=AF.Square), reads=[b.r], writes=[qsq_.r])
                        S.dve(lambda e, qsq_=qsq_, ssg_=ssg_: e.tensor_reduce(out=ssg_.t[:], in_=qsq_.t[:].rearrange("p (g d) -> p g d", d=64), axis=AX.X, op=ALU.add), reads=[qsq_.r], writes=[ssg_.r])
                        rsqrt_act(ssg_.t[:], ssg_.t[:], 1.0 / 64, ssg_, ssg_)
                        qr3 = qraw_.t[:].rearrange("p (g d) -> p g d", d=64)
                        qu3 = qu_.t[:].rearrange("p (g d) -> p g d", d=64)
                        qb3 = qb_.t[:].rearrange("p (g d) -> p g d", d=64)
                        S.dve(lambda e, qr3=qr3, qu3=qu3, ssg_=ssg_: e.tensor_tensor(out=qu3, in0=qr3, in1=ssg_.t[:].unsqueeze(2).to_broadcast([128, 8, 64]), op=ALU.mult), reads=[qraw_.r, ssg_.r], writes=[qu_.r])
                        S.dve(lambda e, qu3=qu3, qb3=qb3, nw=nw: e.tensor_tensor(out=qb3, in0=qu3, in1=nw.t[:].unsqueeze(1).to_broadcast([128, 8, 64]), op=ALU.mult), reads=[qu_.r, nw.r], writes=[qb_.r])
                        S.dve(lambda e, qu3=qu3, nw=nw, w16_=w16_: e.tensor_tensor(out=w16_.t[:], in0=qu3[:, :, 0:16], in1=nw.t[:, 0:16].unsqueeze(1).to_broadcast([128, 8, 16]), op=ALU.mult), reads=[qu_.r, nw.r], writes=[w16_.r], cost=0.15)
                        x1 = w16_.t[:, :, 0:8]
                        x2 = w16_.t[:, :, 8:16]
                        S.dve(lambda e, x1=x1, cosd=cosd, rt_=rt_: e.tensor_tensor(out=rt_[0].t[:], in0=x1, in1=cosd, op=ALU.mult), reads=[w16_.r, pb.r], writes=[rt_[0].r], cost=0.12)
                        S.dve(lambda e, x2=x2, sind=sind, rt_=rt_: e.tensor_tensor(out=rt_[1].t[:], in0=x2, in1=sind, op=ALU.mult), reads=[w16_.r, pb.r], writes=[rt_[1].r], cost=0.12)
                        S.dve(lambda e, x1=x1, sind=sind, rt_=rt_: e.tensor_tensor(out=rt_[2].t[:], in0=x1, in1=sind, op=ALU.mult), reads=[w16_.r, pb.r], writes=[rt_[2].r], cost=0.12)
                        S.dve(lambda e, x2=x2, cosd=cosd, rt_=rt_: e.tensor_tensor(out=rt_[3].t[:], in0=x2, in1=cosd, op=ALU.mult), reads=[w16_.r, pb.r], writes=[rt_[3].r], cost=0.12)
                        S.dve(lambda e, qb3=qb3, rt_=rt_: e.tensor_tensor(out=qb3[:, :, 0:8], in0=rt_[0].t[:], in1=rt_[1].t[:], op=ALU.subtract), reads=[rt_[0].r, rt_[1].r, qb_.r], writes=[qb_.r], cost=0.12)
                        S.dve(lambda e, qb3=qb3, rt_=rt_: e.tensor_tensor(out=qb3[:, :, 8:16], in0=rt_[2].t[:], in1=rt_[3].t[:], op=ALU.add), reads=[rt_[2].r, rt_[3].r, qb_.r], writes=[qb_.r], cost=0.12)
                        transpose_to(qb_, 4, stg, lambda stg=stg, j=j: stg.t[:, :, j * 128:(j + 1) * 128], pT2[0])
                    b = proj(2, hnT_)
                    tv_ = tv[par]
                    S.act(lambda e, b=b, tv_=tv_: e.activation(out=tv_.t[:], in_=b.t[:], func=AF.Copy), reads=[b.r], writes=[tv_.r])
                    if is_s:
                        S.dma(f"s_v{par}", lambda e, tv_=tv_, tl=tl: e.dma_start(out=dv_s[tl:tl + 128, :], in_=tv_.t[:]), reads=[tv_.r])
                    else:
                        S.dma(f"s_v{par}", lambda e, tv_=tv_, tl=tl: e.dma_start(out=dv_l[tl // TPS][tl % TPS:tl % TPS + 128, :], in_=tv_.t[:]), reads=[tv_.r])
                    for which in range(2):
                        b = proj(3 + which, hnT_)
                        rqf_, rta_, rtb_, rqb_ = rqf[which], rta[which], rtb[which], rqb[which][par]
                        b3 = b.t[:].rearrange("p (g d) -> p g d", d=64)
                        rq3 = rqf_.t[:].rearrange("p (g d) -> p g d", d=64)
                        S.dve(lambda e, b3=b3, cosr=cosr, rta_=rta_: e.tensor_tensor(out=rta_.t[:], in0=b3[:, :, 0:32], in1=cosr, op=ALU.mult), reads=[b.r, pb.r], writes=[rta_.r])
                        S.dve(lambda e, b3=b3, sinr=sinr, rtb_=rtb_: e.tensor_tensor(out=rtb_.t[:], in0=b3[:, :, 32:64], in1=sinr, op=ALU.mult), reads=[b.r, pb.r], writes=[rtb_.r])
                        S.dve(lambda e, rq3=rq3, rta_=rta_, rtb_=rtb_: e.tensor_tensor(out=rq3[:, :, 0:32], in0=rta_.t[:], in1=rtb_.t[:], op=ALU.subtract), reads=[rta_.r, rtb_.r], writes=[rqf_.r])
                        S.dve(lambda e, b3=b3, sinr=sinr, rta_=rta_: e.tensor_tensor(out=rta_.t[:], in0=b3[:, :, 0:32], in1=sinr, op=ALU.mult), reads=[b.r, pb.r], writes=[rta_.r])
                        S.dve(lambda e, b3=b3, cosr=cosr, rtb_=rtb_: e.tensor_tensor(out=rtb_.t[:], in0=b3[:, :, 32:64], in1=cosr, op=ALU.mult), reads=[b.r, pb.r], writes=[rtb_.r])
                        S.dve(lambda e, rq3=rq3, rta_=rta_, rtb_=rtb_: e.tensor_tensor(out=rq3[:, :, 32:64], in0=rta_.t[:], in1=rtb_.t[:], op=ALU.add), reads=[rta_.r, rtb_.r, rqf_.r], writes=[rqf_.r])
                        S.act(lambda e, rqb_=rqb_, rqf_=rqf_: e.activation(out=rqb_.t[:], in_=rqf_.t[:], func=AF.Copy), reads=[rqf_.r], writes=[rqb_.r], cost=0.6)
                        dec = qdec if which == 0 else kdec
                        rq4 = rqf_.t[:].rearrange("p (g d) -> p g d", d=64).unsqueeze(2).to_broadcast([128, 8, 2, 64])
                        dc4 = dec.t[:].unsqueeze(3).to_broadcast([128, 8, 2, 64])
                        if which == 0:
                            qd_ = qd[par]
                            S.pool(lambda e, rq4=rq4, dc4=dc4, qd_=qd_: e.tensor_tensor(out=qd_.t[:].rearrange("p (g a d) -> p g a d", g=8, a=2), in0=rq4, in1=dc4, op=ALU.mult),
                                  reads=[rqf_.r, dec.r], writes=[qd_.r], cost=2.5)
                            transpose_to(rqb_, 4, stg_rqT[sl], lambda sl=sl, j=j: stg_rqT[sl].t[:, :, j * 128:(j + 1) * 128], pT2[1])
                            transpose_to(qd_, 8, stg_qdT[sl], lambda sl=sl, j=j: stg_qdT[sl].t[:, :, j * 128:(j + 1) * 128], pT2[0])
                        else:
                            tkd_ = tkd[par]
                            S.pool(lambda e, rq4=rq4, dc4=dc4, tkd_=tkd_: e.tensor_tensor(out=tkd_.t[:].rearrange("p (g a d) -> p g a d", g=8, a=2), in0=rq4, in1=dc4, op=ALU.mult),
                                  reads=[rqf_.r, dec.r], writes=[tkd_.r], cost=2.5)
                            S.dma(f"s_kd{par}", lambda e, tkd_=tkd_, tok0=tok0: e.dma_start(out=rkd[tok0:tok0 + 128, :], in_=tkd_.t[:]), reads=[tkd_.r])
                            transpose_to(rqb_, 4, stg_rkT[sl], lambda sl=sl, j=j: stg_rkT[sl].t[:, :, j * 128:(j + 1) * 128], pT2[1])
                    b = proj(5, hnT_)
                    trv_ = trv[par]
                    S.act(lambda e, b=b, trv_=trv_: e.activation(out=trv_.t[:], in_=b.t[:], func=AF.Copy), reads=[b.r], writes=[trv_.r])
                    S.dma(f"s_rv{par}", lambda e, trv_=trv_, tok0=tok0: e.dma_start(out=rv[tok0:tok0 + 128, :], in_=trv_.t[:]), reads=[trv_.r])
                    b = proj(6, hnT_)
                    esg_, tgs_ = esg[par], tgs[par]
                    S.act(lambda e, b=b, esg_=esg_: e.activation(out=esg_.t[:], in_=b.t[:], func=AF.Exp, scale=-1.0), reads=[b.r], writes=[esg_.r])
                    S.act(lambda e, esg_=esg_: e.activation(out=esg_.t[:], in_=esg_.t[:], func=AF.Ln, bias=ones_f.t[:, 0:1]), reads=[esg_.r, ones_f.r], writes=[esg_.r], cost=0.6)
                    S.act(lambda e, esg_=esg_: e.activation(out=esg_.t[:], in_=esg_.t[:], func=AF.Exp, scale=-1.0), reads=[esg_.r], writes=[esg_.r], cost=0.6)
                    S.dve(lambda e, b=b, esg_=esg_, tgs_=tgs_: e.tensor_tensor(out=tgs_.t[:], in0=b.t[:], in1=esg_.t[:], op=ALU.mult), reads=[b.r, esg_.r], writes=[tgs_.r])
                    S.dma(f"s_gs{par}", lambda e, tgs_=tgs_, tok0=tok0: e.dma_start(out=rgs[tok0:tok0 + 128, :], in_=tgs_.t[:]), reads=[tgs_.r])
                    if j == 3:
                        c0 = sti * 512
                        cl = c0 if is_s else c0 - TS
                        S.dma(f"s_qT{sl}", lambda e, sl=sl, c0=c0: e.dma_start(out=dqT.rearrange("(h f) t -> f h t", f=128)[:, :, c0:c0 + 512], in_=stg_qT[sl].t[:]), reads=[stg_qT[sl].r])
                        if is_s:
                            S.dma(f"s_kT{sl}", lambda e, sl=sl, cl=cl: e.dma_start(out=dkT_s.rearrange("(h f) t -> f h t", f=128)[:, :, cl:cl + 512], in_=stg_kT[sl].t[:]), reads=[stg_kT[sl].r])
                        else:
                            for a in range(NSPL):
                                S.dma(f"s_kT{sl}", lambda e, sl=sl, cl=cl, a=a: e.dma_start(out=dkT_l[a].rearrange("(h f) t -> f h t", f=128)[:, :, cl:cl + 512], in_=stg_kT[sl].t[:, a * HPS:(a + 1) * HPS, :]), reads=[stg_kT[sl].r])
                        S.dma(f"s_rqT{sl}", lambda e, sl=sl, c0=c0: e.dma_start(out=rqT.rearrange("(h f) t -> f h t", f=128)[:, :, c0:c0 + 512], in_=stg_rqT[sl].t[:]), reads=[stg_rqT[sl].r])
                        S.dma(f"s_rkT{sl}", lambda e, sl=sl, c0=c0: e.dma_start(out=rkT.rearrange("(h f) t -> f h t", f=128)[:, :, c0:c0 + 512], in_=stg_rkT[sl].t[:]), reads=[stg_rkT[sl].r])
                        S.dma(f"s_qdT{sl}", lambda e, sl=sl, c0=c0: e.dma_start(out=rqdT.rearrange("(h f) t -> f h t", f=128)[:, :, c0:c0 + 512], in_=stg_qdT[sl].t[:]), reads=[stg_qdT[sl].r])
                S.barrier()

            RG = [[0, 1, 2, 3], [4, 5, 6, 7]]
            Rkg = Res("dkT_g")
            Rvg = Res("dv_g")
            for a in range(NSPL):
                S.dma("cc_k", lambda e, a=a: e.collective_compute("AllGather", ALU.bypass, replica_groups=RG, ins=[dkT_l[a]], outs=[dkT_g[a]]), writes=[Rkg], eng="pool", inc=1, cost=60.0)
                S.dma("cc_v", lambda e, a=a: e.collective_compute("AllGather", ALU.bypass, replica_groups=RG, ins=[dv_l[a]], outs=[dv_g[a]]), writes=[Rvg], eng="pool", inc=1, cost=60.0)

            with ExitStack() as st:
                SKMAX = max(TS, SKP)
                kTh = [sb(st, f"kTh{i}", [128, SKMAX], BF16) for i in range(2)]
                vh = [sb(st, f"vh{i}", [128, SKMAX // 128, 128], BF16) for i in range(2)]
                qA = [sb(st, f"qA{i}", [128, max(TS, TP)], BF16) for i in range(2)]
                qB = [sb(st, f"qB{i}", [128, max(TS, TP)], BF16) for i in range(2)]
                for i in range(2):
                    S.pool(lambda e, i=i: e.memset(qA[i].t[64:128, :], 0.0), writes=[qA[i].r])
                    S.pool(lambda e, i=i: e.memset(qB[i].t[0:64, :], 0.0), writes=[qB[i].r])
                NPX = 6
                pexp2 = [sb(st, f"pexp{i}", [128, 2, 512], BF16) for i in range(NPX)]
                s01 = [[sb(st, f"s01_{c}{i}", [128, 512], BF16) for i in range(2)] for c in range(2)]
                s23 = [[sb(st, f"s23_{c}{i}", [128, 512], BF16) for i in range(2)] for c in range(2)]
                s4 = [[sb(st, f"s4_{c}{i}", [128, 512], BF16) for i in range(2)] for c in range(2)]
                gcount = 0
                r0 = sb(st, "r0", [128, 512], F32)
                r1 = sb(st, "r1", [128, 512], F32)
                a0 = sb(st, "a0", [128, 512], F32)
                a1 = sb(st, "a1", [128, 512], F32)
                osq = sb(st, "osq", [128, 512], F32)
                rsn = sb(st, "rsn", [128, 512], F32)
                aout = [sb(st, f"aout{i}", [128, 512], BF16) for i in range(2)]
                sbk2 = [psb(st, f"sbk{i}", [128, 2, 512], F32) for i in range(2)]
                O = [psb(st, f"Oacc{i}", [128, 512], F32) for i in range(2)]
                L = [psb(st, f"Lacc{i}", [128, 512], F32) for i in range(2)]
                heads = [(job, h) for job in range(2) for h in range(4)]

                def load_head(hi):
                    job, h = heads[hi]
                    hb = hi % 2
                    kb_, vb_ = kTh[hb], vh[hb]
                    Tq = TS if job == 0 else TP
                    qoff = 0 if job == 0 else TS
                    if job == 0:
                        S.dma(f"b_k{hb}", lambda e, kb_=kb_, h=h: e.dma_start(out=kb_.t[:, 0:TS], in_=dkT_s[h * 128:(h + 1) * 128, :]), writes=[kb_.r])
                        S.dma(f"b_v{hb}", lambda e, vb_=vb_, h=h: e.dma_start(out=vb_.t[:, 0:TS // 128, :], in_=dv_s[:, h * 128:(h + 1) * 128].rearrange("(k p) e -> p k e", p=128)), writes=[vb_.r])
                    else:
                        ha = h // HPS
                        hl = h % HPS
                        S.dma(f"b_k{hb}", lambda e, kb_=kb_, ha=ha, hl=hl: e.dma_start(out=kb_.t[:, 0:SKP].rearrange("p (r t) -> p r t", r=NR), in_=dkT_g[ha].rearrange("(r f) t -> f r t", f=HPS * 128)[hl * 128:(hl + 1) * 128, :, :]), reads=[Rkg], writes=[kb_.r])
                        for a in range(NSPL):
                            for r_ in range(NR):
                                S.dma(f"b_v{hb}", lambda e, vb_=vb_, h=h, a=a, r_=r_: e.dma_start(
                                    out=vb_.t[:, (r_ * TP + a * TPS) // 128:(r_ * TP + (a + 1) * TPS) // 128, :],
                                    in_=dv_g[a][r_ * TPS:(r_ + 1) * TPS, h * 128:(h + 1) * 128].rearrange("(k p) e -> p k e", p=128)), reads=[Rvg], writes=[vb_.r])
                    S.dma(f"b_q{hb}", lambda e, hb=hb, h=h, qoff=qoff, Tq=Tq: e.dma_start(out=qA[hb].t[0:64, 0:Tq], in_=dqT[h * 128:h * 128 + 64, qoff:qoff + Tq]), writes=[qA[hb].r])
                    S.dma(f"b_r{hb}", lambda e, hb=hb, h=h, qoff=qoff, Tq=Tq: e.dma_start(out=qB[hb].t[64:128, 0:Tq], in_=dqT[h * 128 + 64:h * 128 + 128, qoff:qoff + Tq]), writes=[qB[hb].r])

                units = []
                for hi, (job, h) in enumerate(heads):
                    Tq = TS if job == 0 else TP
                    Sk = TS if job == 0 else SKP
                    for qc in range(Tq // 512):
                        for kb in range(Sk // 128):
                            units.append((hi, qc, kb, Sk // 128))

                def emit_qk(u):
                    hi, qc, kb, nkb = units[u]
                    hb = hi % 2
                    kb_ = kTh[hb]
                    sbuf_ = sbk2[u % 2]
                    for c in range(2):
                        qb_ = (qA if c == 0 else qB)[hb]
                        S.pe(lambda e, sbuf_=sbuf_, c=c, kb=kb, qc=qc, kb_=kb_, qb_=qb_: e.matmul(sbuf_.t[:, c, :], lhsT=kb_.t[:, kb * 128:(kb + 1) * 128],
                                                                                             rhs=qb_.t[:, qc * 512:(qc + 1) * 512], start=True, stop=True),
                             reads=[kb_.r, qb_.r], writes=[sbuf_.r])

                ocount = 0
                load_head(0)
                emit_qk(0)
                emit_qk(1)
                for u in range(len(units)):
                    hi, qc, kb, nkb = units[u]
                    job, h = heads[hi]
                    hb = hi % 2
                    vb_ = vh[hb]
                    qoff = 0 if job == 0 else TS
                    if qc == 0 and kb == 0 and hi + 1 < len(heads):
                        load_head(hi + 1)
                    pe2 = pexp2[u % NPX]
                    s2_ = sbk2[u % 2]
                    S.act(lambda e, pe2=pe2, s2_=s2_: e.activation(out=pe2.t[:], in_=s2_.t[:], func=AF.Exp), reads=[s2_.r], writes=[pe2.r], cost=1.05)
                    if u + 2 < len(units):
                        if units[u + 2][0] != hi and units[u + 2][1] == 0 and units[u + 2][2] == 0 and units[u + 2][0] + 1 < len(heads):
                            pass
                        emit_qk(u + 2)
                    gp = gcount % 2
                    for c in range(2):
                        pe_ = pexp2[u % NPX]
                        S.pe(lambda e, pe_=pe_, c=c, kb=kb, vb_=vb_, nkb=nkb: e.matmul(O[c].t[:], lhsT=vb_.t[:, kb, :], rhs=pe_.t[:, c, :], start=(kb == 0), stop=(kb == nkb - 1)),
                             reads=[vb_.r, pe_.r], writes=[O[c].r])
                        if kb % 2 == 1:
                            pp_ = pexp2[(u - 1) % NPX]
                            dst = (s01 if kb % 4 == 1 else s23)[c][gp]
                            S.dve(lambda e, dst=dst, pp_=pp_, pe_=pe_, c=c: e.tensor_tensor(out=dst.t[:], in0=pp_.t[:, c, :], in1=pe_.t[:, c, :], op=ALU.add), reads=[pp_.r, pe_.r], writes=[dst.r], cost=0.3)
                        if kb % 4 == 3:
                            a_, b_, d_ = s01[c][gp], s23[c][gp], s4[c][gp]
                            S.dve(lambda e, a_=a_, b_=b_, d_=d_: e.tensor_tensor(out=d_.t[:], in0=a_.t[:], in1=b_.t[:], op=ALU.add), reads=[a_.r, b_.r], writes=[d_.r], cost=0.3)
                            S.pe(lambda e, d_=d_, c=c, kb=kb, nkb=nkb: e.matmul(L[c].t[:], lhsT=ones_b.t[:], rhs=d_.t[:], start=(kb == 3), stop=(kb == nkb - 1)),
                                 reads=[ones_b.r, d_.r], writes=[L[c].r])
                    if kb % 4 == 3:
                        gcount += 1
                    if kb != nkb - 1:
                        continue
                    S.act(lambda e: e.activation(out=a0.t[:], in_=O[0].t[:], func=AF.Copy), reads=[O[0].r], writes=[a0.r])
                    S.dve(lambda e: e.tensor_copy(out=a1.t[:], in_=O[1].t[:]), reads=[O[1].r], writes=[a1.r])
                    S.dve(lambda e: e.reciprocal(out=r0.t[:], in_=L[0].t[:]), reads=[L[0].r], writes=[r0.r])
                    S.dve(lambda e: e.reciprocal(out=r1.t[:], in_=L[1].t[:]), reads=[L[1].r], writes=[r1.r])
                    S.dve(lambda e: e.tensor_tensor(out=a0.t[:], in0=a0.t[:], in1=r0.t[:], op=ALU.mult), reads=[a0.r, r0.r], writes=[a0.r])
                    S.dve(lambda e: e.tensor_tensor(out=a1.t[:], in0=a1.t[:], in1=r1.t[:], op=ALU.mult), reads=[a1.r, r1.r], writes=[a1.r])
                    S.dve(lambda e, l=l: e.scalar_tensor_tensor(out=a0.t[:], in0=a1.t[:], scalar=neglam.t[:, l:l + 1], in1=a0.t[:], op0=ALU.mult, op1=ALU.add),
                          reads=[a1.r, a0.r, neglam.r], writes=[a0.r])
                    S.act(lambda e: e.activation(out=osq.t[:], in_=a0.t[:], func=AF.Square), reads=[a0.r], writes=[osq.r])
                    sB = L[0]
                    S.pe(lambda e, sB=sB: e.matmul(sB.t[:], lhsT=ones_f.t[:], rhs=osq.t[:], start=True, stop=True), reads=[ones_f.r, osq.r], writes=[sB.r])
                    rsqrt_act(rsn.t[:], sB.t[:], 1.0 / 128, sB, rsn)
                    ao = aout[ocount % 2]
                    S.dve(lambda e, ao=ao: e.scalar_tensor_tensor(out=ao.t[:], in0=a0.t[:], scalar=subw.t[:, 0:1], in1=rsn.t[:], op0=ALU.mult, op1=ALU.mult),
                          reads=[a0.r, subw.r, rsn.r], writes=[ao.r])
                    tcol = qoff + qc * 512
                    S.dma(f"b_o{ocount % 2}", lambda e, ao=ao, h=h, tcol=tcol: e.dma_start(out=aT[h * 128:(h + 1) * 128, tcol:tcol + 512], in_=ao.t[:]), reads=[ao.r], eng="act")
                    ocount += 1
                S.barrier()

            with ExitStack() as st:
                SallJ = [sb(st, "Sall0", [128, TS // 128, 512], BF16), sb(st, "Sall1", [128, TP // 128, 512], BF16)]
                sttJ = [sb(st, f"stt{i}", [128, 512], F32) for i in range(2)]
                sttmpJ = [sb(st, f"sttmp{i}", [128, 512], F32) for i in range(2)]
                tg = sb(st, "tg", [128, NR, 512], F32)
                kdl = [sb(st, f"kdl{i}", [128, 4, 1024], BF16) for i in range(4)]
                rvl = [sb(st, f"rvl{i}", [128, 4, 512], BF16) for i in range(4)]
                okT = [sb(st, f"okT{i}", [128, 4, 512], BF16) for i in range(2)]
                oqT = [sb(st, f"oqT{i}", [128, 4, 512], BF16) for i in range(2)]
                oqd = [sb(st, f"oqd{i}", [128, 8, 512], BF16) for i in range(2)]
                orv = [sb(st, f"orv{i}", [128, 4, 512], BF16) for i in range(2)]
                ogs = [sb(st, f"ogs{i}", [128, 4, 512], BF16) for i in range(2)]
                PT = [sb(st, f"PT{i}", [128, 8, 128], BF16) for i in range(2)]
                rsq = sb(st, "rsq", [128, 512], F32)
                rss = sb(st, "rss", [128, 8], F32)
                rn = sb(st, "rn", [128, 512], F32)
                rr = sb(st, "rr", [128, 512], BF16)
                rTst = [sb(st, f"rTst{i}", [128, 4, 512], BF16) for i in range(2)]
                pkvJ = [psb(st, f"pkv{i}", [128, 512], F32) for i in range(2)]
                psc = [psb(st, f"psc{i}", [128, 4, 128], F32) for i in range(2)]
                po = [psb(st, f"pro{i}", [128, 512], F32) for i in range(2)]
                ptr = psb(st, "ptr", [128, 8, 128], BF16)
                Rtg = Res("st_g")
                ldn = [0]

                def sweep(job, use_init):
                    stt, sttmp, Sall = sttJ[job], sttmpJ[job], SallJ[job]
                    Tq = TS if job == 0 else TP
                    qoff = 0 if job == 0 else TS
                    n = Tq // 128
                    nsc = n // 4
                    if use_init:
                        S.dma("r_tg", lambda e: e.dma_start(out=tg.t[:], in_=st_g.rearrange("(r p) c -> p r c", p=128)), reads=[Rtg], writes=[tg.r])
                        for r_ in range(NR):
                            cb = coef.t[:, r_, :].unsqueeze(2).to_broadcast([128, 8, 64])
                            tg3 = tg.t[:, r_, :].rearrange("p (h e) -> p h e", e=64)
                            if r_ == 0:
                                S.dve(lambda e, cb=cb, tg3=tg3: e.tensor_tensor(out=stt.t[:].rearrange("p (h e) -> p h e", e=64), in0=tg3, in1=cb, op=ALU.mult), reads=[tg.r, coef.r], writes=[stt.r])
                            else:
                                S.dve(lambda e, cb=cb, tg3=tg3: e.tensor_tensor(out=sttmp.t[:].rearrange("p (h e) -> p h e", e=64), in0=tg3, in1=cb, op=ALU.mult), reads=[tg.r, coef.r], writes=[sttmp.r])
                                S.dve(lambda e: e.tensor_tensor(out=stt.t[:], in0=stt.t[:], in1=sttmp.t[:], op=ALU.add), reads=[stt.r, sttmp.r], writes=[stt.r])
                    else:
                        S.dve(lambda e: e.memset(stt.t[:], 0.0), writes=[stt.r])
                    cur = {}
                    for t in range(n):
                        tf = t
                        tb = n - 1 - t
                        bufs = {}
                        for nm, ti in (("f", tf), ("b", tb)):
                            sc = ti // 4
                            if (nm, sc) not in cur:
                                slot = job * 2 + (0 if nm == "f" else 1)
                                c0 = qoff + sc * 512
                                S.dma(f"r_kd{slot}", lambda e, slot=slot, c0=c0: e.dma_start(out=kdl[slot].t[:], in_=rkd[c0:c0 + 512, :].rearrange("(j p) c -> p j c", p=128)), writes=[kdl[slot].r])
                                S.dma(f"r_rv{slot}", lambda e, slot=slot, c0=c0: e.dma_start(out=rvl[slot].t[:], in_=rv[c0:c0 + 512, :].rearrange("(j p) c -> p j c", p=128)), writes=[rvl[slot].r])
                                cur = {k: v for k, v in cur.items() if k[0] != nm}
                                cur[(nm, sc)] = slot
                            bufs[nm] = (cur[(nm, sc)], ti % 4)
                        S.act(lambda e, tf=tf: e.activation(out=Sall.t[0:64, tf, :], in_=stt.t[0:64, :], func=AF.Copy), reads=[stt.r], writes=[Sall.r])
                        S.act(lambda e, tb=tb: e.activation(out=Sall.t[64:128, tb, :], in_=stt.t[64:128, :], func=AF.Copy), reads=[stt.r], writes=[Sall.r])
                        pk = pkvJ[job]
                        (sf, jf), (sb_, jb) = bufs["f"], bufs["b"]
                        for h in range(8):
                            kf = kdl[sf].t[:, jf, :].rearrange("p (g a d) -> p g a d", g=8, a=2)
                            kbk = kdl[sb_].t[:, jb, :].rearrange("p (g a d) -> p g a d", g=8, a=2)
                            S.pe(lambda e, pk=pk, h=h, kf=kf, sf=sf, jf=jf: e.matmul(pk.t[0:64, h * 64:(h + 1) * 64], lhsT=kf[:, h, 0, :], rhs=rvl[sf].t[:, jf, h * 64:(h + 1) * 64], start=True, stop=True),
                                 reads=[kdl[sf].r, rvl[sf].r], writes=[pk.r])
                            S.pe(lambda e, pk=pk, h=h, kbk=kbk, sb_=sb_, jb=jb: e.matmul(pk.t[64:128, h * 64:(h + 1) * 64], lhsT=kbk[:, h, 1, :], rhs=rvl[sb_].t[:, jb, h * 64:(h + 1) * 64], start=True, stop=True),
                                 reads=[kdl[sb_].r, rvl[sb_].r], writes=[pk.r])
                        S.dve(lambda e: e.tensor_tensor(out=sttmp.t[:].rearrange("p (h e) -> p h e", e=64), in0=stt.t[:].rearrange("p (h e) -> p h e", e=64),
                                                        in1=cdec.t[:].unsqueeze(2).to_broadcast([128, 8, 64]), op=ALU.mult), reads=[stt.r, cdec.r], writes=[sttmp.r])
                        S.dve(lambda e, pk=pk: e.tensor_tensor(out=stt.t[:], in0=pk.t[:], in1=sttmp.t[:], op=ALU.add), reads=[pk.r, sttmp.r], writes=[stt.r])

                def outputs(job):
                    Sall = SallJ[job]
                    Tq = TS if job == 0 else TP
                    qoff = 0 if job == 0 else TS
                    n = Tq // 128
                    for sc in range(n // 4):
                        sl = sc % 2
                        c0 = qoff + sc * 512
                        S.dma(f"o_kT{sl}", lambda e, sl=sl, c0=c0: e.dma_start(out=okT[sl].t[:], in_=rkT.rearrange("(b p) t -> p b t", p=128)[:, :, c0:c0 + 512]), writes=[okT[sl].r])
                        S.dma(f"o_qT{sl}", lambda e, sl=sl, c0=c0: e.dma_start(out=oqT[sl].t[:], in_=rqT.rearrange("(b p) t -> p b t", p=128)[:, :, c0:c0 + 512]), writes=[oqT[sl].r])
                        S.dma(f"o_qd{sl}", lambda e, sl=sl, c0=c0: e.dma_start(out=oqd[sl].t[:], in_=rqdT.rearrange("(b p) t -> p b t", p=128)[:, :, c0:c0 + 512]), writes=[oqd[sl].r])
                        S.dma(f"o_rv{sl}", lambda e, sl=sl, c0=c0: e.dma_start(out=orv[sl].t[:], in_=rv[c0:c0 + 512, :].rearrange("(j p) c -> p j c", p=128)), writes=[orv[sl].r])
                        S.dma(f"o_gs{sl}", lambda e, sl=sl, c0=c0: e.dma_start(out=ogs[sl].t[:], in_=rgs[c0:c0 + 512, :].rearrange("(j p) c -> p j c", p=128)), writes=[ogs[sl].r])
                        for j in range(4):
                            i = sc * 4 + j
                            ptb = PT[j % 2]
                            for h in range(8):
                                pb_ = psc[h % 2]
                                hp = (h % 2) * 64
                                S.pe(lambda e, pb_=pb_, h=h, hp=hp, sl=sl, j=j: e.matmul(pb_.t[:, h // 2, :], lhsT=okT[sl].t[hp:hp + 64, h // 2, j * 128:(j + 1) * 128],
                                                                                   rhs=oqT[sl].t[hp:hp + 64, h // 2, j * 128:(j + 1) * 128], start=True, stop=True),
                                     reads=[okT[sl].r, oqT[sl].r], writes=[pb_.r])
                            pt4 = ptb.t[:].rearrange("p (b a) n -> p b a n", a=2)
                            S.dve(lambda e, pt4=pt4: e.tensor_tensor(out=pt4[:, :, 0, :], in0=psc[0].t[:], in1=DTe.t[:], op=ALU.mult), reads=[psc[0].r, DTe.r], writes=[ptb.r])
                            S.dve(lambda e, pt4=pt4: e.tensor_tensor(out=pt4[:, :, 1, :], in0=psc[1].t[:], in1=DTo.t[:], op=ALU.mult), reads=[psc[1].r, DTo.r, ptb.r], writes=[ptb.r])
                            pob = po[j % 2]
                            for h in range(8):
                                S.pe(lambda e, pob=pob, h=h, ptb=ptb, sl=sl, j=j: e.matmul(pob.t[:, h * 64:(h + 1) * 64], lhsT=ptb.t[:, h, :], rhs=orv[sl].t[:, j, h * 64:(h + 1) * 64], start=True, stop=False),
                                     reads=[ptb.r, orv[sl].r], writes=[pob.r])
                                S.pe(lambda e, pob=pob, h=h, sl=sl, j=j, i=i: e.matmul(pob.t[:, h * 64:(h + 1) * 64], lhsT=oqd[sl].t[:, h, j * 128:(j + 1) * 128], rhs=Sall.t[:, i, h * 64:(h + 1) * 64], start=False, stop=True),
                                     reads=[oqd[sl].r, Sall.r], writes=[pob.r])
                            S.act(lambda e, pob=pob: e.activation(out=rsq.t[:], in_=pob.t[:], func=AF.Square), reads=[pob.r], writes=[rsq.r])
                            S.dve(lambda e: e.tensor_reduce(out=rss.t[:], in_=rsq.t[:].rearrange("p (g d) -> p g d", d=64), axis=AX.X, op=ALU.add), reads=[rsq.r], writes=[rss.r])
                            rsqrt_act(rss.t[:], rss.t[:], 1.0 / 64, rss, rss)
                            S.dve(lambda e, pob=pob: e.tensor_tensor(out=rn.t[:].rearrange("p (g d) -> p g d", d=64), in0=pob.t[:].rearrange("p (g d) -> p g d", d=64),
                                                                    in1=rss.t[:].unsqueeze(2).to_broadcast([128, 8, 64]), op=ALU.mult), reads=[pob.r, rss.r], writes=[rn.r])
                            S.dve(lambda e: e.tensor_tensor(out=rn.t[:].rearrange("p (g d) -> p g d", d=64), in0=rn.t[:].rearrange("p (g d) -> p g d", d=64),
                                                            in1=gnw.t[:].unsqueeze(1).to_broadcast([128, 8, 64]), op=ALU.mult), reads=[rn.r, gnw.r], writes=[rn.r])
                            S.dve(lambda e, sl=sl, j=j: e.tensor_tensor(out=rr.t[:], in0=rn.t[:], in1=ogs[sl].t[:, j, :], op=ALU.mult), reads=[rn.r, ogs[sl].r], writes=[rr.r])
                            for c in range(4):
                                S.pe(lambda e, c=c: e.transpose(out=ptr.t[:, c, :], in_=rr.t[:, c * 128:(c + 1) * 128], identity=ident.t[:]), reads=[rr.r, ident.r], writes=[ptr.r])
                            S.act(lambda e, sl=sl, j=j: e.activation(out=rTst[sl].t[:, :, j * 128:(j + 1) * 128], in_=ptr.t[:, 0:4, :], func=AF.Copy), reads=[ptr.r], writes=[rTst[sl].r])
                        S.dma(f"o_rT{sl}", lambda e, sl=sl, c0=c0: e.dma_start(out=rT.rearrange("(b p) t -> p b t", p=128)[:, :, c0:c0 + 512], in_=rTst[sl].t[:]), reads=[rTst[sl].r])

                sweep(1, False)
                Rstl = Res("st_l")
                S.dma("r_stl", lambda e: e.dma_start(out=st_l, in_=sttJ[1].t[:]), reads=[sttJ[1].r], writes=[Rstl])
                S.dma("cc_s", lambda e: e.collective_compute("AllGather", ALU.bypass, replica_groups=RG, ins=[st_l], outs=[st_g]), reads=[Rstl], writes=[Rtg], eng="pool", inc=1, cost=250.0)
                sweep(0, False)
                outputs(0)
                sweep(1, True)
                outputs(1)
                S.barrier()

            with ExitStack() as st:
                w_out = load_w(st, "w_out", w_out_in[l], D, D, "w_out")
                w1 = load_w(st, "w1", w1_in[l], D, DFF, "w1")
                arT = [sb(st, f"arT{i}", [128, 8, 512], BF16) for i in range(2)]
                xs_ = [sb(st, f"xs{i}", [128, 4, D], F32) for i in range(2)]
                junk = sb(st, "junk2", [128, D], BF16)
                ss2 = sb(st, "ss2", [128, 1], F32)
                h2l = [sb(st, f"h2_{i}", [128, D], BF16) for i in range(2)]
                h2T = [sb(st, f"h2T{i}", [128, 8, 512], BF16) for i in range(1)] * 2
                rl = [sb(st, f"rl{i}", [128, 512], F32) for i in range(2)]
                uTs = [sb(st, f"uTs{i}", [128, 32, 512], BF16) for i in range(1)] * 2
                pw = [psb(st, f"pw{i}", [128, 512], F32) for i in range(2)]
                ph = psb(st, "ph", [128, 8, 128], BF16)
                pu = [psb(st, f"pu{i}", [128, 512], F32) for i in range(4)]
                for s in range(NST):
                    sl = s % 2
                    c0 = s * 512
                    S.dma(f"c_a{sl}", lambda e, sl=sl, c0=c0: e.dma_start(out=arT[sl].t[:, 0:4, :], in_=aT.rearrange("(b p) t -> p b t", p=128)[:, :, c0:c0 + 512]), writes=[arT[sl].r])
                    S.dma(f"c_r{sl}", lambda e, sl=sl, c0=c0: e.dma_start(out=arT[sl].t[:, 4:8, :], in_=rT.rearrange("(b p) t -> p b t", p=128)[:, :, c0:c0 + 512]), writes=[arT[sl].r])
                    S.dma(f"c_x{sl}", lambda e, sl=sl, c0=c0: e.dma_start(out=xs_[sl].t[:], in_=x_src[c0:c0 + 512, :].rearrange("(j p) c -> p j c", p=128)), writes=[xs_[sl].r])
                    for j in range(4):
                        for cb in range(2):
                            pb_ = pw[cb]
                            for k in range(8):
                                S.pe(lambda e, pb_=pb_, k=k, cb=cb, sl=sl, j=j: e.matmul(pb_.t[:], lhsT=arT[sl].t[:, k, j * 128:(j + 1) * 128], rhs=w_out.t[:, k, cb * 512:(cb + 1) * 512], start=(k == 0), stop=(k == 7)),
                                     reads=[arT[sl].r, w_out.r(k, cb * 512)], writes=[pb_.r])
                            S.dve(lambda e, pb_=pb_, cb=cb, sl=sl, j=j: e.tensor_tensor(out=xs_[sl].t[:, j, cb * 512:(cb + 1) * 512], in0=pb_.t[:], in1=xs_[sl].t[:, j, cb * 512:(cb + 1) * 512], op=ALU.add),
                                  reads=[pb_.r, xs_[sl].r], writes=[xs_[sl].r])
                        S.act(lambda e, sl=sl, j=j: e.activation(out=junk.t[:], in_=xs_[sl].t[:, j, :], func=AF.Square, accum_out=ss2.t[:]), reads=[xs_[sl].r], writes=[junk.r, ss2.r])
                        rsqrt_act(ss2.t[:], ss2.t[:], 1.0 / D, ss2, ss2)
                        h2 = h2l[j % 2]
                        S.dve(lambda e, sl=sl, j=j, h2=h2: e.scalar_tensor_tensor(out=h2.t[:], in0=xs_[sl].t[:, j, :], scalar=ss2.t[:, 0:1], in1=ln2b.t[:], op0=ALU.mult, op1=ALU.mult),
                              reads=[xs_[sl].r, ss2.r, ln2b.r], writes=[h2.r])
                        for c in range(8):
                            S.pe(lambda e, c=c, h2=h2: e.transpose(out=ph.t[:, c, :], in_=h2.t[:, c * 128:(c + 1) * 128], identity=ident.t[:]), reads=[h2.r, ident.r], writes=[ph.r])
                        S.act(lambda e, sl=sl, j=j: e.activation(out=h2T[sl].t[:, :, j * 128:(j + 1) * 128], in_=ph.t[:], func=AF.Copy), reads=[ph.r], writes=[h2T[sl].r])
                    S.dma(f"c_xo{sl}", lambda e, sl=sl, c0=c0: e.dma_start(out=xres[c0:c0 + 512, :].rearrange("(j p) c -> p j c", p=128), in_=xs_[sl].t[:]), reads=[xs_[sl].r], eng="act")
                    for fc in range(32):
                        pb_ = pu[fc % 4]
                        for k in range(8):
                            S.pe(lambda e, pb_=pb_, k=k, fc=fc, sl=sl: e.matmul(pb_.t[:], lhsT=w1.t[:, k, fc * 128:(fc + 1) * 128], rhs=h2T[sl].t[:, k, :], start=(k == 0), stop=(k == 7)),
                                 reads=[w1.r(k, fc * 128), h2T[sl].r], writes=[pb_.r])
                        rb = rl[fc % 2]
                        S.act(lambda e, pb_=pb_, rb=rb: e.activation(out=rb.t[:], in_=pb_.t[:], func=AF.Relu), reads=[pb_.r], writes=[rb.r])
                        S.dve(lambda e, rb=rb, fc=fc, sl=sl: e.tensor_tensor(out=uTs[sl].t[:, fc, :], in0=rb.t[:], in1=rb.t[:], op=ALU.mult), reads=[rb.r], writes=[uTs[sl].r])
                    S.dma(f"c_u{sl}", lambda e, sl=sl, c0=c0: e.dma_start(out=uT.rearrange("(c p) t -> p c t", p=128)[:, :, c0:c0 + 512], in_=uTs[sl].t[:]), reads=[uTs[sl].r], eng="act")
                S.barrier()

            with ExitStack() as st:
                w2 = load_w(st, "w2", w2_in[l], DFF, D, "w2", gn=1024, korder=True)
                wg = load_w(st, "wg", wg_in[l], D, D, "wg")
                wp = load_w(st, "wp", wp_in[l], PLE, D, "wp")
                uTl = [sb(st, f"uTl{i}", [128, 32, 512], BF16) for i in range(2)]
                xs_ = [sb(st, f"xc{i}", [128, 4, D], F32) for i in range(1)] * 2
                pl_ = [sb(st, f"pl{i}", [128, 4, PLE], F32) for i in range(1)] * 2
                x2bl = [sb(st, f"x2b{i}", [128, D], BF16) for i in range(2)]
                x2Tl = [sb(st, f"x2T{i}", [128, 8, 128], BF16) for i in range(2)]
                pbfl = [sb(st, f"pbf{i}", [128, PLE], BF16) for i in range(2)]
                ppTl = [sb(st, f"ppT{i}", [128, 2, 128], BF16) for i in range(2)]
                sg = [sb(st, f"sg{i}", [128, 512], F32) for i in range(2)]
                pm = [psb(st, f"pm{i}", [128, 512], F32) for i in range(2)]
                pg = [psb(st, f"pg{i}", [128, 512], F32) for i in range(2)]
                pq = [psb(st, f"pq{i}", [128, 512], F32) for i in range(2)]
                px = psb(st, "px", [128, 8, 128], BF16)
                pp2 = psb(st, "pp2", [128, 8, 128], BF16)
                for s in range(NST):
                    sl = s % 2
                    c0 = s * 512
                    S.dma(f"d_u{sl}", lambda e, sl=sl, c0=c0: e.dma_start(out=uTl[sl].t[:], in_=uT.rearrange("(c p) t -> p c t", p=128)[:, :, c0:c0 + 512]), writes=[uTl[sl].r])
                    S.dma(f"d_x{sl}", lambda e, sl=sl, c0=c0: e.dma_start(out=xs_[sl].t[:], in_=xres[c0:c0 + 512, :].rearrange("(j p) c -> p j c", p=128)), writes=[xs_[sl].r])
                    S.dma(f"d_p{sl}", lambda e, sl=sl, c0=c0, l=l: e.dma_start(out=pl_[sl].t[:], in_=p_in[l, c0:c0 + 512, :].rearrange("(j p) c -> p j c", p=128)), writes=[pl_[sl].r])
                    for j in range(4):
                        for cb in range(2):
                            pb_ = pm[cb]
                            for fc in range(32):
                                S.pe(lambda e, pb_=pb_, fc=fc, cb=cb, sl=sl, j=j: e.matmul(pb_.t[:], lhsT=uTl[sl].t[:, fc, j * 128:(j + 1) * 128], rhs=w2.t[:, fc, cb * 512:(cb + 1) * 512], start=(fc == 0), stop=(fc == 31)),
                                     reads=[uTl[sl].r, w2.r(fc, cb * 512)], writes=[pb_.r])
                            S.dve(lambda e, pb_=pb_, cb=cb, sl=sl, j=j: e.tensor_tensor(out=xs_[sl].t[:, j, cb * 512:(cb + 1) * 512], in0=pb_.t[:], in1=xs_[sl].t[:, j, cb * 512:(cb + 1) * 512], op=ALU.add),
                                  reads=[pb_.r, xs_[sl].r], writes=[xs_[sl].r])
                        x2b, x2T, pbf, ppT = x2bl[j % 2], x2Tl[j % 2], pbfl[j % 2], ppTl[j % 2]
                        S.act(lambda e, sl=sl, j=j, x2b=x2b: e.activation(out=x2b.t[:], in_=xs_[sl].t[:, j, :], func=AF.Copy), reads=[xs_[sl].r], writes=[x2b.r], cost=1.1)
                        for c in range(8):
                            S.pe(lambda e, c=c, x2b=x2b: e.transpose(out=px.t[:, c, :], in_=x2b.t[:, c * 128:(c + 1) * 128], identity=ident.t[:]), reads=[x2b.r, ident.r], writes=[px.r])
                        S.act(lambda e, x2T=x2T: e.activation(out=x2T.t[:], in_=px.t[:], func=AF.Copy), reads=[px.r], writes=[x2T.r], cost=1.1)
                        S.pool(lambda e, sl=sl, j=j, pbf=pbf: e.tensor_copy(out=pbf.t[:], in_=pl_[sl].t[:, j, :]), reads=[pl_[sl].r], writes=[pbf.r])
                        for c in range(2):
                            S.pe(lambda e, c=c, pbf=pbf: e.transpose(out=pp2.t[:, c, :], in_=pbf.t[:, c * 128:(c + 1) * 128], identity=ident.t[:]), reads=[pbf.r, ident.r], writes=[pp2.r])
                        S.act(lambda e, ppT=ppT: e.activation(out=ppT.t[:], in_=pp2.t[:, 0:2, :], func=AF.Copy), reads=[pp2.r], writes=[ppT.r])
                        for cb in range(2):
                            g_ = pg[cb]
                            q_ = pq[cb]
                            for k in range(8):
                                S.pe(lambda e, g_=g_, k=k, cb=cb: e.matmul(g_.t[:], lhsT=x2T.t[:, k, :], rhs=wg.t[:, k, cb * 512:(cb + 1) * 512], start=(k == 0), stop=(k == 7)),
                                     reads=[x2T.r, wg.r(k, cb * 512)], writes=[g_.r])
                            for k in range(2):
                                S.pe(lambda e, q_=q_, k=k, cb=cb: e.matmul(q_.t[:], lhsT=ppT.t[:, k, :], rhs=wp.t[:, k, cb * 512:(cb + 1) * 512], start=(k == 0), stop=(k == 1)),
                                     reads=[ppT.r, wp.r(k, cb * 512)], writes=[q_.r])
                            sgb = sg[cb]
                            S.act(lambda e, g_=g_, sgb=sgb: e.activation(out=sgb.t[:], in_=g_.t[:], func=AF.Exp, scale=-1.0), reads=[g_.r], writes=[sgb.r])
                            S.dve(lambda e, sgb=sgb: e.tensor_scalar(out=sgb.t[:], in0=sgb.t[:], scalar1=1.0, scalar2=None, op0=ALU.add), reads=[sgb.r], writes=[sgb.r])
                            S.dve(lambda e, sgb=sgb: e.reciprocal(out=sgb.t[:], in_=sgb.t[:]), reads=[sgb.r], writes=[sgb.r])
                            S.dve(lambda e, sgb=sgb, q_=q_: e.tensor_tensor(out=sgb.t[:], in0=q_.t[:], in1=sgb.t[:], op=ALU.mult), reads=[q_.r, sgb.r], writes=[sgb.r])
                            S.dve(lambda e, sgb=sgb, cb=cb, sl=sl, j=j: e.tensor_tensor(out=xs_[sl].t[:, j, cb * 512:(cb + 1) * 512], in0=sgb.t[:], in1=xs_[sl].t[:, j, cb * 512:(cb + 1) * 512], op=ALU.add),
                                  reads=[sgb.r, xs_[sl].r], writes=[xs_[sl].r])
                    S.dma(f"d_xo{sl}", lambda e, sl=sl, c0=c0: e.dma_start(out=x_dst[c0:c0 + 512, :].rearrange("(j p) c -> p j c", p=128), in_=xs_[sl].t[:]), reads=[xs_[sl].r], eng="act")
                S.barrier()

        if DEBUG:
            for nm, src in (("rgs", rgs), ("rv", rv), ("dv_s", dv_s), ("dqT", dqT), ("aT", aT), ("rT", rT), ("xres", xres), ("uT", uT), ("rqT", rqT), ("rkd", rkd), ("rqdT", rqdT), ("rkT", rkT), ("dkT_s", dkT_s)):
                dbg = dram("dbg_" + nm, list(src.shape), src.dtype, "ExternalOutput")
                S.dma("dbg", lambda e, dbg=dbg, src=src: e.dma_start(out=dbg, in_=src))
        S.emit(top)
        nc._n_ops = len(S.ops)
        nc._n_sem = S.nsem
    return nc


ROPE_THETA = 500000.0
RET_THETA = 10000.0


def host_tables(TS, TP, rank):
    posv = np.concatenate([np.arange(TS), rank * TP + np.arange(TP)]).astype(np.float32)
    inv_d = (np.float32(1.0) / (np.float32(ROPE_THETA) ** (np.arange(0, 16, 2, dtype=np.float32) / np.float32(16)))).astype(np.float32)
    inv_r = (np.float32(1.0) / (np.float32(RET_THETA) ** (np.arange(0, 64, 2, dtype=np.float32) / np.float32(64)))).astype(np.float32)
    ang_d = (posv[:, None] * inv_d[None, :]).astype(np.float32)
    ang_r = (posv[:, None] * inv_r[None, :]).astype(np.float32)
    pos = np.concatenate([np.cos(ang_d), np.sin(ang_d), np.cos(ang_r), np.sin(ang_r)], axis=1).astype(np.float32)
    m = np.arange(128)[:, None].astype(np.float32)
    n = np.arange(128)[None, :].astype(np.float32)
    relF = np.maximum(n - m, 0.0)
    mskF = (n >= m).astype(np.float32) * 0.125
    relB = np.maximum(m - n, 0.0)
    mskB = (m > n).astype(np.float32) * 0.125
    p = np.arange(128, dtype=np.float32)[:, None]
    cst = np.concatenate([relF, mskF, relB, mskB, p + 1, 128 - p, 127 - p, p], axis=1).astype(np.float32)
    rkt = np.zeros((128, 8), np.float32)
    for r in range(NR):
        if r < rank:
            rkt[0:64, r] = TP * (rank - 1 - r)
            rkt[0:64, 4 + r] = 1.0
        if r > rank:
            rkt[64:128, r] = TP * (r - rank - 1)
            rkt[64:128, 4 + r] = 1.0
    return pos, cst, rkt


_NC_CACHE = {}


def run_cores(inputs, TS, TP, DEPTH):
    key = (TS, TP, DEPTH)
    if key not in _NC_CACHE:
        _NC_CACHE[key] = build(TS, TP, DEPTH)
    nc = _NC_CACHE[key]
    f = lambda a: np.ascontiguousarray(np.asarray(a, dtype=np.float32))
    xp = f(inputs["x_prompt"])
    xs = f(inputs["x_sample"])
    pp = f(inputs["p_prompt"])
    ps = f(inputs["p_sample"])
    shared = {
        "ln1_w": f(inputs["ln1_w"]), "w_in": f(inputs["w_in"]), "diff_q_norm": f(inputs["diff_q_norm"]),
        "diff_k_norm": f(inputs["diff_k_norm"]), "diff_lambda": f(inputs["diff_lambda"]).reshape(1, DEPTH * 256),
        "diff_subln": f(inputs["diff_subln"]), "ret_decay_logit": f(inputs["ret_decay_logit"]).reshape(1, DEPTH * 16),
        "ret_gn": f(inputs["ret_gn"]), "w_out": f(inputs["w_out"]), "ln2_w": f(inputs["ln2_w"]),
        "w_mlp1": f(inputs["w_mlp1"]), "w_mlp2": f(inputs["w_mlp2"]), "w_ple_gate": f(inputs["w_ple_gate"]),
        "w_ple_proj": f(inputs["w_ple_proj"]),
    }
    in_maps = []
    for c in range(8):
        g, r = c // NR, c % NR
        pos, cst, rkt = host_tables(TS, TP, r)
        m = dict(shared)
        m["x"] = np.ascontiguousarray(np.concatenate([xs[c], xp[g, r * TP:(r + 1) * TP]], axis=0))
        m["p"] = np.ascontiguousarray(np.concatenate([ps[:, c], pp[:, g, r * TP:(r + 1) * TP]], axis=1))
        m["pos"] = pos
        m["cst"] = cst
        m["rkt"] = rkt
        in_maps.append(m)
    res = run_bass_kernel_spmd(nc, in_maps, core_ids=list(range(8)))
    y_s = np.stack([np.asarray(res.results[c]["y"][:TS]) for c in range(8)], axis=0)
    y_p = np.stack([np.concatenate([np.asarray(res.results[g * NR + r]["y"][TS:]) for r in range(NR)], axis=0) for g in range(2)], axis=0)
    return y_p.astype(np.float32), y_s.astype(np.float32)


def kernel(**inputs):
    return run_cores(inputs, 4096, 2048, 4)
```

```python
import math
import types
import numpy as np
import concourse.bass as bass
import concourse.mybir as mybir
from concourse.bass_utils import run_bass_kernel_spmd
from contextlib import ExitStack

F32 = mybir.dt.float32
BF16 = mybir.dt.bfloat16
AF = mybir.ActivationFunctionType
ALU = mybir.AluOpType
AX = mybir.AxisListType

COMPUTE = ("pe", "act", "dve", "pool")
ALL_ENG = ("pe", "act", "dve", "pool", "sp")
SAME_ENG_SYNC = True

D = 1024
INW = 3584
DFF = 4096
PLE = 256
EPS = 1e-6
NR = 4


class Res:
    __slots__ = ("name", "w", "rs")

    def __init__(self, name=""):
        self.name = name
        self.w = None
        self.rs = []


class Op:
    __slots__ = ("eng", "fn", "deps", "mark", "val", "key", "is_mm", "inc", "cost", "idx", "fin", "succ", "npend", "ready")

    def __init__(self, eng, fn, key=None, is_mm=False, cost=None):
        self.eng = eng
        self.fn = fn
        self.deps = []
        self.mark = False
        self.val = 0
        self.key = key
        self.is_mm = is_mm
        self.inc = 16
        self.cost = cost
        self.idx = 0
        self.fin = 0.0
        self.succ = None
        self.npend = 0
        self.ready = 0.0


def _freeze(fn):
    if fn is None or fn.__closure__ is None:
        return fn
    cells = []
    for c in fn.__closure__:
        try:
            cells.append(types.CellType(c.cell_contents))
        except ValueError:
            cells.append(c)
    return types.FunctionType(fn.__code__, fn.__globals__, fn.__name__, fn.__defaults__, tuple(cells))


DEF_COST = {"pe": 0.27, "act": 0.45, "dve": 0.60, "pool": 0.90, "sp": 0.30}
SLAT = 0.5
XLAT = 0.45
DMA_LAT = 3.0
RESCHEDULE = True


class _Probe:
    def __init__(self):
        self.name = None
        self.args = ()
        self.kw = {}

    def __getattr__(self, name):
        def f(*a, **k):
            if self.name is None:
                self.name, self.args, self.kw = name, a, k
            return self
        return f


def _esize(ap):
    n = 1
    for d in ap.shape[1:]:
        n *= int(d)
    return n


def _estimate(eng, fn, key):
    try:
        p = _Probe()
        fn(p)
        out = p.kw.get("out", p.args[0] if p.args else None)
        if out is None or not hasattr(out, "shape"):
            return None
        n = _esize(out)
        if key is not None:
            nbytes = n * int(out.shape[0]) * (2 if out.dtype == BF16 else 4)
            return 2.5 + nbytes / 150e3
        if eng == "pe":
            return 0.03 + n / 2100.0
        if eng == "act":
            return 0.08 + n / 960.0
        if eng == "dve":
            return 0.07 + n / 960.0
        if eng == "pool":
            return 0.6 + n / 700.0
    except Exception:
        return None
    return None


class Sched:
    def __init__(self, nc):
        self.nc = nc
        self.ops = []

    def op(self, eng, fn, reads=(), writes=(), key=None, is_mm=False, cost=None):
        fn = _freeze(fn)
        if cost is None and fn is not None:
            cost = _estimate(eng, fn, key)
        o = Op(eng, fn, key, is_mm, cost)
        deps = o.deps
        for r in reads:
            if r.w is not None:
                deps.append(r.w)
        for w in writes:
            if w.w is not None:
                deps.append(w.w)
            deps.extend(w.rs)
        for r in reads:
            r.rs.append(o)
        for w in writes:
            w.w = o
            w.rs = []
        o.idx = len(self.ops)
        self.ops.append(o)
        return o

    def pe(self, fn, reads=(), writes=(), cost=None):
        return self.op("pe", fn, reads, writes, is_mm=True, cost=cost)

    def act(self, fn, reads=(), writes=(), cost=None):
        return self.op("act", fn, reads, writes, cost=cost)

    def dve(self, fn, reads=(), writes=(), cost=None):
        return self.op("dve", fn, reads, writes, cost=cost)

    def pool(self, fn, reads=(), writes=(), cost=None):
        return self.op("pool", fn, reads, writes, cost=cost)

    def dma(self, key, fn, reads=(), writes=(), eng="sp", inc=16, cost=None):
        o = self.op(eng, fn, reads, writes, key=key, cost=cost)
        o.inc = inc
        return o

    def barrier(self):
        o = Op(None, None)
        o.idx = len(self.ops)
        self.ops.append(o)

    @staticmethod
    def _skip(p, o):
        return p.key is None and p.eng == o.eng and (not SAME_ENG_SYNC or (p.is_mm and o.is_mm))

    def _schedule_segment(self, seg):
        import heapq
        inseg = set(id(o) for o in seg)
        for o in seg:
            o.succ = []
            o.npend = 0
            o.ready = 0.0
        for o in seg:
            for p in o.deps:
                if id(p) in inseg:
                    p.succ.append(o)
                    o.npend += 1
        free = {e: 0.0 for e in ALL_ENG}
        heaps = {e: [] for e in ALL_ENG}
        for o in seg:
            if o.npend == 0:
                heapq.heappush(heaps[o.eng], (0.0, o.idx, o))
        order = {e: [] for e in ALL_ENG}
        remaining = len(seg)
        while remaining:
            best = None
            for e in ALL_ENG:
                h = heaps[e]
                if not h:
                    continue
                t_free = free[e]
                cands = []
                while h and h[0][0] <= t_free:
                    cands.append(heapq.heappop(h))
                if cands:
                    c = min(cands, key=lambda x: x[1])
                    for x in cands:
                        if x is not c:
                            heapq.heappush(h, x)
                    heapq.heappush(h, c)
                    start = t_free
                    pick = c
                else:
                    pick = h[0]
                    start = pick[0]
                if best is None or start < best[0] or (start == best[0] and pick[1] < best[2][1]):
                    best = (start, e, pick)
            start, e, pick = best
            h = heaps[e]
            h.remove(pick)
            heapq.heapify(h)
            o = pick[2]
            cost = o.cost if o.cost is not None else DEF_COST[e]
            if o.key is not None:
                free[e] = start + DEF_COST["sp"]
                o.fin = start + (cost if o.cost is not None else DMA_LAT)
            else:
                free[e] = start + cost
                o.fin = free[e]
            order[e].append(o)
            remaining -= 1
            for q in o.succ:
                if q.eng == o.eng and o.key is None:
                    lat = 0.0 if (o.is_mm and q.is_mm) else SLAT
                else:
                    lat = XLAT
                r = o.fin + lat
                if r > q.ready:
                    q.ready = r
                q.npend -= 1
                if q.npend == 0:
                    heapq.heappush(heaps[q.eng], (q.ready, q.idx, q))
        return order

    def emit(self, stack):
        nc = self.nc
        segs = []
        cur = []
        for o in self.ops:
            if o.eng is None:
                if cur:
                    segs.append(cur)
                    cur = []
            else:
                cur.append(o)
        if cur:
            segs.append(cur)
        streams = {e: [] for e in ALL_ENG}
        for seg in segs:
            if RESCHEDULE:
                order = self._schedule_segment(seg)
            else:
                order = {e: [o for o in seg if o.eng == e] for e in ALL_ENG}
            lastops = {}
            for e in ALL_ENG:
                for o in order[e]:
                    lastops[o.key if o.key is not None else e] = o
            for e in ALL_ENG:
                streams[e].extend(order[e])
            deps = list(lastops.values())
            for e in ALL_ENG:
                b = Op(e, None)
                b.deps = list(deps)
                streams[e].append(b)
        for e in ALL_ENG:
            for o in streams[e]:
                for p in o.deps:
                    if p.key is None and not self._skip(p, o):
                        p.mark = True
        kcnt = {}
        for e in ALL_ENG:
            cnt = 0
            for o in streams[e]:
                if o.fn is None:
                    continue
                if o.key is not None:
                    kcnt[o.key] = kcnt.get(o.key, 0) + o.inc
                    o.val = kcnt[o.key]
                elif o.mark:
                    cnt += 1
                    o.val = cnt
        sems = {}
        for e in COMPUTE:
            sems[e] = stack.enter_context(nc.semaphore("s_" + e))
        for k in kcnt:
            sems[k] = stack.enter_context(nc.semaphore("d_" + k))
        self.nsem = len(sems)
        block = stack.enter_context(nc.Block())

        def run(engname, eng):
            seen = {}
            for o in streams[engname]:
                need = {}
                for p in o.deps:
                    if self._skip(p, o):
                        continue
                    sk = p.eng if p.key is None else p.key
                    if seen.get(sk, 0) >= p.val:
                        continue
                    if need.get(sk, 0) < p.val:
                        need[sk] = p.val
                for sk, v in need.items():
                    eng.wait_ge(sems[sk], v)
                    seen[sk] = v
                if o.fn is None:
                    continue
                ins = o.fn(eng)
                if o.key is not None:
                    ins.then_inc(sems[o.key], o.inc)
                elif o.mark:
                    ins.then_inc(sems[o.eng], 1)
            if engname == "sp":
                for k, v in kcnt.items():
                    if seen.get(k, 0) < v:
                        eng.wait_ge(sems[k], v)

        @block.tensor
        def _(e):
            run("pe", e)

        @block.scalar
        def _(e):
            run("act", e)

        @block.vector
        def _(e):
            run("dve", e)

        @block.gpsimd
        def _(e):
            run("pool", e)

        @block.sync
        def _(e):
            run("sp", e)


class Buf:
    __slots__ = ("t", "r")

    def __init__(self, t, name):
        self.t = t
        self.r = Res(name)


def build(TS, TP, DEPTH, DEBUG=False):
    T = TS + TP
    SKP = NR * TP
    NT = T // 128
    NST = T // 512
    assert TS % 512 == 0 and TP % 512 == 0
    nc = bass.Bass("TRN2", target_bir_lowering=False)

    def dram(name, shape, dtype, kind="Internal"):
        return nc.dram_tensor(name, shape, dtype, kind=kind).ap()

    x_in = dram("x", [T, D], F32, "ExternalInput")
    p_in = dram("p", [DEPTH, T, PLE], F32, "ExternalInput")
    pos_in = dram("pos", [T, 80], F32, "ExternalInput")
    cst_in = dram("cst", [128, 516], F32, "ExternalInput")
    rkt_in = dram("rkt", [128, 8], F32, "ExternalInput")
    ln1_in = dram("ln1_w", [DEPTH, D], F32, "ExternalInput")
    w_in_in = dram("w_in", [DEPTH, D, INW], F32, "ExternalInput")
    qn_in = dram("diff_q_norm", [DEPTH, 64], F32, "ExternalInput")
    kn_in = dram("diff_k_norm", [DEPTH, 64], F32, "ExternalInput")
    lam_in = dram("diff_lambda", [1, DEPTH * 256], F32, "ExternalInput")
    sub_in = dram("diff_subln", [DEPTH, 128], F32, "ExternalInput")
    dec_in = dram("ret_decay_logit", [1, DEPTH * 16], F32, "ExternalInput")
    gn_in = dram("ret_gn", [DEPTH, 64], F32, "ExternalInput")
    w_out_in = dram("w_out", [DEPTH, D, D], F32, "ExternalInput")
    ln2_in = dram("ln2_w", [DEPTH, D], F32, "ExternalInput")
    w1_in = dram("w_mlp1", [DEPTH, D, DFF], F32, "ExternalInput")
    w2_in = dram("w_mlp2", [DEPTH, DFF, D], F32, "ExternalInput")
    wg_in = dram("w_ple_gate", [DEPTH, D, D], F32, "ExternalInput")
    wp_in = dram("w_ple_proj", [DEPTH, PLE, D], F32, "ExternalInput")
    y_out = dram("y", [T, D], F32, "ExternalOutput")

    xres = dram("xres", [T, D], F32)
    dqT = dram("dqT", [512, T], BF16)
    dkT_s = dram("dkT_s", [512, TS], BF16)
    NSPL = max(1, (512 * TP * 2) // (1 << 20))
    HPS = 4 // NSPL
    TPS = TP // NSPL
    dkT_l = [dram(f"dkT_l{a}", [HPS * 128, TP], BF16) for a in range(NSPL)]
    dkT_g = [dram(f"dkT_g{a}", [NR * HPS * 128, TP], BF16) for a in range(NSPL)]
    dv_s = dram("dv_s", [TS, 512], BF16)
    dv_l = [dram(f"dv_l{a}", [TPS, 512], BF16) for a in range(NSPL)]
    dv_g = [dram(f"dv_g{a}", [NR * TPS, 512], BF16) for a in range(NSPL)]
    rqT = dram("rqT", [512, T], BF16)
    rkT = dram("rkT", [512, T], BF16)
    rqdT = dram("rqdT", [8 * 128, T], BF16)
    rkd = dram("rkd", [T, 1024], BF16)
    rv = dram("rv", [T, 512], BF16)
    rgs = dram("rgs", [T, 512], BF16)
    aT = dram("aT", [512, T], BF16)
    rT = dram("rT", [512, T], BF16)
    uT = dram("uT", [DFF, T], BF16)
    st_l = dram("st_l", [128, 512], F32)
    st_g = dram("st_g", [NR * 128, 512], F32)

    with ExitStack() as top:
        S = Sched(nc)

        uid = [0]

        def sb(st, name, shape, dt):
            uid[0] += 1
            return Buf(st.enter_context(nc.sbuf_tensor(f"sb{uid[0]}_{name}", shape, dt)), name)

        def psb(st, name, shape, dt):
            uid[0] += 1
            return Buf(st.enter_context(nc.psum_tensor(f"ps{uid[0]}_{name}", shape, dt)), name)

        ident = sb(top, "ident", [128, 128], BF16)
        identf = sb(top, "identf", [128, 128], F32)
        ones_b = sb(top, "ones_b", [128, 128], BF16)
        ones_f = sb(top, "ones_f", [128, 128], F32)
        cst = sb(top, "cst", [128, 516], F32)
        rkt = sb(top, "rkt", [128, 8], F32)
        lg = sb(top, "lg", [128, DEPTH * 16], F32)
        lgcol = sb(top, "lgcol", [128, DEPTH * 8], F32)
        lam = sb(top, "lam", [128, DEPTH], F32)
        neglam = sb(top, "neglam", [128, DEPTH], F32)
        epsc = sb(top, "epsc", [128, 1], F32)
        ln1b = sb(top, "ln1b", [128, D], F32)
        ln2b = sb(top, "ln2b", [128, D], F32)
        qnw = sb(top, "qnw", [128, 64], F32)
        knw = sb(top, "knw", [128, 64], F32)
        gnw = sb(top, "gnw", [128, 64], F32)
        subw = sb(top, "subw", [128, 1], F32)
        qdec = sb(top, "qdec", [128, 8, 2], F32)
        kdec = sb(top, "kdec", [128, 8, 2], F32)
        cdec = sb(top, "cdec", [128, 8], F32)
        DTe = sb(top, "DTe", [128, 4, 128], F32)
        DTo = sb(top, "DTo", [128, 4, 128], F32)
        coef = sb(top, "coef", [128, 4, 8], F32)

        relF = cst.t[:, 0:128]
        mskF = cst.t[:, 128:256]
        relB = cst.t[:, 256:384]
        mskB = cst.t[:, 384:512]
        idx_p1 = cst.t[:, 512:513]
        idx_128m = cst.t[:, 513:514]
        idx_127m = cst.t[:, 514:515]
        idx_p = cst.t[:, 515:516]

        S.dma("c_init", lambda e: e.dma_start(out=cst.t[:], in_=cst_in), writes=[cst.r])
        S.dma("c_init", lambda e: e.dma_start(out=rkt.t[:], in_=rkt_in), writes=[rkt.r])
        S.pool(lambda e: e.memset(identf.t[:], 0.0), writes=[identf.r])
        S.pool(lambda e: e.affine_select(out=identf.t[:], in_=identf.t[:], compare_op=ALU.not_equal, fill=1.0, base=0,
                                         pattern=[[-1, 128]], channel_multiplier=1), reads=[identf.r], writes=[identf.r])
        S.dve(lambda e: e.tensor_copy(out=ident.t[:], in_=identf.t[:]), reads=[identf.r], writes=[ident.r])
        S.dve(lambda e: e.memset(ones_b.t[:], 1.0), writes=[ones_b.r])
        S.dve(lambda e: e.memset(ones_f.t[:], 1.0), writes=[ones_f.r])
        S.dve(lambda e: e.memset(epsc.t[:], EPS), writes=[epsc.r])

        with ExitStack() as st0:
            NL = DEPTH * 16
            xx = sb(st0, "ls_x", [128, NL], F32)
            ax = sb(st0, "ls_ax", [128, NL], F32)
            uu = sb(st0, "ls_u", [128, NL], F32)
            ss_ = sb(st0, "ls_s", [128, NL], F32)
            s2 = sb(st0, "ls_s2", [128, NL], F32)
            pl = sb(st0, "ls_pl", [128, NL], F32)
            lp = sb(st0, "lp", [128, DEPTH * 256], F32)
            S.dma("c_init", lambda e: e.dma_start(out=xx.t[:], in_=dec_in.to_broadcast([128, NL])), writes=[xx.r])
            S.dma("c_init", lambda e: e.dma_start(out=lp.t[:], in_=lam_in.to_broadcast([128, DEPTH * 256])), writes=[lp.r])
            S.barrier()
            S.dve(lambda e: e.tensor_scalar(out=ax.t[:], in0=xx.t[:], scalar1=-1.0, scalar2=None, op0=ALU.mult), reads=[xx.r], writes=[ax.r])
            S.dve(lambda e: e.tensor_tensor(out=ax.t[:], in0=ax.t[:], in1=xx.t[:], op=ALU.max), reads=[xx.r, ax.r], writes=[ax.r])
            S.act(lambda e: e.activation(out=uu.t[:], in_=ax.t[:], func=AF.Exp, scale=-1.0), reads=[ax.r], writes=[uu.r])
            S.dve(lambda e: e.tensor_scalar(out=ss_.t[:], in0=uu.t[:], scalar1=2.0, scalar2=None, op0=ALU.add), reads=[uu.r], writes=[ss_.r])
            S.dve(lambda e: e.reciprocal(out=ss_.t[:], in_=ss_.t[:]), reads=[ss_.r], writes=[ss_.r])
            S.dve(lambda e: e.tensor_tensor(out=ss_.t[:], in0=ss_.t[:], in1=uu.t[:], op=ALU.mult), reads=[ss_.r, uu.r], writes=[ss_.r])
            S.dve(lambda e: e.tensor_tensor(out=s2.t[:], in0=ss_.t[:], in1=ss_.t[:], op=ALU.mult), reads=[ss_.r], writes=[s2.r])
            S.dve(lambda e: e.tensor_scalar(out=pl.t[:], in0=s2.t[:], scalar1=1.0 / 13, scalar2=1.0 / 11, op0=ALU.mult, op1=ALU.add), reads=[s2.r], writes=[pl.r])
            for cc in (1.0 / 9, 1.0 / 7, 1.0 / 5, 1.0 / 3, 1.0):
                S.dve(lambda e: e.tensor_tensor(out=pl.t[:], in0=pl.t[:], in1=s2.t[:], op=ALU.mult), reads=[pl.r, s2.r], writes=[pl.r])
                S.dve(lambda e, cc=cc: e.tensor_scalar(out=pl.t[:], in0=pl.t[:], scalar1=cc, scalar2=None, op0=ALU.add), reads=[pl.r], writes=[pl.r])
            S.dve(lambda e: e.tensor_tensor(out=pl.t[:], in0=pl.t[:], in1=ss_.t[:], op=ALU.mult), reads=[pl.r, ss_.r], writes=[pl.r])
            S.dve(lambda e: e.tensor_scalar(out=ax.t[:], in0=xx.t[:], scalar1=0.0, scalar2=None, op0=ALU.min), reads=[xx.r], writes=[ax.r])
            S.dve(lambda e: e.scalar_tensor_tensor(out=lg.t[:], in0=pl.t[:], scalar=-2.0, in1=ax.t[:], op0=ALU.mult, op1=ALU.add), reads=[pl.r, ax.r], writes=[lg.r])
            lg4 = lg.t[:].rearrange("p (l a h) -> p l a h", l=DEPTH, a=2)
            lgc3 = lgcol.t[:].rearrange("p (l h) -> p l h", l=DEPTH)
            S.dve(lambda e: e.tensor_copy(out=lgc3[0:64], in_=lg4[0:64, :, 0, :]), reads=[lg.r], writes=[lgcol.r])
            S.dve(lambda e: e.tensor_copy(out=lgc3[64:128], in_=lg4[64:128, :, 1, :]), reads=[lg.r], writes=[lgcol.r])
            pr = sb(st0, "lpr", [128, DEPTH * 2 * 64], F32)
            sm = sb(st0, "lsm", [128, DEPTH * 2], F32)
            lp5 = lp.t[:].rearrange("p (l a b d) -> p l a b d", l=DEPTH, a=2, b=2)
            pr4 = pr.t[:].rearrange("p (l a d) -> p l a d", l=DEPTH, a=2)
            S.dve(lambda e: e.tensor_tensor(out=pr4, in0=lp5[:, :, :, 0, :], in1=lp5[:, :, :, 1, :], op=ALU.mult), reads=[lp.r], writes=[pr.r])
            S.dve(lambda e: e.tensor_reduce(out=sm.t[:], in_=pr.t[:].rearrange("p (g d) -> p g d", d=64), axis=AX.X, op=ALU.add), reads=[pr.r], writes=[sm.r])
            S.act(lambda e: e.activation(out=sm.t[:], in_=sm.t[:], func=AF.Exp), reads=[sm.r], writes=[sm.r])
            sm3 = sm.t[:].rearrange("p (l a) -> p l a", a=2)
            S.dve(lambda e: e.tensor_tensor(out=lam.t[:], in0=sm3[:, :, 0], in1=sm3[:, :, 1], op=ALU.subtract), reads=[sm.r], writes=[lam.r])
            for l in range(DEPTH):
                li = 0.8 - 0.6 * math.exp(-0.3 * l)
                S.dve(lambda e, l=l, li=li: e.tensor_scalar(out=lam.t[:, l:l + 1], in0=lam.t[:, l:l + 1], scalar1=li, scalar2=None, op0=ALU.add), reads=[lam.r], writes=[lam.r])
            S.dve(lambda e: e.tensor_scalar(out=neglam.t[:], in0=lam.t[:], scalar1=-1.0, scalar2=None, op0=ALU.mult), reads=[lam.r], writes=[neglam.r])
            S.barrier()

        def rsqrt_act(out_ap, in_ap, scale, rbuf, wbuf):
            S.act(lambda e: e.activation(out=out_ap, in_=in_ap, func=AF.Ln, scale=scale, bias=epsc.t[0:out_ap.shape[0], 0:1]), reads=[rbuf.r, epsc.r], writes=[wbuf.r])
            S.act(lambda e: e.activation(out=out_ap, in_=out_ap, func=AF.Exp, scale=-0.5), reads=[wbuf.r], writes=[wbuf.r])

        class WGroups:
            def __init__(self, t, gn):
                self.t = t
                self.gn = gn
                self.res = {}

            def r(self, k, n0):
                return self.res[(k // 4, n0 // self.gn)]

        def load_w(st, name, src, K, N, key, gn=512, korder=False):
            kc = K // 128
            gn = min(gn, N, 2048)
            uid[0] += 1
            t = st.enter_context(nc.sbuf_tensor(f"sb{uid[0]}_{name}", [128, kc, N], BF16))
            w = WGroups(t, gn)
            srcv = src.rearrange("(c p) n -> p c n", p=128)
            chain = [Res(name + "_chA"), Res(name + "_chB")]
            groups = [(c0, n0) for n0 in range(0, N, gn) for c0 in range(0, kc, 4)]
            if korder:
                groups = [(c0, n0) for c0 in range(0, kc, 4) for n0 in range(0, N, gn)]
            for i, (c0, n0) in enumerate(groups):
                c1 = min(kc, c0 + 4)
                n1 = min(N, n0 + gn)
                rr_ = Res(f"{name}_{c0}_{n0}")
                w.res[(c0 // 4, n0 // gn)] = rr_
                S.dma(key + "AB"[i % 2], lambda e, c0=c0, c1=c1, n0=n0, n1=n1: e.dma_start(out=t[:, c0:c1, n0:n1], in_=srcv[:, c0:c1, n0:n1]),
                      writes=[rr_, chain[i % 2]], eng="pool", cost=8.0)
            return w

        for l in range(DEPTH):
            lam_init = 0.8 - 0.6 * math.exp(-0.3 * l)
            x_src = x_in if l == 0 else xres
            x_dst = y_out if l == DEPTH - 1 else xres

            S.dma("c_lay", lambda e, l=l: e.dma_start(out=ln1b.t[:], in_=ln1_in[l:l + 1, :].to_broadcast([128, D])), writes=[ln1b.r])
            S.dma("c_lay", lambda e, l=l: e.dma_start(out=ln2b.t[:], in_=ln2_in[l:l + 1, :].to_broadcast([128, D])), writes=[ln2b.r])
            S.dma("c_lay", lambda e, l=l: e.dma_start(out=qnw.t[:], in_=qn_in[l:l + 1, :].to_broadcast([128, 64])), writes=[qnw.r])
            S.dma("c_lay", lambda e, l=l: e.dma_start(out=knw.t[:], in_=kn_in[l:l + 1, :].to_broadcast([128, 64])), writes=[knw.r])
            S.dma("c_lay", lambda e, l=l: e.dma_start(out=gnw.t[:], in_=gn_in[l:l + 1, :].to_broadcast([128, 64])), writes=[gnw.r])
            S.dma("c_lay", lambda e, l=l: e.dma_start(out=subw.t[:], in_=sub_in[l:l + 1, :].rearrange("o e -> e o"), allow_slow_non_contiguous=True), writes=[subw.r])
            S.barrier()
            S.dve(lambda e: e.tensor_scalar(out=qnw.t[:], in0=qnw.t[:], scalar1=0.125, scalar2=None, op0=ALU.mult), reads=[qnw.r], writes=[qnw.r])
            S.dve(lambda e, li=lam_init: e.tensor_scalar(out=subw.t[:], in0=subw.t[:], scalar1=1.0 - li, scalar2=None, op0=ALU.mult), reads=[subw.r], writes=[subw.r])
            lgl = lg.t[:, l * 16:(l + 1) * 16]
            S.act(lambda e, lgl=lgl: e.activation(out=qdec.t[:, :, 0], in_=lgl[:, 0:8], func=AF.Exp, scale=idx_p1), reads=[lg.r, cst.r], writes=[qdec.r])
            S.act(lambda e, lgl=lgl: e.activation(out=qdec.t[:, :, 1], in_=lgl[:, 8:16], func=AF.Exp, scale=idx_128m), reads=[lg.r, cst.r], writes=[qdec.r])
            S.act(lambda e, lgl=lgl: e.activation(out=kdec.t[:, :, 0], in_=lgl[:, 0:8], func=AF.Exp, scale=idx_127m), reads=[lg.r, cst.r], writes=[kdec.r])
            S.act(lambda e, lgl=lgl: e.activation(out=kdec.t[:, :, 1], in_=lgl[:, 8:16], func=AF.Exp, scale=idx_p), reads=[lg.r, cst.r], writes=[kdec.r])
            S.dve(lambda e: e.tensor_scalar(out=kdec.t[:], in0=kdec.t[:], scalar1=0.125, scalar2=None, op0=ALU.mult), reads=[kdec.r], writes=[kdec.r])
            S.act(lambda e, l=l: e.activation(out=cdec.t[:], in_=lgcol.t[:, l * 8:(l + 1) * 8], func=AF.Exp, scale=128.0), reads=[lgcol.r], writes=[cdec.r])
            with ExitStack() as stc:
                tmpa = sb(stc, "dt_a", [128, 128], F32)
                tmpb = sb(stc, "dt_b", [128, 128], F32)
                for h in range(8):
                    dst = (DTe if h % 2 == 0 else DTo)
                    dsl = dst.t[:, h // 2, :]
                    S.act(lambda e, h=h: e.activation(out=tmpa.t[:], in_=relF, func=AF.Exp, scale=lgl[:, h:h + 1]), reads=[cst.r, lg.r], writes=[tmpa.r])
                    S.act(lambda e, h=h: e.activation(out=tmpb.t[:], in_=relB, func=AF.Exp, scale=lgl[:, 8 + h:9 + h]), reads=[cst.r, lg.r], writes=[tmpb.r])
                    S.dve(lambda e: e.tensor_tensor(out=tmpa.t[:], in0=tmpa.t[:], in1=mskF, op=ALU.mult), reads=[tmpa.r, cst.r], writes=[tmpa.r])
                    S.dve(lambda e: e.tensor_tensor(out=tmpb.t[:], in0=tmpb.t[:], in1=mskB, op=ALU.mult), reads=[tmpb.r, cst.r], writes=[tmpb.r])
                    S.dve(lambda e, dsl=dsl: e.tensor_tensor(out=dsl, in0=tmpa.t[:], in1=tmpb.t[:], op=ALU.add), reads=[tmpa.r, tmpb.r], writes=[dst.r])
                for r_ in range(NR):
                    S.act(lambda e, r_=r_, l=l: e.activation(out=coef.t[:, r_, :], in_=lgcol.t[:, l * 8:(l + 1) * 8], func=AF.Exp, scale=rkt.t[:, r_:r_ + 1]), reads=[lgcol.r, rkt.r], writes=[coef.r])
                    S.dve(lambda e, r_=r_: e.tensor_scalar(out=coef.t[:, r_, :], in0=coef.t[:, r_, :], scalar1=rkt.t[:, 4 + r_:5 + r_], scalar2=None, op0=ALU.mult), reads=[coef.r, rkt.r], writes=[coef.r])
                S.barrier()

            with ExitStack() as st:
                w_in = load_w(st, "w_in", w_in_in[l], D, INW, "w_in")
                P2 = range(2)
                xt = [sb(st, f"xt{i}", [128, D], F32) for i in P2]
                post = [sb(st, f"post{i}", [128, 80], F32) for i in P2]
                junk = sb(st, "junk", [128, D], BF16)
                ssx = [sb(st, f"ssx{i}", [128, 1], F32) for i in P2]
                hn = [sb(st, f"hn{i}", [128, D], BF16) for i in P2]
                hnT = [sb(st, f"hnT{i}", [128, 8, 128], BF16) for i in P2]
                qraw = [[sb(st, f"qraw{w}{i}", [128, 512], F32) for i in P2] for w in P2]
                qsq = [sb(st, f"qsq{w}", [128, 512], F32) for w in P2]
                ssg = [[sb(st, f"ssg{w}{i}", [128, 8], F32) for i in P2] for w in P2]
                qu = [sb(st, f"qu{w}", [128, 512], F32) for w in P2]
                w16 = [sb(st, f"w16{w}", [128, 8, 16], F32) for w in P2]
                rt = [[sb(st, f"rt{w}{i}", [128, 8, 8], F32) for i in range(4)] for w in P2]
                qb = [[sb(st, f"qb{w}{i}", [128, 512], BF16) for i in P2] for w in P2]
                rqf = [sb(st, f"rqf{w}", [128, 512], F32) for w in P2]
                rta = [sb(st, f"rta{w}", [128, 8, 32], F32) for w in P2]
                rtb = [sb(st, f"rtb{w}", [128, 8, 32], F32) for w in P2]
                rqb = [[sb(st, f"rqb{w}{i}", [128, 512], BF16) for i in P2] for w in P2]
                qd = [sb(st, f"qd{i}", [128, 1024], BF16) for i in P2]
                esg = [sb(st, f"esg{i}", [128, 512], F32) for i in P2]
                stg_qT = [sb(st, f"sg_qT{i}", [128, 4, 512], BF16) for i in P2]
                stg_kT = [sb(st, f"sg_kT{i}", [128, 4, 512], BF16) for i in P2]
                stg_rqT = [sb(st, f"sg_rqT{i}", [128, 4, 512], BF16) for i in P2]
                stg_rkT = [sb(st, f"sg_rkT{i}", [128, 4, 512], BF16) for i in P2]
                stg_qdT = [sb(st, f"sg_qdT{i}", [128, 8, 512], BF16) for i in P2]
                tv = [sb(st, f"tv{i}", [128, 512], BF16) for i in P2]
                trv = [sb(st, f"trv{i}", [128, 512], BF16) for i in P2]
                tgs = [sb(st, f"tgs{i}", [128, 512], BF16) for i in P2]
                tkd = [sb(st, f"tkd{i}", [128, 1024], BF16) for i in P2]
                pT = psb(st, "pT", [128, 8, 128], BF16)
                pT2 = [psb(st, f"pT2_{i}", [128, 8, 128], BF16) for i in P2]
                pj = [psb(st, f"pj{i}", [128, 512], F32) for i in range(5)]
                pjn = [0]

                def proj(cb, hnT_):
                    b = pj[pjn[0] % 5]
                    pjn[0] += 1
                    for k in range(8):
                        S.pe(lambda e, k=k, b=b, cb=cb: e.matmul(b.t[:], lhsT=hnT_.t[:, k, :], rhs=w_in.t[:, k, cb * 512:(cb + 1) * 512], start=(k == 0), stop=(k == 7)),
                             reads=[hnT_.r, w_in.r(k, cb * 512)], writes=[b.r])
                    return b

                def transpose_to(src, nblk, dst_stage, dst_ap_fn, pbuf):
                    for c in range(nblk):
                        S.pe(lambda e, c=c: e.transpose(out=pbuf.t[:, c, :], in_=src.t[:, c * 128:(c + 1) * 128], identity=ident.t[:]),
                             reads=[src.r, ident.r], writes=[pbuf.r])
                    S.act(lambda e: e.activation(out=dst_ap_fn(), in_=pbuf.t[:, 0:nblk, :], func=AF.Copy), reads=[pbuf.r], writes=[dst_stage.r])

                for t in range(NT):
                    sti = t // 4
                    j = t % 4
                    sl = sti % 2
                    par = t % 2
                    tok0 = t * 128
                    is_s = tok0 < TS
                    tl = tok0 if is_s else tok0 - TS
                    xb = xt[par]
                    pb = post[par]
                    hn_, hnT_, ssx_ = hn[par], hnT[par], ssx[par]
                    S.dma(f"a_x{par}", lambda e, xb=xb, tok0=tok0: e.dma_start(out=xb.t[:], in_=x_src[tok0:tok0 + 128, :]), writes=[xb.r])
                    S.dma(f"a_pos{par}", lambda e, pb=pb, tok0=tok0: e.dma_start(out=pb.t[:], in_=pos_in[tok0:tok0 + 128, :]), writes=[pb.r])
                    S.act(lambda e, xb=xb, ssx_=ssx_: e.activation(out=junk.t[:], in_=xb.t[:], func=AF.Square, accum_out=ssx_.t[:]), reads=[xb.r], writes=[junk.r, ssx_.r])
                    rsqrt_act(ssx_.t[:], ssx_.t[:], 1.0 / D, ssx_, ssx_)
                    S.dve(lambda e, xb=xb, hn_=hn_, ssx_=ssx_: e.scalar_tensor_tensor(out=hn_.t[:], in0=xb.t[:], scalar=ssx_.t[:, 0:1], in1=ln1b.t[:], op0=ALU.mult, op1=ALU.mult),
                          reads=[xb.r, ssx_.r, ln1b.r], writes=[hn_.r])
                    for c in range(8):
                        S.pe(lambda e, c=c, hn_=hn_: e.transpose(out=pT.t[:, c, :], in_=hn_.t[:, c * 128:(c + 1) * 128], identity=ident.t[:]), reads=[hn_.r, ident.r], writes=[pT.r])
                    S.act(lambda e, hnT_=hnT_: e.activation(out=hnT_.t[:], in_=pT.t[:], func=AF.Copy), reads=[pT.r], writes=[hnT_.r], cost=1.1)
                    cosd = pb.t[:, 0:8].unsqueeze(1).to_broadcast([128, 8, 8])
                    sind = pb.t[:, 8:16].unsqueeze(1).to_broadcast([128, 8, 8])
                    cosr = pb.t[:, 16:48].unsqueeze(1).to_broadcast([128, 8, 32])
                    sinr = pb.t[:, 48:80].unsqueeze(1).to_broadcast([128, 8, 32])
                    for which in range(2):
                        b = proj(which, hnT_)
                        nw = qnw if which == 0 else knw
                        stg = (stg_qT if which == 0 else stg_kT)[sl]
                        qraw_, qsq_, ssg_, qu_, w16_, rt_, qb_ = qraw[which][par], qsq[which], ssg[which][par], qu[which], w16[which], rt[which], qb[which][par]
                        S.act(lambda e, b=b, qraw_=qraw_: e.activation(out=qraw_.t[:], in_=b.t[:], func=AF.Copy), reads=[b.r], writes=[qraw_.r])
                        S.act(lambda e, b=b, qsq_=qsq_: e.activation(out=qsq_.t[:], in_=b.t[:], func=AF.Square), reads=[b.r], writes=[qsq_.r])
                        S.dve(lambda e, qsq_=qsq_, ssg_=ssg_: e.tensor_reduce(out=ssg_.t[:], in_=qsq_.t[:].rearrange("p (g d) -> p g d", d=64), axis=AX.X, op=ALU.add), reads=[qsq_.r], writes=[ssg_.r])
                        rsqrt_act(ssg_.t[:], ssg_.t[:], 1.0 / 64, ssg_, ssg_)
                        qr3 = qraw_.t[:].rearrange("p (g d) -> p g d", d=64)
                        qu3 = qu_.t[:].rearrange("p (g d) -> p g d", d=64)
                        qb3 = qb_.t[:].rearrange("p (g d) -> p g d", d=64)
                        S.dve(lambda e, qr3=qr3, qu3=qu3, ssg_=ssg_: e.tensor_tensor(out=qu3, in0=qr3, in1=ssg_.t[:].unsqueeze(2).to_broadcast([128, 8, 64]), op=ALU.mult), reads=[qraw_.r, ssg_.r], writes=[qu_.r])
                        S.dve(lambda e, qu3=qu3, qb3=qb3, nw=nw: e.tensor_tensor(out=qb3, in0=qu3, in1=nw.t[:].unsqueeze(1).to_broadcast([128, 8, 64]), op=ALU.mult), reads=[qu_.r, nw.r], writes=[qb_.r])
                        S.dve(lambda e, qu3=qu3, nw=nw, w16_=w16_: e.tensor_tensor(out=w16_.t[:], in0=qu3[:, :, 0:16], in1=nw.t[:, 0:16].unsqueeze(1).to_broadcast([128, 8, 16]), op=ALU.mult), reads=[qu_.r, nw.r], writes=[w16_.r], cost=0.15)
                        x1 = w16_.t[:, :, 0:8]
                        x2 = w16_.t[:, :, 8:16]
                        S.dve(lambda e, x1=x1, cosd=cosd, rt_=rt_: e.tensor_tensor(out=rt_[0].t[:], in0=x1, in1=cosd, op=ALU.mult), reads=[w16_.r, pb.r], writes=[rt_[0].r], cost=0.12)
                        S.dve(lambda e, x2=x2, sind=sind, rt_=rt_: e.tensor_tensor(out=rt_[1].t[:], in0=x2, in1=sind, op=ALU.mult), reads=[w16_.r, pb.r], writes=[rt_[1].r], cost=0.12)
                        S.dve(lambda e, x1=x1, sind=sind, rt_=rt_: e.tensor_tensor(out=rt_[2].t[:], in0=x1, in1=sind, op=ALU.mult), reads=[w16_.r, pb.r], writes=[rt_[2].r], cost=0.12)
                        S.dve(lambda e, x2=x2, cosd=cosd, rt_=rt_: e.tensor_tensor(out=rt_[3].t[:], in0=x2, in1=cosd, op=ALU.mult), reads=[w16_.r, pb.r], writes=[rt_[3].r], cost=0.12)
                        S.dve(lambda e, qb3=qb3, rt_=rt_: e.tensor_tensor(out=qb3[:, :, 0:8], in0=rt_[0].t[:], in1=rt_[1].t[:], op=ALU.subtract), reads=[rt_[0].r, rt_[1].r, qb_.r], writes=[qb_.r], cost=0.12)
                        S.dve(lambda e, qb3=qb3, rt_=rt_: e.tensor_tensor(out=qb3[:, :, 8:16], in0=rt_[2].t[:], in1=rt_[3].t[:], op=ALU.add), reads=[rt_[2].r, rt_[3].r, qb_.r], writes=[qb_.r], cost=0.12)
                        transpose_to(qb_, 4, stg, lambda stg=stg, j=j: stg.t[:, :, j * 128:(j + 1) * 128], pT2[0])
                    b = proj(2, hnT_)
                    tv_ = tv[par]
                    S.act(lambda e, b=b, tv_=tv_: e.activation(out=tv_.t[:], in_=b.t[:], func=AF.Copy), reads=[b.r], writes=[tv_.r])
                    if is_s:
                        S.dma(f"s_v{par}", lambda e, tv_=tv_, tl=tl: e.dma_start(out=dv_s[tl:tl + 128, :], in_=tv_.t[:]), reads=[tv_.r])
                    else:
                        S.dma(f"s_v{par}", lambda e, tv_=tv_, tl=tl: e.dma_start(out=dv_l[tl // TPS][tl % TPS:tl % TPS + 128, :], in_=tv_.t[:]), reads=[tv_.r])
                    for which in range(2):
                        b = proj(3 + which, hnT_)
                        rqf_, rta_, rtb_, rqb_ = rqf[which], rta[which], rtb[which], rqb[which][par]
                        b3 = b.t[:].rearrange("p (g d) -> p g d", d=64)
                        rq3 = rqf_.t[:].rearrange("p (g d) -> p g d", d=64)
                        S.dve(lambda e, b3=b3, cosr=cosr, rta_=rta_: e.tensor_tensor(out=rta_.t[:], in0=b3[:, :, 0:32], in1=cosr, op=ALU.mult), reads=[b.r, pb.r], writes=[rta_.r])
                        S.dve(lambda e, b3=b3, sinr=sinr, rtb_=rtb_: e.tensor_tensor(out=rtb_.t[:], in0=b3[:, :, 32:64], in1=sinr, op=ALU.mult), reads=[b.r, pb.r], writes=[rtb_.r])
                        S.dve(lambda e, rq3=rq3, rta_=rta_, rtb_=rtb_: e.tensor_tensor(out=rq3[:, :, 0:32], in0=rta_.t[:], in1=rtb_.t[:], op=ALU.subtract), reads=[rta_.r, rtb_.r], writes=[rqf_.r])
                        S.dve(lambda e, b3=b3, sinr=sinr, rta_=rta_: e.tensor_tensor(out=rta_.t[:], in0=b3[:, :, 0:32], in1=sinr, op=ALU.mult), reads=[b.r, pb.r], writes=[rta_.r])
                        S.dve(lambda e, b3=b3, cosr=cosr, rtb_=rtb_: e.tensor_tensor(out=rtb_.t[:], in0=b3[:, :, 32:64], in1=cosr, op=ALU.mult), reads=[b.r, pb.r], writes=[rtb_.r])
                        S.dve(lambda e, rq3=rq3, rta_=rta_, rtb_=rtb_: e.tensor_tensor(out=rq3[:, :, 32:64], in0=rta_.t[:], in1=rtb_.t[:], op=ALU.add), reads=[rta_.r, rtb_.r, rqf_.r], writes=[rqf_.r])
                        S.act(lambda e, rqb_=rqb_, rqf_=rqf_: e.activation(out=rqb_.t[:], in_=rqf_.t[:], func=AF.Copy), reads=[rqf_.r], writes=[rqb_.r], cost=0.6)
                        dec = qdec if which == 0 else kdec
                        rq4 = rqf_.t[:].rearrange("p (g d) -> p g d", d=64).unsqueeze(2).to_broadcast([128, 8, 2, 64])
                        dc4 = dec.t[:].unsqueeze(3).to_broadcast([128, 8, 2, 64])
                        if which == 0:
                            qd_ = qd[par]
                            S.pool(lambda e, rq4=rq4, dc4=dc4, qd_=qd_: e.tensor_tensor(out=qd_.t[:].rearrange("p (g a d) -> p g a d", g=8, a=2), in0=rq4, in1=dc4, op=ALU.mult),
                                  reads=[rqf_.r, dec.r], writes=[qd_.r], cost=2.5)
                            transpose_to(rqb_, 4, stg_rqT[sl], lambda sl=sl, j=j: stg_rqT[sl].t[:, :, j * 128:(j + 1) * 128], pT2[1])
                            transpose_to(qd_, 8, stg_qdT[sl], lambda sl=sl, j=j: stg_qdT[sl].t[:, :, j * 128:(j + 1) * 128], pT2[0])
                        else:
                            tkd_ = tkd[par]
                            S.pool(lambda e, rq4=rq4, dc4=dc4, tkd_=tkd_: e.tensor_tensor(out=tkd_.t[:].rearrange("p (g a d) -> p g a d", g=8, a=2), in0=rq4, in1=dc4, op=ALU.mult),
                                  reads=[rqf_.r, dec.r], writes=[tkd_.r], cost=2.5)
                            S.dma(f"s_kd{par}", lambda e, tkd_=tkd_, tok0=tok0: e.dma_start(out=rkd[tok0:tok0 + 128, :], in_=tkd_.t[:]), reads=[tkd_.r])
                            transpose_to(rqb_, 4, stg_rkT[sl], lambda sl=sl, j=j: stg_rkT[sl].t[:, :, j * 128:(j + 1) * 128], pT2[1])
                    b = proj(5, hnT_)
                    trv_ = trv[par]
                    S.act(lambda e, b=b, trv_=trv_: e.activation(out=trv_.t[:], in_=b.t[:], func=AF.Copy), reads=[b.r], writes=[trv_.r])
                    S.dma(f"s_rv{par}", lambda e, trv_=trv_, tok0=tok0: e.dma_start(out=rv[tok0:tok0 + 128, :], in_=trv_.t[:]), reads=[trv_.r])
                    b = proj(6, hnT_)
                    esg_, tgs_ = esg[par], tgs[par]
                    S.act(lambda e, b=b, esg_=esg_: e.activation(out=esg_.t[:], in_=b.t[:], func=AF.Exp, scale=-1.0), reads=[b.r], writes=[esg_.r])
                    S.act(lambda e, esg_=esg_: e.activation(out=esg_.t[:], in_=esg_.t[:], func=AF.Ln, bias=ones_f.t[:, 0:1]), reads=[esg_.r, ones_f.r], writes=[esg_.r], cost=0.6)
                    S.act(lambda e, esg_=esg_: e.activation(out=esg_.t[:], in_=esg_.t[:], func=AF.Exp, scale=-1.0), reads=[esg_.r], writes=[esg_.r], cost=0.6)
                    S.dve(lambda e, b=b, esg_=esg_, tgs_=tgs_: e.tensor_tensor(out=tgs_.t[:], in0=b.t[:], in1=esg_.t[:], op=ALU.mult), reads=[b.r, esg_.r], writes=[tgs_.r])
                    S.dma(f"s_gs{par}", lambda e, tgs_=tgs_, tok0=tok0: e.dma_start(out=rgs[tok0:tok0 + 128, :], in_=tgs_.t[:]), reads=[tgs_.r])
                    if j == 3:
                        c0 = sti * 512
                        cl = c0 if is_s else c0 - TS
                        S.dma(f"s_qT{sl}", lambda e, sl=sl, c0=c0: e.dma_start(out=dqT.rearrange("(h f) t -> f h t", f=128)[:, :, c0:c0 + 512], in_=stg_qT[sl].t[:]), reads=[stg_qT[sl].r])
                        if is_s:
                            S.dma(f"s_kT{sl}", lambda e, sl=sl, cl=cl: e.dma_start(out=dkT_s.rearrange("(h f) t -> f h t", f=128)[:, :, cl:cl + 512], in_=stg_kT[sl].t[:]), reads=[stg_kT[sl].r])
                        else:
                            for a in range(NSPL):
                                S.dma(f"s_kT{sl}", lambda e, sl=sl, cl=cl, a=a: e.dma_start(out=dkT_l[a].rearrange("(h f) t -> f h t", f=128)[:, :, cl:cl + 512], in_=stg_kT[sl].t[:, a * HPS:(a + 1) * HPS, :]), reads=[stg_kT[sl].r])
                        S.dma(f"s_rqT{sl}", lambda e, sl=sl, c0=c0: e.dma_start(out=rqT.rearrange("(h f) t -> f h t", f=128)[:, :, c0:c0 + 512], in_=stg_rqT[sl].t[:]), reads=[stg_rqT[sl].r])
                        S.dma(f"s_rkT{sl}", lambda e, sl=sl, c0=c0: e.dma_start(out=rkT.rearrange("(h f) t -> f h t", f=128)[:, :, c0:c0 + 512], in_=stg_rkT[sl].t[:]), reads=[stg_rkT[sl].r])
                        S.dma(f"s_qdT{sl}", lambda e, sl=sl, c0=c0: e.dma_start(out=rqdT.rearrange("(h f) t -> f h t", f=128)[:, :, c0:c0 + 512], in_=stg_qdT[sl].t[:]), reads=[stg_qdT[sl].r])
                S.barrier()

            RG = [[0, 1, 2, 3], [4, 5, 6, 7]]
            Rkg = Res("dkT_g")
            Rvg = Res("dv_g")
            for a in range(NSPL):
                S.dma("cc_k", lambda e, a=a: e.collective_compute("AllGather", ALU.bypass, replica_groups=RG, ins=[dkT_l[a]], outs=[dkT_g[a]]), writes=[Rkg], eng="pool", inc=1, cost=60.0)
                S.dma("cc_v", lambda e, a=a: e.collective_compute("AllGather", ALU.bypass, replica_groups=RG, ins=[dv_l[a]], outs=[dv_g[a]]), writes=[Rvg], eng="pool", inc=1, cost=60.0)

            with ExitStack() as st:
                SKMAX = max(TS, SKP)
                kTh = [sb(st, f"kTh{i}", [128, SKMAX], BF16) for i in range(2)]
                vh = [sb(st, f"vh{i}", [128, SKMAX // 128, 128], BF16) for i in range(2)]
                qA = [sb(st, f"qA{i}", [128, max(TS, TP)], BF16) for i in range(2)]
                qB = [sb(st, f"qB{i}", [128, max(TS, TP)], BF16) for i in range(2)]
                for i in range(2):
                    S.pool(lambda e, i=i: e.memset(qA[i].t[64:128, :], 0.0), writes=[qA[i].r])
                    S.pool(lambda e, i=i: e.memset(qB[i].t[0:64, :], 0.0), writes=[qB[i].r])
                NPX = 6
                pexp2 = [sb(st, f"pexp{i}", [128, 2, 512], BF16) for i in range(NPX)]
                s01 = [[sb(st, f"s01_{c}{i}", [128, 512], BF16) for i in range(2)] for c in range(2)]
                s23 = [[sb(st, f"s23_{c}{i}", [128, 512], BF16) for i in range(2)] for c in range(2)]
                s4 = [[sb(st, f"s4_{c}{i}", [128, 512], BF16) for i in range(2)] for c in range(2)]
                gcount = 0
                r0 = sb(st, "r0", [128, 512], F32)
                r1 = sb(st, "r1", [128, 512], F32)
                a0 = sb(st, "a0", [128, 512], F32)
                a1 = sb(st, "a1", [128, 512], F32)
                osq = sb(st, "osq", [128, 512], F32)
                rsn = sb(st, "rsn", [128, 512], F32)
                aout = [sb(st, f"aout{i}", [128, 512], BF16) for i in range(2)]
                sbk2 = [psb(st, f"sbk{i}", [128, 2, 512], F32) for i in range(2)]
                O = [psb(st, f"Oacc{i}", [128, 512], F32) for i in range(2)]
                L = [psb(st, f"Lacc{i}", [128, 512], F32) for i in range(2)]
                heads = [(job, h) for job in range(2) for h in range(4)]

                def load_head(hi):
                    job, h = heads[hi]
                    hb = hi % 2
                    kb_, vb_ = kTh[hb], vh[hb]
                    Tq = TS if job == 0 else TP
                    qoff = 0 if job == 0 else TS
                    if job == 0:
                        S.dma(f"b_k{hb}", lambda e, kb_=kb_, h=h: e.dma_start(out=kb_.t[:, 0:TS], in_=dkT_s[h * 128:(h + 1) * 128, :]), writes=[kb_.r])
                        S.dma(f"b_v{hb}", lambda e, vb_=vb_, h=h: e.dma_start(out=vb_.t[:, 0:TS // 128, :], in_=dv_s[:, h * 128:(h + 1) * 128].rearrange("(k p) e -> p k e", p=128)), writes=[vb_.r])
                    else:
                        ha = h // HPS
                        hl = h % HPS
                        S.dma(f"b_k{hb}", lambda e, kb_=kb_, ha=ha, hl=hl: e.dma_start(out=kb_.t[:, 0:SKP].rearrange("p (r t) -> p r t", r=NR), in_=dkT_g[ha].rearrange("(r f) t -> f r t", f=HPS * 128)[hl * 128:(hl + 1) * 128, :, :]), reads=[Rkg], writes=[kb_.r])
                        for a in range(NSPL):
                            for r_ in range(NR):
                                S.dma(f"b_v{hb}", lambda e, vb_=vb_, h=h, a=a, r_=r_: e.dma_start(
                                    out=vb_.t[:, (r_ * TP + a * TPS) // 128:(r_ * TP + (a + 1) * TPS) // 128, :],
                                    in_=dv_g[a][r_ * TPS:(r_ + 1) * TPS, h * 128:(h + 1) * 128].rearrange("(k p) e -> p k e", p=128)), reads=[Rvg], writes=[vb_.r])
                    S.dma(f"b_q{hb}", lambda e, hb=hb, h=h, qoff=qoff, Tq=Tq: e.dma_start(out=qA[hb].t[0:64, 0:Tq], in_=dqT[h * 128:h * 128 + 64, qoff:qoff + Tq]), writes=[qA[hb].r])
                    S.dma(f"b_r{hb}", lambda e, hb=hb, h=h, qoff=qoff, Tq=Tq: e.dma_start(out=qB[hb].t[64:128, 0:Tq], in_=dqT[h * 128 + 64:h * 128 + 128, qoff:qoff + Tq]), writes=[qB[hb].r])

                units = []
                for hi, (job, h) in enumerate(heads):
                    Tq = TS if job == 0 else TP
                    Sk = TS if job == 0 else SKP
                    for qc in range(Tq // 512):
                        for kb in range(Sk // 128):
                            units.append((hi, qc, kb, Sk // 128))

                def emit_qk(u):
                    hi, qc, kb, nkb = units[u]
                    hb = hi % 2
                    kb_ = kTh[hb]
                    sbuf_ = sbk2[u % 2]
                    for c in range(2):
                        qb_ = (qA if c == 0 else qB)[hb]
                        S.pe(lambda e, sbuf_=sbuf_, c=c, kb=kb, qc=qc, kb_=kb_, qb_=qb_: e.matmul(sbuf_.t[:, c, :], lhsT=kb_.t[:, kb * 128:(kb + 1) * 128],
                                                                                             rhs=qb_.t[:, qc * 512:(qc + 1) * 512], start=True, stop=True),
                             reads=[kb_.r, qb_.r], writes=[sbuf_.r])

                ocount = 0
                load_head(0)
                emit_qk(0)
                emit_qk(1)
                for u in range(len(units)):
                    hi, qc, kb, nkb = units[u]
                    job, h = heads[hi]
                    hb = hi % 2
                    vb_ = vh[hb]
                    qoff = 0 if job == 0 else TS
                    if qc == 0 and kb == 0 and hi + 1 < len(heads):
                        load_head(hi + 1)
                    pe2 = pexp2[u % NPX]
                    s2_ = sbk2[u % 2]
                    S.act(lambda e, pe2=pe2, s2_=s2_: e.activation(out=pe2.t[:], in_=s2_.t[:], func=AF.Exp), reads=[s2_.r], writes=[pe2.r], cost=1.05)
                    if u + 2 < len(units):
                        if units[u + 2][0] != hi and units[u + 2][1] == 0 and units[u + 2][2] == 0 and units[u + 2][0] + 1 < len(heads):
                            pass
                        emit_qk(u + 2)
                    gp = gcount % 2
                    for c in range(2):
                        pe_ = pexp2[u % NPX]
                        S.pe(lambda e, pe_=pe_, c=c, kb=kb, vb_=vb_, nkb=nkb: e.matmul(O[c].t[:], lhsT=vb_.t[:, kb, :], rhs=pe_.t[:, c, :], start=(kb == 0), stop=(kb == nkb - 1)),
                             reads=[vb_.r, pe_.r], writes=[O[c].r])
                        if kb % 2 == 1:
                            pp_ = pexp2[(u - 1) % NPX]
                            dst = (s01 if kb % 4 == 1 else s23)[c][gp]
                            S.dve(lambda e, dst=dst, pp_=pp_, pe_=pe_, c=c: e.tensor_tensor(out=dst.t[:], in0=pp_.t[:, c, :], in1=pe_.t[:, c, :], op=ALU.add), reads=[pp_.r, pe_.r], writes=[dst.r], cost=0.3)
                        if kb % 4 == 3:
                            a_, b_, d_ = s01[c][gp], s23[c][gp], s4[c][gp]
                            S.dve(lambda e, a_=a_, b_=b_, d_=d_: e.tensor_tensor(out=d_.t[:], in0=a_.t[:], in1=b_.t[:], op=ALU.add), reads=[a_.r, b_.r], writes=[d_.r], cost=0.3)
                            S.pe(lambda e, d_=d_, c=c, kb=kb, nkb=nkb: e.matmul(L[c].t[:], lhsT=ones_b.t[:], rhs=d_.t[:], start=(kb == 3), stop=(kb == nkb - 1)),
                                 reads=[ones_b.r, d_.r], writes=[L[c].r])
                    if kb % 4 == 3:
                        gcount += 1
                    if kb != nkb - 1:
                        continue
                    S.act(lambda e: e.activation(out=a0.t[:], in_=O[0].t[:], func=AF.Copy), reads=[O[0].r], writes=[a0.r])
                    S.dve(lambda e: e.tensor_copy(out=a1.t[:], in_=O[1].t[:]), reads=[O[1].r], writes=[a1.r])
                    S.dve(lambda e: e.reciprocal(out=r0.t[:], in_=L[0].t[:]), reads=[L[0].r], writes=[r0.r])
                    S.dve(lambda e: e.reciprocal(out=r1.t[:], in_=L[1].t[:]), reads=[L[1].r], writes=[r1.r])
                    S.dve(lambda e: e.tensor_tensor(out=a0.t[:], in0=a0.t[:], in1=r0.t[:], op=ALU.mult), reads=[a0.r, r0.r], writes=[a0.r])
                    S.dve(lambda e: e.tensor_tensor(out=a1.t[:], in0=a1.t[:], in1=r1.t[:], op=ALU.mult), reads=[a1.r, r1.r], writes=[a1.r])
                    S.dve(lambda e, l=l: e.scalar_tensor_tensor(out=a0.t[:], in0=a1.t[:], scalar=neglam.t[:, l:l + 1], in1=a0.t[:], op0=ALU.mult, op1=ALU.add),
                          reads=[a1.r, a0.r, neglam.r], writes=[a0.r])
                    S.act(lambda e: e.activation(out=osq.t[:], in_=a0.t[:], func=AF.Square), reads=[a0.r], writes=[osq.r])
                    sB = L[0]
                    S.pe(lambda e, sB=sB: e.matmul(sB.t[:], lhsT=ones_f.t[:], rhs=osq.t[:], start=True, stop=True), reads=[ones_f.r, osq.r], writes=[sB.r])
                    rsqrt_act(rsn.t[:], sB.t[:], 1.0 / 128, sB, rsn)
                    ao = aout[ocount % 2]
                    S.dve(lambda e, ao=ao: e.scalar_tensor_tensor(out=ao.t[:], in0=a0.t[:], scalar=subw.t[:, 0:1], in1=rsn.t[:], op0=ALU.mult, op1=ALU.mult),
                          reads=[a0.r, subw.r, rsn.r], writes=[ao.r])
                    tcol = qoff + qc * 512
                    S.dma(f"b_o{ocount % 2}", lambda e, ao=ao, h=h, tcol=tcol: e.dma_start(out=aT[h * 128:(h + 1) * 128, tcol:tcol + 512], in_=ao.t[:]), reads=[ao.r], eng="act")
                    ocount += 1
                S.barrier()

            with ExitStack() as st:
                SallJ = [sb(st, "Sall0", [128, TS // 128, 512], BF16), sb(st, "Sall1", [128, TP // 128, 512], BF16)]
                sttJ = [sb(st, f"stt{i}", [128, 512], F32) for i in range(2)]
                sttmpJ = [sb(st, f"sttmp{i}", [128, 512], F32) for i in range(2)]
                tg = sb(st, "tg", [128, NR, 512], F32)
                kdl = [sb(st, f"kdl{i}", [128, 4, 1024], BF16) for i in range(4)]
                rvl = [sb(st, f"rvl{i}", [128, 4, 512], BF16) for i in range(4)]
                okT = [sb(st, f"okT{i}", [128, 4, 512], BF16) for i in range(2)]
                oqT = [sb(st, f"oqT{i}", [128, 4, 512], BF16) for i in range(2)]
                oqd = [sb(st, f"oqd{i}", [128, 8, 512], BF16) for i in range(2)]
                orv = [sb(st, f"orv{i}", [128, 4, 512], BF16) for i in range(2)]
                ogs = [sb(st, f"ogs{i}", [128, 4, 512], BF16) for i in range(2)]
                PT = [sb(st, f"PT{i}", [128, 8, 128], BF16) for i in range(2)]
                rsq = sb(st, "rsq", [128, 512], F32)
                rss = sb(st, "rss", [128, 8], F32)
                rn = sb(st, "rn", [128, 512], F32)
                rr = sb(st, "rr", [128, 512], BF16)
                rTst = [sb(st, f"rTst{i}", [128, 4, 512], BF16) for i in range(2)]
                pkvJ = [psb(st, f"pkv{i}", [128, 512], F32) for i in range(2)]
                psc = [psb(st, f"psc{i}", [128, 4, 128], F32) for i in range(2)]
                po = [psb(st, f"pro{i}", [128, 512], F32) for i in range(2)]
                ptr = psb(st, "ptr", [128, 8, 128], BF16)
                Rtg = Res("st_g")
                ldn = [0]

                def sweep(job, use_init):
                    stt, sttmp, Sall = sttJ[job], sttmpJ[job], SallJ[job]
                    Tq = TS if job == 0 else TP
                    qoff = 0 if job == 0 else TS
                    n = Tq // 128
                    nsc = n // 4
                    if use_init:
                        S.dma("r_tg", lambda e: e.dma_start(out=tg.t[:], in_=st_g.rearrange("(r p) c -> p r c", p=128)), reads=[Rtg], writes=[tg.r])
                        for r_ in range(NR):
                            cb = coef.t[:, r_, :].unsqueeze(2).to_broadcast([128, 8, 64])
                            tg3 = tg.t[:, r_, :].rearrange("p (h e) -> p h e", e=64)
                            if r_ == 0:
                                S.dve(lambda e, cb=cb, tg3=tg3: e.tensor_tensor(out=stt.t[:].rearrange("p (h e) -> p h e", e=64), in0=tg3, in1=cb, op=ALU.mult), reads=[tg.r, coef.r], writes=[stt.r])
                            else:
                                S.dve(lambda e, cb=cb, tg3=tg3: e.tensor_tensor(out=sttmp.t[:].rearrange("p (h e) -> p h e", e=64), in0=tg3, in1=cb, op=ALU.mult), reads=[tg.r, coef.r], writes=[sttmp.r])
                                S.dve(lambda e: e.tensor_tensor(out=stt.t[:], in0=stt.t[:], in1=sttmp.t[:], op=ALU.add), reads=[stt.r, sttmp.r], writes=[stt.r])
                    else:
                        S.dve(lambda e: e.memset(stt.t[:], 0.0), writes=[stt.r])
                    cur = {}
                    for t in range(n):
                        tf = t
                        tb = n - 1 - t
                        bufs = {}
                        for nm, ti in (("f", tf), ("b", tb)):
                            sc = ti // 4
                            if (nm, sc) not in cur:
                                slot = job * 2 + (0 if nm == "f" else 1)
                                c0 = qoff + sc * 512
                                S.dma(f"r_kd{slot}", lambda e, slot=slot, c0=c0: e.dma_start(out=kdl[slot].t[:], in_=rkd[c0:c0 + 512, :].rearrange("(j p) c -> p j c", p=128)), writes=[kdl[slot].r])
                                S.dma(f"r_rv{slot}", lambda e, slot=slot, c0=c0: e.dma_start(out=rvl[slot].t[:], in_=rv[c0:c0 + 512, :].rearrange("(j p) c -> p j c", p=128)), writes=[rvl[slot].r])
                                cur = {k: v for k, v in cur.items() if k[0] != nm}
                                cur[(nm, sc)] = slot
                            bufs[nm] = (cur[(nm, sc)], ti % 4)
                        S.act(lambda e, tf=tf: e.activation(out=Sall.t[0:64, tf, :], in_=stt.t[0:64, :], func=AF.Copy), reads=[stt.r], writes=[Sall.r])
                        S.act(lambda e, tb=tb: e.activation(out=Sall.t[64:128, tb, :], in_=stt.t[64:128, :], func=AF.Copy), reads=[stt.r], writes=[Sall.r])
                        pk = pkvJ[job]
                        (sf, jf), (sb_, jb) = bufs["f"], bufs["b"]
                        for h in range(8):
                            kf = kdl[sf].t[:, jf, :].rearrange("p (g a d) -> p g a d", g=8, a=2)
                            kbk = kdl[sb_].t[:, jb, :].rearrange("p (g a d) -> p g a d", g=8, a=2)
                            S.pe(lambda e, pk=pk, h=h, kf=kf, sf=sf, jf=jf: e.matmul(pk.t[0:64, h * 64:(h + 1) * 64], lhsT=kf[:, h, 0, :], rhs=rvl[sf].t[:, jf, h * 64:(h + 1) * 64], start=True, stop=True),
                                 reads=[kdl[sf].r, rvl[sf].r], writes=[pk.r])
                            S.pe(lambda e, pk=pk, h=h, kbk=kbk, sb_=sb_, jb=jb: e.matmul(pk.t[64:128, h * 64:(h + 1) * 64], lhsT=kbk[:, h, 1, :], rhs=rvl[sb_].t[:, jb, h * 64:(h + 1) * 64], start=True, stop=True),
                                 reads=[kdl[sb_].r, rvl[sb_].r], writes=[pk.r])
                        S.dve(lambda e: e.tensor_tensor(out=sttmp.t[:].rearrange("p (h e) -> p h e", e=64), in0=stt.t[:].rearrange("p (h e) -> p h e", e=64),
                                                        in1=cdec.t[:].unsqueeze(2).to_broadcast([128, 8, 64]), op=ALU.mult), reads=[stt.r, cdec.r], writes=[sttmp.r])
                        S.dve(lambda e, pk=pk: e.tensor_tensor(out=stt.t[:], in0=pk.t[:], in1=sttmp.t[:], op=ALU.add), reads=[pk.r, sttmp.r], writes=[stt.r])

                def outputs(job):
                    Sall = SallJ[job]
                    Tq = TS if job == 0 else TP
                    qoff = 0 if job == 0 else TS
                    n = Tq // 128
                    for sc in range(n // 4):
                        sl = sc % 2
                        c0 = qoff + sc * 512
                        S.dma(f"o_kT{sl}", lambda e, sl=sl, c0=c0: e.dma_start(out=okT[sl].t[:], in_=rkT.rearrange("(b p) t -> p b t", p=128)[:, :, c0:c0 + 512]), writes=[okT[sl].r])
                        S.dma(f"o_qT{sl}", lambda e, sl=sl, c0=c0: e.dma_start(out=oqT[sl].t[:], in_=rqT.rearrange("(b p) t -> p b t", p=128)[:, :, c0:c0 + 512]), writes=[oqT[sl].r])
                        S.dma(f"o_qd{sl}", lambda e, sl=sl, c0=c0: e.dma_start(out=oqd[sl].t[:], in_=rqdT.rearrange("(b p) t -> p b t", p=128)[:, :, c0:c0 + 512]), writes=[oqd[sl].r])
                        S.dma(f"o_rv{sl}", lambda e, sl=sl, c0=c0: e.dma_start(out=orv[sl].t[:], in_=rv[c0:c0 + 512, :].rearrange("(j p) c -> p j c", p=128)), writes=[orv[sl].r])
                        S.dma(f"o_gs{sl}", lambda e, sl=sl, c0=c0: e.dma_start(out=ogs[sl].t[:], in_=rgs[c0:c0 + 512, :].rearrange("(j p) c -> p j c", p=128)), writes=[ogs[sl].r])
                        for j in range(4):
                            i = sc * 4 + j
                            ptb = PT[j % 2]
                            for h in range(8):
                                pb_ = psc[h % 2]
                                hp = (h % 2) * 64
                                S.pe(lambda e, pb_=pb_, h=h, hp=hp, sl=sl, j=j: e.matmul(pb_.t[:, h // 2, :], lhsT=okT[sl].t[hp:hp + 64, h // 2, j * 128:(j + 1) * 128],
                                                                                   rhs=oqT[sl].t[hp:hp + 64, h // 2, j * 128:(j + 1) * 128], start=True, stop=True),
                                     reads=[okT[sl].r, oqT[sl].r], writes=[pb_.r])
                            pt4 = ptb.t[:].rearrange("p (b a) n -> p b a n", a=2)
                            S.dve(lambda e, pt4=pt4: e.tensor_tensor(out=pt4[:, :, 0, :], in0=psc[0].t[:], in1=DTe.t[:], op=ALU.mult), reads=[psc[0].r, DTe.r], writes=[ptb.r])
                            S.dve(lambda e, pt4=pt4: e.tensor_tensor(out=pt4[:, :, 1, :], in0=psc[1].t[:], in1=DTo.t[:], op=ALU.mult), reads=[psc[1].r, DTo.r, ptb.r], writes=[ptb.r])
                            pob = po[j % 2]
                            for h in range(8):
                                S.pe(lambda e, pob=pob, h=h, ptb=ptb, sl=sl, j=j: e.matmul(pob.t[:, h * 64:(h + 1) * 64], lhsT=ptb.t[:, h, :], rhs=orv[sl].t[:, j, h * 64:(h + 1) * 64], start=True, stop=False),
                                     reads=[ptb.r, orv[sl].r], writes=[pob.r])
                                S.pe(lambda e, pob=pob, h=h, sl=sl, j=j, i=i: e.matmul(pob.t[:, h * 64:(h + 1) * 64], lhsT=oqd[sl].t[:, h, j * 128:(j + 1) * 128], rhs=Sall.t[:, i, h * 64:(h + 1) * 64], start=False, stop=True),
                                     reads=[oqd[sl].r, Sall.r], writes=[pob.r])
                            S.act(lambda e, pob=pob: e.activation(out=rsq.t[:], in_=pob.t[:], func=AF.Square), reads=[pob.r], writes=[rsq.r])
                            S.dve(lambda e: e.tensor_reduce(out=rss.t[:], in_=rsq.t[:].rearrange("p (g d) -> p g d", d=64), axis=AX.X, op=ALU.add), reads=[rsq.r], writes=[rss.r])
                            rsqrt_act(rss.t[:], rss.t[:], 1.0 / 64, rss, rss)
                            S.dve(lambda e, pob=pob: e.tensor_tensor(out=rn.t[:].rearrange("p (g d) -> p g d", d=64), in0=pob.t[:].rearrange("p (g d) -> p g d", d=64),
                                                                    in1=rss.t[:].unsqueeze(2).to_broadcast([128, 8, 64]), op=ALU.mult), reads=[pob.r, rss.r], writes=[rn.r])
                            S.dve(lambda e: e.tensor_tensor(out=rn.t[:].rearrange("p (g d) -> p g d", d=64), in0=rn.t[:].rearrange("p (g d) -> p g d", d=64),
                                                            in1=gnw.t[:].unsqueeze(1).to_broadcast([128, 8, 64]), op=ALU.mult), reads=[rn.r, gnw.r], writes=[rn.r])
                            S.dve(lambda e, sl=sl, j=j: e.tensor_tensor(out=rr.t[:], in0=rn.t[:], in1=ogs[sl].t[:, j, :], op=ALU.mult), reads=[rn.r, ogs[sl].r], writes=[rr.r])
                            for c in range(4):
                                S.pe(lambda e, c=c: e.transpose(out=ptr.t[:, c, :], in_=rr.t[:, c * 128:(c + 1) * 128], identity=ident.t[:]), reads=[rr.r, ident.r], writes=[ptr.r])
                            S.act(lambda e, sl=sl, j=j: e.activation(out=rTst[sl].t[:, :, j * 128:(j + 1) * 128], in_=ptr.t[:, 0:4, :], func=AF.Copy), reads=[ptr.r], writes=[rTst[sl].r])
                        S.dma(f"o_rT{sl}", lambda e, sl=sl, c0=c0: e.dma_start(out=rT.rearrange("(b p) t -> p b t", p=128)[:, :, c0:c0 + 512], in_=rTst[sl].t[:]), reads=[rTst[sl].r])

                sweep(1, False)
                Rstl = Res("st_l")
                S.dma("r_stl", lambda e: e.dma_start(out=st_l, in_=sttJ[1].t[:]), reads=[sttJ[1].r], writes=[Rstl])
                S.dma("cc_s", lambda e: e.collective_compute("AllGather", ALU.bypass, replica_groups=RG, ins=[st_l], outs=[st_g]), reads=[Rstl], writes=[Rtg], eng="pool", inc=1, cost=250.0)
                sweep(0, False)
                outputs(0)
                sweep(1, True)
                outputs(1)
                S.barrier()

            with ExitStack() as st:
                w_out = load_w(st, "w_out", w_out_in[l], D, D, "w_out")
                w1 = load_w(st, "w1", w1_in[l], D, DFF, "w1")
                arT = [sb(st, f"arT{i}", [128, 8, 512], BF16) for i in range(2)]
                xs_ = [sb(st, f"xs{i}", [128, 4, D], F32) for i in range(2)]
                junk = sb(st, "junk2", [128, D], BF16)
                ss2 = sb(st, "ss2", [128, 1], F32)
                h2l = [sb(st, f"h2_{i}", [128, D], BF16) for i in range(2)]
                h2T = [sb(st, f"h2T{i}", [128, 8, 512], BF16) for i in range(1)] * 2
                rl = [sb(st, f"rl{i}", [128, 512], F32) for i in range(2)]
                uTs = [sb(st, f"uTs{i}", [128, 32, 512], BF16) for i in range(1)] * 2
                pw = [psb(st, f"pw{i}", [128, 512], F32) for i in range(2)]
                ph = psb(st, "ph", [128, 8, 128], BF16)
                pu = [psb(st, f"pu{i}", [128, 512], F32) for i in range(4)]
                for s in range(NST):
                    sl = s % 2
                    c0 = s * 512
                    S.dma(f"c_a{sl}", lambda e, sl=sl, c0=c0: e.dma_start(out=arT[sl].t[:, 0:4, :], in_=aT.rearrange("(b p) t -> p b t", p=128)[:, :, c0:c0 + 512]), writes=[arT[sl].r])
                    S.dma(f"c_r{sl}", lambda e, sl=sl, c0=c0: e.dma_start(out=arT[sl].t[:, 4:8, :], in_=rT.rearrange("(b p) t -> p b t", p=128)[:, :, c0:c0 + 512]), writes=[arT[sl].r])
                    S.dma(f"c_x{sl}", lambda e, sl=sl, c0=c0: e.dma_start(out=xs_[sl].t[:], in_=x_src[c0:c0 + 512, :].rearrange("(j p) c -> p j c", p=128)), writes=[xs_[sl].r])
                    for j in range(4):
                        for cb in range(2):
                            pb_ = pw[cb]
                            for k in range(8):
                                S.pe(lambda e, pb_=pb_, k=k, cb=cb, sl=sl, j=j: e.matmul(pb_.t[:], lhsT=arT[sl].t[:, k, j * 128:(j + 1) * 128], rhs=w_out.t[:, k, cb * 512:(cb + 1) * 512], start=(k == 0), stop=(k == 7)),
                                     reads=[arT[sl].r, w_out.r(k, cb * 512)], writes=[pb_.r])
                            S.dve(lambda e, pb_=pb_, cb=cb, sl=sl, j=j: e.tensor_tensor(out=xs_[sl].t[:, j, cb * 512:(cb + 1) * 512], in0=pb_.t[:], in1=xs_[sl].t[:, j, cb * 512:(cb + 1) * 512], op=ALU.add),
                                  reads=[pb_.r, xs_[sl].r], writes=[xs_[sl].r])
                        S.act(lambda e, sl=sl, j=j: e.activation(out=junk.t[:], in_=xs_[sl].t[:, j, :], func=AF.Square, accum_out=ss2.t[:]), reads=[xs_[sl].r], writes=[junk.r, ss2.r])
                        rsqrt_act(ss2.t[:], ss2.t[:], 1.0 / D, ss2, ss2)
                        h2 = h2l[j % 2]
                        S.dve(lambda e, sl=sl, j=j, h2=h2: e.scalar_tensor_tensor(out=h2.t[:], in0=xs_[sl].t[:, j, :], scalar=ss2.t[:, 0:1], in1=ln2b.t[:], op0=ALU.mult, op1=ALU.mult),
                              reads=[xs_[sl].r, ss2.r, ln2b.r], writes=[h2.r])
                        for c in range(8):
                            S.pe(lambda e, c=c, h2=h2: e.transpose(out=ph.t[:, c, :], in_=h2.t[:, c * 128:(c + 1) * 128], identity=ident.t[:]), reads=[h2.r, ident.r], writes=[ph.r])
                        S.act(lambda e, sl=sl, j=j: e.activation(out=h2T[sl].t[:, :, j * 128:(j + 1) * 128], in_=ph.t[:], func=AF.Copy), reads=[ph.r], writes=[h2T[sl].r])
                    S.dma(f"c_xo{sl}", lambda e, sl=sl, c0=c0: e.dma_start(out=xres[c0:c0 + 512, :].rearrange("(j p) c -> p j c", p=128), in_=xs_[sl].t[:]), reads=[xs_[sl].r], eng="act")
                    for fc in range(32):
                        pb_ = pu[fc % 4]
                        for k in range(8):
                            S.pe(lambda e, pb_=pb_, k=k, fc=fc, sl=sl: e.matmul(pb_.t[:], lhsT=w1.t[:, k, fc * 128:(fc + 1) * 128], rhs=h2T[sl].t[:, k, :], start=(k == 0), stop=(k == 7)),
                                 reads=[w1.r(k, fc * 128), h2T[sl].r], writes=[pb_.r])
                        rb = rl[fc % 2]
                        S.act(lambda e, pb_=pb_, rb=rb: e.activation(out=rb.t[:], in_=pb_.t[:], func=AF.Relu), reads=[pb_.r], writes=[rb.r])
                        S.dve(lambda e, rb=rb, fc=fc, sl=sl: e.tensor_tensor(out=uTs[sl].t[:, fc, :], in0=rb.t[:], in1=rb.t[:], op=ALU.mult), reads=[rb.r], writes=[uTs[sl].r])
                    S.dma(f"c_u{sl}", lambda e, sl=sl, c0=c0: e.dma_start(out=uT.rearrange("(c p) t -> p c t", p=128)[:, :, c0:c0 + 512], in_=uTs[sl].t[:]), reads=[uTs[sl].r], eng="act")
                S.barrier()

            with ExitStack() as st:
                w2 = load_w(st, "w2", w2_in[l], DFF, D, "w2", gn=1024, korder=True)
                wg = load_w(st, "wg", wg_in[l], D, D, "wg")
                wp = load_w(st, "wp", wp_in[l], PLE, D, "wp")
                uTl = [sb(st, f"uTl{i}", [128, 32, 512], BF16) for i in range(2)]
                xs_ = [sb(st, f"xc{i}", [128, 4, D], F32) for i in range(1)] * 2
                pl_ = [sb(st, f"pl{i}", [128, 4, PLE], F32) for i in range(1)] * 2
                x2bl = [sb(st, f"x2b{i}", [128, D], BF16) for i in range(2)]
                x2Tl = [sb(st, f"x2T{i}", [128, 8, 128], BF16) for i in range(2)]
                pbfl = [sb(st, f"pbf{i}", [128, PLE], BF16) for i in range(2)]
                ppTl = [sb(st, f"ppT{i}", [128, 2, 128], BF16) for i in range(2)]
                sg = [sb(st, f"sg{i}", [128, 512], F32) for i in range(2)]
                pm = [psb(st, f"pm{i}", [128, 512], F32) for i in range(2)]
                pg = [psb(st, f"pg{i}", [128, 512], F32) for i in range(2)]
                pq = [psb(st, f"pq{i}", [128, 512], F32) for i in range(2)]
                px = psb(st, "px", [128, 8, 128], BF16)
                pp2 = psb(st, "pp2", [128, 8, 128], BF16)
                for s in range(NST):
                    sl = s % 2
                    c0 = s * 512
                    S.dma(f"d_u{sl}", lambda e, sl=sl, c0=c0: e.dma_start(out=uTl[sl].t[:], in_=uT.rearrange("(c p) t -> p c t", p=128)[:, :, c0:c0 + 512]), writes=[uTl[sl].r])
                    S.dma(f"d_x{sl}", lambda e, sl=sl, c0=c0: e.dma_start(out=xs_[sl].t[:], in_=xres[c0:c0 + 512, :].rearrange("(j p) c -> p j c", p=128)), writes=[xs_[sl].r])
                    S.dma(f"d_p{sl}", lambda e, sl=sl, c0=c0, l=l: e.dma_start(out=pl_[sl].t[:], in_=p_in[l, c0:c0 + 512, :].rearrange("(j p) c -> p j c", p=128)), writes=[pl_[sl].r])
                    for j in range(4):
                        for cb in range(2):
                            pb_ = pm[cb]
                            for fc in range(32):
                                S.pe(lambda e, pb_=pb_, fc=fc, cb=cb, sl=sl, j=j: e.matmul(pb_.t[:], lhsT=uTl[sl].t[:, fc, j * 128:(j + 1) * 128], rhs=w2.t[:, fc, cb * 512:(cb + 1) * 512], start=(fc == 0), stop=(fc == 31)),
                                     reads=[uTl[sl].r, w2.r(fc, cb * 512)], writes=[pb_.r])
                            S.dve(lambda e, pb_=pb_, cb=cb, sl=sl, j=j: e.tensor_tensor(out=xs_[sl].t[:, j, cb * 512:(cb + 1) * 512], in0=pb_.t[:], in1=xs_[sl].t[:, j, cb * 512:(cb + 1) * 512], op=ALU.add),
                                  reads=[pb_.r, xs_[sl].r], writes=[xs_[sl].r])
                        x2b, x2T, pbf, ppT = x2bl[j % 2], x2Tl[j % 2], pbfl[j % 2], ppTl[j % 2]
                        S.act(lambda e, sl=sl, j=j, x2b=x2b: e.activation(out=x2b.t[:], in_=xs_[sl].t[:, j, :], func=AF.Copy), reads=[xs_[sl].r], writes=[x2b.r], cost=1.1)
                        for c in range(8):
                            S.pe(lambda e, c=c, x2b=x2b: e.transpose(out=px.t[:, c, :], in_=x2b.t[:, c * 128:(c + 1) * 128], identity=ident.t[:]), reads=[x2b.r, ident.r], writes=[px.r])
                        S.act(lambda e, x2T=x2T: e.activation(out=x2T.t[:], in_=px.t[:], func=AF.Copy), reads=[px.r], writes=[x2T.r], cost=1.1)
                        S.pool(lambda e, sl=sl, j=j, pbf=pbf: e.tensor_copy(out=pbf.t[:], in_=pl_[sl].t[:, j, :]), reads=[pl_[sl].r], writes=[pbf.r])
                        for c in range(2):
                            S.pe(lambda e, c=c, pbf=pbf: e.transpose(out=pp2.t[:, c, :], in_=pbf.t[:, c * 128:(c + 1) * 128], identity=ident.t[:]), reads=[pbf.r, ident.r], writes=[pp2.r])
                        S.act(lambda e, ppT=ppT: e.activation(out=ppT.t[:], in_=pp2.t[:, 0:2, :], func=AF.Copy), reads=[pp2.r], writes=[ppT.r])
                        for cb in range(2):
                            g_ = pg[cb]
                            q_ = pq[cb]
                            for k in range(8):
                                S.pe(lambda e, g_=g_, k=k, cb=cb: e.matmul(g_.t[:], lhsT=x2T.t[:, k, :], rhs=wg.t[:, k, cb * 512:(cb + 1) * 512], start=(k == 0), stop=(k == 7)),
                                     reads=[x2T.r, wg.r(k, cb * 512)], writes=[g_.r])
                            for k in range(2):
                                S.pe(lambda e, q_=q_, k=k, cb=cb: e.matmul(q_.t[:], lhsT=ppT.t[:, k, :], rhs=wp.t[:, k, cb * 512:(cb + 1) * 512], start=(k == 0), stop=(k == 1)),
                                     reads=[ppT.r, wp.r(k, cb * 512)], writes=[q_.r])
                            sgb = sg[cb]
                            S.act(lambda e, g_=g_, sgb=sgb: e.activation(out=sgb.t[:], in_=g_.t[:], func=AF.Exp, scale=-1.0), reads=[g_.r], writes=[sgb.r])
                            S.dve(lambda e, sgb=sgb: e.tensor_scalar(out=sgb.t[:], in0=sgb.t[:], scalar1=1.0, scalar2=None, op0=ALU.add), reads=[sgb.r], writes=[sgb.r])
                            S.dve(lambda e, sgb=sgb: e.reciprocal(out=sgb.t[:], in_=sgb.t[:]), reads=[sgb.r], writes=[sgb.r])
                            S.dve(lambda e, sgb=sgb, q_=q_: e.tensor_tensor(out=sgb.t[:], in0=q_.t[:], in1=sgb.t[:], op=ALU.mult), reads=[q_.r, sgb.r], writes=[sgb.r])
                            S.dve(lambda e, sgb=sgb, cb=cb, sl=sl, j=j: e.tensor_tensor(out=xs_[sl].t[:, j, cb * 512:(cb + 1) * 512], in0=sgb.t[:], in1=xs_[sl].t[:, j, cb * 512:(cb + 1) * 512], op=ALU.add),
                                  reads=[sgb.r, xs_[sl].r], writes=[xs_[sl].r])
                    S.dma(f"d_xo{sl}", lambda e, sl=sl, c0=c0: e.dma_start(out=x_dst[c0:c0 + 512, :].rearrange("(j p) c -> p j c", p=128), in_=xs_[sl].t[:]), reads=[xs_[sl].r], eng="act")
                S.barrier()

        if DEBUG:
            for nm, src in (("rgs", rgs), ("rv", rv), ("dv_s", dv_s), ("dqT", dqT), ("aT", aT), ("rT", rT), ("xres", xres), ("uT", uT), ("rqT", rqT), ("rkd", rkd), ("rqdT", rqdT), ("rkT", rkT), ("dkT_s", dkT_s)):
                dbg = dram("dbg_" + nm, list(src.shape), src.dtype, "ExternalOutput")
                S.dma("dbg", lambda e, dbg=dbg, src=src: e.dma_start(out=dbg, in_=src))
        S.emit(top)
        nc._n_ops = len(S.ops)
        nc._n_sem = S.nsem
    return nc


ROPE_THETA = 500000.0
RET_THETA = 10000.0


def host_tables(TS, TP, rank):
    posv = np.concatenate([np.arange(TS), rank * TP + np.arange(TP)]).astype(np.float32)
    inv_d = (np.float32(1.0) / (np.float32(ROPE_THETA) ** (np.arange(0, 16, 2, dtype=np.float32) / np.float32(16)))).astype(np.float32)
    inv_r = (np.float32(1.0) / (np.float32(RET_THETA) ** (np.arange(0, 64, 2, dtype=np.float32) / np.float32(64)))).astype(np.float32)
    ang_d = (posv[:, None] * inv_d[None, :]).astype(np.float32)
    ang_r = (posv[:, None] * inv_r[None, :]).astype(np.float32)
    pos = np.concatenate([np.cos(ang_d), np.sin(ang_d), np.cos(ang_r), np.sin(ang_r)], axis=1).astype(np.float32)
    m = np.arange(128)[:, None].astype(np.float32)
    n = np.arange(128)[None, :].astype(np.float32)
    relF = np.maximum(n - m, 0.0)
    mskF = (n >= m).astype(np.float32) * 0.125
    relB = np.maximum(m - n, 0.0)
    mskB = (m > n).astype(np.float32) * 0.125
    p = np.arange(128, dtype=np.float32)[:, None]
    cst = np.concatenate([relF, mskF, relB, mskB, p + 1, 128 - p, 127 - p, p], axis=1).astype(np.float32)
    rkt = np.zeros((128, 8), np.float32)
    for r in range(NR):
        if r < rank:
            rkt[0:64, r] = TP * (rank - 1 - r)
            rkt[0:64, 4 + r] = 1.0
        if r > rank:
            rkt[64:128, r] = TP * (r - rank - 1)
            rkt[64:128, 4 + r] = 1.0
    return pos, cst, rkt


_NC_CACHE = {}


def run_cores(inputs, TS, TP, DEPTH):
    key = (TS, TP, DEPTH)
    if key not in _NC_CACHE:
        _NC_CACHE[key] = build(TS, TP, DEPTH)
    nc = _NC_CACHE[key]
    f = lambda a: np.ascontiguousarray(np.asarray(a, dtype=np.float32))
    xp = f(inputs["x_prompt"])
    xs = f(inputs["x_sample"])
    pp = f(inputs["p_prompt"])
    ps = f(inputs["p_sample"])
    shared = {
        "ln1_w": f(inputs["ln1_w"]), "w_in": f(inputs["w_in"]), "diff_q_norm": f(inputs["diff_q_norm"]),
        "diff_k_norm": f(inputs["diff_k_norm"]), "diff_lambda": f(inputs["diff_lambda"]).reshape(1, DEPTH * 256),
        "diff_subln": f(inputs["diff_subln"]), "ret_decay_logit": f(inputs["ret_decay_logit"]).reshape(1, DEPTH * 16),
        "ret_gn": f(inputs["ret_gn"]), "w_out": f(inputs["w_out"]), "ln2_w": f(inputs["ln2_w"]),
        "w_mlp1": f(inputs["w_mlp1"]), "w_mlp2": f(inputs["w_mlp2"]), "w_ple_gate": f(inputs["w_ple_gate"]),
        "w_ple_proj": f(inputs["w_ple_proj"]),
    }
    in_maps = []
    for c in range(8):
        g, r = c // NR, c % NR
        pos, cst, rkt = host_tables(TS, TP, r)
        m = dict(shared)
        m["x"] = np.ascontiguousarray(np.concatenate([xs[c], xp[g, r * TP:(r + 1) * TP]], axis=0))
        m["p"] = np.ascontiguousarray(np.concatenate([ps[:, c], pp[:, g, r * TP:(r + 1) * TP]], axis=1))
        m["pos"] = pos
        m["cst"] = cst
        m["rkt"] = rkt
        in_maps.append(m)
    res = run_bass_kernel_spmd(nc, in_maps, core_ids=list(range(8)))
    y_s = np.stack([np.asarray(res.results[c]["y"][:TS]) for c in range(8)], axis=0)
    y_p = np.stack([np.concatenate([np.asarray(res.results[g * NR + r]["y"][TS:]) for r in range(NR)], axis=0) for g in range(2)], axis=0)
    return y_p.astype(np.float32), y_s.astype(np.float32)


def kernel(**inputs):
    return run_cores(inputs, 4096, 2048, 4)
```

```python
import math
import types
import numpy as np
import concourse.bass as bass
import concourse.mybir as mybir
from concourse.bass_utils import run_bass_kernel_spmd
from contextlib import ExitStack

F32 = mybir.dt.float32
BF16 = mybir.dt.bfloat16
AF = mybir.ActivationFunctionType
ALU = mybir.AluOpType
AX = mybir.AxisListType

COMPUTE = ("pe", "act", "dve", "pool")
ALL_ENG = ("pe", "act", "dve", "pool", "sp")
SAME_ENG_SYNC = True

D = 1024
INW = 3584
DFF = 4096
PLE = 256
EPS = 1e-6
NR = 4


class Res:
    __slots__ = ("name", "w", "rs")

    def __init__(self, name=""):
        self.name = name
        self.w = None
        self.rs = []


class Op:
    __slots__ = ("eng", "fn", "deps", "mark", "val", "key", "is_mm", "inc", "cost", "idx", "fin", "succ", "npend", "ready")

    def __init__(self, eng, fn, key=None, is_mm=False, cost=None):
        self.eng = eng
        self.fn = fn
        self.deps = []
        self.mark = False
        self.val = 0
        self.key = key
        self.is_mm = is_mm
        self.inc = 16
        self.cost = cost
        self.idx = 0
        self.fin = 0.0
        self.succ = None
        self.npend = 0
        self.ready = 0.0


def _freeze(fn):
    if fn is None or fn.__closure__ is None:
        return fn
    cells = []
    for c in fn.__closure__:
        try:
            cells.append(types.CellType(c.cell_contents))
        except ValueError:
            cells.append(c)
    return types.FunctionType(fn.__code__, fn.__globals__, fn.__name__, fn.__defaults__, tuple(cells))


DEF_COST = {"pe": 0.27, "act": 0.45, "dve": 0.60, "pool": 0.90, "sp": 0.30}
SLAT = 0.5
XLAT = 0.45
DMA_LAT = 3.0
RESCHEDULE = True


class _Probe:
    def __init__(self):
        self.name = None
        self.args = ()
        self.kw = {}

    def __getattr__(self, name):
        def f(*a, **k):
            if self.name is None:
                self.name, self.args, self.kw = name, a, k
            return self
        return f


def _esize(ap):
    n = 1
    for d in ap.shape[1:]:
        n *= int(d)
    return n


def _estimate(eng, fn, key):
    try:
        p = _Probe()
        fn(p)
        out = p.kw.get("out", p.args[0] if p.args else None)
        if out is None or not hasattr(out, "shape"):
            return None
        n = _esize(out)
        if key is not None:
            nbytes = n * int(out.shape[0]) * (2 if out.dtype == BF16 else 4)
            return 2.5 + nbytes / 150e3
        if eng == "pe":
            return 0.03 + n / 2100.0
        if eng == "act":
            return 0.08 + n / 960.0
        if eng == "dve":
            return 0.07 + n / 960.0
        if eng == "pool":
            return 0.6 + n / 700.0
    except Exception:
        return None
    return None


class Sched:
    def __init__(self, nc):
        self.nc = nc
        self.ops = []

    def op(self, eng, fn, reads=(), writes=(), key=None, is_mm=False, cost=None):
        fn = _freeze(fn)
        if cost is None and fn is not None:
            cost = _estimate(eng, fn, key)
        o = Op(eng, fn, key, is_mm, cost)
        deps = o.deps
        for r in reads:
            if r.w is not None:
                deps.append(r.w)
        for w in writes:
            if w.w is not None:
                deps.append(w.w)
            deps.extend(w.rs)
        for r in reads:
            r.rs.append(o)
        for w in writes:
            w.w = o
            w.rs = []
        o.idx = len(self.ops)
        self.ops.append(o)
        return o

    def pe(self, fn, reads=(), writes=(), cost=None):
        return self.op("pe", fn, reads, writes, is_mm=True, cost=cost)

    def act(self, fn, reads=(), writes=(), cost=None):
        return self.op("act", fn, reads, writes, cost=cost)

    def dve(self, fn, reads=(), writes=(), cost=None):
        return self.op("dve", fn, reads, writes, cost=cost)

    def pool(self, fn, reads=(), writes=(), cost=None):
        return self.op("pool", fn, reads, writes, cost=cost)

    def dma(self, key, fn, reads=(), writes=(), eng="sp", inc=16, cost=None):
        o = self.op(eng, fn, reads, writes, key=key, cost=cost)
        o.inc = inc
        return o

    def barrier(self):
        o = Op(None, None)
        o.idx = len(self.ops)
        self.ops.append(o)

    @staticmethod
    def _skip(p, o):
        return p.key is None and p.eng == o.eng and (not SAME_ENG_SYNC or (p.is_mm and o.is_mm))

    def _schedule_segment(self, seg):
        import heapq
        inseg = set(id(o) for o in seg)
        for o in seg:
            o.succ = []
            o.npend = 0
            o.ready = 0.0
        for o in seg:
            for p in o.deps:
                if id(p) in inseg:
                    p.succ.append(o)
                    o.npend += 1
        free = {e: 0.0 for e in ALL_ENG}
        heaps = {e: [] for e in ALL_ENG}
        for o in seg:
            if o.npend == 0:
                heapq.heappush(heaps[o.eng], (0.0, o.idx, o))
        order = {e: [] for e in ALL_ENG}
        remaining = len(seg)
        while remaining:
            best = None
            for e in ALL_ENG:
                h = heaps[e]
                if not h:
                    continue
                t_free = free[e]
                cands = []
                while h and h[0][0] <= t_free:
                    cands.append(heapq.heappop(h))
                if cands:
                    c = min(cands, key=lambda x: x[1])
                    for x in cands:
                        if x is not c:
                            heapq.heappush(h, x)
                    heapq.heappush(h, c)
                    start = t_free
                    pick = c
                else:
                    pick = h[0]
                    start = pick[0]
                if best is None or start < best[0] or (start == best[0] and pick[1] < best[2][1]):
                    best = (start, e, pick)
            start, e, pick = best
            h = heaps[e]
            h.remove(pick)
            heapq.heapify(h)
            o = pick[2]
            cost = o.cost if o.cost is not None else DEF_COST[e]
            if o.key is not None:
                free[e] = start + DEF_COST["sp"]
                o.fin = start + (cost if o.cost is not None else DMA_LAT)
            else:
                free[e] = start + cost
                o.fin = free[e]
            order[e].append(o)
            remaining -= 1
            for q in o.succ:
                if q.eng == o.eng and o.key is None:
                    lat = 0.0 if (o.is_mm and q.is_mm) else SLAT
                else:
                    lat = XLAT
                r = o.fin + lat
                if r > q.ready:
                    q.ready = r
                q.npend -= 1
                if q.npend == 0:
                    heapq.heappush(heaps[q.eng], (q.ready, q.idx, q))
        return order

    def emit(self, stack):
        nc = self.nc
        segs = []
        cur = []
        for o in self.ops:
            if o.eng is None:
                if cur:
                    segs.append(cur)
                    cur = []
            else:
                cur.append(o)
        if cur:
            segs.append(cur)
        streams = {e: [] for e in ALL_ENG}
        for seg in segs:
            if RESCHEDULE:
                order = self._schedule_segment(seg)
            else:
                order = {e: [o for o in seg if o.eng == e] for e in ALL_ENG}
            lastops = {}
            for e in ALL_ENG:
                for o in order[e]:
                    lastops[o.key if o.key is not None else e] = o
            for e in ALL_ENG:
                streams[e].extend(order[e])
            deps = list(lastops.values())
            for e in ALL_ENG:
                b = Op(e, None)
                b.deps = list(deps)
                streams[e].append(b)
        for e in ALL_ENG:
            for o in streams[e]:
                for p in o.deps:
                    if p.key is None and not self._skip(p, o):
                        p.mark = True
        kcnt = {}
        for e in ALL_ENG:
            cnt = 0
            for o in streams[e]:
                if o.fn is None:
                    continue
                if o.key is not None:
                    kcnt[o.key] = kcnt.get(o.key, 0) + o.inc
                    o.val = kcnt[o.key]
                elif o.mark:
                    cnt += 1
                    o.val = cnt
        sems = {}
        for e in COMPUTE:
            sems[e] = stack.enter_context(nc.semaphore("s_" + e))
        for k in kcnt:
            sems[k] = stack.enter_context(nc.semaphore("d_" + k))
        self.nsem = len(sems)
        block = stack.enter_context(nc.Block())

        def run(engname, eng):
            seen = {}
            for o in streams[engname]:
                need = {}
                for p in o.deps:
                    if self._skip(p, o):
                        continue
                    sk = p.eng if p.key is None else p.key
                    if seen.get(sk, 0) >= p.val:
                        continue
                    if need.get(sk, 0) < p.val:
                        need[sk] = p.val
                for sk, v in need.items():
                    eng.wait_ge(sems[sk], v)
                    seen[sk] = v
                if o.fn is None:
                    continue
                ins = o.fn(eng)
                if o.key is not None:
                    ins.then_inc(sems[o.key], o.inc)
                elif o.mark:
                    ins.then_inc(sems[o.eng], 1)
            if engname == "sp":
                for k, v in kcnt.items():
                    if seen.get(k, 0) < v:
                        eng.wait_ge(sems[k], v)

        @block.tensor
        def _(e):
            run("pe", e)

        @block.scalar
        def _(e):
            run("act", e)

        @block.vector
        def _(e):
            run("dve", e)

        @block.gpsimd
        def _(e):
            run("pool", e)

        @block.sync
        def _(e):
            run("sp", e)


class Buf:
    __slots__ = ("t", "r")

    def __init__(self, t, name):
        self.t = t
        self.r = Res(name)


def build(TS, TP, DEPTH, DEBUG=False):
    T = TS + TP
    SKP = NR * TP
    NT = T // 128
    NST = T // 512
    assert TS % 512 == 0 and TP % 512 == 0
    nc = bass.Bass("TRN2", target_bir_lowering=False)

    def dram(name, shape, dtype, kind="Internal"):
        return nc.dram_tensor(name, shape, dtype, kind=kind).ap()

    x_in = dram("x", [T, D], F32, "ExternalInput")
    p_in = dram("p", [DEPTH, T, PLE], F32, "ExternalInput")
    pos_in = dram("pos", [T, 80], F32, "ExternalInput")
    cst_in = dram("cst", [128, 516], F32, "ExternalInput")
    rkt_in = dram("rkt", [128, 8], F32, "ExternalInput")
    ln1_in = dram("ln1_w", [DEPTH, D], F32, "ExternalInput")
    w_in_in = dram("w_in", [DEPTH, D, INW], F32, "ExternalInput")
    qn_in = dram("diff_q_norm", [DEPTH, 64], F32, "ExternalInput")
    kn_in = dram("diff_k_norm", [DEPTH, 64], F32, "ExternalInput")
    lam_in = dram("diff_lambda", [1, DEPTH * 256], F32, "ExternalInput")
    sub_in = dram("diff_subln", [DEPTH, 128], F32, "ExternalInput")
    dec_in = dram("ret_decay_logit", [1, DEPTH * 16], F32, "ExternalInput")
    gn_in = dram("ret_gn", [DEPTH, 64], F32, "ExternalInput")
    w_out_in = dram("w_out", [DEPTH, D, D], F32, "ExternalInput")
    ln2_in = dram("ln2_w", [DEPTH, D], F32, "ExternalInput")
    w1_in = dram("w_mlp1", [DEPTH, D, DFF], F32, "ExternalInput")
    w2_in = dram("w_mlp2", [DEPTH, DFF, D], F32, "ExternalInput")
    wg_in = dram("w_ple_gate", [DEPTH, D, D], F32, "ExternalInput")
    wp_in = dram("w_ple_proj", [DEPTH, PLE, D], F32, "ExternalInput")
    y_out = dram("y", [T, D], F32, "ExternalOutput")

    xres = dram("xres", [T, D], F32)
    dqT = dram("dqT", [512, T], BF16)
    dkT_s = dram("dkT_s", [512, TS], BF16)
    NSPL = max(1, (512 * TP * 2) // (1 << 20))
    HPS = 4 // NSPL
    TPS = TP // NSPL
    dkT_l = [dram(f"dkT_l{a}", [HPS * 128, TP], BF16) for a in range(NSPL)]
    dkT_g = [dram(f"dkT_g{a}", [NR * HPS * 128, TP], BF16) for a in range(NSPL)]
    dv_s = dram("dv_s", [TS, 512], BF16)
    dv_l = [dram(f"dv_l{a}", [TPS, 512], BF16) for a in range(NSPL)]
    dv_g = [dram(f"dv_g{a}", [NR * TPS, 512], BF16) for a in range(NSPL)]
    rqT = dram("rqT", [512, T], BF16)
    rkT = dram("rkT", [512, T], BF16)
    rqdT = dram("rqdT", [8 * 128, T], BF16)
    rkd = dram("rkd", [T, 1024], BF16)
    rv = dram("rv", [T, 512], BF16)
    rgs = dram("rgs", [T, 512], BF16)
    aT = dram("aT", [512, T], BF16)
    rT = dram("rT", [512, T], BF16)
    uT = dram("uT", [DFF, T], BF16)
    st_l = dram("st_l", [128, 512], F32)
    st_g = dram("st_g", [NR * 128, 512], F32)

    with ExitStack() as top:
        S = Sched(nc)

        uid = [0]

        def sb(st, name, shape, dt):
            uid[0] += 1
            return Buf(st.enter_context(nc.sbuf_tensor(f"sb{uid[0]}_{name}", shape, dt)), name)

        def psb(st, name, shape, dt):
            uid[0] += 1
            return Buf(st.enter_context(nc.psum_tensor(f"ps{uid[0]}_{name}", shape, dt)), name)

        ident = sb(top, "ident", [128, 128], BF16)
        identf = sb(top, "identf", [128, 128], F32)
        ones_b = sb(top, "ones_b", [128, 128], BF16)
        ones_f = sb(top, "ones_f", [128, 128], F32)
        cst = sb(top, "cst", [128, 516], F32)
        rkt = sb(top, "rkt", [128, 8], F32)
        lg = sb(top, "lg", [128, DEPTH * 16], F32)
        lgcol = sb(top, "lgcol", [128, DEPTH * 8], F32)
        lam = sb(top, "lam", [128, DEPTH], F32)
        neglam = sb(top, "neglam", [128, DEPTH], F32)
        epsc = sb(top, "epsc", [128, 1], F32)
        ln1b = sb(top, "ln1b", [128, D], F32)
        ln2b = sb(top, "ln2b", [128, D], F32)
        qnw = sb(top, "qnw", [128, 64], F32)
        knw = sb(top, "knw", [128, 64], F32)
        gnw = sb(top, "gnw", [128, 64], F32)
        subw = sb(top, "subw", [128, 1], F32)
        qdec = sb(top, "qdec", [128, 8, 2], F32)
        kdec = sb(top, "kdec", [128, 8, 2], F32)
        cdec = sb(top, "cdec", [128, 8], F32)
        DTe = sb(top, "DTe", [128, 4, 128], F32)
        DTo = sb(top, "DTo", [128, 4, 128], F32)
        coef = sb(top, "coef", [128, 4, 8], F32)

        relF = cst.t[:, 0:128]
        mskF = cst.t[:, 128:256]
        relB = cst.t[:, 256:384]
        mskB = cst.t[:, 384:512]
        idx_p1 = cst.t[:, 512:513]
        idx_128m = cst.t[:, 513:514]
        idx_127m = cst.t[:, 514:515]
        idx_p = cst.t[:, 515:516]

        S.dma("c_init", lambda e: e.dma_start(out=cst.t[:], in_=cst_in), writes=[cst.r])
        S.dma("c_init", lambda e: e.dma_start(out=rkt.t[:], in_=rkt_in), writes=[rkt.r])
        S.pool(lambda e: e.memset(identf.t[:], 0.0), writes=[identf.r])
        S.pool(lambda e: e.affine_select(out=identf.t[:], in_=identf.t[:], compare_op=ALU.not_equal, fill=1.0, base=0,
                                         pattern=[[-1, 128]], channel_multiplier=1), reads=[identf.r], writes=[identf.r])
        S.dve(lambda e: e.tensor_copy(out=ident.t[:], in_=identf.t[:]), reads=[identf.r], writes=[ident.r])
        S.dve(lambda e: e.memset(ones_b.t[:], 1.0), writes=[ones_b.r])
        S.dve(lambda e: e.memset(ones_f.t[:], 1.0), writes=[ones_f.r])
        S.dve(lambda e: e.memset(epsc.t[:], EPS), writes=[epsc.r])

        with ExitStack() as st0:
            NL = DEPTH * 16
            xx = sb(st0, "ls_x", [128, NL], F32)
            ax = sb(st0, "ls_ax", [128, NL], F32)
            uu = sb(st0, "ls_u", [128, NL], F32)
            ss_ = sb(st0, "ls_s", [128, NL], F32)
            s2 = sb(st0, "ls_s2", [128, NL], F32)
            pl = sb(st0, "ls_pl", [128, NL], F32)
            lp = sb(st0, "lp", [128, DEPTH * 256], F32)
            S.dma("c_init", lambda e: e.dma_start(out=xx.t[:], in_=dec_in.to_broadcast([128, NL])), writes=[xx.r])
            S.dma("c_init", lambda e: e.dma_start(out=lp.t[:], in_=lam_in.to_broadcast([128, DEPTH * 256])), writes=[lp.r])
            S.barrier()
            S.dve(lambda e: e.tensor_scalar(out=ax.t[:], in0=xx.t[:], scalar1=-1.0, scalar2=None, op0=ALU.mult), reads=[xx.r], writes=[ax.r])
            S.dve(lambda e: e.tensor_tensor(out=ax.t[:], in0=ax.t[:], in1=xx.t[:], op=ALU.max), reads=[xx.r, ax.r], writes=[ax.r])
            S.act(lambda e: e.activation(out=uu.t[:], in_=ax.t[:], func=AF.Exp, scale=-1.0), reads=[ax.r], writes=[uu.r])
            S.dve(lambda e: e.tensor_scalar(out=ss_.t[:], in0=uu.t[:], scalar1=2.0, scalar2=None, op0=ALU.add), reads=[uu.r], writes=[ss_.r])
            S.dve(lambda e: e.reciprocal(out=ss_.t[:], in_=ss_.t[:]), reads=[ss_.r], writes=[ss_.r])
            S.dve(lambda e: e.tensor_tensor(out=ss_.t[:], in0=ss_.t[:], in1=uu.t[:], op=ALU.mult), reads=[ss_.r, uu.r], writes=[ss_.r])
            S.dve(lambda e: e.tensor_tensor(out=s2.t[:], in0=ss_.t[:], in1=ss_.t[:], op=ALU.mult), reads=[ss_.r], writes=[s2.r])
            S.dve(lambda e: e.tensor_scalar(out=pl.t[:], in0=s2.t[:], scalar1=1.0 / 13, scalar2=1.0 / 11, op0=ALU.mult, op1=ALU.add), reads=[s2.r], writes=[pl.r])
            for cc in (1.0 / 9, 1.0 / 7, 1.0 / 5, 1.0 / 3, 1.0):
                S.dve(lambda e: e.tensor_tensor(out=pl.t[:], in0=pl.t[:], in1=s2.t[:], op=ALU.mult), reads=[pl.r, s2.r], writes=[pl.r])
                S.dve(lambda e, cc=cc: e.tensor_scalar(out=pl.t[:], in0=pl.t[:], scalar1=cc, scalar2=None, op0=ALU.add), reads=[pl.r], writes=[pl.r])
            S.dve(lambda e: e.tensor_tensor(out=pl.t[:], in0=pl.t[:], in1=ss_.t[:], op=ALU.mult), reads=[pl.r, ss_.r], writes=[pl.r])
            S.dve(lambda e: e.tensor_scalar(out=ax.t[:], in0=xx.t[:], scalar1=0.0, scalar2=None, op0=ALU.min), reads=[xx.r], writes=[ax.r])
            S.dve(lambda e: e.scalar_tensor_tensor(out=lg.t[:], in0=pl.t[:], scalar=-2.0, in1=ax.t[:], op0=ALU.mult, op1=ALU.add), reads=[pl.r, ax.r], writes=[lg.r])
            lg4 = lg.t[:].rearrange("p (l a h) -> p l a h", l=DEPTH, a=2)
            lgc3 = lgcol.t[:].rearrange("p (l h) -> p l h", l=DEPTH)
            S.dve(lambda e: e.tensor_copy(out=lgc3[0:64], in_=lg4[0:64, :, 0, :]), reads=[lg.r], writes=[lgcol.r])
            S.dve(lambda e: e.tensor_copy(out=lgc3[64:128], in_=lg4[64:128, :, 1, :]), reads=[lg.r], writes=[lgcol.r])
            pr = sb(st0, "lpr", [128, DEPTH * 2 * 64], F32)
            sm = sb(st0, "lsm", [128, DEPTH * 2], F32)
            lp5 = lp.t[:].rearrange("p (l a b d) -> p l a b d", l=DEPTH, a=2, b=2)
            pr4 = pr.t[:].rearrange("p (l a d) -> p l a d", l=DEPTH, a=2)
            S.dve(lambda e: e.tensor_tensor(out=pr4, in0=lp5[:, :, :, 0, :], in1=lp5[:, :, :, 1, :], op=ALU.mult), reads=[lp.r], writes=[pr.r])
            S.dve(lambda e: e.tensor_reduce(out=sm.t[:], in_=pr.t[:].rearrange("p (g d) -> p g d", d=64), axis=AX.X, op=ALU.add), reads=[pr.r], writes=[sm.r])
            S.act(lambda e: e.activation(out=sm.t[:], in_=sm.t[:], func=AF.Exp), reads=[sm.r], writes=[sm.r])
            sm3 = sm.t[:].rearrange("p (l a) -> p l a", a=2)
            S.dve(lambda e: e.tensor_tensor(out=lam.t[:], in0=sm3[:, :, 0], in1=sm3[:, :, 1], op=ALU.subtract), reads=[sm.r], writes=[lam.r])
            for l in range(DEPTH):
                li = 0.8 - 0.6 * math.exp(-0.3 * l)
                S.dve(lambda e, l=l, li=li: e.tensor_scalar(out=lam.t[:, l:l + 1], in0=lam.t[:, l:l + 1], scalar1=li, scalar2=None, op0=ALU.add), reads=[lam.r], writes=[lam.r])
            S.dve(lambda e: e.tensor_scalar(out=neglam.t[:], in0=lam.t[:], scalar1=-1.0, scalar2=None, op0=ALU.mult), reads=[lam.r], writes=[neglam.r])
            S.barrier()

        def rsqrt_act(out_ap, in_ap, scale, rbuf, wbuf):
            S.act(lambda e: e.activation(out=out_ap, in_=in_ap, func=AF.Ln, scale=scale, bias=epsc.t[0:out_ap.shape[0], 0:1]), reads=[rbuf.r, epsc.r], writes=[wbuf.r])
            S.act(lambda e: e.activation(out=out_ap, in_=out_ap, func=AF.Exp, scale=-0.5), reads=[wbuf.r], writes=[wbuf.r])

        class WGroups:
            def __init__(self, t, gn):
                self.t = t
                self.gn = gn
                self.res = {}

            def r(self, k, n0):
                return self.res[(k // 4, n0 // self.gn)]

        def load_w(st, name, src, K, N, key, gn=512, korder=False):
            kc = K // 128
            gn = min(gn, N, 2048)
            uid[0] += 1
            t = st.enter_context(nc.sbuf_tensor(f"sb{uid[0]}_{name}", [128, kc, N], BF16))
            w = WGroups(t, gn)
            srcv = src.rearrange("(c p) n -> p c n", p=128)
            chain = [Res(name + "_chA"), Res(name + "_chB")]
            groups = [(c0, n0) for n0 in range(0, N, gn) for c0 in range(0, kc, 4)]
            if korder:
                groups = [(c0, n0) for c0 in range(0, kc, 4) for n0 in range(0, N, gn)]
            for i, (c0, n0) in enumerate(groups):
                c1 = min(kc, c0 + 4)
                n1 = min(N, n0 + gn)
                rr_ = Res(f"{name}_{c0}_{n0}")
                w.res[(c0 // 4, n0 // gn)] = rr_
                S.dma(key + "AB"[i % 2], lambda e, c0=c0, c1=c1, n0=n0, n1=n1: e.dma_start(out=t[:, c0:c1, n0:n1], in_=srcv[:, c0:c1, n0:n1]),
                      writes=[rr_, chain[i % 2]], eng="pool", cost=8.0)
            return w

        for l in range(DEPTH):
            lam_init = 0.8 - 0.6 * math.exp(-0.3 * l)
            x_src = x_in if l == 0 else xres
            x_dst = y_out if l == DEPTH - 1 else xres

            S.dma("c_lay", lambda e, l=l: e.dma_start(out=ln1b.t[:], in_=ln1_in[l:l + 1, :].to_broadcast([128, D])), writes=[ln1b.r])
            S.dma("c_lay", lambda e, l=l: e.dma_start(out=ln2b.t[:], in_=ln2_in[l:l + 1, :].to_broadcast([128, D])), writes=[ln2b.r])
            S.dma("c_lay", lambda e, l=l: e.dma_start(out=qnw.t[:], in_=qn_in[l:l + 1, :].to_broadcast([128, 64])), writes=[qnw.r])
            S.dma("c_lay", lambda e, l=l: e.dma_start(out=knw.t[:], in_=kn_in[l:l + 1, :].to_broadcast([128, 64])), writes=[knw.r])
            S.dma("c_lay", lambda e, l=l: e.dma_start(out=gnw.t[:], in_=gn_in[l:l + 1, :].to_broadcast([128, 64])), writes=[gnw.r])
            S.dma("c_lay", lambda e, l=l: e.dma_start(out=subw.t[:], in_=sub_in[l:l + 1, :].rearrange("o e -> e o"), allow_slow_non_contiguous=True), writes=[subw.r])
            S.barrier()
            S.dve(lambda e: e.tensor_scalar(out=qnw.t[:], in0=qnw.t[:], scalar1=0.125, scalar2=None, op0=ALU.mult), reads=[qnw.r], writes=[qnw.r])
            S.dve(lambda e, li=lam_init: e.tensor_scalar(out=subw.t[:], in0=subw.t[:], scalar1=1.0 - li, scalar2=None, op0=ALU.mult), reads=[subw.r], writes=[subw.r])
            lgl = lg.t[:, l * 16:(l + 1) * 16]
            S.act(lambda e, lgl=lgl: e.activation(out=qdec.t[:, :, 0], in_=lgl[:, 0:8], func=AF.Exp, scale=idx_p1), reads=[lg.r, cst.r], writes=[qdec.r])
            S.act(lambda e, lgl=lgl: e.activation(out=qdec.t[:, :, 1], in_=lgl[:, 8:16], func=AF.Exp, scale=idx_128m), reads=[lg.r, cst.r], writes=[qdec.r])
            S.act(lambda e, lgl=lgl: e.activation(out=kdec.t[:, :, 0], in_=lgl[:, 0:8], func=AF.Exp, scale=idx_127m), reads=[lg.r, cst.r], writes=[kdec.r])
            S.act(lambda e, lgl=lgl: e.activation(out=kdec.t[:, :, 1], in_=lgl[:, 8:16], func=AF.Exp, scale=idx_p), reads=[lg.r, cst.r], writes=[kdec.r])
            S.dve(lambda e: e.tensor_scalar(out=kdec.t[:], in0=kdec.t[:], scalar1=0.125, scalar2=None, op0=ALU.mult), reads=[kdec.r], writes=[kdec.r])
            S.act(lambda e, l=l: e.activation(out=cdec.t[:], in_=lgcol.t[:, l * 8:(l + 1) * 8], func=AF.Exp, scale=128.0), reads=[lgcol.r], writes=[cdec.r])
            with ExitStack() as stc:
                tmpa = sb(stc, "dt_a", [128, 128], F32)
                tmpb = sb(stc, "dt_b", [128, 128], F32)
                for h in range(8):
                    dst = (DTe if h % 2 == 0 else DTo)
                    dsl = dst.t[:, h // 2, :]
                    S.act(lambda e, h=h: e.activation(out=tmpa.t[:], in_=relF, func=AF.Exp, scale=lgl[:, h:h + 1]), reads=[cst.r, lg.r], writes=[tmpa.r])
                    S.act(lambda e, h=h: e.activation(out=tmpb.t[:], in_=relB, func=AF.Exp, scale=lgl[:, 8 + h:9 + h]), reads=[cst.r, lg.r], writes=[tmpb.r])
                    S.dve(lambda e: e.tensor_tensor(out=tmpa.t[:], in0=tmpa.t[:], in1=mskF, op=ALU.mult), reads=[tmpa.r, cst.r], writes=[tmpa.r])
                    S.dve(lambda e: e.tensor_tensor(out=tmpb.t[:], in0=tmpb.t[:], in1=mskB, op=ALU.mult), reads=[tmpb.r, cst.r], writes=[tmpb.r])
                    S.dve(lambda e, dsl=dsl: e.tensor_tensor(out=dsl, in0=tmpa.t[:], in1=tmpb.t[:], op=ALU.add), reads=[tmpa.r, tmpb.r], writes=[dst.r])
                for r_ in range(NR):
                    S.act(lambda e, r_=r_, l=l: e.activation(out=coef.t[:, r_, :], in_=lgcol.t[:, l * 8:(l + 1) * 8], func=AF.Exp, scale=rkt.t[:, r_:r_ + 1]), reads=[lgcol.r, rkt.r], writes=[coef.r])
                    S.dve(lambda e, r_=r_: e.tensor_scalar(out=coef.t[:, r_, :], in0=coef.t[:, r_, :], scalar1=rkt.t[:, 4 + r_:5 + r_], scalar2=None, op0=ALU.mult), reads=[coef.r, rkt.r], writes=[coef.r])
                S.barrier()

            with ExitStack() as st:
                w_in = load_w(st, "w_in", w_in_in[l], D, INW, "w_in")
                P2 = range(2)
                xt = [sb(st, f"xt{i}", [128, D], F32) for i in P2]
                post = [sb(st, f"post{i}", [128, 80], F32) for i in P2]
                junk = sb(st, "junk", [128, D], BF16)
                ssx = [sb(st, f"ssx{i}", [128, 1], F32) for i in P2]
                hn = [sb(st, f"hn{i}", [128, D], BF16) for i in P2]
                hnT = [sb(st, f"hnT{i}", [128, 8, 128], BF16) for i in P2]
                qraw = [[sb(st, f"qraw{w}{i}", [128, 512], F32) for i in P2] for w in P2]
                qsq = [sb(st, f"qsq{w}", [128, 512], F32) for w in P2]
                ssg = [[sb(st, f"ssg{w}{i}", [128, 8], F32) for i in P2] for w in P2]
                qu = [sb(st, f"qu{w}", [128, 512], F32) for w in P2]
                w16 = [sb(st, f"w16{w}", [128, 8, 16], F32) for w in P2]
                rt = [[sb(st, f"rt{w}{i}", [128, 8, 8], F32) for i in range(4)] for w in P2]
                qb = [[sb(st, f"qb{w}{i}", [128, 512], BF16) for i in P2] for w in P2]
                rqf = [sb(st, f"rqf{w}", [128, 512], F32) for w in P2]
                rta = [sb(st, f"rta{w}", [128, 8, 32], F32) for w in P2]
                rtb = [sb(st, f"rtb{w}", [128, 8, 32], F32) for w in P2]
                rqb = [[sb(st, f"rqb{w}{i}", [128, 512], BF16) for i in P2] for w in P2]
                qd = [sb(st, f"qd{i}", [128, 1024], BF16) for i in P2]
                esg = [sb(st, f"esg{i}", [128, 512], F32) for i in P2]
                stg_qT = [sb(st, f"sg_qT{i}", [128, 4, 512], BF16) for i in P2]
                stg_kT = [sb(st, f"sg_kT{i}", [128, 4, 512], BF16) for i in P2]
                stg_rqT = [sb(st, f"sg_rqT{i}", [128, 4, 512], BF16) for i in P2]
                stg_rkT = [sb(st, f"sg_rkT{i}", [128, 4, 512], BF16) for i in P2]
                stg_qdT = [sb(st, f"sg_qdT{i}", [128, 8, 512], BF16) for i in P2]
                tv = [sb(st, f"tv{i}", [128, 512], BF16) for i in P2]
                trv = [sb(st, f"trv{i}", [128, 512], BF16) for i in P2]
                tgs = [sb(st, f"tgs{i}", [128, 512], BF16) for i in P2]
                tkd = [sb(st, f"tkd{i}", [128, 1024], BF16) for i in P2]
                pT = psb(st, "pT", [128, 8, 128], BF16)
                pT2 = [psb(st, f"pT2_{i}", [128, 8, 128], BF16) for i in P2]
                pj = [psb(st, f"pj{i}", [128, 512], F32) for i in range(5)]
                pjn = [0]

                def proj(cb, hnT_):
                    b = pj[pjn[0] % 5]
                    pjn[0] += 1
                    for k in range(8):
                        S.pe(lambda e, k=k, b=b, cb=cb: e.matmul(b.t[:], lhsT=hnT_.t[:, k, :], rhs=w_in.t[:, k, cb * 512:(cb + 1) * 512], start=(k == 0), stop=(k == 7)),
                             reads=[hnT_.r, w_in.r(k, cb * 512)], writes=[b.r])
                    return b

                def transpose_to(src, nblk, dst_stage, dst_ap_fn, pbuf):
                    for c in range(nblk):
                        S.pe(lambda e, c=c: e.transpose(out=pbuf.t[:, c, :], in_=src.t[:, c * 128:(c + 1) * 128], identity=ident.t[:]),
                             reads=[src.r, ident.r], writes=[pbuf.r])
                    S.act(lambda e: e.activation(out=dst_ap_fn(), in_=pbuf.t[:, 0:nblk, :], func=AF.Copy), reads=[pbuf.r], writes=[dst_stage.r])

                for t in range(NT):
                    sti = t // 4
                    j = t % 4
                    sl = sti % 2
                    par = t % 2
                    tok0 = t * 128
                    is_s = tok0 < TS
                    tl = tok0 if is_s else tok0 - TS
                    xb = xt[par]
                    pb = post[par]
                    hn_, hnT_, ssx_ = hn[par], hnT[par], ssx[par]
                    S.dma(f"a_x{par}", lambda e, xb=xb, tok0=tok0: e.dma_start(out=xb.t[:], in_=x_src[tok0:tok0 + 128, :]), writes=[xb.r])
                    S.dma(f"a_pos{par}", lambda e, pb=pb, tok0=tok0: e.dma_start(out=pb.t[:], in_=pos_in[tok0:tok0 + 128, :]), writes=[pb.r])
                    S.act(lambda e, xb=xb, ssx_=ssx_: e.activation(out=junk.t[:], in_=xb.t[:], func=AF.Square, accum_out=ssx_.t[:]), reads=[xb.r], writes=[junk.r, ssx_.r])
                    rsqrt_act(ssx_.t[:], ssx_.t[:], 1.0 / D, ssx_, ssx_)
                    S.dve(lambda e, xb=xb, hn_=hn_, ssx_=ssx_: e.scalar_tensor_tensor(out=hn_.t[:], in0=xb.t[:], scalar=ssx_.t[:, 0:1], in1=ln1b.t[:], op0=ALU.mult, op1=ALU.mult),
                          reads=[xb.r, ssx_.r, ln1b.r], writes=[hn_.r])
                    for c in range(8):
                        S.pe(lambda e, c=c, hn_=hn_: e.transpose(out=pT.t[:, c, :], in_=hn_.t[:, c * 128:(c + 1) * 128], identity=ident.t[:]), reads=[hn_.r, ident.r], writes=[pT.r])
                    S.act(lambda e, hnT_=hnT_: e.activation(out=hnT_.t[:], in_=pT.t[:], func=AF.Copy), reads=[pT.r], writes=[hnT_.r], cost=1.1)
                    cosd = pb.t[:, 0:8].unsqueeze(1).to_broadcast([128, 8, 8])
                    sind = pb.t[:, 8:16].unsqueeze(1).to_broadcast([128, 8, 8])
                    cosr = pb.t[:, 16:48].unsqueeze(1).to_broadcast([128, 8, 32])
                    sinr = pb.t[:, 48:80].unsqueeze(1).to_broadcast([128, 8, 32])
                    for which in range(2):
                        b = proj(which, hnT_)
                        nw = qnw if which == 0 else knw
                        stg = (stg_qT if which == 0 else stg_kT)[sl]
                        qraw_, qsq_, ssg_, qu_, w16_, rt_, qb_ = qraw[which][par], qsq[which], ssg[which][par], qu[which], w16[which], rt[which], qb[which][par]
                        S.act(lambda e, b=b, qraw_=qraw_: e.activation(out=qraw_.t[:], in_=b.t[:], func=AF.Copy), reads=[b.r], writes=[qraw_.r])
                        S.act(lambda e, b=b, qsq_=qsq_: e.activation(out=qsq_.t[:], in_=b.t[:], func=AF.Square), reads=[b.r], writes=[qsq_.r])
                        S.dve(lambda e, qsq_=qsq_, ssg_=ssg_: e.tensor_reduce(out=ssg_.t[:], in_=qsq_.t[:].rearrange("p (g d) -> p g d", d=64), axis=AX.X, op=ALU.add), reads=[qsq_.r], writes=[ssg_.r])
                        rsqrt_act(ssg_.t[:], ssg_.t[:], 1.0 / 64, ssg_, ssg_)
                        qr3 = qraw_.t[:].rearrange("p (g d) -> p g d", d=64)
                        qu3 = qu_.t[:].rearrange("p (g d) -> p g d", d=64)
                        qb3 = qb_.t[:].rearrange("p (g d) -> p g d", d=64)
                        S.dve(lambda e, qr3=qr3, qu3=qu3, ssg_=ssg_: e.tensor_tensor(out=qu3, in0=qr3, in1=ssg_.t[:].unsqueeze(2).to_broadcast([128, 8, 64]), op=ALU.mult), reads=[qraw_.r, ssg_.r], writes=[qu_.r])
                        S.dve(lambda e, qu3=qu3, qb3=qb3, nw=nw: e.tensor_tensor(out=qb3, in0=qu3, in1=nw.t[:].unsqueeze(1).to_broadcast([128, 8, 64]), op=ALU.mult), reads=[qu_.r, nw.r], writes=[qb_.r])
                        S.dve(lambda e, qu3=qu3, nw=nw, w16_=w16_: e.tensor_tensor(out=w16_.t[:], in0=qu3[:, :, 0:16], in1=nw.t[:, 0:16].unsqueeze(1).to_broadcast([128, 8, 16]), op=ALU.mult), reads=[qu_.r, nw.r], writes=[w16_.r], cost=0.15)
                        x1 = w16_.t[:, :, 0:8]
                        x2 = w16_.t[:, :, 8:16]
                        S.dve(lambda e, x1=x1, cosd=cosd, rt_=rt_: e.tensor_tensor(out=rt_[0].t[:], in0=x1, in1=cosd, op=ALU.mult), reads=[w16_.r, pb.r], writes=[rt_[0].r], cost=0.12)
                        S.dve(lambda e, x2=x2, sind=sind, rt_=rt_: e.tensor_tensor(out=rt_[1].t[:], in0=x2, in1=sind, op=ALU.mult), reads=[w16_.r, pb.r], writes=[rt_[1].r], cost=0.12)
                        S.dve(lambda e, x1=x1, sind=sind, rt_=rt_: e.tensor_tensor(out=rt_[2].t[:], in0=x1, in1=sind, op=ALU.mult), reads=[w16_.r, pb.r], writes=[rt_[2].r], cost=0.12)
                        S.dve(lambda e, x2=x2, cosd=cosd, rt_=rt_: e.tensor_tensor(out=rt_[3].t[:], in0=x2, in1=cosd, op=ALU.mult), reads=[w16_.r, pb.r], writes=[rt_[3].r], cost=0.12)
                        S.dve(lambda e, qb3=qb3, rt_=rt_: e.tensor_tensor(out=qb3[:, :, 0:8], in0=rt_[0].t[:], in1=rt_[1].t[:], op=ALU.subtract), reads=[rt_[0].r, rt_[1].r, qb_.r], writes=[qb_.r], cost=0.12)
                        S.dve(lambda e, qb3=qb3, rt_=rt_: e.tensor_tensor(out=qb3[:, :, 8:16], in0=rt_[2].t[:], in1=rt_[3].t[:], op=ALU.add), reads=[rt_[2].r, rt_[3].r, qb_.r], writes=[qb_.r], cost=0.12)
                        transpose_to(qb_, 4, stg, lambda stg=stg, j=j: stg.t[:, :, j * 128:(j + 1) * 128], pT2[0])
                    b = proj(2, hnT_)
                    tv_ = tv[par]
                    S.act(lambda e, b=b, tv_=tv_: e.activation(out=tv_.t[:], in_=b.t[:], func=AF.Copy), reads=[b.r], writes=[tv_.r])
                    if is_s:
                        S.dma(f"s_v{par}", lambda e, tv_=tv_, tl=tl: e.dma_start(out=dv_s[tl:tl + 128, :], in_=tv_.t[:]), reads=[tv_.r])
                    else:
                        S.dma(f"s_v{par}", lambda e, tv_=tv_, tl=tl: e.dma_start(out=dv_l[tl // TPS][tl % TPS:tl % TPS + 128, :], in_=tv_.t[:]), reads=[tv_.r])
                    for which in range(2):
                        b = proj(3 + which, hnT_)
                        rqf_, rta_, rtb_, rqb_ = rqf[which], rta[which], rtb[which], rqb[which][par]
                        b3 = b.t[:].rearrange("p (g d) -> p g d", d=64)
                        rq3 = rqf_.t[:].rearrange("p (g d) -> p g d", d=64)
                        S.dve(lambda e, b3=b3, cosr=cosr, rta_=rta_: e.tensor_tensor(out=rta_.t[:], in0=b3[:, :, 0:32], in1=cosr, op=ALU.mult), reads=[b.r, pb.r], writes=[rta_.r])
                        S.dve(lambda e, b3=b3, sinr=sinr, rtb_=rtb_: e.tensor_tensor(out=rtb_.t[:], in0=b3[:, :, 32:64], in1=sinr, op=ALU.mult), reads=[b.r, pb.r], writes=[rtb_.r])
                        S.dve(lambda e, rq3=rq3, rta_=rta_, rtb_=rtb_: e.tensor_tensor(out=rq3[:, :, 0:32], in0=rta_.t[:], in1=rtb_.t[:], op=ALU.subtract), reads=[rta_.r, rtb_.r], writes=[rqf_.r])
                        S.dve(lambda e, b3=b3, sinr=sinr, rta_=rta_: e.tensor_tensor(out=rta_.t[:], in0=b3[:, :, 0:32], in1=sinr, op=ALU.mult), reads=[b.r, pb.r], writes=[rta_.r])
                        S.dve(lambda e, b3=b3, cosr=cosr, rtb_=rtb_: e.tensor_tensor(out=rtb_.t[:], in0=b3[:, :, 32:64], in1=cosr, op=ALU.mult), reads=[b.r, pb.r], writes=[rtb_.r])
                        S.dve(lambda e, rq3=rq3, rta_=rta_, rtb_=rtb_: e.tensor_tensor(out=rq3[:, :, 32:64], in0=rta_.t[:], in1=rtb_.t[:], op=ALU.add), reads=[rta_.r, rtb_.r, rqf_.r], writes=[rqf_.r])
                        S.act(lambda e, rqb_=rqb_, rqf_=rqf_: e.activation(out=rqb_.t[:], in_=rqf_.t[:], func=AF.Copy), reads=[rqf_.r], writes=[rqb_.r], cost=0.6)
                        dec = qdec if which == 0 else kdec
                        rq4 = rqf_.t[:].rearrange("p (g d) -> p g d", d=64).unsqueeze(2).to_broadcast([128, 8, 2, 64])
                        dc4 = dec.t[:].unsqueeze(3).to_broadcast([128, 8, 2, 64])
                        if which == 0:
                            qd_ = qd[par]
                            S.pool(lambda e, rq4=rq4, dc4=dc4, qd_=qd_: e.tensor_tensor(out=qd_.t[:].rearrange("p (g a d) -> p g a d", g=8, a=2), in0=rq4, in1=dc4, op=ALU.mult),
                                  reads=[rqf_.r, dec.r], writes=[qd_.r], cost=2.5)
                            transpose_to(rqb_, 4, stg_rqT[sl], lambda sl=sl, j=j: stg_rqT[sl].t[:, :, j * 128:(j + 1) * 128], pT2[1])
                            transpose_to(qd_, 8, stg_qdT[sl], lambda sl=sl, j=j: stg_qdT[sl].t[:, :, j * 128:(j + 1) * 128], pT2[0])
                        else:
                            tkd_ = tkd[par]
                            S.pool(lambda e, rq4=rq4, dc4=dc4, tkd_=tkd_: e.tensor_tensor(out=tkd_.t[:].rearrange("p (g a d) -> p g a d", g=8, a=2), in0=rq4, in1=dc4, op=ALU.mult),
                                  reads=[rqf_.r, dec.r], writes=[tkd_.r], cost=2.5)
                            S.dma(f"s_kd{par}", lambda e, tkd_=tkd_, tok0=tok0: e.dma_start(out=rkd[tok0:tok0 + 128, :], in_=tkd_.t[:]), reads=[tkd_.r])
                            transpose_to(rqb_, 4, stg_rkT[sl], lambda sl=sl, j=j: stg_rkT[sl].t[:, :, j * 128:(j + 1) * 128], pT2[1])
                    b = proj(5, hnT_)
                    trv_ = trv[par]
                    S.act(lambda e, b=b, trv_=trv_: e.activation(out=trv_.t[:], in_=b.t[:], func=AF.Copy), reads=[b.r], writes=[trv_.r])
                    S.dma(f"s_rv{par}", lambda e, trv_=trv_, tok0=tok0: e.dma_start(out=rv[tok0:tok0 + 128, :], in_=trv_.t[:]), reads=[trv_.r])
                    b = proj(6, hnT_)
                    esg_, tgs_ = esg[par], tgs[par]
                    S.act(lambda e, b=b, esg_=esg_: e.activation(out=esg_.t[:], in_=b.t[:], func=AF.Exp, scale=-1.0), reads=[b.r], writes=[esg_.r])
                    S.act(lambda e, esg_=esg_: e.activation(out=esg_.t[:], in_=esg_.t[:], func=AF.Ln, bias=ones_f.t[:, 0:1]), reads=[esg_.r, ones_f.r], writes=[esg_.r], cost=0.6)
                    S.act(lambda e, esg_=esg_: e.activation(out=esg_.t[:], in_=esg_.t[:], func=AF.Exp, scale=-1.0), reads=[esg_.r], writes=[esg_.r], cost=0.6)
                    S.dve(lambda e, b=b, esg_=esg_, tgs_=tgs_: e.tensor_tensor(out=tgs_.t[:], in0=b.t[:], in1=esg_.t[:], op=ALU.mult), reads=[b.r, esg_.r], writes=[tgs_.r])
                    S.dma(f"s_gs{par}", lambda e, tgs_=tgs_, tok0=tok0: e.dma_start(out=rgs[tok0:tok0 + 128, :], in_=tgs_.t[:]), reads=[tgs_.r])
                    if j == 3:
                        c0 = sti * 512
                        cl = c0 if is_s else c0 - TS
                        S.dma(f"s_qT{sl}", lambda e, sl=sl, c0=c0: e.dma_start(out=dqT.rearrange("(h f) t -> f h t", f=128)[:, :, c0:c0 + 512], in_=stg_qT[sl].t[:]), reads=[stg_qT[sl].r])
                        if is_s:
                            S.dma(f"s_kT{sl}", lambda e, sl=sl, cl=cl: e.dma_start(out=dkT_s.rearrange("(h f) t -> f h t", f=128)[:, :, cl:cl + 512], in_=stg_kT[sl].t[:]), reads=[stg_kT[sl].r])
                        else:
                            for a in range(NSPL):
                                S.dma(f"s_kT{sl}", lambda e, sl=sl, cl=cl, a=a: e.dma_start(out=dkT_l[a].rearrange("(h f) t -> f h t", f=128)[:, :, cl:cl + 512], in_=stg_kT[sl].t[:, a * HPS:(a + 1) * HPS, :]), reads=[stg_kT[sl].r])
                        S.dma(f"s_rqT{sl}", lambda e, sl=sl, c0=c0: e.dma_start(out=rqT.rearrange("(h f) t -> f h t", f=128)[:, :, c0:c0 + 512], in_=stg_rqT[sl].t[:]), reads=[stg_rqT[sl].r])
                        S.dma(f"s_rkT{sl}", lambda e, sl=sl, c0=c0: e.dma_start(out=rkT.rearrange("(h f) t -> f h t", f=128)[:, :, c0:c0 + 512], in_=stg_rkT[sl].t[:]), reads=[stg_rkT[sl].r])
                        S.dma(f"s_qdT{sl}", lambda e, sl=sl, c0=c0: e.dma_start(out=rqdT.rearrange("(h f) t -> f h t", f=128)[:, :, c0:c0 + 512], in_=stg_qdT[sl].t[:]), reads=[stg_qdT[sl].r])
                S.barrier()

            RG = [[0, 1, 2, 3], [4, 5, 6, 7]]
            Rkg = Res("dkT_g")
            Rvg = Res("dv_g")
            for a in range(NSPL):
                S.dma("cc_k", lambda e, a=a: e.collective_compute("AllGather", ALU.bypass, replica_groups=RG, ins=[dkT_l[a]], outs=[dkT_g[a]]), writes=[Rkg], eng="pool", inc=1, cost=60.0)
                S.dma("cc_v", lambda e, a=a: e.collective_compute("AllGather", ALU.bypass, replica_groups=RG, ins=[dv_l[a]], outs=[dv_g[a]]), writes=[Rvg], eng="pool", inc=1, cost=60.0)

            with ExitStack() as st:
                SKMAX = max(TS, SKP)
                kTh = [sb(st, f"kTh{i}", [128, SKMAX], BF16) for i in range(2)]
                vh = [sb(st, f"vh{i}", [128, SKMAX // 128, 128], BF16) for i in range(2)]
                qA = [sb(st, f"qA{i}", [128, max(TS, TP)], BF16) for i in range(2)]
                qB = [sb(st, f"qB{i}", [128, max(TS, TP)], BF16) for i in range(2)]
                for i in range(2):
                    S.pool(lambda e, i=i: e.memset(qA[i].t[64:128, :], 0.0), writes=[qA[i].r])
                    S.pool(lambda e, i=i: e.memset(qB[i].t[0:64, :], 0.0), writes=[qB[i].r])
                NPX = 6
                pexp2 = [sb(st, f"pexp{i}", [128, 2, 512], BF16) for i in range(NPX)]
                s01 = [[sb(st, f"s01_{c}{i}", [128, 512], BF16) for i in range(2)] for c in range(2)]
                s23 = [[sb(st, f"s23_{c}{i}", [128, 512], BF16) for i in range(2)] for c in range(2)]
                s4 = [[sb(st, f"s4_{c}{i}", [128, 512], BF16) for i in range(2)] for c in range(2)]
                s8 = [[sb(st, f"s8_{c}{i}", [128, 512], BF16) for i in range(2)] for c in range(2)]
                gcount = 0
                r0 = sb(st, "r0", [128, 512], F32)
                r1 = sb(st, "r1", [128, 512], F32)
                a0 = sb(st, "a0", [128, 512], F32)
                a1 = sb(st, "a1", [128, 512], F32)
                osq = sb(st, "osq", [128, 512], F32)
                rsn = sb(st, "rsn", [128, 512], F32)
                aout = [sb(st, f"aout{i}", [128, 512], BF16) for i in range(2)]
                sbk2 = [psb(st, f"sbk{i}", [128, 2, 512], F32) for i in range(2)]
                O = [psb(st, f"Oacc{i}", [128, 512], F32) for i in range(2)]
                L = [psb(st, f"Lacc{i}", [128, 512], F32) for i in range(2)]
                heads = [(job, h) for job in range(2) for h in range(4)]

                def load_head(hi):
                    job, h = heads[hi]
                    hb = hi % 2
                    kb_, vb_ = kTh[hb], vh[hb]
                    Tq = TS if job == 0 else TP
                    qoff = 0 if job == 0 else TS
                    if job == 0:
                        S.dma(f"b_k{hb}", lambda e, kb_=kb_, h=h: e.dma_start(out=kb_.t[:, 0:TS], in_=dkT_s[h * 128:(h + 1) * 128, :]), writes=[kb_.r])
                        S.dma(f"b_v{hb}", lambda e, vb_=vb_, h=h: e.dma_start(out=vb_.t[:, 0:TS // 128, :], in_=dv_s[:, h * 128:(h + 1) * 128].rearrange("(k p) e -> p k e", p=128)), writes=[vb_.r])
                    else:
                        ha = h // HPS
                        hl = h % HPS
                        S.dma(f"b_k{hb}", lambda e, kb_=kb_, ha=ha, hl=hl: e.dma_start(out=kb_.t[:, 0:SKP].rearrange("p (r t) -> p r t", r=NR), in_=dkT_g[ha].rearrange("(r f) t -> f r t", f=HPS * 128)[hl * 128:(hl + 1) * 128, :, :]), reads=[Rkg], writes=[kb_.r])
                        for a in range(NSPL):
                            for r_ in range(NR):
                                S.dma(f"b_v{hb}", lambda e, vb_=vb_, h=h, a=a, r_=r_: e.dma_start(
                                    out=vb_.t[:, (r_ * TP + a * TPS) // 128:(r_ * TP + (a + 1) * TPS) // 128, :],
                                    in_=dv_g[a][r_ * TPS:(r_ + 1) * TPS, h * 128:(h + 1) * 128].rearrange("(k p) e -> p k e", p=128)), reads=[Rvg], writes=[vb_.r])
                    S.dma(f"b_q{hb}", lambda e, hb=hb, h=h, qoff=qoff, Tq=Tq: e.dma_start(out=qA[hb].t[0:64, 0:Tq], in_=dqT[h * 128:h * 128 + 64, qoff:qoff + Tq]), writes=[qA[hb].r])
                    S.dma(f"b_r{hb}", lambda e, hb=hb, h=h, qoff=qoff, Tq=Tq: e.dma_start(out=qB[hb].t[64:128, 0:Tq], in_=dqT[h * 128 + 64:h * 128 + 128, qoff:qoff + Tq]), writes=[qB[hb].r])

                units = []
                for hi, (job, h) in enumerate(heads):
                    Tq = TS if job == 0 else TP
                    Sk = TS if job == 0 else SKP
                    for qc in range(Tq // 512):
                        for kb in range(Sk // 128):
                            units.append((hi, qc, kb, Sk // 128))

                def emit_qk(u):
                    hi, qc, kb, nkb = units[u]
                    hb = hi % 2
                    kb_ = kTh[hb]
                    sbuf_ = sbk2[u % 2]
                    for c in range(2):
                        qb_ = (qA if c == 0 else qB)[hb]
                        S.pe(lambda e, sbuf_=sbuf_, c=c, kb=kb, qc=qc, kb_=kb_, qb_=qb_: e.matmul(sbuf_.t[:, c, :], lhsT=kb_.t[:, kb * 128:(kb + 1) * 128],
                                                                                             rhs=qb_.t[:, qc * 512:(qc + 1) * 512], start=True, stop=True),
                             reads=[kb_.r, qb_.r], writes=[sbuf_.r])

                ocount = 0
                load_head(0)
                emit_qk(0)
                emit_qk(1)
                for u in range(len(units)):
                    hi, qc, kb, nkb = units[u]
                    job, h = heads[hi]
                    hb = hi % 2
                    vb_ = vh[hb]
                    qoff = 0 if job == 0 else TS
                    if qc == 0 and kb == 0 and hi + 1 < len(heads):
                        load_head(hi + 1)
                    pe2 = pexp2[u % NPX]
                    s2_ = sbk2[u % 2]
                    S.act(lambda e, pe2=pe2, s2_=s2_: e.activation(out=pe2.t[:], in_=s2_.t[:], func=AF.Exp), reads=[s2_.r], writes=[pe2.r], cost=1.05)
                    if u + 2 < len(units):
                        if units[u + 2][0] != hi and units[u + 2][1] == 0 and units[u + 2][2] == 0 and units[u + 2][0] + 1 < len(heads):
                            pass
                        emit_qk(u + 2)
                    gp = gcount % 2
                    for c in range(2):
                        pe_ = pexp2[u % NPX]
                        S.pe(lambda e, pe_=pe_, c=c, kb=kb, vb_=vb_, nkb=nkb: e.matmul(O[c].t[:], lhsT=vb_.t[:, kb, :], rhs=pe_.t[:, c, :], start=(kb == 0), stop=(kb == nkb - 1)),
                             reads=[vb_.r, pe_.r], writes=[O[c].r])
                        if kb % 2 == 1:
                            pp_ = pexp2[(u - 1) % NPX]
                            dst = (s01 if kb % 4 == 1 else s23)[c][gp]
                            S.dve(lambda e, dst=dst, pp_=pp_, pe_=pe_, c=c: e.tensor_tensor(out=dst.t[:], in0=pp_.t[:, c, :], in1=pe_.t[:, c, :], op=ALU.add), reads=[pp_.r, pe_.r], writes=[dst.r], cost=0.3)
                        if kb % 4 == 3:
                            a_, b_, d_ = s01[c][gp], s23[c][gp], s4[c][gp]
                            S.dve(lambda e, a_=a_, b_=b_, d_=d_: e.tensor_tensor(out=d_.t[:], in0=a_.t[:], in1=b_.t[:], op=ALU.add), reads=[a_.r, b_.r], writes=[d_.r], cost=0.3)
                        if kb % 8 == 7:
                            a_, b_, d_ = s4[c][gp ^ 1], s4[c][gp], s8[c][(gcount // 2) % 2]
                            S.dve(lambda e, a_=a_, b_=b_, d_=d_: e.tensor_tensor(out=d_.t[:], in0=a_.t[:], in1=b_.t[:], op=ALU.add), reads=[a_.r, b_.r], writes=[d_.r], cost=0.3)
                            S.pe(lambda e, d_=d_, c=c, kb=kb, nkb=nkb: e.matmul(L[c].t[:], lhsT=ones_b.t[:], rhs=d_.t[:], start=(kb == 7), stop=(kb == nkb - 1)),
                                 reads=[ones_b.r, d_.r], writes=[L[c].r])
                    if kb % 4 == 3:
                        gcount += 1
                    if kb != nkb - 1:
                        continue
                    S.act(lambda e: e.activation(out=a0.t[:], in_=O[0].t[:], func=AF.Copy), reads=[O[0].r], writes=[a0.r])
                    S.dve(lambda e: e.tensor_copy(out=a1.t[:], in_=O[1].t[:]), reads=[O[1].r], writes=[a1.r])
                    S.dve(lambda e: e.reciprocal(out=r0.t[:], in_=L[0].t[:]), reads=[L[0].r], writes=[r0.r])
                    S.dve(lambda e: e.reciprocal(out=r1.t[:], in_=L[1].t[:]), reads=[L[1].r], writes=[r1.r])
                    S.dve(lambda e: e.tensor_tensor(out=a0.t[:], in0=a0.t[:], in1=r0.t[:], op=ALU.mult), reads=[a0.r, r0.r], writes=[a0.r])
                    S.dve(lambda e: e.tensor_tensor(out=a1.t[:], in0=a1.t[:], in1=r1.t[:], op=ALU.mult), reads=[a1.r, r1.r], writes=[a1.r])
                    S.dve(lambda e, l=l: e.scalar_tensor_tensor(out=a0.t[:], in0=a1.t[:], scalar=neglam.t[:, l:l + 1], in1=a0.t[:], op0=ALU.mult, op1=ALU.add),
                          reads=[a1.r, a0.r, neglam.r], writes=[a0.r])
                    S.act(lambda e: e.activation(out=osq.t[:], in_=a0.t[:], func=AF.Square), reads=[a0.r], writes=[osq.r])
                    sB = L[0]
                    S.pe(lambda e, sB=sB: e.matmul(sB.t[:], lhsT=ones_f.t[:], rhs=osq.t[:], start=True, stop=True), reads=[ones_f.r, osq.r], writes=[sB.r])
                    rsqrt_act(rsn.t[:], sB.t[:], 1.0 / 128, sB, rsn)
                    ao = aout[ocount % 2]
                    S.dve(lambda e, ao=ao: e.scalar_tensor_tensor(out=ao.t[:], in0=a0.t[:], scalar=subw.t[:, 0:1], in1=rsn.t[:], op0=ALU.mult, op1=ALU.mult),
                          reads=[a0.r, subw.r, rsn.r], writes=[ao.r])
                    tcol = qoff + qc * 512
                    S.dma(f"b_o{ocount % 2}", lambda e, ao=ao, h=h, tcol=tcol: e.dma_start(out=aT[h * 128:(h + 1) * 128, tcol:tcol + 512], in_=ao.t[:]), reads=[ao.r], eng="act")
                    ocount += 1
                S.barrier()

            with ExitStack() as st:
                SallJ = [sb(st, "Sall0", [128, TS // 128, 512], BF16), sb(st, "Sall1", [128, TP // 128, 512], BF16)]
                sttJ = [sb(st, f"stt{i}", [128, 512], F32) for i in range(2)]
                sttmpJ = [sb(st, f"sttmp{i}", [128, 512], F32) for i in range(2)]
                tg = sb(st, "tg", [128, NR, 512], F32)
                kdl = [sb(st, f"kdl{i}", [128, 4, 1024], BF16) for i in range(4)]
                rvl = [sb(st, f"rvl{i}", [128, 4, 512], BF16) for i in range(4)]
                okT = [sb(st, f"okT{i}", [128, 4, 512], BF16) for i in range(2)]
                oqT = [sb(st, f"oqT{i}", [128, 4, 512], BF16) for i in range(2)]
                oqd = [sb(st, f"oqd{i}", [128, 8, 512], BF16) for i in range(2)]
                orv = [sb(st, f"orv{i}", [128, 4, 512], BF16) for i in range(2)]
                ogs = [sb(st, f"ogs{i}", [128, 4, 512], BF16) for i in range(2)]
                PT = [sb(st, f"PT{i}", [128, 8, 128], BF16) for i in range(2)]
                rsq = sb(st, "rsq", [128, 512], F32)
                rss = sb(st, "rss", [128, 8], F32)
                rn = sb(st, "rn", [128, 512], F32)
                rr = sb(st, "rr", [128, 512], BF16)
                rTst = [sb(st, f"rTst{i}", [128, 4, 512], BF16) for i in range(2)]
                pkvJ = [psb(st, f"pkv{i}", [128, 512], F32) for i in range(2)]
                psc = [psb(st, f"psc{i}", [128, 4, 128], F32) for i in range(2)]
                po = [psb(st, f"pro{i}", [128, 512], F32) for i in range(2)]
                ptr = psb(st, "ptr", [128, 8, 128], BF16)
                Rtg = Res("st_g")
                ldn = [0]

                def sweep(job, use_init):
                    stt, sttmp, Sall = sttJ[job], sttmpJ[job], SallJ[job]
                    Tq = TS if job == 0 else TP
                    qoff = 0 if job == 0 else TS
                    n = Tq // 128
                    nsc = n // 4
                    if use_init:
                        S.dma("r_tg", lambda e: e.dma_start(out=tg.t[:], in_=st_g.rearrange("(r p) c -> p r c", p=128)), reads=[Rtg], writes=[tg.r])
                        for r_ in range(NR):
                            cb = coef.t[:, r_, :].unsqueeze(2).to_broadcast([128, 8, 64])
                            tg3 = tg.t[:, r_, :].rearrange("p (h e) -> p h e", e=64)
                            if r_ == 0:
                                S.dve(lambda e, cb=cb, tg3=tg3: e.tensor_tensor(out=stt.t[:].rearrange("p (h e) -> p h e", e=64), in0=tg3, in1=cb, op=ALU.mult), reads=[tg.r, coef.r], writes=[stt.r])
                            else:
                                S.dve(lambda e, cb=cb, tg3=tg3: e.tensor_tensor(out=sttmp.t[:].rearrange("p (h e) -> p h e", e=64), in0=tg3, in1=cb, op=ALU.mult), reads=[tg.r, coef.r], writes=[sttmp.r])
                                S.dve(lambda e: e.tensor_tensor(out=stt.t[:], in0=stt.t[:], in1=sttmp.t[:], op=ALU.add), reads=[stt.r, sttmp.r], writes=[stt.r])
                    else:
                        S.dve(lambda e: e.memset(stt.t[:], 0.0), writes=[stt.r])
                    cur = {}
                    for t in range(n):
                        tf = t
                        tb = n - 1 - t
                        bufs = {}
                        for nm, ti in (("f", tf), ("b", tb)):
                            sc = ti // 4
                            if (nm, sc) not in cur:
                                slot = job * 2 + (0 if nm == "f" else 1)
                                c0 = qoff + sc * 512
                                S.dma(f"r_kd{slot}", lambda e, slot=slot, c0=c0: e.dma_start(out=kdl[slot].t[:], in_=rkd[c0:c0 + 512, :].rearrange("(j p) c -> p j c", p=128)), writes=[kdl[slot].r])
                                S.dma(f"r_rv{slot}", lambda e, slot=slot, c0=c0: e.dma_start(out=rvl[slot].t[:], in_=rv[c0:c0 + 512, :].rearrange("(j p) c -> p j c", p=128)), writes=[rvl[slot].r])
                                cur = {k: v for k, v in cur.items() if k[0] != nm}
                                cur[(nm, sc)] = slot
                            bufs[nm] = (cur[(nm, sc)], ti % 4)
                        S.act(lambda e, tf=tf: e.activation(out=Sall.t[0:64, tf, :], in_=stt.t[0:64, :], func=AF.Copy), reads=[stt.r], writes=[Sall.r])
                        S.act(lambda e, tb=tb: e.activation(out=Sall.t[64:128, tb, :], in_=stt.t[64:128, :], func=AF.Copy), reads=[stt.r], writes=[Sall.r])
                        pk = pkvJ[job]
                        (sf, jf), (sb_, jb) = bufs["f"], bufs["b"]
                        for h in range(8):
                            kf = kdl[sf].t[:, jf, :].rearrange("p (g a d) -> p g a d", g=8, a=2)
                            kbk = kdl[sb_].t[:, jb, :].rearrange("p (g a d) -> p g a d", g=8, a=2)
                            S.pe(lambda e, pk=pk, h=h, kf=kf, sf=sf, jf=jf: e.matmul(pk.t[0:64, h * 64:(h + 1) * 64], lhsT=kf[:, h, 0, :], rhs=rvl[sf].t[:, jf, h * 64:(h + 1) * 64], start=True, stop=True),
                                 reads=[kdl[sf].r, rvl[sf].r], writes=[pk.r])
                            S.pe(lambda e, pk=pk, h=h, kbk=kbk, sb_=sb_, jb=jb: e.matmul(pk.t[64:128, h * 64:(h + 1) * 64], lhsT=kbk[:, h, 1, :], rhs=rvl[sb_].t[:, jb, h * 64:(h + 1) * 64], start=True, stop=True),
                                 reads=[kdl[sb_].r, rvl[sb_].r], writes=[pk.r])
                        S.dve(lambda e: e.tensor_tensor(out=sttmp.t[:].rearrange("p (h e) -> p h e", e=64), in0=stt.t[:].rearrange("p (h e) -> p h e", e=64),
                                                        in1=cdec.t[:].unsqueeze(2).to_broadcast([128, 8, 64]), op=ALU.mult), reads=[stt.r, cdec.r], writes=[sttmp.r])
                        S.dve(lambda e, pk=pk: e.tensor_tensor(out=stt.t[:], in0=pk.t[:], in1=sttmp.t[:], op=ALU.add), reads=[pk.r, sttmp.r], writes=[stt.r])

                def outputs(job):
                    Sall = SallJ[job]
                    Tq = TS if job == 0 else TP
                    qoff = 0 if job == 0 else TS
                    n = Tq // 128
                    for sc in range(n // 4):
                        sl = sc % 2
                        c0 = qoff + sc * 512
                        S.dma(f"o_kT{sl}", lambda e, sl=sl, c0=c0: e.dma_start(out=okT[sl].t[:], in_=rkT.rearrange("(b p) t -> p b t", p=128)[:, :, c0:c0 + 512]), writes=[okT[sl].r])
                        S.dma(f"o_qT{sl}", lambda e, sl=sl, c0=c0: e.dma_start(out=oqT[sl].t[:], in_=rqT.rearrange("(b p) t -> p b t", p=128)[:, :, c0:c0 + 512]), writes=[oqT[sl].r])
                        S.dma(f"o_qd{sl}", lambda e, sl=sl, c0=c0: e.dma_start(out=oqd[sl].t[:], in_=rqdT.rearrange("(b p) t -> p b t", p=128)[:, :, c0:c0 + 512]), writes=[oqd[sl].r])
                        S.dma(f"o_rv{sl}", lambda e, sl=sl, c0=c0: e.dma_start(out=orv[sl].t[:], in_=rv[c0:c0 + 512, :].rearrange("(j p) c -> p j c", p=128)), writes=[orv[sl].r])
                        S.dma(f"o_gs{sl}", lambda e, sl=sl, c0=c0: e.dma_start(out=ogs[sl].t[:], in_=rgs[c0:c0 + 512, :].rearrange("(j p) c -> p j c", p=128)), writes=[ogs[sl].r])
                        for j in range(4):
                            i = sc * 4 + j
                            ptb = PT[j % 2]
                            for h in range(8):
                                pb_ = psc[h % 2]
                                hp = (h % 2) * 64
                                S.pe(lambda e, pb_=pb_, h=h, hp=hp, sl=sl, j=j: e.matmul(pb_.t[:, h // 2, :], lhsT=okT[sl].t[hp:hp + 64, h // 2, j * 128:(j + 1) * 128],
                                                                                   rhs=oqT[sl].t[hp:hp + 64, h // 2, j * 128:(j + 1) * 128], start=True, stop=True),
                                     reads=[okT[sl].r, oqT[sl].r], writes=[pb_.r])
                            pt4 = ptb.t[:].rearrange("p (b a) n -> p b a n", a=2)
                            S.dve(lambda e, pt4=pt4: e.tensor_tensor(out=pt4[:, :, 0, :], in0=psc[0].t[:], in1=DTe.t[:], op=ALU.mult), reads=[psc[0].r, DTe.r], writes=[ptb.r])
                            S.dve(lambda e, pt4=pt4: e.tensor_tensor(out=pt4[:, :, 1, :], in0=psc[1].t[:], in1=DTo.t[:], op=ALU.mult), reads=[psc[1].r, DTo.r, ptb.r], writes=[ptb.r])
                            pob = po[j % 2]
                            for h in range(8):
                                S.pe(lambda e, pob=pob, h=h, ptb=ptb, sl=sl, j=j: e.matmul(pob.t[:, h * 64:(h + 1) * 64], lhsT=ptb.t[:, h, :], rhs=orv[sl].t[:, j, h * 64:(h + 1) * 64], start=True, stop=False),
                                     reads=[ptb.r, orv[sl].r], writes=[pob.r])
                                S.pe(lambda e, pob=pob, h=h, sl=sl, j=j, i=i: e.matmul(pob.t[:, h * 64:(h + 1) * 64], lhsT=oqd[sl].t[:, h, j * 128:(j + 1) * 128], rhs=Sall.t[:, i, h * 64:(h + 1) * 64], start=False, stop=True),
                                     reads=[oqd[sl].r, Sall.r], writes=[pob.r])
                            S.act(lambda e, pob=pob: e.activation(out=rsq.t[:], in_=pob.t[:], func=AF.Square), reads=[pob.r], writes=[rsq.r])
                            S.dve(lambda e: e.tensor_reduce(out=rss.t[:], in_=rsq.t[:].rearrange("p (g d) -> p g d", d=64), axis=AX.X, op=ALU.add), reads=[rsq.r], writes=[rss.r])
                            rsqrt_act(rss.t[:], rss.t[:], 1.0 / 64, rss, rss)
                            S.dve(lambda e, pob=pob: e.tensor_tensor(out=rn.t[:].rearrange("p (g d) -> p g d", d=64), in0=pob.t[:].rearrange("p (g d) -> p g d", d=64),
                                                                    in1=rss.t[:].unsqueeze(2).to_broadcast([128, 8, 64]), op=ALU.mult), reads=[pob.r, rss.r], writes=[rn.r])
                            S.dve(lambda e: e.tensor_tensor(out=rn.t[:].rearrange("p (g d) -> p g d", d=64), in0=rn.t[:].rearrange("p (g d) -> p g d", d=64),
                                                            in1=gnw.t[:].unsqueeze(1).to_broadcast([128, 8, 64]), op=ALU.mult), reads=[rn.r, gnw.r], writes=[rn.r])
                            S.dve(lambda e, sl=sl, j=j: e.tensor_tensor(out=rr.t[:], in0=rn.t[:], in1=ogs[sl].t[:, j, :], op=ALU.mult), reads=[rn.r, ogs[sl].r], writes=[rr.r])
                            for c in range(4):
                                S.pe(lambda e, c=c: e.transpose(out=ptr.t[:, c, :], in_=rr.t[:, c * 128:(c + 1) * 128], identity=ident.t[:]), reads=[rr.r, ident.r], writes=[ptr.r])
                            S.act(lambda e, sl=sl, j=j: e.activation(out=rTst[sl].t[:, :, j * 128:(j + 1) * 128], in_=ptr.t[:, 0:4, :], func=AF.Copy), reads=[ptr.r], writes=[rTst[sl].r])
                        S.dma(f"o_rT{sl}", lambda e, sl=sl, c0=c0: e.dma_start(out=rT.rearrange("(b p) t -> p b t", p=128)[:, :, c0:c0 + 512], in_=rTst[sl].t[:]), reads=[rTst[sl].r])

                sweep(1, False)
                Rstl = Res("st_l")
                S.dma("r_stl", lambda e: e.dma_start(out=st_l, in_=sttJ[1].t[:]), reads=[sttJ[1].r], writes=[Rstl])
                S.dma("cc_s", lambda e: e.collective_compute("AllGather", ALU.bypass, replica_groups=RG, ins=[st_l], outs=[st_g]), reads=[Rstl], writes=[Rtg], eng="pool", inc=1, cost=250.0)
                sweep(0, False)
                outputs(0)
                sweep(1, True)
                outputs(1)
                S.barrier()

            with ExitStack() as st:
                w_out = load_w(st, "w_out", w_out_in[l], D, D, "w_out")
                w1 = load_w(st, "w1", w1_in[l], D, DFF, "w1")
                arT = [sb(st, f"arT{i}", [128, 8, 512], BF16) for i in range(2)]
                xs_ = [sb(st, f"xs{i}", [128, 4, D], F32) for i in range(2)]
                junk = sb(st, "junk2", [128, D], BF16)
                ss2 = sb(st, "ss2", [128, 1], F32)
                h2l = [sb(st, f"h2_{i}", [128, D], BF16) for i in range(2)]
                h2T = [sb(st, f"h2T{i}", [128, 8, 512], BF16) for i in range(1)] * 2
                rl = [sb(st, f"rl{i}", [128, 512], F32) for i in range(2)]
                uTs = [sb(st, f"uTs{i}", [128, 32, 512], BF16) for i in range(1)] * 2
                pw = [psb(st, f"pw{i}", [128, 512], F32) for i in range(2)]
                ph = psb(st, "ph", [128, 8, 128], BF16)
                pu = [psb(st, f"pu{i}", [128, 512], F32) for i in range(4)]
                for s in range(NST):
                    sl = s % 2
                    c0 = s * 512
                    S.dma(f"c_a{sl}", lambda e, sl=sl, c0=c0: e.dma_start(out=arT[sl].t[:, 0:4, :], in_=aT.rearrange("(b p) t -> p b t", p=128)[:, :, c0:c0 + 512]), writes=[arT[sl].r])
                    S.dma(f"c_r{sl}", lambda e, sl=sl, c0=c0: e.dma_start(out=arT[sl].t[:, 4:8, :], in_=rT.rearrange("(b p) t -> p b t", p=128)[:, :, c0:c0 + 512]), writes=[arT[sl].r])
                    S.dma(f"c_x{sl}", lambda e, sl=sl, c0=c0: e.dma_start(out=xs_[sl].t[:], in_=x_src[c0:c0 + 512, :].rearrange("(j p) c -> p j c", p=128)), writes=[xs_[sl].r])
                    for j in range(4):
                        for cb in range(2):
                            pb_ = pw[cb]
                            for k in range(8):
                                S.pe(lambda e, pb_=pb_, k=k, cb=cb, sl=sl, j=j: e.matmul(pb_.t[:], lhsT=arT[sl].t[:, k, j * 128:(j + 1) * 128], rhs=w_out.t[:, k, cb * 512:(cb + 1) * 512], start=(k == 0), stop=(k == 7)),
                                     reads=[arT[sl].r, w_out.r(k, cb * 512)], writes=[pb_.r])
                            S.dve(lambda e, pb_=pb_, cb=cb, sl=sl, j=j: e.tensor_tensor(out=xs_[sl].t[:, j, cb * 512:(cb + 1) * 512], in0=pb_.t[:], in1=xs_[sl].t[:, j, cb * 512:(cb + 1) * 512], op=ALU.add),
                                  reads=[pb_.r, xs_[sl].r], writes=[xs_[sl].r])
                        S.act(lambda e, sl=sl, j=j: e.activation(out=junk.t[:], in_=xs_[sl].t[:, j, :], func=AF.Square, accum_out=ss2.t[:]), reads=[xs_[sl].r], writes=[junk.r, ss2.r])
                        rsqrt_act(ss2.t[:], ss2.t[:], 1.0 / D, ss2, ss2)
                        h2 = h2l[j % 2]
                        S.dve(lambda e, sl=sl, j=j, h2=h2: e.scalar_tensor_tensor(out=h2.t[:], in0=xs_[sl].t[:, j, :], scalar=ss2.t[:, 0:1], in1=ln2b.t[:], op0=ALU.mult, op1=ALU.mult),
                              reads=[xs_[sl].r, ss2.r, ln2b.r], writes=[h2.r])
                        for c in range(8):
                            S.pe(lambda e, c=c, h2=h2: e.transpose(out=ph.t[:, c, :], in_=h2.t[:, c * 128:(c + 1) * 128], identity=ident.t[:]), reads=[h2.r, ident.r], writes=[ph.r])
                        S.act(lambda e, sl=sl, j=j: e.activation(out=h2T[sl].t[:, :, j * 128:(j + 1) * 128], in_=ph.t[:], func=AF.Copy), reads=[ph.r], writes=[h2T[sl].r])
                    S.dma(f"c_xo{sl}", lambda e, sl=sl, c0=c0: e.dma_start(out=xres[c0:c0 + 512, :].rearrange("(j p) c -> p j c", p=128), in_=xs_[sl].t[:]), reads=[xs_[sl].r], eng="act")
                    for fc in range(32):
                        pb_ = pu[fc % 4]
                        for k in range(8):
                            S.pe(lambda e, pb_=pb_, k=k, fc=fc, sl=sl: e.matmul(pb_.t[:], lhsT=w1.t[:, k, fc * 128:(fc + 1) * 128], rhs=h2T[sl].t[:, k, :], start=(k == 0), stop=(k == 7)),
                                 reads=[w1.r(k, fc * 128), h2T[sl].r], writes=[pb_.r])
                        rb = rl[fc % 2]
                        S.act(lambda e, pb_=pb_, rb=rb: e.activation(out=rb.t[:], in_=pb_.t[:], func=AF.Relu), reads=[pb_.r], writes=[rb.r])
                        S.dve(lambda e, rb=rb, fc=fc, sl=sl: e.tensor_tensor(out=uTs[sl].t[:, fc, :], in0=rb.t[:], in1=rb.t[:], op=ALU.mult), reads=[rb.r], writes=[uTs[sl].r])
                    S.dma(f"c_u{sl}", lambda e, sl=sl, c0=c0: e.dma_start(out=uT.rearrange("(c p) t -> p c t", p=128)[:, :, c0:c0 + 512], in_=uTs[sl].t[:]), reads=[uTs[sl].r], eng="act")
                S.barrier()

            with ExitStack() as st:
                w2 = load_w(st, "w2", w2_in[l], DFF, D, "w2", gn=1024, korder=True)
                wg = load_w(st, "wg", wg_in[l], D, D, "wg")
                wp = load_w(st, "wp", wp_in[l], PLE, D, "wp")
                uTl = [sb(st, f"uTl{i}", [128, 32, 512], BF16) for i in range(2)]
                xs_ = [sb(st, f"xc{i}", [128, 4, D], F32) for i in range(1)] * 2
                pl_ = [sb(st, f"pl{i}", [128, 4, PLE], F32) for i in range(1)] * 2
                x2bl = [sb(st, f"x2b{i}", [128, D], BF16) for i in range(2)]
                x2Tl = [sb(st, f"x2T{i}", [128, 8, 128], BF16) for i in range(2)]
                pbfl = [sb(st, f"pbf{i}", [128, PLE], BF16) for i in range(2)]
                ppTl = [sb(st, f"ppT{i}", [128, 2, 128], BF16) for i in range(2)]
                sg = [sb(st, f"sg{i}", [128, 512], F32) for i in range(2)]
                pm = [psb(st, f"pm{i}", [128, 512], F32) for i in range(2)]
                pg = [psb(st, f"pg{i}", [128, 512], F32) for i in range(2)]
                pq = [psb(st, f"pq{i}", [128, 512], F32) for i in range(2)]
                px = psb(st, "px", [128, 8, 128], BF16)
                pp2 = psb(st, "pp2", [128, 8, 128], BF16)
                for s in range(NST):
                    sl = s % 2
                    c0 = s * 512
                    S.dma(f"d_u{sl}", lambda e, sl=sl, c0=c0: e.dma_start(out=uTl[sl].t[:], in_=uT.rearrange("(c p) t -> p c t", p=128)[:, :, c0:c0 + 512]), writes=[uTl[sl].r])
                    S.dma(f"d_x{sl}", lambda e, sl=sl, c0=c0: e.dma_start(out=xs_[sl].t[:], in_=xres[c0:c0 + 512, :].rearrange("(j p) c -> p j c", p=128)), writes=[xs_[sl].r])
                    S.dma(f"d_p{sl}", lambda e, sl=sl, c0=c0, l=l: e.dma_start(out=pl_[sl].t[:], in_=p_in[l, c0:c0 + 512, :].rearrange("(j p) c -> p j c", p=128)), writes=[pl_[sl].r])
                    for j in range(4):
                        for cb in range(2):
                            pb_ = pm[cb]
                            for fc in range(32):
                                S.pe(lambda e, pb_=pb_, fc=fc, cb=cb, sl=sl, j=j: e.matmul(pb_.t[:], lhsT=uTl[sl].t[:, fc, j * 128:(j + 1) * 128], rhs=w2.t[:, fc, cb * 512:(cb + 1) * 512], start=(fc == 0), stop=(fc == 31)),
                                     reads=[uTl[sl].r, w2.r(fc, cb * 512)], writes=[pb_.r])
                            S.dve(lambda e, pb_=pb_, cb=cb, sl=sl, j=j: e.tensor_tensor(out=xs_[sl].t[:, j, cb * 512:(cb + 1) * 512], in0=pb_.t[:], in1=xs_[sl].t[:, j, cb * 512:(cb + 1) * 512], op=ALU.add),
                                  reads=[pb_.r, xs_[sl].r], writes=[xs_[sl].r])
                        x2b, x2T, pbf, ppT = x2bl[j % 2], x2Tl[j % 2], pbfl[j % 2], ppTl[j % 2]
                        S.act(lambda e, sl=sl, j=j, x2b=x2b: e.activation(out=x2b.t[:], in_=xs_[sl].t[:, j, :], func=AF.Copy), reads=[xs_[sl].r], writes=[x2b.r], cost=1.1)
                        for c in range(8):
                            S.pe(lambda e, c=c, x2b=x2b: e.transpose(out=px.t[:, c, :], in_=x2b.t[:, c * 128:(c + 1) * 128], identity=ident.t[:]), reads=[x2b.r, ident.r], writes=[px.r])
                        S.act(lambda e, x2T=x2T: e.activation(out=x2T.t[:], in_=px.t[:], func=AF.Copy), reads=[px.r], writes=[x2T.r], cost=1.1)
                        S.pool(lambda e, sl=sl, j=j, pbf=pbf: e.tensor_copy(out=pbf.t[:], in_=pl_[sl].t[:, j, :]), reads=[pl_[sl].r], writes=[pbf.r])
                        for c in range(2):
                            S.pe(lambda e, c=c, pbf=pbf: e.transpose(out=pp2.t[:, c, :], in_=pbf.t[:, c * 128:(c + 1) * 128], identity=ident.t[:]), reads=[pbf.r, ident.r], writes=[pp2.r])
                        S.act(lambda e, ppT=ppT: e.activation(out=ppT.t[:], in_=pp2.t[:, 0:2, :], func=AF.Copy), reads=[pp2.r], writes=[ppT.r])
                        for cb in range(2):
                            g_ = pg[cb]
                            q_ = pq[cb]
                            for k in range(8):
                                S.pe(lambda e, g_=g_, k=k, cb=cb: e.matmul(g_.t[:], lhsT=x2T.t[:, k, :], rhs=wg.t[:, k, cb * 512:(cb + 1) * 512], start=(k == 0), stop=(k == 7)),
                                     reads=[x2T.r, wg.r(k, cb * 512)], writes=[g_.r])
                            for k in range(2):
                                S.pe(lambda e, q_=q_, k=k, cb=cb: e.matmul(q_.t[:], lhsT=ppT.t[:, k, :], rhs=wp.t[:, k, cb * 512:(cb + 1) * 512], start=(k == 0), stop=(k == 1)),
                                     reads=[ppT.r, wp.r(k, cb * 512)], writes=[q_.r])
                            sgb = sg[cb]
                            S.act(lambda e, g_=g_, sgb=sgb: e.activation(out=sgb.t[:], in_=g_.t[:], func=AF.Exp, scale=-1.0), reads=[g_.r], writes=[sgb.r])
                            S.dve(lambda e, sgb=sgb: e.tensor_scalar(out=sgb.t[:], in0=sgb.t[:], scalar1=1.0, scalar2=None, op0=ALU.add), reads=[sgb.r], writes=[sgb.r])
                            S.dve(lambda e, sgb=sgb: e.reciprocal(out=sgb.t[:], in_=sgb.t[:]), reads=[sgb.r], writes=[sgb.r])
                            S.dve(lambda e, sgb=sgb, q_=q_: e.tensor_tensor(out=sgb.t[:], in0=q_.t[:], in1=sgb.t[:], op=ALU.mult), reads=[q_.r, sgb.r], writes=[sgb.r])
                            S.dve(lambda e, sgb=sgb, cb=cb, sl=sl, j=j: e.tensor_tensor(out=xs_[sl].t[:, j, cb * 512:(cb + 1) * 512], in0=sgb.t[:], in1=xs_[sl].t[:, j, cb * 512:(cb + 1) * 512], op=ALU.add),
                                  reads=[sgb.r, xs_[sl].r], writes=[xs_[sl].r])
                    S.dma(f"d_xo{sl}", lambda e, sl=sl, c0=c0: e.dma_start(out=x_dst[c0:c0 + 512, :].rearrange("(j p) c -> p j c", p=128), in_=xs_[sl].t[:]), reads=[xs_[sl].r], eng="act")
                S.barrier()

        if DEBUG:
            for nm, src in (("rgs", rgs), ("rv", rv), ("dv_s", dv_s), ("dqT", dqT), ("aT", aT), ("rT", rT), ("xres", xres), ("uT", uT), ("rqT", rqT), ("rkd", rkd), ("rqdT", rqdT), ("rkT", rkT), ("dkT_s", dkT_s)):
                dbg = dram("dbg_" + nm, list(src.shape), src.dtype, "ExternalOutput")
                S.dma("dbg", lambda e, dbg=dbg, src=src: e.dma_start(out=dbg, in_=src))
        S.emit(top)
        nc._n_ops = len(S.ops)
        nc._n_sem = S.nsem
    return nc


ROPE_THETA = 500000.0
RET_THETA = 10000.0


def host_tables(TS, TP, rank):
    posv = np.concatenate([np.arange(TS), rank * TP + np.arange(TP)]).astype(np.float32)
    inv_d = (np.float32(1.0) / (np.float32(ROPE_THETA) ** (np.arange(0, 16, 2, dtype=np.float32) / np.float32(16)))).astype(np.float32)
    inv_r = (np.float32(1.0) / (np.float32(RET_THETA) ** (np.arange(0, 64, 2, dtype=np.float32) / np.float32(64)))).astype(np.float32)
    ang_d = (posv[:, None] * inv_d[None, :]).astype(np.float32)
    ang_r = (posv[:, None] * inv_r[None, :]).astype(np.float32)
    pos = np.concatenate([np.cos(ang_d), np.sin(ang_d), np.cos(ang_r), np.sin(ang_r)], axis=1).astype(np.float32)
    m = np.arange(128)[:, None].astype(np.float32)
    n = np.arange(128)[None, :].astype(np.float32)
    relF = np.maximum(n - m, 0.0)
    mskF = (n >= m).astype(np.float32) * 0.125
    relB = np.maximum(m - n, 0.0)
    mskB = (m > n).astype(np.float32) * 0.125
    p = np.arange(128, dtype=np.float32)[:, None]
    cst = np.concatenate([relF, mskF, relB, mskB, p + 1, 128 - p, 127 - p, p], axis=1).astype(np.float32)
    rkt = np.zeros((128, 8), np.float32)
    for r in range(NR):
        if r < rank:
            rkt[0:64, r] = TP * (rank - 1 - r)
            rkt[0:64, 4 + r] = 1.0
        if r > rank:
            rkt[64:128, r] = TP * (r - rank - 1)
            rkt[64:128, 4 + r] = 1.0
    return pos, cst, rkt


_NC_CACHE = {}


def run_cores(inputs, TS, TP, DEPTH):
    key = (TS, TP, DEPTH)
    if key not in _NC_CACHE:
        _NC_CACHE[key] = build(TS, TP, DEPTH)
    nc = _NC_CACHE[key]
    f = lambda a: np.ascontiguousarray(np.asarray(a, dtype=np.float32))
    xp = f(inputs["x_prompt"])
    xs = f(inputs["x_sample"])
    pp = f(inputs["p_prompt"])
    ps = f(inputs["p_sample"])
    shared = {
        "ln1_w": f(inputs["ln1_w"]), "w_in": f(inputs["w_in"]), "diff_q_norm": f(inputs["diff_q_norm"]),
        "diff_k_norm": f(inputs["diff_k_norm"]), "diff_lambda": f(inputs["diff_lambda"]).reshape(1, DEPTH * 256),
        "diff_subln": f(inputs["diff_subln"]), "ret_decay_logit": f(inputs["ret_decay_logit"]).reshape(1, DEPTH * 16),
        "ret_gn": f(inputs["ret_gn"]), "w_out": f(inputs["w_out"]), "ln2_w": f(inputs["ln2_w"]),
        "w_mlp1": f(inputs["w_mlp1"]), "w_mlp2": f(inputs["w_mlp2"]), "w_ple_gate": f(inputs["w_ple_gate"]),
        "w_ple_proj": f(inputs["w_ple_proj"]),
    }
    in_maps = []
    for c in range(8):
        g, r = c // NR, c % NR
        pos, cst, rkt = host_tables(TS, TP, r)
        m = dict(shared)
        m["x"] = np.ascontiguousarray(np.concatenate([xs[c], xp[g, r * TP:(r + 1) * TP]], axis=0))
        m["p"] = np.ascontiguousarray(np.concatenate([ps[:, c], pp[:, g, r * TP:(r + 1) * TP]], axis=1))
        m["pos"] = pos
        m["cst"] = cst
        m["rkt"] = rkt
        in_maps.append(m)
    res = run_bass_kernel_spmd(nc, in_maps, core_ids=list(range(8)))
    y_s = np.stack([np.asarray(res.results[c]["y"][:TS]) for c in range(8)], axis=0)
    y_p = np.stack([np.concatenate([np.asarray(res.results[g * NR + r]["y"][TS:]) for r in range(NR)], axis=0) for g in range(2)], axis=0)
    return y_p.astype(np.float32), y_s.astype(np.float32)


def kernel(**inputs):
    return run_cores(inputs, 4096, 2048, 4)
```

```python
import math
import types
import numpy as np
import concourse.bass as bass
import concourse.mybir as mybir
from concourse.bass_utils import run_bass_kernel_spmd
from contextlib import ExitStack

F32 = mybir.dt.float32
BF16 = mybir.dt.bfloat16
AF = mybir.ActivationFunctionType
ALU = mybir.AluOpType
AX = mybir.AxisListType

COMPUTE = ("pe", "act", "dve", "pool")
ALL_ENG = ("pe", "act", "dve", "pool", "sp")
SAME_ENG_SYNC = True

D = 1024
INW = 3584
DFF = 4096
PLE = 256
EPS = 1e-6
NR = 4


class Res:
    __slots__ = ("name", "w", "rs")

    def __init__(self, name=""):
        self.name = name
        self.w = None
        self.rs = []


class Op:
    __slots__ = ("eng", "fn", "deps", "mark", "val", "key", "is_mm", "inc", "cost", "idx", "fin", "succ", "npend", "ready")

    def __init__(self, eng, fn, key=None, is_mm=False, cost=None):
        self.eng = eng
        self.fn = fn
        self.deps = []
        self.mark = False
        self.val = 0
        self.key = key
        self.is_mm = is_mm
        self.inc = 16
        self.cost = cost
        self.idx = 0
        self.fin = 0.0
        self.succ = None
        self.npend = 0
        self.ready = 0.0


def _freeze(fn):
    if fn is None or fn.__closure__ is None:
        return fn
    cells = []
    for c in fn.__closure__:
        try:
            cells.append(types.CellType(c.cell_contents))
        except ValueError:
            cells.append(c)
    return types.FunctionType(fn.__code__, fn.__globals__, fn.__name__, fn.__defaults__, tuple(cells))


DEF_COST = {"pe": 0.27, "act": 0.45, "dve": 0.60, "pool": 0.90, "sp": 0.30}
SLAT = 0.8
XLAT = 0.6
DMA_LAT = 3.0
RESCHEDULE = True


class _Probe:
    def __init__(self):
        self.name = None
        self.args = ()
        self.kw = {}

    def __getattr__(self, name):
        def f(*a, **k):
            if self.name is None:
                self.name, self.args, self.kw = name, a, k
            return self
        return f


def _esize(ap):
    n = 1
    for d in ap.shape[1:]:
        n *= int(d)
    return n


def _estimate(eng, fn, key):
    try:
        p = _Probe()
        fn(p)
        out = p.kw.get("out", p.args[0] if p.args else None)
        if out is None or not hasattr(out, "shape"):
            return None
        n = _esize(out)
        if key is not None:
            nbytes = n * int(out.shape[0]) * (2 if out.dtype == BF16 else 4)
            return 2.5 + nbytes / 150e3
        if eng == "pe":
            return 0.03 + n / 2100.0
        if eng == "act":
            return 0.08 + n / 960.0
        if eng == "dve":
            return 0.07 + n / 960.0
        if eng == "pool":
            return 0.6 + n / 700.0
    except Exception:
        return None
    return None


class Sched:
    def __init__(self, nc):
        self.nc = nc
        self.ops = []

    def op(self, eng, fn, reads=(), writes=(), key=None, is_mm=False, cost=None):
        fn = _freeze(fn)
        if cost is None and fn is not None:
            cost = _estimate(eng, fn, key)
        o = Op(eng, fn, key, is_mm, cost)
        deps = o.deps
        for r in reads:
            if r.w is not None:
                deps.append(r.w)
        for w in writes:
            if w.w is not None:
                deps.append(w.w)
            deps.extend(w.rs)
        for r in reads:
            r.rs.append(o)
        for w in writes:
            w.w = o
            w.rs = []
        o.idx = len(self.ops)
        self.ops.append(o)
        return o

    def pe(self, fn, reads=(), writes=(), cost=None):
        return self.op("pe", fn, reads, writes, is_mm=True, cost=cost)

    def act(self, fn, reads=(), writes=(), cost=None):
        return self.op("act", fn, reads, writes, cost=cost)

    def dve(self, fn, reads=(), writes=(), cost=None):
        return self.op("dve", fn, reads, writes, cost=cost)

    def pool(self, fn, reads=(), writes=(), cost=None):
        return self.op("pool", fn, reads, writes, cost=cost)

    def dma(self, key, fn, reads=(), writes=(), eng="sp", inc=16, cost=None):
        o = self.op(eng, fn, reads, writes, key=key, cost=cost)
        o.inc = inc
        return o

    def barrier(self):
        o = Op(None, None)
        o.idx = len(self.ops)
        self.ops.append(o)

    @staticmethod
    def _skip(p, o):
        return p.key is None and p.eng == o.eng and (not SAME_ENG_SYNC or (p.is_mm and o.is_mm))

    def _schedule_segment(self, seg):
        import heapq
        inseg = set(id(o) for o in seg)
        for o in seg:
            o.succ = []
            o.npend = 0
            o.ready = 0.0
        for o in seg:
            for p in o.deps:
                if id(p) in inseg:
                    p.succ.append(o)
                    o.npend += 1
        free = {e: 0.0 for e in ALL_ENG}
        heaps = {e: [] for e in ALL_ENG}
        for o in seg:
            if o.npend == 0:
                heapq.heappush(heaps[o.eng], (0.0, o.idx, o))
        order = {e: [] for e in ALL_ENG}
        remaining = len(seg)
        while remaining:
            best = None
            for e in ALL_ENG:
                h = heaps[e]
                if not h:
                    continue
                t_free = free[e]
                cands = []
                while h and h[0][0] <= t_free:
                    cands.append(heapq.heappop(h))
                if cands:
                    c = min(cands, key=lambda x: x[1])
                    for x in cands:
                        if x is not c:
                            heapq.heappush(h, x)
                    heapq.heappush(h, c)
                    start = t_free
                    pick = c
                else:
                    pick = h[0]
                    start = pick[0]
                if best is None or start < best[0] or (start == best[0] and pick[1] < best[2][1]):
                    best = (start, e, pick)
            start, e, pick = best
            h = heaps[e]
            h.remove(pick)
            heapq.heapify(h)
            o = pick[2]
            cost = o.cost if o.cost is not None else DEF_COST[e]
            if o.key is not None:
                free[e] = start + DEF_COST["sp"]
                o.fin = start + (cost if o.cost is not None else DMA_LAT)
            else:
                free[e] = start + cost
                o.fin = free[e]
            order[e].append(o)
            remaining -= 1
            for q in o.succ:
                if q.eng == o.eng and o.key is None:
                    lat = 0.0 if (o.is_mm and q.is_mm) else SLAT
                else:
                    lat = XLAT
                r = o.fin + lat
                if r > q.ready:
                    q.ready = r
                q.npend -= 1
                if q.npend == 0:
                    heapq.heappush(heaps[q.eng], (q.ready, q.idx, q))
        return order

    def emit(self, stack):
        nc = self.nc
        segs = []
        cur = []
        for o in self.ops:
            if o.eng is None:
                if cur:
                    segs.append(cur)
                    cur = []
            else:
                cur.append(o)
        if cur:
            segs.append(cur)
        streams = {e: [] for e in ALL_ENG}
        for seg in segs:
            if RESCHEDULE:
                order = self._schedule_segment(seg)
            else:
                order = {e: [o for o in seg if o.eng == e] for e in ALL_ENG}
            lastops = {}
            for e in ALL_ENG:
                for o in order[e]:
                    lastops[o.key if o.key is not None else e] = o
            for e in ALL_ENG:
                streams[e].extend(order[e])
            deps = list(lastops.values())
            for e in ALL_ENG:
                b = Op(e, None)
                b.deps = list(deps)
                streams[e].append(b)
        for e in ALL_ENG:
            for o in streams[e]:
                for p in o.deps:
                    if p.key is None and not self._skip(p, o):
                        p.mark = True
        kcnt = {}
        for e in ALL_ENG:
            cnt = 0
            for o in streams[e]:
                if o.fn is None:
                    continue
                if o.key is not None:
                    kcnt[o.key] = kcnt.get(o.key, 0) + o.inc
                    o.val = kcnt[o.key]
                elif o.mark:
                    cnt += 1
                    o.val = cnt
        sems = {}
        for e in COMPUTE:
            sems[e] = stack.enter_context(nc.semaphore("s_" + e))
        for k in kcnt:
            sems[k] = stack.enter_context(nc.semaphore("d_" + k))
        self.nsem = len(sems)
        block = stack.enter_context(nc.Block())

        def run(engname, eng):
            seen = {}
            for o in streams[engname]:
                need = {}
                for p in o.deps:
                    if self._skip(p, o):
                        continue
                    sk = p.eng if p.key is None else p.key
                    if seen.get(sk, 0) >= p.val:
                        continue
                    if need.get(sk, 0) < p.val:
                        need[sk] = p.val
                for sk, v in need.items():
                    eng.wait_ge(sems[sk], v)
                    seen[sk] = v
                if o.fn is None:
                    continue
                ins = o.fn(eng)
                if o.key is not None:
                    ins.then_inc(sems[o.key], o.inc)
                elif o.mark:
                    ins.then_inc(sems[o.eng], 1)
            if engname == "sp":
                for k, v in kcnt.items():
                    if seen.get(k, 0) < v:
                        eng.wait_ge(sems[k], v)

        @block.tensor
        def _(e):
            run("pe", e)

        @block.scalar
        def _(e):
            run("act", e)

        @block.vector
        def _(e):
            run("dve", e)

        @block.gpsimd
        def _(e):
            run("pool", e)

        @block.sync
        def _(e):
            run("sp", e)


class Buf:
    __slots__ = ("t", "r")

    def __init__(self, t, name):
        self.t = t
        self.r = Res(name)


def build(TS, TP, DEPTH, DEBUG=False):
    T = TS + TP
    SKP = NR * TP
    NT = T // 128
    NST = T // 512
    assert TS % 512 == 0 and TP % 512 == 0
    nc = bass.Bass("TRN2", target_bir_lowering=False)

    def dram(name, shape, dtype, kind="Internal"):
        return nc.dram_tensor(name, shape, dtype, kind=kind).ap()

    x_in = dram("x", [T, D], F32, "ExternalInput")
    p_in = dram("p", [DEPTH, T, PLE], F32, "ExternalInput")
    pos_in = dram("pos", [T, 80], F32, "ExternalInput")
    cst_in = dram("cst", [128, 516], F32, "ExternalInput")
    rkt_in = dram("rkt", [128, 8], F32, "ExternalInput")
    ln1_in = dram("ln1_w", [DEPTH, D], F32, "ExternalInput")
    w_in_in = dram("w_in", [DEPTH, D, INW], F32, "ExternalInput")
    qn_in = dram("diff_q_norm", [DEPTH, 64], F32, "ExternalInput")
    kn_in = dram("diff_k_norm", [DEPTH, 64], F32, "ExternalInput")
    lam_in = dram("diff_lambda", [1, DEPTH * 256], F32, "ExternalInput")
    sub_in = dram("diff_subln", [DEPTH, 128], F32, "ExternalInput")
    dec_in = dram("ret_decay_logit", [1, DEPTH * 16], F32, "ExternalInput")
    gn_in = dram("ret_gn", [DEPTH, 64], F32, "ExternalInput")
    w_out_in = dram("w_out", [DEPTH, D, D], F32, "ExternalInput")
    ln2_in = dram("ln2_w", [DEPTH, D], F32, "ExternalInput")
    w1_in = dram("w_mlp1", [DEPTH, D, DFF], F32, "ExternalInput")
    w2_in = dram("w_mlp2", [DEPTH, DFF, D], F32, "ExternalInput")
    wg_in = dram("w_ple_gate", [DEPTH, D, D], F32, "ExternalInput")
    wp_in = dram("w_ple_proj", [DEPTH, PLE, D], F32, "ExternalInput")
    y_out = dram("y", [T, D], F32, "ExternalOutput")

    xres = dram("xres", [T, D], F32)
    dqT = dram("dqT", [512, T], BF16)
    dkT_s = dram("dkT_s", [512, TS], BF16)
    NSPL = max(1, (512 * TP * 2) // (1 << 20))
    HPS = 4 // NSPL
    TPS = TP // NSPL
    dkT_l = [dram(f"dkT_l{a}", [HPS * 128, TP], BF16) for a in range(NSPL)]
    dkT_g = [dram(f"dkT_g{a}", [NR * HPS * 128, TP], BF16) for a in range(NSPL)]
    dv_s = dram("dv_s", [TS, 512], BF16)
    dv_l = [dram(f"dv_l{a}", [TPS, 512], BF16) for a in range(NSPL)]
    dv_g = [dram(f"dv_g{a}", [NR * TPS, 512], BF16) for a in range(NSPL)]
    rqT = dram("rqT", [512, T], BF16)
    rkT = dram("rkT", [512, T], BF16)
    rqdT = dram("rqdT", [8 * 128, T], BF16)
    rkd = dram("rkd", [T, 1024], BF16)
    rv = dram("rv", [T, 512], BF16)
    rgs = dram("rgs", [T, 512], BF16)
    aT = dram("aT", [512, T], BF16)
    rT = dram("rT", [512, T], BF16)
    uT = dram("uT", [DFF, T], BF16)
    st_l = dram("st_l", [128, 512], F32)
    st_g = dram("st_g", [NR * 128, 512], F32)

    with ExitStack() as top:
        S = Sched(nc)

        uid = [0]

        def sb(st, name, shape, dt):
            uid[0] += 1
            return Buf(st.enter_context(nc.sbuf_tensor(f"sb{uid[0]}_{name}", shape, dt)), name)

        def psb(st, name, shape, dt):
            uid[0] += 1
            return Buf(st.enter_context(nc.psum_tensor(f"ps{uid[0]}_{name}", shape, dt)), name)

        ident = sb(top, "ident", [128, 128], BF16)
        identf = sb(top, "identf", [128, 128], F32)
        ones_b = sb(top, "ones_b", [128, 128], BF16)
        ones_f = sb(top, "ones_f", [128, 128], F32)
        cst = sb(top, "cst", [128, 516], F32)
        rkt = sb(top, "rkt", [128, 8], F32)
        lg = sb(top, "lg", [128, DEPTH * 16], F32)
        lgcol = sb(top, "lgcol", [128, DEPTH * 8], F32)
        lam = sb(top, "lam", [128, DEPTH], F32)
        neglam = sb(top, "neglam", [128, DEPTH], F32)
        epsc = sb(top, "epsc", [128, 1], F32)
        ln1b = sb(top, "ln1b", [128, D], F32)
        ln2b = sb(top, "ln2b", [128, D], F32)
        qnw = sb(top, "qnw", [128, 64], F32)
        knw = sb(top, "knw", [128, 64], F32)
        gnw = sb(top, "gnw", [128, 64], F32)
        subw = sb(top, "subw", [128, 1], F32)
        qdec = sb(top, "qdec", [128, 8, 2], F32)
        kdec = sb(top, "kdec", [128, 8, 2], F32)
        cdec = sb(top, "cdec", [128, 8], F32)
        DTe = sb(top, "DTe", [128, 4, 128], F32)
        DTo = sb(top, "DTo", [128, 4, 128], F32)
        coef = sb(top, "coef", [128, 4, 8], F32)

        relF = cst.t[:, 0:128]
        mskF = cst.t[:, 128:256]
        relB = cst.t[:, 256:384]
        mskB = cst.t[:, 384:512]
        idx_p1 = cst.t[:, 512:513]
        idx_128m = cst.t[:, 513:514]
        idx_127m = cst.t[:, 514:515]
        idx_p = cst.t[:, 515:516]

        S.dma("c_init", lambda e: e.dma_start(out=cst.t[:], in_=cst_in), writes=[cst.r])
        S.dma("c_init", lambda e: e.dma_start(out=rkt.t[:], in_=rkt_in), writes=[rkt.r])
        S.pool(lambda e: e.memset(identf.t[:], 0.0), writes=[identf.r])
        S.pool(lambda e: e.affine_select(out=identf.t[:], in_=identf.t[:], compare_op=ALU.not_equal, fill=1.0, base=0,
                                         pattern=[[-1, 128]], channel_multiplier=1), reads=[identf.r], writes=[identf.r])
        S.dve(lambda e: e.tensor_copy(out=ident.t[:], in_=identf.t[:]), reads=[identf.r], writes=[ident.r])
        S.dve(lambda e: e.memset(ones_b.t[:], 1.0), writes=[ones_b.r])
        S.dve(lambda e: e.memset(ones_f.t[:], 1.0), writes=[ones_f.r])
        S.dve(lambda e: e.memset(epsc.t[:], EPS), writes=[epsc.r])

        with ExitStack() as st0:
            NL = DEPTH * 16
            xx = sb(st0, "ls_x", [128, NL], F32)
            ax = sb(st0, "ls_ax", [128, NL], F32)
            uu = sb(st0, "ls_u", [128, NL], F32)
            ss_ = sb(st0, "ls_s", [128, NL], F32)
            s2 = sb(st0, "ls_s2", [128, NL], F32)
            pl = sb(st0, "ls_pl", [128, NL], F32)
            lp = sb(st0, "lp", [128, DEPTH * 256], F32)
            S.dma("c_init", lambda e: e.dma_start(out=xx.t[:], in_=dec_in.to_broadcast([128, NL])), writes=[xx.r])
            S.dma("c_init", lambda e: e.dma_start(out=lp.t[:], in_=lam_in.to_broadcast([128, DEPTH * 256])), writes=[lp.r])
            S.barrier()
            S.dve(lambda e: e.tensor_scalar(out=ax.t[:], in0=xx.t[:], scalar1=-1.0, scalar2=None, op0=ALU.mult), reads=[xx.r], writes=[ax.r])
            S.dve(lambda e: e.tensor_tensor(out=ax.t[:], in0=ax.t[:], in1=xx.t[:], op=ALU.max), reads=[xx.r, ax.r], writes=[ax.r])
            S.act(lambda e: e.activation(out=uu.t[:], in_=ax.t[:], func=AF.Exp, scale=-1.0), reads=[ax.r], writes=[uu.r])
            S.dve(lambda e: e.tensor_scalar(out=ss_.t[:], in0=uu.t[:], scalar1=2.0, scalar2=None, op0=ALU.add), reads=[uu.r], writes=[ss_.r])
            S.dve(lambda e: e.reciprocal(out=ss_.t[:], in_=ss_.t[:]), reads=[ss_.r], writes=[ss_.r])
            S.dve(lambda e: e.tensor_tensor(out=ss_.t[:], in0=ss_.t[:], in1=uu.t[:], op=ALU.mult), reads=[ss_.r, uu.r], writes=[ss_.r])
            S.dve(lambda e: e.tensor_tensor(out=s2.t[:], in0=ss_.t[:], in1=ss_.t[:], op=ALU.mult), reads=[ss_.r], writes=[s2.r])
            S.dve(lambda e: e.tensor_scalar(out=pl.t[:], in0=s2.t[:], scalar1=1.0 / 13, scalar2=1.0 / 11, op0=ALU.mult, op1=ALU.add), reads=[s2.r], writes=[pl.r])
            for cc in (1.0 / 9, 1.0 / 7, 1.0 / 5, 1.0 / 3, 1.0):
                S.dve(lambda e: e.tensor_tensor(out=pl.t[:], in0=pl.t[:], in1=s2.t[:], op=ALU.mult), reads=[pl.r, s2.r], writes=[pl.r])
                S.dve(lambda e, cc=cc: e.tensor_scalar(out=pl.t[:], in0=pl.t[:], scalar1=cc, scalar2=None, op0=ALU.add), reads=[pl.r], writes=[pl.r])
            S.dve(lambda e: e.tensor_tensor(out=pl.t[:], in0=pl.t[:], in1=ss_.t[:], op=ALU.mult), reads=[pl.r, ss_.r], writes=[pl.r])
            S.dve(lambda e: e.tensor_scalar(out=ax.t[:], in0=xx.t[:], scalar1=0.0, scalar2=None, op0=ALU.min), reads=[xx.r], writes=[ax.r])
            S.dve(lambda e: e.scalar_tensor_tensor(out=lg.t[:], in0=pl.t[:], scalar=-2.0, in1=ax.t[:], op0=ALU.mult, op1=ALU.add), reads=[pl.r, ax.r], writes=[lg.r])
            lg4 = lg.t[:].rearrange("p (l a h) -> p l a h", l=DEPTH, a=2)
            lgc3 = lgcol.t[:].rearrange("p (l h) -> p l h", l=DEPTH)
            S.dve(lambda e: e.tensor_copy(out=lgc3[0:64], in_=lg4[0:64, :, 0, :]), reads=[lg.r], writes=[lgcol.r])
            S.dve(lambda e: e.tensor_copy(out=lgc3[64:128], in_=lg4[64:128, :, 1, :]), reads=[lg.r], writes=[lgcol.r])
            pr = sb(st0, "lpr", [128, DEPTH * 2 * 64], F32)
            sm = sb(st0, "lsm", [128, DEPTH * 2], F32)
            lp5 = lp.t[:].rearrange("p (l a b d) -> p l a b d", l=DEPTH, a=2, b=2)
            pr4 = pr.t[:].rearrange("p (l a d) -> p l a d", l=DEPTH, a=2)
            S.dve(lambda e: e.tensor_tensor(out=pr4, in0=lp5[:, :, :, 0, :], in1=lp5[:, :, :, 1, :], op=ALU.mult), reads=[lp.r], writes=[pr.r])
            S.dve(lambda e: e.tensor_reduce(out=sm.t[:], in_=pr.t[:].rearrange("p (g d) -> p g d", d=64), axis=AX.X, op=ALU.add), reads=[pr.r], writes=[sm.r])
            S.act(lambda e: e.activation(out=sm.t[:], in_=sm.t[:], func=AF.Exp), reads=[sm.r], writes=[sm.r])
            sm3 = sm.t[:].rearrange("p (l a) -> p l a", a=2)
            S.dve(lambda e: e.tensor_tensor(out=lam.t[:], in0=sm3[:, :, 0], in1=sm3[:, :, 1], op=ALU.subtract), reads=[sm.r], writes=[lam.r])
            for l in range(DEPTH):
                li = 0.8 - 0.6 * math.exp(-0.3 * l)
                S.dve(lambda e, l=l, li=li: e.tensor_scalar(out=lam.t[:, l:l + 1], in0=lam.t[:, l:l + 1], scalar1=li, scalar2=None, op0=ALU.add), reads=[lam.r], writes=[lam.r])
            S.dve(lambda e: e.tensor_scalar(out=neglam.t[:], in0=lam.t[:], scalar1=-1.0, scalar2=None, op0=ALU.mult), reads=[lam.r], writes=[neglam.r])
            S.barrier()

        def rsqrt_act(out_ap, in_ap, scale, rbuf, wbuf):
            S.act(lambda e: e.activation(out=out_ap, in_=in_ap, func=AF.Ln, scale=scale, bias=epsc.t[0:out_ap.shape[0], 0:1]), reads=[rbuf.r, epsc.r], writes=[wbuf.r])
            S.act(lambda e: e.activation(out=out_ap, in_=out_ap, func=AF.Exp, scale=-0.5), reads=[wbuf.r], writes=[wbuf.r])

        class WGroups:
            def __init__(self, t, gn):
                self.t = t
                self.gn = gn
                self.res = {}

            def r(self, k, n0):
                return self.res[(k // 4, n0 // self.gn)]

        def load_w(st, name, src, K, N, key, gn=512, korder=False):
            kc = K // 128
            gn = min(gn, N, 2048)
            uid[0] += 1
            t = st.enter_context(nc.sbuf_tensor(f"sb{uid[0]}_{name}", [128, kc, N], BF16))
            w = WGroups(t, gn)
            srcv = src.rearrange("(c p) n -> p c n", p=128)
            chain = [Res(name + "_chA"), Res(name + "_chB")]
            groups = [(c0, n0) for n0 in range(0, N, gn) for c0 in range(0, kc, 4)]
            if korder:
                groups = [(c0, n0) for c0 in range(0, kc, 4) for n0 in range(0, N, gn)]
            for i, (c0, n0) in enumerate(groups):
                c1 = min(kc, c0 + 4)
                n1 = min(N, n0 + gn)
                rr_ = Res(f"{name}_{c0}_{n0}")
                w.res[(c0 // 4, n0 // gn)] = rr_
                S.dma(key + "AB"[i % 2], lambda e, c0=c0, c1=c1, n0=n0, n1=n1: e.dma_start(out=t[:, c0:c1, n0:n1], in_=srcv[:, c0:c1, n0:n1]),
                      writes=[rr_, chain[i % 2]], eng="pool", cost=8.0)
            return w

        for l in range(DEPTH):
            lam_init = 0.8 - 0.6 * math.exp(-0.3 * l)
            x_src = x_in if l == 0 else xres
            x_dst = y_out if l == DEPTH - 1 else xres

            S.dma("c_lay", lambda e, l=l: e.dma_start(out=ln1b.t[:], in_=ln1_in[l:l + 1, :].to_broadcast([128, D])), writes=[ln1b.r])
            S.dma("c_lay", lambda e, l=l: e.dma_start(out=ln2b.t[:], in_=ln2_in[l:l + 1, :].to_broadcast([128, D])), writes=[ln2b.r])
            S.dma("c_lay", lambda e, l=l: e.dma_start(out=qnw.t[:], in_=qn_in[l:l + 1, :].to_broadcast([128, 64])), writes=[qnw.r])
            S.dma("c_lay", lambda e, l=l: e.dma_start(out=knw.t[:], in_=kn_in[l:l + 1, :].to_broadcast([128, 64])), writes=[knw.r])
            S.dma("c_lay", lambda e, l=l: e.dma_start(out=gnw.t[:], in_=gn_in[l:l + 1, :].to_broadcast([128, 64])), writes=[gnw.r])
            S.dma("c_lay", lambda e, l=l: e.dma_start(out=subw.t[:], in_=sub_in[l:l + 1, :].rearrange("o e -> e o"), allow_slow_non_contiguous=True), writes=[subw.r])
            S.barrier()
            S.dve(lambda e: e.tensor_scalar(out=qnw.t[:], in0=qnw.t[:], scalar1=0.125, scalar2=None, op0=ALU.mult), reads=[qnw.r], writes=[qnw.r])
            S.dve(lambda e, li=lam_init: e.tensor_scalar(out=subw.t[:], in0=subw.t[:], scalar1=1.0 - li, scalar2=None, op0=ALU.mult), reads=[subw.r], writes=[subw.r])
            lgl = lg.t[:, l * 16:(l + 1) * 16]
            S.act(lambda e, lgl=lgl: e.activation(out=qdec.t[:, :, 0], in_=lgl[:, 0:8], func=AF.Exp, scale=idx_p1), reads=[lg.r, cst.r], writes=[qdec.r])
            S.act(lambda e, lgl=lgl: e.activation(out=qdec.t[:, :, 1], in_=lgl[:, 8:16], func=AF.Exp, scale=idx_128m), reads=[lg.r, cst.r], writes=[qdec.r])
            S.act(lambda e, lgl=lgl: e.activation(out=kdec.t[:, :, 0], in_=lgl[:, 0:8], func=AF.Exp, scale=idx_127m), reads=[lg.r, cst.r], writes=[kdec.r])
            S.act(lambda e, lgl=lgl: e.activation(out=kdec.t[:, :, 1], in_=lgl[:, 8:16], func=AF.Exp, scale=idx_p), reads=[lg.r, cst.r], writes=[kdec.r])
            S.dve(lambda e: e.tensor_scalar(out=kdec.t[:], in0=kdec.t[:], scalar1=0.125, scalar2=None, op0=ALU.mult), reads=[kdec.r], writes=[kdec.r])
            S.act(lambda e, l=l: e.activation(out=cdec.t[:], in_=lgcol.t[:, l * 8:(l + 1) * 8], func=AF.Exp, scale=128.0), reads=[lgcol.r], writes=[cdec.r])
            with ExitStack() as stc:
                tmpa = sb(stc, "dt_a", [128, 128], F32)
                tmpb = sb(stc, "dt_b", [128, 128], F32)
                for h in range(8):
                    dst = (DTe if h % 2 == 0 else DTo)
                    dsl = dst.t[:, h // 2, :]
                    S.act(lambda e, h=h: e.activation(out=tmpa.t[:], in_=relF, func=AF.Exp, scale=lgl[:, h:h + 1]), reads=[cst.r, lg.r], writes=[tmpa.r])
                    S.act(lambda e, h=h: e.activation(out=tmpb.t[:], in_=relB, func=AF.Exp, scale=lgl[:, 8 + h:9 + h]), reads=[cst.r, lg.r], writes=[tmpb.r])
                    S.dve(lambda e: e.tensor_tensor(out=tmpa.t[:], in0=tmpa.t[:], in1=mskF, op=ALU.mult), reads=[tmpa.r, cst.r], writes=[tmpa.r])
                    S.dve(lambda e: e.tensor_tensor(out=tmpb.t[:], in0=tmpb.t[:], in1=mskB, op=ALU.mult), reads=[tmpb.r, cst.r], writes=[tmpb.r])
                    S.dve(lambda e, dsl=dsl: e.tensor_tensor(out=dsl, in0=tmpa.t[:], in1=tmpb.t[:], op=ALU.add), reads=[tmpa.r, tmpb.r], writes=[dst.r])
                for r_ in range(NR):
                    S.act(lambda e, r_=r_, l=l: e.activation(out=coef.t[:, r_, :], in_=lgcol.t[:, l * 8:(l + 1) * 8], func=AF.Exp, scale=rkt.t[:, r_:r_ + 1]), reads=[lgcol.r, rkt.r], writes=[coef.r])
                    S.dve(lambda e, r_=r_: e.tensor_scalar(out=coef.t[:, r_, :], in0=coef.t[:, r_, :], scalar1=rkt.t[:, 4 + r_:5 + r_], scalar2=None, op0=ALU.mult), reads=[coef.r, rkt.r], writes=[coef.r])
                S.barrier()

            with ExitStack() as st:
                w_in = load_w(st, "w_in", w_in_in[l], D, INW, "w_in")
                P2 = range(2)
                xt = [sb(st, f"xt{i}", [128, D], F32) for i in P2]
                post = [sb(st, f"post{i}", [128, 80], F32) for i in P2]
                junk = sb(st, "junk", [128, D], BF16)
                ssx = [sb(st, f"ssx{i}", [128, 1], F32) for i in P2]
                hn = [sb(st, f"hn{i}", [128, D], BF16) for i in P2]
                hnT = [sb(st, f"hnT{i}", [128, 8, 128], BF16) for i in P2]
                qraw = [[sb(st, f"qraw{w}{i}", [128, 512], F32) for i in P2] for w in P2]
                qsq = [sb(st, f"qsq{w}", [128, 512], F32) for w in P2]
                ssg = [[sb(st, f"ssg{w}{i}", [128, 8], F32) for i in P2] for w in P2]
                qu = [sb(st, f"qu{w}", [128, 512], F32) for w in P2]
                w16 = [sb(st, f"w16{w}", [128, 8, 16], F32) for w in P2]
                rt = [[sb(st, f"rt{w}{i}", [128, 8, 8], F32) for i in range(4)] for w in P2]
                qb = [[sb(st, f"qb{w}{i}", [128, 512], BF16) for i in P2] for w in P2]
                rqf = [sb(st, f"rqf{w}", [128, 512], F32) for w in P2]
                rta = [sb(st, f"rta{w}", [128, 8, 32], F32) for w in P2]
                rtb = [sb(st, f"rtb{w}", [128, 8, 32], F32) for w in P2]
                rqb = [[sb(st, f"rqb{w}{i}", [128, 512], BF16) for i in P2] for w in P2]
                qd = [sb(st, f"qd{i}", [128, 1024], BF16) for i in P2]
                esg = [sb(st, f"esg{i}", [128, 512], F32) for i in P2]
                stg_qT = [sb(st, f"sg_qT{i}", [128, 4, 512], BF16) for i in P2]
                stg_kT = [sb(st, f"sg_kT{i}", [128, 4, 512], BF16) for i in P2]
                stg_rqT = [sb(st, f"sg_rqT{i}", [128, 4, 512], BF16) for i in P2]
                stg_rkT = [sb(st, f"sg_rkT{i}", [128, 4, 512], BF16) for i in P2]
                stg_qdT = [sb(st, f"sg_qdT{i}", [128, 8, 512], BF16) for i in P2]
                tv = [sb(st, f"tv{i}", [128, 512], BF16) for i in P2]
                trv = [sb(st, f"trv{i}", [128, 512], BF16) for i in P2]
                tgs = [sb(st, f"tgs{i}", [128, 512], BF16) for i in P2]
                tkd = [sb(st, f"tkd{i}", [128, 1024], BF16) for i in P2]
                pT = psb(st, "pT", [128, 8, 128], BF16)
                pT2 = [psb(st, f"pT2_{i}", [128, 8, 128], BF16) for i in P2]
                pj = [psb(st, f"pj{i}", [128, 512], F32) for i in range(5)]
                pjn = [0]

                def proj(cb, hnT_):
                    b = pj[pjn[0] % 5]
                    pjn[0] += 1
                    for k in range(8):
                        S.pe(lambda e, k=k, b=b, cb=cb: e.matmul(b.t[:], lhsT=hnT_.t[:, k, :], rhs=w_in.t[:, k, cb * 512:(cb + 1) * 512], start=(k == 0), stop=(k == 7)),
                             reads=[hnT_.r, w_in.r(k, cb * 512)], writes=[b.r])
                    return b

                def transpose_to(src, nblk, dst_stage, dst_ap_fn, pbuf):
                    for c in range(nblk):
                        S.pe(lambda e, c=c: e.transpose(out=pbuf.t[:, c, :], in_=src.t[:, c * 128:(c + 1) * 128], identity=ident.t[:]),
                             reads=[src.r, ident.r], writes=[pbuf.r])
                    S.act(lambda e: e.activation(out=dst_ap_fn(), in_=pbuf.t[:, 0:nblk, :], func=AF.Copy), reads=[pbuf.r], writes=[dst_stage.r])

                for t in range(NT):
                    sti = t // 4
                    j = t % 4
                    sl = sti % 2
                    par = t % 2
                    tok0 = t * 128
                    is_s = tok0 < TS
                    tl = tok0 if is_s else tok0 - TS
                    xb = xt[par]
                    pb = post[par]
                    hn_, hnT_, ssx_ = hn[par], hnT[par], ssx[par]
                    S.dma(f"a_x{par}", lambda e, xb=xb, tok0=tok0: e.dma_start(out=xb.t[:], in_=x_src[tok0:tok0 + 128, :]), writes=[xb.r])
                    S.dma(f"a_pos{par}", lambda e, pb=pb, tok0=tok0: e.dma_start(out=pb.t[:], in_=pos_in[tok0:tok0 + 128, :]), writes=[pb.r])
                    S.act(lambda e, xb=xb, ssx_=ssx_: e.activation(out=junk.t[:], in_=xb.t[:], func=AF.Square, accum_out=ssx_.t[:]), reads=[xb.r], writes=[junk.r, ssx_.r])
                    rsqrt_act(ssx_.t[:], ssx_.t[:], 1.0 / D, ssx_, ssx_)
                    S.dve(lambda e, xb=xb, hn_=hn_, ssx_=ssx_: e.scalar_tensor_tensor(out=hn_.t[:], in0=xb.t[:], scalar=ssx_.t[:, 0:1], in1=ln1b.t[:], op0=ALU.mult, op1=ALU.mult),
                          reads=[xb.r, ssx_.r, ln1b.r], writes=[hn_.r])
                    for c in range(8):
                        S.pe(lambda e, c=c, hn_=hn_: e.transpose(out=pT.t[:, c, :], in_=hn_.t[:, c * 128:(c + 1) * 128], identity=ident.t[:]), reads=[hn_.r, ident.r], writes=[pT.r])
                    S.act(lambda e, hnT_=hnT_: e.activation(out=hnT_.t[:], in_=pT.t[:], func=AF.Copy), reads=[pT.r], writes=[hnT_.r], cost=1.1)
                    cosd = pb.t[:, 0:8].unsqueeze(1).to_broadcast([128, 8, 8])
                    sind = pb.t[:, 8:16].unsqueeze(1).to_broadcast([128, 8, 8])
                    cosr = pb.t[:, 16:48].unsqueeze(1).to_broadcast([128, 8, 32])
                    sinr = pb.t[:, 48:80].unsqueeze(1).to_broadcast([128, 8, 32])
                    for which in range(2):
                        b = proj(which, hnT_)
                        nw = qnw if which == 0 else knw
                        stg = (stg_qT if which == 0 else stg_kT)[sl]
                        qraw_, qsq_, ssg_, qu_, w16_, rt_, qb_ = qraw[which][par], qsq[which], ssg[which][par], qu[which], w16[which], rt[which], qb[which][par]
                        S.act(lambda e, b=b, qraw_=qraw_: e.activation(out=qraw_.t[:], in_=b.t[:], func=AF.Copy), reads=[b.r], writes=[qraw_.r])
                        S.act(lambda e, b=b, qsq_=qsq_: e.activation(out=qsq_.t[:], in_=b.t[:], func=AF.Square), reads=[b.r], writes=[qsq_.r])
                        S.dve(lambda e, qsq_=qsq_, ssg_=ssg_: e.tensor_reduce(out=ssg_.t[:], in_=qsq_.t[:].rearrange("p (g d) -> p g d", d=64), axis=AX.X, op=ALU.add), reads=[qsq_.r], writes=[ssg_.r])
                        rsqrt_act(ssg_.t[:], ssg_.t[:], 1.0 / 64, ssg_, ssg_)
                        qr3 = qraw_.t[:].rearrange("p (g d) -> p g d", d=64)
                        qu3 = qu_.t[:].rearrange("p (g d) -> p g d", d=64)
                        qb3 = qb_.t[:].rearrange("p (g d) -> p g d", d=64)
                        S.dve(lambda e, qr3=qr3, qu3=qu3, ssg_=ssg_: e.tensor_tensor(out=qu3, in0=qr3, in1=ssg_.t[:].unsqueeze(2).to_broadcast([128, 8, 64]), op=ALU.mult), reads=[qraw_.r, ssg_.r], writes=[qu_.r])
                        S.dve(lambda e, qu3=qu3, qb3=qb3, nw=nw: e.tensor_tensor(out=qb3, in0=qu3, in1=nw.t[:].unsqueeze(1).to_broadcast([128, 8, 64]), op=ALU.mult), reads=[qu_.r, nw.r], writes=[qb_.r])
                        S.dve(lambda e, qu3=qu3, nw=nw, w16_=w16_: e.tensor_tensor(out=w16_.t[:], in0=qu3[:, :, 0:16], in1=nw.t[:, 0:16].unsqueeze(1).to_broadcast([128, 8, 16]), op=ALU.mult), reads=[qu_.r, nw.r], writes=[w16_.r], cost=0.15)
                        x1 = w16_.t[:, :, 0:8]
                        x2 = w16_.t[:, :, 8:16]
                        S.dve(lambda e, x1=x1, cosd=cosd, rt_=rt_: e.tensor_tensor(out=rt_[0].t[:], in0=x1, in1=cosd, op=ALU.mult), reads=[w16_.r, pb.r], writes=[rt_[0].r], cost=0.12)
                        S.dve(lambda e, x2=x2, sind=sind, rt_=rt_: e.tensor_tensor(out=rt_[1].t[:], in0=x2, in1=sind, op=ALU.mult), reads=[w16_.r, pb.r], writes=[rt_[1].r], cost=0.12)
                        S.dve(lambda e, x1=x1, sind=sind, rt_=rt_: e.tensor_tensor(out=rt_[2].t[:], in0=x1, in1=sind, op=ALU.mult), reads=[w16_.r, pb.r], writes=[rt_[2].r], cost=0.12)
                        S.dve(lambda e, x2=x2, cosd=cosd, rt_=rt_: e.tensor_tensor(out=rt_[3].t[:], in0=x2, in1=cosd, op=ALU.mult), reads=[w16_.r, pb.r], writes=[rt_[3].r], cost=0.12)
                        S.dve(lambda e, qb3=qb3, rt_=rt_: e.tensor_tensor(out=qb3[:, :, 0:8], in0=rt_[0].t[:], in1=rt_[1].t[:], op=ALU.subtract), reads=[rt_[0].r, rt_[1].r, qb_.r], writes=[qb_.r], cost=0.12)
                        S.dve(lambda e, qb3=qb3, rt_=rt_: e.tensor_tensor(out=qb3[:, :, 8:16], in0=rt_[2].t[:], in1=rt_[3].t[:], op=ALU.add), reads=[rt_[2].r, rt_[3].r, qb_.r], writes=[qb_.r], cost=0.12)
                        transpose_to(qb_, 4, stg, lambda stg=stg, j=j: stg.t[:, :, j * 128:(j + 1) * 128], pT2[0])
                    b = proj(2, hnT_)
                    tv_ = tv[par]
                    S.act(lambda e, b=b, tv_=tv_: e.activation(out=tv_.t[:], in_=b.t[:], func=AF.Copy), reads=[b.r], writes=[tv_.r])
                    if is_s:
                        S.dma(f"s_v{par}", lambda e, tv_=tv_, tl=tl: e.dma_start(out=dv_s[tl:tl + 128, :], in_=tv_.t[:]), reads=[tv_.r])
                    else:
                        S.dma(f"s_v{par}", lambda e, tv_=tv_, tl=tl: e.dma_start(out=dv_l[tl // TPS][tl % TPS:tl % TPS + 128, :], in_=tv_.t[:]), reads=[tv_.r])
                    for which in range(2):
                        b = proj(3 + which, hnT_)
                        rqf_, rta_, rtb_, rqb_ = rqf[which], rta[which], rtb[which], rqb[which][par]
                        b3 = b.t[:].rearrange("p (g d) -> p g d", d=64)
                        rq3 = rqf_.t[:].rearrange("p (g d) -> p g d", d=64)
                        S.dve(lambda e, b3=b3, cosr=cosr, rta_=rta_: e.tensor_tensor(out=rta_.t[:], in0=b3[:, :, 0:32], in1=cosr, op=ALU.mult), reads=[b.r, pb.r], writes=[rta_.r])
                        S.dve(lambda e, b3=b3, sinr=sinr, rtb_=rtb_: e.tensor_tensor(out=rtb_.t[:], in0=b3[:, :, 32:64], in1=sinr, op=ALU.mult), reads=[b.r, pb.r], writes=[rtb_.r])
                        S.dve(lambda e, rq3=rq3, rta_=rta_, rtb_=rtb_: e.tensor_tensor(out=rq3[:, :, 0:32], in0=rta_.t[:], in1=rtb_.t[:], op=ALU.subtract), reads=[rta_.r, rtb_.r], writes=[rqf_.r])
                        S.dve(lambda e, b3=b3, sinr=sinr, rta_=rta_: e.tensor_tensor(out=rta_.t[:], in0=b3[:, :, 0:32], in1=sinr, op=ALU.mult), reads=[b.r, pb.r], writes=[rta_.r])
                        S.dve(lambda e, b3=b3, cosr=cosr, rtb_=rtb_: e.tensor_tensor(out=rtb_.t[:], in0=b3[:, :, 32:64], in1=cosr, op=ALU.mult), reads=[b.r, pb.r], writes=[rtb_.r])
                        S.dve(lambda e, rq3=rq3, rta_=rta_, rtb_=rtb_: e.tensor_tensor(out=rq3[:, :, 32:64], in0=rta_.t[:], in1=rtb_.t[:], op=ALU.add), reads=[rta_.r, rtb_.r, rqf_.r], writes=[rqf_.r])
                        S.act(lambda e, rqb_=rqb_, rqf_=rqf_: e.activation(out=rqb_.t[:], in_=rqf_.t[:], func=AF.Copy), reads=[rqf_.r], writes=[rqb_.r], cost=0.6)
                        dec = qdec if which == 0 else kdec
                        rq4 = rqf_.t[:].rearrange("p (g d) -> p g d", d=64).unsqueeze(2).to_broadcast([128, 8, 2, 64])
                        dc4 = dec.t[:].unsqueeze(3).to_broadcast([128, 8, 2, 64])
                        if which == 0:
                            qd_ = qd[par]
                            S.pool(lambda e, rq4=rq4, dc4=dc4, qd_=qd_: e.tensor_tensor(out=qd_.t[:].rearrange("p (g a d) -> p g a d", g=8, a=2), in0=rq4, in1=dc4, op=ALU.mult),
                                  reads=[rqf_.r, dec.r], writes=[qd_.r], cost=2.5)
                            transpose_to(rqb_, 4, stg_rqT[sl], lambda sl=sl, j=j: stg_rqT[sl].t[:, :, j * 128:(j + 1) * 128], pT2[1])
                            transpose_to(qd_, 8, stg_qdT[sl], lambda sl=sl, j=j: stg_qdT[sl].t[:, :, j * 128:(j + 1) * 128], pT2[0])
                        else:
                            tkd_ = tkd[par]
                            S.pool(lambda e, rq4=rq4, dc4=dc4, tkd_=tkd_: e.tensor_tensor(out=tkd_.t[:].rearrange("p (g a d) -> p g a d", g=8, a=2), in0=rq4, in1=dc4, op=ALU.mult),
                                  reads=[rqf_.r, dec.r], writes=[tkd_.r], cost=2.5)
                            S.dma(f"s_kd{par}", lambda e, tkd_=tkd_, tok0=tok0: e.dma_start(out=rkd[tok0:tok0 + 128, :], in_=tkd_.t[:]), reads=[tkd_.r])
                            transpose_to(rqb_, 4, stg_rkT[sl], lambda sl=sl, j=j: stg_rkT[sl].t[:, :, j * 128:(j + 1) * 128], pT2[1])
                    b = proj(5, hnT_)
                    trv_ = trv[par]
                    S.act(lambda e, b=b, trv_=trv_: e.activation(out=trv_.t[:], in_=b.t[:], func=AF.Copy), reads=[b.r], writes=[trv_.r])
                    S.dma(f"s_rv{par}", lambda e, trv_=trv_, tok0=tok0: e.dma_start(out=rv[tok0:tok0 + 128, :], in_=trv_.t[:]), reads=[trv_.r])
                    b = proj(6, hnT_)
                    esg_, tgs_ = esg[par], tgs[par]
                    S.act(lambda e, b=b, esg_=esg_: e.activation(out=esg_.t[:], in_=b.t[:], func=AF.Exp, scale=-1.0), reads=[b.r], writes=[esg_.r])
                    S.act(lambda e, esg_=esg_: e.activation(out=esg_.t[:], in_=esg_.t[:], func=AF.Ln, bias=ones_f.t[:, 0:1]), reads=[esg_.r, ones_f.r], writes=[esg_.r], cost=0.6)
                    S.act(lambda e, esg_=esg_: e.activation(out=esg_.t[:], in_=esg_.t[:], func=AF.Exp, scale=-1.0), reads=[esg_.r], writes=[esg_.r], cost=0.6)
                    S.dve(lambda e, b=b, esg_=esg_, tgs_=tgs_: e.tensor_tensor(out=tgs_.t[:], in0=b.t[:], in1=esg_.t[:], op=ALU.mult), reads=[b.r, esg_.r], writes=[tgs_.r])
                    S.dma(f"s_gs{par}", lambda e, tgs_=tgs_, tok0=tok0: e.dma_start(out=rgs[tok0:tok0 + 128, :], in_=tgs_.t[:]), reads=[tgs_.r])
                    if j == 3:
                        c0 = sti * 512
                        cl = c0 if is_s else c0 - TS
                        S.dma(f"s_qT{sl}", lambda e, sl=sl, c0=c0: e.dma_start(out=dqT.rearrange("(h f) t -> f h t", f=128)[:, :, c0:c0 + 512], in_=stg_qT[sl].t[:]), reads=[stg_qT[sl].r])
                        if is_s:
                            S.dma(f"s_kT{sl}", lambda e, sl=sl, cl=cl: e.dma_start(out=dkT_s.rearrange("(h f) t -> f h t", f=128)[:, :, cl:cl + 512], in_=stg_kT[sl].t[:]), reads=[stg_kT[sl].r])
                        else:
                            for a in range(NSPL):
                                S.dma(f"s_kT{sl}", lambda e, sl=sl, cl=cl, a=a: e.dma_start(out=dkT_l[a].rearrange("(h f) t -> f h t", f=128)[:, :, cl:cl + 512], in_=stg_kT[sl].t[:, a * HPS:(a + 1) * HPS, :]), reads=[stg_kT[sl].r])
                        S.dma(f"s_rqT{sl}", lambda e, sl=sl, c0=c0: e.dma_start(out=rqT.rearrange("(h f) t -> f h t", f=128)[:, :, c0:c0 + 512], in_=stg_rqT[sl].t[:]), reads=[stg_rqT[sl].r])
                        S.dma(f"s_rkT{sl}", lambda e, sl=sl, c0=c0: e.dma_start(out=rkT.rearrange("(h f) t -> f h t", f=128)[:, :, c0:c0 + 512], in_=stg_rkT[sl].t[:]), reads=[stg_rkT[sl].r])
                        S.dma(f"s_qdT{sl}", lambda e, sl=sl, c0=c0: e.dma_start(out=rqdT.rearrange("(h f) t -> f h t", f=128)[:, :, c0:c0 + 512], in_=stg_qdT[sl].t[:]), reads=[stg_qdT[sl].r])
                S.barrier()

            RG = [[0, 1, 2, 3], [4, 5, 6, 7]]
            Rkg = Res("dkT_g")
            Rvg = Res("dv_g")
            for a in range(NSPL):
                S.dma("cc_k", lambda e, a=a: e.collective_compute("AllGather", ALU.bypass, replica_groups=RG, ins=[dkT_l[a]], outs=[dkT_g[a]]), writes=[Rkg], eng="pool", inc=1, cost=60.0)
                S.dma("cc_v", lambda e, a=a: e.collective_compute("AllGather", ALU.bypass, replica_groups=RG, ins=[dv_l[a]], outs=[dv_g[a]]), writes=[Rvg], eng="pool", inc=1, cost=60.0)

            with ExitStack() as st:
                SKMAX = max(TS, SKP)
                kTh = [sb(st, f"kTh{i}", [128, SKMAX], BF16) for i in range(2)]
                vh = [sb(st, f"vh{i}", [128, SKMAX // 128, 128], BF16) for i in range(2)]
                qA = [sb(st, f"qA{i}", [128, max(TS, TP)], BF16) for i in range(2)]
                qB = [sb(st, f"qB{i}", [128, max(TS, TP)], BF16) for i in range(2)]
                for i in range(2):
                    S.pool(lambda e, i=i: e.memset(qA[i].t[64:128, :], 0.0), writes=[qA[i].r])
                    S.pool(lambda e, i=i: e.memset(qB[i].t[0:64, :], 0.0), writes=[qB[i].r])
                NPX = 6
                pexp2 = [sb(st, f"pexp{i}", [128, 2, 512], BF16) for i in range(NPX)]
                s01 = [[sb(st, f"s01_{c}{i}", [128, 512], BF16) for i in range(2)] for c in range(2)]
                s23 = [[sb(st, f"s23_{c}{i}", [128, 512], BF16) for i in range(2)] for c in range(2)]
                s4 = [[sb(st, f"s4_{c}{i}", [128, 512], BF16) for i in range(2)] for c in range(2)]
                s8 = [[sb(st, f"s8_{c}{i}", [128, 512], BF16) for i in range(2)] for c in range(2)]
                gcount = 0
                r0 = sb(st, "r0", [128, 512], F32)
                r1 = sb(st, "r1", [128, 512], F32)
                a0 = sb(st, "a0", [128, 512], F32)
                a1 = sb(st, "a1", [128, 512], F32)
                osq = sb(st, "osq", [128, 512], F32)
                rsn = sb(st, "rsn", [128, 512], F32)
                aout = [sb(st, f"aout{i}", [128, 512], BF16) for i in range(2)]
                sbk2 = [psb(st, f"sbk{i}", [128, 2, 512], F32) for i in range(2)]
                O = [psb(st, f"Oacc{i}", [128, 512], F32) for i in range(2)]
                L = [psb(st, f"Lacc{i}", [128, 512], F32) for i in range(2)]
                heads = [(job, h) for job in range(2) for h in range(4)]

                def load_head(hi):
                    job, h = heads[hi]
                    hb = hi % 2
                    kb_, vb_ = kTh[hb], vh[hb]
                    Tq = TS if job == 0 else TP
                    qoff = 0 if job == 0 else TS
                    if job == 0:
                        S.dma(f"b_k{hb}", lambda e, kb_=kb_, h=h: e.dma_start(out=kb_.t[:, 0:TS], in_=dkT_s[h * 128:(h + 1) * 128, :]), writes=[kb_.r])
                        S.dma(f"b_v{hb}", lambda e, vb_=vb_, h=h: e.dma_start(out=vb_.t[:, 0:TS // 128, :], in_=dv_s[:, h * 128:(h + 1) * 128].rearrange("(k p) e -> p k e", p=128)), writes=[vb_.r])
                    else:
                        ha = h // HPS
                        hl = h % HPS
                        S.dma(f"b_k{hb}", lambda e, kb_=kb_, ha=ha, hl=hl: e.dma_start(out=kb_.t[:, 0:SKP].rearrange("p (r t) -> p r t", r=NR), in_=dkT_g[ha].rearrange("(r f) t -> f r t", f=HPS * 128)[hl * 128:(hl + 1) * 128, :, :]), reads=[Rkg], writes=[kb_.r])
                        for a in range(NSPL):
                            for r_ in range(NR):
                                S.dma(f"b_v{hb}", lambda e, vb_=vb_, h=h, a=a, r_=r_: e.dma_start(
                                    out=vb_.t[:, (r_ * TP + a * TPS) // 128:(r_ * TP + (a + 1) * TPS) // 128, :],
                                    in_=dv_g[a][r_ * TPS:(r_ + 1) * TPS, h * 128:(h + 1) * 128].rearrange("(k p) e -> p k e", p=128)), reads=[Rvg], writes=[vb_.r])
                    S.dma(f"b_q{hb}", lambda e, hb=hb, h=h, qoff=qoff, Tq=Tq: e.dma_start(out=qA[hb].t[0:64, 0:Tq], in_=dqT[h * 128:h * 128 + 64, qoff:qoff + Tq]), writes=[qA[hb].r])
                    S.dma(f"b_r{hb}", lambda e, hb=hb, h=h, qoff=qoff, Tq=Tq: e.dma_start(out=qB[hb].t[64:128, 0:Tq], in_=dqT[h * 128 + 64:h * 128 + 128, qoff:qoff + Tq]), writes=[qB[hb].r])

                units = []
                for hi, (job, h) in enumerate(heads):
                    Tq = TS if job == 0 else TP
                    Sk = TS if job == 0 else SKP
                    for qc in range(Tq // 512):
                        for kb in range(Sk // 128):
                            units.append((hi, qc, kb, Sk // 128))

                def emit_qk(u):
                    hi, qc, kb, nkb = units[u]
                    hb = hi % 2
                    kb_ = kTh[hb]
                    sbuf_ = sbk2[u % 2]
                    for c in range(2):
                        qb_ = (qA if c == 0 else qB)[hb]
                        S.pe(lambda e, sbuf_=sbuf_, c=c, kb=kb, qc=qc, kb_=kb_, qb_=qb_: e.matmul(sbuf_.t[:, c, :], lhsT=kb_.t[:, kb * 128:(kb + 1) * 128],
                                                                                             rhs=qb_.t[:, qc * 512:(qc + 1) * 512], start=True, stop=True),
                             reads=[kb_.r, qb_.r], writes=[sbuf_.r])

                ocount = 0
                load_head(0)
                emit_qk(0)
                emit_qk(1)
                for u in range(len(units)):
                    hi, qc, kb, nkb = units[u]
                    job, h = heads[hi]
                    hb = hi % 2
                    vb_ = vh[hb]
                    qoff = 0 if job == 0 else TS
                    if qc == 0 and kb == 0 and hi + 1 < len(heads):
                        load_head(hi + 1)
                    pe2 = pexp2[u % NPX]
                    s2_ = sbk2[u % 2]
                    S.act(lambda e, pe2=pe2, s2_=s2_: e.activation(out=pe2.t[:], in_=s2_.t[:], func=AF.Exp), reads=[s2_.r], writes=[pe2.r], cost=1.05)
                    if u + 2 < len(units):
                        if units[u + 2][0] != hi and units[u + 2][1] == 0 and units[u + 2][2] == 0 and units[u + 2][0] + 1 < len(heads):
                            pass
                        emit_qk(u + 2)
                    gp = gcount % 2
                    for c in range(2):
                        pe_ = pexp2[u % NPX]
                        S.pe(lambda e, pe_=pe_, c=c, kb=kb, vb_=vb_, nkb=nkb: e.matmul(O[c].t[:], lhsT=vb_.t[:, kb, :], rhs=pe_.t[:, c, :], start=(kb == 0), stop=(kb == nkb - 1)),
                             reads=[vb_.r, pe_.r], writes=[O[c].r])
                        if kb % 2 == 1:
                            pp_ = pexp2[(u - 1) % NPX]
                            dst = (s01 if kb % 4 == 1 else s23)[c][gp]
                            S.dve(lambda e, dst=dst, pp_=pp_, pe_=pe_, c=c: e.tensor_tensor(out=dst.t[:], in0=pp_.t[:, c, :], in1=pe_.t[:, c, :], op=ALU.add), reads=[pp_.r, pe_.r], writes=[dst.r], cost=0.3)
                        if kb % 4 == 3:
                            a_, b_, d_ = s01[c][gp], s23[c][gp], s4[c][gp]
                            S.dve(lambda e, a_=a_, b_=b_, d_=d_: e.tensor_tensor(out=d_.t[:], in0=a_.t[:], in1=b_.t[:], op=ALU.add), reads=[a_.r, b_.r], writes=[d_.r], cost=0.3)
                        if kb % 8 == 7:
                            a_, b_, d_ = s4[c][gp ^ 1], s4[c][gp], s8[c][(gcount // 2) % 2]
                            S.dve(lambda e, a_=a_, b_=b_, d_=d_: e.tensor_tensor(out=d_.t[:], in0=a_.t[:], in1=b_.t[:], op=ALU.add), reads=[a_.r, b_.r], writes=[d_.r], cost=0.3)
                            S.pe(lambda e, d_=d_, c=c, kb=kb, nkb=nkb: e.matmul(L[c].t[:], lhsT=ones_b.t[:], rhs=d_.t[:], start=(kb == 7), stop=(kb == nkb - 1)),
                                 reads=[ones_b.r, d_.r], writes=[L[c].r])
                    if kb % 4 == 3:
                        gcount += 1
                    if kb != nkb - 1:
                        continue
                    S.act(lambda e: e.activation(out=a0.t[:], in_=O[0].t[:], func=AF.Copy), reads=[O[0].r], writes=[a0.r])
                    S.dve(lambda e: e.tensor_copy(out=a1.t[:], in_=O[1].t[:]), reads=[O[1].r], writes=[a1.r])
                    S.dve(lambda e: e.reciprocal(out=r0.t[:], in_=L[0].t[:]), reads=[L[0].r], writes=[r0.r])
                    S.dve(lambda e: e.reciprocal(out=r1.t[:], in_=L[1].t[:]), reads=[L[1].r], writes=[r1.r])
                    S.dve(lambda e: e.tensor_tensor(out=a0.t[:], in0=a0.t[:], in1=r0.t[:], op=ALU.mult), reads=[a0.r, r0.r], writes=[a0.r])
                    S.dve(lambda e: e.tensor_tensor(out=a1.t[:], in0=a1.t[:], in1=r1.t[:], op=ALU.mult), reads=[a1.r, r1.r], writes=[a1.r])
                    S.dve(lambda e, l=l: e.scalar_tensor_tensor(out=a0.t[:], in0=a1.t[:], scalar=neglam.t[:, l:l + 1], in1=a0.t[:], op0=ALU.mult, op1=ALU.add),
                          reads=[a1.r, a0.r, neglam.r], writes=[a0.r])
                    S.act(lambda e: e.activation(out=osq.t[:], in_=a0.t[:], func=AF.Square), reads=[a0.r], writes=[osq.r])
                    sB = L[0]
                    S.pe(lambda e, sB=sB: e.matmul(sB.t[:], lhsT=ones_f.t[:], rhs=osq.t[:], start=True, stop=True), reads=[ones_f.r, osq.r], writes=[sB.r])
                    rsqrt_act(rsn.t[:], sB.t[:], 1.0 / 128, sB, rsn)
                    ao = aout[ocount % 2]
                    S.dve(lambda e, ao=ao: e.scalar_tensor_tensor(out=ao.t[:], in0=a0.t[:], scalar=subw.t[:, 0:1], in1=rsn.t[:], op0=ALU.mult, op1=ALU.mult),
                          reads=[a0.r, subw.r, rsn.r], writes=[ao.r])
                    tcol = qoff + qc * 512
                    S.dma(f"b_o{ocount % 2}", lambda e, ao=ao, h=h, tcol=tcol: e.dma_start(out=aT[h * 128:(h + 1) * 128, tcol:tcol + 512], in_=ao.t[:]), reads=[ao.r], eng="act")
                    ocount += 1
                S.barrier()

            with ExitStack() as st:
                SallJ = [sb(st, "Sall0", [128, TS // 128, 512], BF16), sb(st, "Sall1", [128, TP // 128, 512], BF16)]
                sttJ = [sb(st, f"stt{i}", [128, 512], F32) for i in range(2)]
                sttmpJ = [sb(st, f"sttmp{i}", [128, 512], F32) for i in range(2)]
                tg = sb(st, "tg", [128, NR, 512], F32)
                kdl = [sb(st, f"kdl{i}", [128, 4, 1024], BF16) for i in range(4)]
                rvl = [sb(st, f"rvl{i}", [128, 4, 512], BF16) for i in range(4)]
                okT = [sb(st, f"okT{i}", [128, 4, 512], BF16) for i in range(2)]
                oqT = [sb(st, f"oqT{i}", [128, 4, 512], BF16) for i in range(2)]
                oqd = [sb(st, f"oqd{i}", [128, 8, 512], BF16) for i in range(2)]
                orv = [sb(st, f"orv{i}", [128, 4, 512], BF16) for i in range(2)]
                ogs = [sb(st, f"ogs{i}", [128, 4, 512], BF16) for i in range(2)]
                PT = [sb(st, f"PT{i}", [128, 8, 128], BF16) for i in range(2)]
                rsq = sb(st, "rsq", [128, 512], F32)
                rss = sb(st, "rss", [128, 8], F32)
                rn = sb(st, "rn", [128, 512], F32)
                rr = sb(st, "rr", [128, 512], BF16)
                rTst = [sb(st, f"rTst{i}", [128, 4, 512], BF16) for i in range(2)]
                pkvJ = [psb(st, f"pkv{i}", [128, 512], F32) for i in range(2)]
                psc = [psb(st, f"psc{i}", [128, 4, 128], F32) for i in range(2)]
                po = [psb(st, f"pro{i}", [128, 512], F32) for i in range(2)]
                ptr = psb(st, "ptr", [128, 8, 128], BF16)
                Rtg = Res("st_g")
                ldn = [0]

                def sweep(job, use_init):
                    stt, sttmp, Sall = sttJ[job], sttmpJ[job], SallJ[job]
                    Tq = TS if job == 0 else TP
                    qoff = 0 if job == 0 else TS
                    n = Tq // 128
                    nsc = n // 4
                    if use_init:
                        S.dma("r_tg", lambda e: e.dma_start(out=tg.t[:], in_=st_g.rearrange("(r p) c -> p r c", p=128)), reads=[Rtg], writes=[tg.r])
                        for r_ in range(NR):
                            cb = coef.t[:, r_, :].unsqueeze(2).to_broadcast([128, 8, 64])
                            tg3 = tg.t[:, r_, :].rearrange("p (h e) -> p h e", e=64)
                            if r_ == 0:
                                S.dve(lambda e, cb=cb, tg3=tg3: e.tensor_tensor(out=stt.t[:].rearrange("p (h e) -> p h e", e=64), in0=tg3, in1=cb, op=ALU.mult), reads=[tg.r, coef.r], writes=[stt.r])
                            else:
                                S.dve(lambda e, cb=cb, tg3=tg3: e.tensor_tensor(out=sttmp.t[:].rearrange("p (h e) -> p h e", e=64), in0=tg3, in1=cb, op=ALU.mult), reads=[tg.r, coef.r], writes=[sttmp.r])
                                S.dve(lambda e: e.tensor_tensor(out=stt.t[:], in0=stt.t[:], in1=sttmp.t[:], op=ALU.add), reads=[stt.r, sttmp.r], writes=[stt.r])
                    else:
                        S.dve(lambda e: e.memset(stt.t[:], 0.0), writes=[stt.r])
                    cur = {}
                    for t in range(n):
                        tf = t
                        tb = n - 1 - t
                        bufs = {}
                        for nm, ti in (("f", tf), ("b", tb)):
                            sc = ti // 4
                            if (nm, sc) not in cur:
                                slot = job * 2 + (0 if nm == "f" else 1)
                                c0 = qoff + sc * 512
                                S.dma(f"r_kd{slot}", lambda e, slot=slot, c0=c0: e.dma_start(out=kdl[slot].t[:], in_=rkd[c0:c0 + 512, :].rearrange("(j p) c -> p j c", p=128)), writes=[kdl[slot].r])
                                S.dma(f"r_rv{slot}", lambda e, slot=slot, c0=c0: e.dma_start(out=rvl[slot].t[:], in_=rv[c0:c0 + 512, :].rearrange("(j p) c -> p j c", p=128)), writes=[rvl[slot].r])
                                cur = {k: v for k, v in cur.items() if k[0] != nm}
                                cur[(nm, sc)] = slot
                            bufs[nm] = (cur[(nm, sc)], ti % 4)
                        S.act(lambda e, tf=tf: e.activation(out=Sall.t[0:64, tf, :], in_=stt.t[0:64, :], func=AF.Copy), reads=[stt.r], writes=[Sall.r])
                        S.act(lambda e, tb=tb: e.activation(out=Sall.t[64:128, tb, :], in_=stt.t[64:128, :], func=AF.Copy), reads=[stt.r], writes=[Sall.r])
                        pk = pkvJ[job]
                        (sf, jf), (sb_, jb) = bufs["f"], bufs["b"]
                        for h in range(8):
                            kf = kdl[sf].t[:, jf, :].rearrange("p (g a d) -> p g a d", g=8, a=2)
                            kbk = kdl[sb_].t[:, jb, :].rearrange("p (g a d) -> p g a d", g=8, a=2)
                            S.pe(lambda e, pk=pk, h=h, kf=kf, sf=sf, jf=jf: e.matmul(pk.t[0:64, h * 64:(h + 1) * 64], lhsT=kf[:, h, 0, :], rhs=rvl[sf].t[:, jf, h * 64:(h + 1) * 64], start=True, stop=True),
                                 reads=[kdl[sf].r, rvl[sf].r], writes=[pk.r])
                            S.pe(lambda e, pk=pk, h=h, kbk=kbk, sb_=sb_, jb=jb: e.matmul(pk.t[64:128, h * 64:(h + 1) * 64], lhsT=kbk[:, h, 1, :], rhs=rvl[sb_].t[:, jb, h * 64:(h + 1) * 64], start=True, stop=True),
                                 reads=[kdl[sb_].r, rvl[sb_].r], writes=[pk.r])
                        S.dve(lambda e: e.tensor_tensor(out=sttmp.t[:].rearrange("p (h e) -> p h e", e=64), in0=stt.t[:].rearrange("p (h e) -> p h e", e=64),
                                                        in1=cdec.t[:].unsqueeze(2).to_broadcast([128, 8, 64]), op=ALU.mult), reads=[stt.r, cdec.r], writes=[sttmp.r])
                        S.dve(lambda e, pk=pk: e.tensor_tensor(out=stt.t[:], in0=pk.t[:], in1=sttmp.t[:], op=ALU.add), reads=[pk.r, sttmp.r], writes=[stt.r])

                def outputs(job):
                    Sall = SallJ[job]
                    Tq = TS if job == 0 else TP
                    qoff = 0 if job == 0 else TS
                    n = Tq // 128
                    for sc in range(n // 4):
                        sl = sc % 2
                        c0 = qoff + sc * 512
                        S.dma(f"o_kT{sl}", lambda e, sl=sl, c0=c0: e.dma_start(out=okT[sl].t[:], in_=rkT.rearrange("(b p) t -> p b t", p=128)[:, :, c0:c0 + 512]), writes=[okT[sl].r])
                        S.dma(f"o_qT{sl}", lambda e, sl=sl, c0=c0: e.dma_start(out=oqT[sl].t[:], in_=rqT.rearrange("(b p) t -> p b t", p=128)[:, :, c0:c0 + 512]), writes=[oqT[sl].r])
                        S.dma(f"o_qd{sl}", lambda e, sl=sl, c0=c0: e.dma_start(out=oqd[sl].t[:], in_=rqdT.rearrange("(b p) t -> p b t", p=128)[:, :, c0:c0 + 512]), writes=[oqd[sl].r])
                        S.dma(f"o_rv{sl}", lambda e, sl=sl, c0=c0: e.dma_start(out=orv[sl].t[:], in_=rv[c0:c0 + 512, :].rearrange("(j p) c -> p j c", p=128)), writes=[orv[sl].r])
                        S.dma(f"o_gs{sl}", lambda e, sl=sl, c0=c0: e.dma_start(out=ogs[sl].t[:], in_=rgs[c0:c0 + 512, :].rearrange("(j p) c -> p j c", p=128)), writes=[ogs[sl].r])
                        for j in range(4):
                            i = sc * 4 + j
                            ptb = PT[j % 2]
                            for h in range(8):
                                pb_ = psc[h % 2]
                                hp = (h % 2) * 64
                                S.pe(lambda e, pb_=pb_, h=h, hp=hp, sl=sl, j=j: e.matmul(pb_.t[:, h // 2, :], lhsT=okT[sl].t[hp:hp + 64, h // 2, j * 128:(j + 1) * 128],
                                                                                   rhs=oqT[sl].t[hp:hp + 64, h // 2, j * 128:(j + 1) * 128], start=True, stop=True),
                                     reads=[okT[sl].r, oqT[sl].r], writes=[pb_.r])
                            pt4 = ptb.t[:].rearrange("p (b a) n -> p b a n", a=2)
                            S.dve(lambda e, pt4=pt4: e.tensor_tensor(out=pt4[:, :, 0, :], in0=psc[0].t[:], in1=DTe.t[:], op=ALU.mult), reads=[psc[0].r, DTe.r], writes=[ptb.r])
                            S.dve(lambda e, pt4=pt4: e.tensor_tensor(out=pt4[:, :, 1, :], in0=psc[1].t[:], in1=DTo.t[:], op=ALU.mult), reads=[psc[1].r, DTo.r, ptb.r], writes=[ptb.r])
                            pob = po[j % 2]
                            for h in range(8):
                                S.pe(lambda e, pob=pob, h=h, ptb=ptb, sl=sl, j=j: e.matmul(pob.t[:, h * 64:(h + 1) * 64], lhsT=ptb.t[:, h, :], rhs=orv[sl].t[:, j, h * 64:(h + 1) * 64], start=True, stop=False),
                                     reads=[ptb.r, orv[sl].r], writes=[pob.r])
                                S.pe(lambda e, pob=pob, h=h, sl=sl, j=j, i=i: e.matmul(pob.t[:, h * 64:(h + 1) * 64], lhsT=oqd[sl].t[:, h, j * 128:(j + 1) * 128], rhs=Sall.t[:, i, h * 64:(h + 1) * 64], start=False, stop=True),
                                     reads=[oqd[sl].r, Sall.r], writes=[pob.r])
                            S.act(lambda e, pob=pob: e.activation(out=rsq.t[:], in_=pob.t[:], func=AF.Square), reads=[pob.r], writes=[rsq.r])
                            S.dve(lambda e: e.tensor_reduce(out=rss.t[:], in_=rsq.t[:].rearrange("p (g d) -> p g d", d=64), axis=AX.X, op=ALU.add), reads=[rsq.r], writes=[rss.r])
                            rsqrt_act(rss.t[:], rss.t[:], 1.0 / 64, rss, rss)
                            S.dve(lambda e, pob=pob: e.tensor_tensor(out=rn.t[:].rearrange("p (g d) -> p g d", d=64), in0=pob.t[:].rearrange("p (g d) -> p g d", d=64),
                                                                    in1=rss.t[:].unsqueeze(2).to_broadcast([128, 8, 64]), op=ALU.mult), reads=[pob.r, rss.r], writes=[rn.r])
                            S.dve(lambda e: e.tensor_tensor(out=rn.t[:].rearrange("p (g d) -> p g d", d=64), in0=rn.t[:].rearrange("p (g d) -> p g d", d=64),
                                                            in1=gnw.t[:].unsqueeze(1).to_broadcast([128, 8, 64]), op=ALU.mult), reads=[rn.r, gnw.r], writes=[rn.r])
                            S.dve(lambda e, sl=sl, j=j: e.tensor_tensor(out=rr.t[:], in0=rn.t[:], in1=ogs[sl].t[:, j, :], op=ALU.mult), reads=[rn.r, ogs[sl].r], writes=[rr.r])
                            for c in range(4):
                                S.pe(lambda e, c=c: e.transpose(out=ptr.t[:, c, :], in_=rr.t[:, c * 128:(c + 1) * 128], identity=ident.t[:]), reads=[rr.r, ident.r], writes=[ptr.r])
                            S.act(lambda e, sl=sl, j=j: e.activation(out=rTst[sl].t[:, :, j * 128:(j + 1) * 128], in_=ptr.t[:, 0:4, :], func=AF.Copy), reads=[ptr.r], writes=[rTst[sl].r])
                        S.dma(f"o_rT{sl}", lambda e, sl=sl, c0=c0: e.dma_start(out=rT.rearrange("(b p) t -> p b t", p=128)[:, :, c0:c0 + 512], in_=rTst[sl].t[:]), reads=[rTst[sl].r])

                sweep(1, False)
                Rstl = Res("st_l")
                S.dma("r_stl", lambda e: e.dma_start(out=st_l, in_=sttJ[1].t[:]), reads=[sttJ[1].r], writes=[Rstl])
                S.dma("cc_s", lambda e: e.collective_compute("AllGather", ALU.bypass, replica_groups=RG, ins=[st_l], outs=[st_g]), reads=[Rstl], writes=[Rtg], eng="pool", inc=1, cost=250.0)
                sweep(0, False)
                outputs(0)
                sweep(1, True)
                outputs(1)
                S.barrier()

            with ExitStack() as st:
                w_out = load_w(st, "w_out", w_out_in[l], D, D, "w_out")
                w1 = load_w(st, "w1", w1_in[l], D, DFF, "w1")
                arT = [sb(st, f"arT{i}", [128, 8, 512], BF16) for i in range(2)]
                xs_ = [sb(st, f"xs{i}", [128, 4, D], F32) for i in range(2)]
                junk = sb(st, "junk2", [128, D], BF16)
                ss2 = sb(st, "ss2", [128, 1], F32)
                h2l = [sb(st, f"h2_{i}", [128, D], BF16) for i in range(2)]
                h2T = [sb(st, f"h2T{i}", [128, 8, 512], BF16) for i in range(1)] * 2
                rl = [sb(st, f"rl{i}", [128, 512], F32) for i in range(2)]
                uTs = [sb(st, f"uTs{i}", [128, 32, 512], BF16) for i in range(1)] * 2
                pw = [psb(st, f"pw{i}", [128, 512], F32) for i in range(2)]
                ph = psb(st, "ph", [128, 8, 128], BF16)
                pu = [psb(st, f"pu{i}", [128, 512], F32) for i in range(4)]
                for s in range(NST):
                    sl = s % 2
                    c0 = s * 512
                    S.dma(f"c_a{sl}", lambda e, sl=sl, c0=c0: e.dma_start(out=arT[sl].t[:, 0:4, :], in_=aT.rearrange("(b p) t -> p b t", p=128)[:, :, c0:c0 + 512]), writes=[arT[sl].r])
                    S.dma(f"c_r{sl}", lambda e, sl=sl, c0=c0: e.dma_start(out=arT[sl].t[:, 4:8, :], in_=rT.rearrange("(b p) t -> p b t", p=128)[:, :, c0:c0 + 512]), writes=[arT[sl].r])
                    S.dma(f"c_x{sl}", lambda e, sl=sl, c0=c0: e.dma_start(out=xs_[sl].t[:], in_=x_src[c0:c0 + 512, :].rearrange("(j p) c -> p j c", p=128)), writes=[xs_[sl].r])
                    for j in range(4):
                        for cb in range(2):
                            pb_ = pw[cb]
                            for k in range(8):
                                S.pe(lambda e, pb_=pb_, k=k, cb=cb, sl=sl, j=j: e.matmul(pb_.t[:], lhsT=arT[sl].t[:, k, j * 128:(j + 1) * 128], rhs=w_out.t[:, k, cb * 512:(cb + 1) * 512], start=(k == 0), stop=(k == 7)),
                                     reads=[arT[sl].r, w_out.r(k, cb * 512)], writes=[pb_.r])
                            S.dve(lambda e, pb_=pb_, cb=cb, sl=sl, j=j: e.tensor_tensor(out=xs_[sl].t[:, j, cb * 512:(cb + 1) * 512], in0=pb_.t[:], in1=xs_[sl].t[:, j, cb * 512:(cb + 1) * 512], op=ALU.add),
                                  reads=[pb_.r, xs_[sl].r], writes=[xs_[sl].r])
                        S.act(lambda e, sl=sl, j=j: e.activation(out=junk.t[:], in_=xs_[sl].t[:, j, :], func=AF.Square, accum_out=ss2.t[:]), reads=[xs_[sl].r], writes=[junk.r, ss2.r])
                        rsqrt_act(ss2.t[:], ss2.t[:], 1.0 / D, ss2, ss2)
                        h2 = h2l[j % 2]
                        S.dve(lambda e, sl=sl, j=j, h2=h2: e.scalar_tensor_tensor(out=h2.t[:], in0=xs_[sl].t[:, j, :], scalar=ss2.t[:, 0:1], in1=ln2b.t[:], op0=ALU.mult, op1=ALU.mult),
                              reads=[xs_[sl].r, ss2.r, ln2b.r], writes=[h2.r])
                        for c in range(8):
                            S.pe(lambda e, c=c, h2=h2: e.transpose(out=ph.t[:, c, :], in_=h2.t[:, c * 128:(c + 1) * 128], identity=ident.t[:]), reads=[h2.r, ident.r], writes=[ph.r])
                        S.act(lambda e, sl=sl, j=j: e.activation(out=h2T[sl].t[:, :, j * 128:(j + 1) * 128], in_=ph.t[:], func=AF.Copy), reads=[ph.r], writes=[h2T[sl].r])
                    S.dma(f"c_xo{sl}", lambda e, sl=sl, c0=c0: e.dma_start(out=xres[c0:c0 + 512, :].rearrange("(j p) c -> p j c", p=128), in_=xs_[sl].t[:]), reads=[xs_[sl].r], eng="act")
                    for fc in range(32):
                        pb_ = pu[fc % 4]
                        for k in range(8):
                            S.pe(lambda e, pb_=pb_, k=k, fc=fc, sl=sl: e.matmul(pb_.t[:], lhsT=w1.t[:, k, fc * 128:(fc + 1) * 128], rhs=h2T[sl].t[:, k, :], start=(k == 0), stop=(k == 7)),
                                 reads=[w1.r(k, fc * 128), h2T[sl].r], writes=[pb_.r])
                        rb = rl[fc % 2]
                        S.act(lambda e, pb_=pb_, rb=rb: e.activation(out=rb.t[:], in_=pb_.t[:], func=AF.Relu), reads=[pb_.r], writes=[rb.r])
                        S.dve(lambda e, rb=rb, fc=fc, sl=sl: e.tensor_tensor(out=uTs[sl].t[:, fc, :], in0=rb.t[:], in1=rb.t[:], op=ALU.mult), reads=[rb.r], writes=[uTs[sl].r])
                    S.dma(f"c_u{sl}", lambda e, sl=sl, c0=c0: e.dma_start(out=uT.rearrange("(c p) t -> p c t", p=128)[:, :, c0:c0 + 512], in_=uTs[sl].t[:]), reads=[uTs[sl].r], eng="act")
                S.barrier()

            with ExitStack() as st:
                w2 = load_w(st, "w2", w2_in[l], DFF, D, "w2", gn=1024, korder=True)
                wg = load_w(st, "wg", wg_in[l], D, D, "wg")
                wp = load_w(st, "wp", wp_in[l], PLE, D, "wp")
                uTl = [sb(st, f"uTl{i}", [128, 32, 512], BF16) for i in range(2)]
                xs_ = [sb(st, f"xc{i}", [128, 4, D], F32) for i in range(1)] * 2
                pl_ = [sb(st, f"pl{i}", [128, 4, PLE], F32) for i in range(1)] * 2
                x2bl = [sb(st, f"x2b{i}", [128, D], BF16) for i in range(2)]
                x2Tl = [sb(st, f"x2T{i}", [128, 8, 128], BF16) for i in range(2)]
                pbfl = [sb(st, f"pbf{i}", [128, PLE], BF16) for i in range(2)]
                ppTl = [sb(st, f"ppT{i}", [128, 2, 128], BF16) for i in range(2)]
                sg = [sb(st, f"sg{i}", [128, 512], F32) for i in range(2)]
                pm = [psb(st, f"pm{i}", [128, 512], F32) for i in range(2)]
                pg = [psb(st, f"pg{i}", [128, 512], F32) for i in range(2)]
                pq = [psb(st, f"pq{i}", [128, 512], F32) for i in range(2)]
                px = psb(st, "px", [128, 8, 128], BF16)
                pp2 = psb(st, "pp2", [128, 8, 128], BF16)
                for s in range(NST):
                    sl = s % 2
                    c0 = s * 512
                    S.dma(f"d_u{sl}", lambda e, sl=sl, c0=c0: e.dma_start(out=uTl[sl].t[:], in_=uT.rearrange("(c p) t -> p c t", p=128)[:, :, c0:c0 + 512]), writes=[uTl[sl].r])
                    S.dma(f"d_x{sl}", lambda e, sl=sl, c0=c0: e.dma_start(out=xs_[sl].t[:], in_=xres[c0:c0 + 512, :].rearrange("(j p) c -> p j c", p=128)), writes=[xs_[sl].r])
                    S.dma(f"d_p{sl}", lambda e, sl=sl, c0=c0, l=l: e.dma_start(out=pl_[sl].t[:], in_=p_in[l, c0:c0 + 512, :].rearrange("(j p) c -> p j c", p=128)), writes=[pl_[sl].r])
                    for j in range(4):
                        for cb in range(2):
                            pb_ = pm[cb]
                            for fc in range(32):
                                S.pe(lambda e, pb_=pb_, fc=fc, cb=cb, sl=sl, j=j: e.matmul(pb_.t[:], lhsT=uTl[sl].t[:, fc, j * 128:(j + 1) * 128], rhs=w2.t[:, fc, cb * 512:(cb + 1) * 512], start=(fc == 0), stop=(fc == 31)),
                                     reads=[uTl[sl].r, w2.r(fc, cb * 512)], writes=[pb_.r])
                            S.dve(lambda e, pb_=pb_, cb=cb, sl=sl, j=j: e.tensor_tensor(out=xs_[sl].t[:, j, cb * 512:(cb + 1) * 512], in0=pb_.t[:], in1=xs_[sl].t[:, j, cb * 512:(cb + 1) * 512], op=ALU.add),
                                  reads=[pb_.r, xs_[sl].r], writes=[xs_[sl].r])
                        x2b, x2T, pbf, ppT = x2bl[j % 2], x2Tl[j % 2], pbfl[j % 2], ppTl[j % 2]
                        S.act(lambda e, sl=sl, j=j, x2b=x2b: e.activation(out=x2b.t[:], in_=xs_[sl].t[:, j, :], func=AF.Copy), reads=[xs_[sl].r], writes=[x2b.r], cost=1.1)
                        for c in range(8):
                            S.pe(lambda e, c=c, x2b=x2b: e.transpose(out=px.t[:, c, :], in_=x2b.t[:, c * 128:(c + 1) * 128], identity=ident.t[:]), reads=[x2b.r, ident.r], writes=[px.r])
                        S.act(lambda e, x2T=x2T: e.activation(out=x2T.t[:], in_=px.t[:], func=AF.Copy), reads=[px.r], writes=[x2T.r], cost=1.1)
                        S.pool(lambda e, sl=sl, j=j, pbf=pbf: e.tensor_copy(out=pbf.t[:], in_=pl_[sl].t[:, j, :]), reads=[pl_[sl].r], writes=[pbf.r])
                        for c in range(2):
                            S.pe(lambda e, c=c, pbf=pbf: e.transpose(out=pp2.t[:, c, :], in_=pbf.t[:, c * 128:(c + 1) * 128], identity=ident.t[:]), reads=[pbf.r, ident.r], writes=[pp2.r])
                        S.act(lambda e, ppT=ppT: e.activation(out=ppT.t[:], in_=pp2.t[:, 0:2, :], func=AF.Copy), reads=[pp2.r], writes=[ppT.r])
                        for cb in range(2):
                            g_ = pg[cb]
                            q_ = pq[cb]
                            for k in range(8):
                                S.pe(lambda e, g_=g_, k=k, cb=cb: e.matmul(g_.t[:], lhsT=x2T.t[:, k, :], rhs=wg.t[:, k, cb * 512:(cb + 1) * 512], start=(k == 0), stop=(k == 7)),
                                     reads=[x2T.r, wg.r(k, cb * 512)], writes=[g_.r])
                            for k in range(2):
                                S.pe(lambda e, q_=q_, k=k, cb=cb: e.matmul(q_.t[:], lhsT=ppT.t[:, k, :], rhs=wp.t[:, k, cb * 512:(cb + 1) * 512], start=(k == 0), stop=(k == 1)),
                                     reads=[ppT.r, wp.r(k, cb * 512)], writes=[q_.r])
                            sgb = sg[cb]
                            S.act(lambda e, g_=g_, sgb=sgb: e.activation(out=sgb.t[:], in_=g_.t[:], func=AF.Exp, scale=-1.0), reads=[g_.r], writes=[sgb.r])
                            S.dve(lambda e, sgb=sgb: e.tensor_scalar(out=sgb.t[:], in0=sgb.t[:], scalar1=1.0, scalar2=None, op0=ALU.add), reads=[sgb.r], writes=[sgb.r])
                            S.dve(lambda e, sgb=sgb: e.reciprocal(out=sgb.t[:], in_=sgb.t[:]), reads=[sgb.r], writes=[sgb.r])
                            S.dve(lambda e, sgb=sgb, q_=q_: e.tensor_tensor(out=sgb.t[:], in0=q_.t[:], in1=sgb.t[:], op=ALU.mult), reads=[q_.r, sgb.r], writes=[sgb.r])
                            S.dve(lambda e, sgb=sgb, cb=cb, sl=sl, j=j: e.tensor_tensor(out=xs_[sl].t[:, j, cb * 512:(cb + 1) * 512], in0=sgb.t[:], in1=xs_[sl].t[:, j, cb * 512:(cb + 1) * 512], op=ALU.add),
                                  reads=[sgb.r, xs_[sl].r], writes=[xs_[sl].r])
                    S.dma(f"d_xo{sl}", lambda e, sl=sl, c0=c0: e.dma_start(out=x_dst[c0:c0 + 512, :].rearrange("(j p) c -> p j c", p=128), in_=xs_[sl].t[:]), reads=[xs_[sl].r], eng="act")
                S.barrier()

        if DEBUG:
            for nm, src in (("rgs", rgs), ("rv", rv), ("dv_s", dv_s), ("dqT", dqT), ("aT", aT), ("rT", rT), ("xres", xres), ("uT", uT), ("rqT", rqT), ("rkd", rkd), ("rqdT", rqdT), ("rkT", rkT), ("dkT_s", dkT_s)):
                dbg = dram("dbg_" + nm, list(src.shape), src.dtype, "ExternalOutput")
                S.dma("dbg", lambda e, dbg=dbg, src=src: e.dma_start(out=dbg, in_=src))
        S.emit(top)
        nc._n_ops = len(S.ops)
        nc._n_sem = S.nsem
    return nc


ROPE_THETA = 500000.0
RET_THETA = 10000.0


def host_tables(TS, TP, rank):
    posv = np.concatenate([np.arange(TS), rank * TP + np.arange(TP)]).astype(np.float32)
    inv_d = (np.float32(1.0) / (np.float32(ROPE_THETA) ** (np.arange(0, 16, 2, dtype=np.float32) / np.float32(16)))).astype(np.float32)
    inv_r = (np.float32(1.0) / (np.float32(RET_THETA) ** (np.arange(0, 64, 2, dtype=np.float32) / np.float32(64)))).astype(np.float32)
    ang_d = (posv[:, None] * inv_d[None, :]).astype(np.float32)
    ang_r = (posv[:, None] * inv_r[None, :]).astype(np.float32)
    pos = np.concatenate([np.cos(ang_d), np.sin(ang_d), np.cos(ang_r), np.sin(ang_r)], axis=1).astype(np.float32)
    m = np.arange(128)[:, None].astype(np.float32)
    n = np.arange(128)[None, :].astype(np.float32)
    relF = np.maximum(n - m, 0.0)
    mskF = (n >= m).astype(np.float32) * 0.125
    relB = np.maximum(m - n, 0.0)
    mskB = (m > n).astype(np.float32) * 0.125
    p = np.arange(128, dtype=np.float32)[:, None]
    cst = np.concatenate([relF, mskF, relB, mskB, p + 1, 128 - p, 127 - p, p], axis=1).astype(np.float32)
    rkt = np.zeros((128, 8), np.float32)
    for r in range(NR):
        if r < rank:
            rkt[0:64, r] = TP * (rank - 1 - r)
            rkt[0:64, 4 + r] = 1.0
        if r > rank:
            rkt[64:128, r] = TP * (r - rank - 1)
            rkt[64:128, 4 + r] = 1.0
    return pos, cst, rkt


_NC_CACHE = {}


def run_cores(inputs, TS, TP, DEPTH):
    key = (TS, TP, DEPTH)
    if key not in _NC_CACHE:
        _NC_CACHE[key] = build(TS, TP, DEPTH)
    nc = _NC_CACHE[key]
    f = lambda a: np.ascontiguousarray(np.asarray(a, dtype=np.float32))
    xp = f(inputs["x_prompt"])
    xs = f(inputs["x_sample"])
    pp = f(inputs["p_prompt"])
    ps = f(inputs["p_sample"])
    shared = {
        "ln1_w": f(inputs["ln1_w"]), "w_in": f(inputs["w_in"]), "diff_q_norm": f(inputs["diff_q_norm"]),
        "diff_k_norm": f(inputs["diff_k_norm"]), "diff_lambda": f(inputs["diff_lambda"]).reshape(1, DEPTH * 256),
        "diff_subln": f(inputs["diff_subln"]), "ret_decay_logit": f(inputs["ret_decay_logit"]).reshape(1, DEPTH * 16),
        "ret_gn": f(inputs["ret_gn"]), "w_out": f(inputs["w_out"]), "ln2_w": f(inputs["ln2_w"]),
        "w_mlp1": f(inputs["w_mlp1"]), "w_mlp2": f(inputs["w_mlp2"]), "w_ple_gate": f(inputs["w_ple_gate"]),
        "w_ple_proj": f(inputs["w_ple_proj"]),
    }
    in_maps = []
    for c in range(8):
        g, r = c // NR, c % NR
        pos, cst, rkt = host_tables(TS, TP, r)
        m = dict(shared)
        m["x"] = np.ascontiguousarray(np.concatenate([xs[c], xp[g, r * TP:(r + 1) * TP]], axis=0))
        m["p"] = np.ascontiguousarray(np.concatenate([ps[:, c], pp[:, g, r * TP:(r + 1) * TP]], axis=1))
        m["pos"] = pos
        m["cst"] = cst
        m["rkt"] = rkt
        in_maps.append(m)
    res = run_bass_kernel_spmd(nc, in_maps, core_ids=list(range(8)))
    y_s = np.stack([np.asarray(res.results[c]["y"][:TS]) for c in range(8)], axis=0)
    y_p = np.stack([np.concatenate([np.asarray(res.results[g * NR + r]["y"][TS:]) for r in range(NR)], axis=0) for g in range(2)], axis=0)
    return y_p.astype(np.float32), y_s.astype(np.float32)


def kernel(**inputs):
    return run_cores(inputs, 4096, 2048, 4)
```

```python
import math
import types
import numpy as np
import concourse.bass as bass
import concourse.mybir as mybir
from concourse.bass_utils import run_bass_kernel_spmd
from contextlib import ExitStack

F32 = mybir.dt.float32
BF16 = mybir.dt.bfloat16
AF = mybir.ActivationFunctionType
ALU = mybir.AluOpType
AX = mybir.AxisListType

COMPUTE = ("pe", "act", "dve", "pool")
ALL_ENG = ("pe", "act", "dve", "pool", "sp")
SAME_ENG_SYNC = True

D = 1024
INW = 3584
DFF = 4096
PLE = 256
EPS = 1e-6
NR = 4


class Res:
    __slots__ = ("name", "w", "rs")

    def __init__(self, name=""):
        self.name = name
        self.w = None
        self.rs = []


class Op:
    __slots__ = ("eng", "fn", "deps", "mark", "val", "key", "is_mm", "inc", "cost", "idx", "fin", "succ", "npend", "ready")

    def __init__(self, eng, fn, key=None, is_mm=False, cost=None):
        self.eng = eng
        self.fn = fn
        self.deps = []
        self.mark = False
        self.val = 0
        self.key = key
        self.is_mm = is_mm
        self.inc = 16
        self.cost = cost
        self.idx = 0
        self.fin = 0.0
        self.succ = None
        self.npend = 0
        self.ready = 0.0


def _freeze(fn):
    if fn is None or fn.__closure__ is None:
        return fn
    cells = []
    for c in fn.__closure__:
        try:
            cells.append(types.CellType(c.cell_contents))
        except ValueError:
            cells.append(c)
    return types.FunctionType(fn.__code__, fn.__globals__, fn.__name__, fn.__defaults__, tuple(cells))


DEF_COST = {"pe": 0.27, "act": 0.45, "dve": 0.60, "pool": 0.90, "sp": 0.30}
SLAT = 0.8
XLAT = 0.6
DMA_LAT = 3.0
RESCHEDULE = True


class _Probe:
    def __init__(self):
        self.name = None
        self.args = ()
        self.kw = {}

    def __getattr__(self, name):
        def f(*a, **k):
            if self.name is None:
                self.name, self.args, self.kw = name, a, k
            return self
        return f


def _esize(ap):
    n = 1
    for d in ap.shape[1:]:
        n *= int(d)
    return n


def _estimate(eng, fn, key):
    try:
        p = _Probe()
        fn(p)
        out = p.kw.get("out", p.args[0] if p.args else None)
        if out is None or not hasattr(out, "shape"):
            return None
        n = _esize(out)
        if key is not None:
            nbytes = n * int(out.shape[0]) * (2 if out.dtype == BF16 else 4)
            return 2.5 + nbytes / 150e3
        if eng == "pe":
            return 0.03 + n / 2100.0
        if eng == "act":
            return 0.08 + n / 960.0
        if eng == "dve":
            return 0.07 + n / 960.0
        if eng == "pool":
            return 0.6 + n / 700.0
    except Exception:
        return None
    return None


class Sched:
    def __init__(self, nc):
        self.nc = nc
        self.ops = []

    def op(self, eng, fn, reads=(), writes=(), key=None, is_mm=False, cost=None):
        fn = _freeze(fn)
        if cost is None and fn is not None:
            cost = _estimate(eng, fn, key)
        o = Op(eng, fn, key, is_mm, cost)
        deps = o.deps
        for r in reads:
            if r.w is not None:
                deps.append(r.w)
        for w in writes:
            if w.w is not None:
                deps.append(w.w)
            deps.extend(w.rs)
        for r in reads:
            r.rs.append(o)
        for w in writes:
            w.w = o
            w.rs = []
        o.idx = len(self.ops)
        self.ops.append(o)
        return o

    def pe(self, fn, reads=(), writes=(), cost=None):
        return self.op("pe", fn, reads, writes, is_mm=True, cost=cost)

    def act(self, fn, reads=(), writes=(), cost=None):
        return self.op("act", fn, reads, writes, cost=cost)

    def dve(self, fn, reads=(), writes=(), cost=None):
        return self.op("dve", fn, reads, writes, cost=cost)

    def pool(self, fn, reads=(), writes=(), cost=None):
        return self.op("pool", fn, reads, writes, cost=cost)

    def dma(self, key, fn, reads=(), writes=(), eng="sp", inc=16, cost=None):
        o = self.op(eng, fn, reads, writes, key=key, cost=cost)
        o.inc = inc
        return o

    def barrier(self):
        o = Op(None, None)
        o.idx = len(self.ops)
        self.ops.append(o)

    @staticmethod
    def _skip(p, o):
        return p.key is None and p.eng == o.eng and (not SAME_ENG_SYNC or (p.is_mm and o.is_mm))

    def _schedule_segment(self, seg):
        import heapq
        inseg = set(id(o) for o in seg)
        for o in seg:
            o.succ = []
            o.npend = 0
            o.ready = 0.0
        for o in seg:
            for p in o.deps:
                if id(p) in inseg:
                    p.succ.append(o)
                    o.npend += 1
        free = {e: 0.0 for e in ALL_ENG}
        heaps = {e: [] for e in ALL_ENG}
        for o in seg:
            if o.npend == 0:
                heapq.heappush(heaps[o.eng], (0.0, o.idx, o))
        order = {e: [] for e in ALL_ENG}
        remaining = len(seg)
        while remaining:
            best = None
            for e in ALL_ENG:
                h = heaps[e]
                if not h:
                    continue
                t_free = free[e]
                cands = []
                while h and h[0][0] <= t_free:
                    cands.append(heapq.heappop(h))
                if cands:
                    c = min(cands, key=lambda x: x[1])
                    for x in cands:
                        if x is not c:
                            heapq.heappush(h, x)
                    heapq.heappush(h, c)
                    start = t_free
                    pick = c
                else:
                    pick = h[0]
                    start = pick[0]
                if best is None or start < best[0] or (start == best[0] and pick[1] < best[2][1]):
                    best = (start, e, pick)
            start, e, pick = best
            h = heaps[e]
            h.remove(pick)
            heapq.heapify(h)
            o = pick[2]
            cost = o.cost if o.cost is not None else DEF_COST[e]
            if o.key is not None:
                free[e] = start + DEF_COST["sp"]
                o.fin = start + (cost if o.cost is not None else DMA_LAT)
            else:
                free[e] = start + cost
                o.fin = free[e]
            order[e].append(o)
            remaining -= 1
            for q in o.succ:
                if q.eng == o.eng and o.key is None:
                    lat = 0.0 if (o.is_mm and q.is_mm) else SLAT
                else:
                    lat = XLAT
                r = o.fin + lat
                if r > q.ready:
                    q.ready = r
                q.npend -= 1
                if q.npend == 0:
                    heapq.heappush(heaps[q.eng], (q.ready, q.idx, q))
        return order

    def emit(self, stack):
        nc = self.nc
        segs = []
        cur = []
        for o in self.ops:
            if o.eng is None:
                if cur:
                    segs.append(cur)
                    cur = []
            else:
                cur.append(o)
        if cur:
            segs.append(cur)
        streams = {e: [] for e in ALL_ENG}
        for seg in segs:
            if RESCHEDULE:
                order = self._schedule_segment(seg)
            else:
                order = {e: [o for o in seg if o.eng == e] for e in ALL_ENG}
            lastops = {}
            for e in ALL_ENG:
                for o in order[e]:
                    lastops[o.key if o.key is not None else e] = o
            for e in ALL_ENG:
                streams[e].extend(order[e])
            deps = list(lastops.values())
            for e in ALL_ENG:
                b = Op(e, None)
                b.deps = list(deps)
                streams[e].append(b)
        for e in ALL_ENG:
            for o in streams[e]:
                for p in o.deps:
                    if p.key is None and not self._skip(p, o):
                        p.mark = True
        kcnt = {}
        for e in ALL_ENG:
            cnt = 0
            for o in streams[e]:
                if o.fn is None:
                    continue
                if o.key is not None:
                    kcnt[o.key] = kcnt.get(o.key, 0) + o.inc
                    o.val = kcnt[o.key]
                elif o.mark:
                    cnt += 1
                    o.val = cnt
        sems = {}
        for e in COMPUTE:
            sems[e] = stack.enter_context(nc.semaphore("s_" + e))
        for k in kcnt:
            sems[k] = stack.enter_context(nc.semaphore("d_" + k))
        self.nsem = len(sems)
        block = stack.enter_context(nc.Block())

        def run(engname, eng):
            seen = {}
            for o in streams[engname]:
                need = {}
                for p in o.deps:
                    if self._skip(p, o):
                        continue
                    sk = p.eng if p.key is None else p.key
                    if seen.get(sk, 0) >= p.val:
                        continue
                    if need.get(sk, 0) < p.val:
                        need[sk] = p.val
                for sk, v in need.items():
                    eng.wait_ge(sems[sk], v)
                    seen[sk] = v
                if o.fn is None:
                    continue
                ins = o.fn(eng)
                if o.key is not None:
                    ins.then_inc(sems[o.key], o.inc)
                elif o.mark:
                    ins.then_inc(sems[o.eng], 1)
            if engname == "sp":
                for k, v in kcnt.items():
                    if seen.get(k, 0) < v:
                        eng.wait_ge(sems[k], v)

        @block.tensor
        def _(e):
            run("pe", e)

        @block.scalar
        def _(e):
            run("act", e)

        @block.vector
        def _(e):
            run("dve", e)

        @block.gpsimd
        def _(e):
            run("pool", e)

        @block.sync
        def _(e):
            run("sp", e)


class Buf:
    __slots__ = ("t", "r")

    def __init__(self, t, name):
        self.t = t
        self.r = Res(name)


def build(TS, TP, DEPTH, DEBUG=False):
    T = TS + TP
    SKP = NR * TP
    NT = T // 128
    NST = T // 512
    assert TS % 512 == 0 and TP % 512 == 0
    nc = bass.Bass("TRN2", target_bir_lowering=False)

    def dram(name, shape, dtype, kind="Internal"):
        return nc.dram_tensor(name, shape, dtype, kind=kind).ap()

    x_in = dram("x", [T, D], F32, "ExternalInput")
    p_in = dram("p", [DEPTH, T, PLE], F32, "ExternalInput")
    pos_in = dram("pos", [T, 80], F32, "ExternalInput")
    cst_in = dram("cst", [128, 516], F32, "ExternalInput")
    rkt_in = dram("rkt", [128, 8], F32, "ExternalInput")
    ln1_in = dram("ln1_w", [DEPTH, D], F32, "ExternalInput")
    w_in_in = dram("w_in", [DEPTH, D, INW], F32, "ExternalInput")
    qn_in = dram("diff_q_norm", [DEPTH, 64], F32, "ExternalInput")
    kn_in = dram("diff_k_norm", [DEPTH, 64], F32, "ExternalInput")
    lam_in = dram("diff_lambda", [1, DEPTH * 256], F32, "ExternalInput")
    sub_in = dram("diff_subln", [DEPTH, 128], F32, "ExternalInput")
    dec_in = dram("ret_decay_logit", [1, DEPTH * 16], F32, "ExternalInput")
    gn_in = dram("ret_gn", [DEPTH, 64], F32, "ExternalInput")
    w_out_in = dram("w_out", [DEPTH, D, D], F32, "ExternalInput")
    ln2_in = dram("ln2_w", [DEPTH, D], F32, "ExternalInput")
    w1_in = dram("w_mlp1", [DEPTH, D, DFF], F32, "ExternalInput")
    w2_in = dram("w_mlp2", [DEPTH, DFF, D], F32, "ExternalInput")
    wg_in = dram("w_ple_gate", [DEPTH, D, D], F32, "ExternalInput")
    wp_in = dram("w_ple_proj", [DEPTH, PLE, D], F32, "ExternalInput")
    y_out = dram("y", [T, D], F32, "ExternalOutput")

    xres = dram("xres", [T, D], F32)
    dqT = dram("dqT", [512, T], BF16)
    dkT_s = dram("dkT_s", [512, TS], BF16)
    NSPL = max(1, (512 * TP * 2) // (1 << 20))
    HPS = 4 // NSPL
    TPS = TP // NSPL
    dkT_l = [dram(f"dkT_l{a}", [HPS * 128, TP], BF16) for a in range(NSPL)]
    dkT_g = [dram(f"dkT_g{a}", [NR * HPS * 128, TP], BF16) for a in range(NSPL)]
    dv_s = dram("dv_s", [TS, 512], BF16)
    dv_l = [dram(f"dv_l{a}", [TPS, 512], BF16) for a in range(NSPL)]
    dv_g = [dram(f"dv_g{a}", [NR * TPS, 512], BF16) for a in range(NSPL)]
    rqT = dram("rqT", [512, T], BF16)
    rkT = dram("rkT", [512, T], BF16)
    rqdT = dram("rqdT", [8 * 128, T], BF16)
    rkd = dram("rkd", [T, 1024], BF16)
    rv = dram("rv", [T, 512], BF16)
    rgs = dram("rgs", [T, 512], BF16)
    aT = dram("aT", [512, T], BF16)
    rT = dram("rT", [512, T], BF16)
    uT = dram("uT", [DFF, T], BF16)
    st_l = dram("st_l", [128, 512], F32)
    st_g = dram("st_g", [NR * 128, 512], F32)

    with ExitStack() as top:
        S = Sched(nc)

        uid = [0]

        def sb(st, name, shape, dt):
            uid[0] += 1
            return Buf(st.enter_context(nc.sbuf_tensor(f"sb{uid[0]}_{name}", shape, dt)), name)

        def psb(st, name, shape, dt):
            uid[0] += 1
            return Buf(st.enter_context(nc.psum_tensor(f"ps{uid[0]}_{name}", shape, dt)), name)

        ident = sb(top, "ident", [128, 128], BF16)
        identf = sb(top, "identf", [128, 128], F32)
        ones_b = sb(top, "ones_b", [128, 128], BF16)
        ones_f = sb(top, "ones_f", [128, 128], F32)
        cst = sb(top, "cst", [128, 516], F32)
        rkt = sb(top, "rkt", [128, 8], F32)
        lg = sb(top, "lg", [128, DEPTH * 16], F32)
        lgcol = sb(top, "lgcol", [128, DEPTH * 8], F32)
        lam = sb(top, "lam", [128, DEPTH], F32)
        neglam = sb(top, "neglam", [128, DEPTH], F32)
        epsc = sb(top, "epsc", [128, 1], F32)
        ln1b = sb(top, "ln1b", [128, D], F32)
        ln2b = sb(top, "ln2b", [128, D], F32)
        qnw = sb(top, "qnw", [128, 64], F32)
        knw = sb(top, "knw", [128, 64], F32)
        gnw = sb(top, "gnw", [128, 64], F32)
        subw = sb(top, "subw", [128, 1], F32)
        qdec = sb(top, "qdec", [128, 8, 2], F32)
        kdec = sb(top, "kdec", [128, 8, 2], F32)
        cdec = sb(top, "cdec", [128, 8], F32)
        DTe = sb(top, "DTe", [128, 4, 128], F32)
        DTo = sb(top, "DTo", [128, 4, 128], F32)
        coef = sb(top, "coef", [128, 4, 8], F32)

        relF = cst.t[:, 0:128]
        mskF = cst.t[:, 128:256]
        relB = cst.t[:, 256:384]
        mskB = cst.t[:, 384:512]
        idx_p1 = cst.t[:, 512:513]
        idx_128m = cst.t[:, 513:514]
        idx_127m = cst.t[:, 514:515]
        idx_p = cst.t[:, 515:516]

        S.dma("c_init", lambda e: e.dma_start(out=cst.t[:], in_=cst_in), writes=[cst.r])
        S.dma("c_init", lambda e: e.dma_start(out=rkt.t[:], in_=rkt_in), writes=[rkt.r])
        S.pool(lambda e: e.memset(identf.t[:], 0.0), writes=[identf.r])
        S.pool(lambda e: e.affine_select(out=identf.t[:], in_=identf.t[:], compare_op=ALU.not_equal, fill=1.0, base=0,
                                         pattern=[[-1, 128]], channel_multiplier=1), reads=[identf.r], writes=[identf.r])
        S.dve(lambda e: e.tensor_copy(out=ident.t[:], in_=identf.t[:]), reads=[identf.r], writes=[ident.r])
        S.dve(lambda e: e.memset(ones_b.t[:], 1.0), writes=[ones_b.r])
        S.dve(lambda e: e.memset(ones_f.t[:], 1.0), writes=[ones_f.r])
        S.dve(lambda e: e.memset(epsc.t[:], EPS), writes=[epsc.r])

        with ExitStack() as st0:
            NL = DEPTH * 16
            xx = sb(st0, "ls_x", [128, NL], F32)
            ax = sb(st0, "ls_ax", [128, NL], F32)
            uu = sb(st0, "ls_u", [128, NL], F32)
            ss_ = sb(st0, "ls_s", [128, NL], F32)
            s2 = sb(st0, "ls_s2", [128, NL], F32)
            pl = sb(st0, "ls_pl", [128, NL], F32)
            lp = sb(st0, "lp", [128, DEPTH * 256], F32)
            S.dma("c_init", lambda e: e.dma_start(out=xx.t[:], in_=dec_in.to_broadcast([128, NL])), writes=[xx.r])
            S.dma("c_init", lambda e: e.dma_start(out=lp.t[:], in_=lam_in.to_broadcast([128, DEPTH * 256])), writes=[lp.r])
            S.barrier()
            S.dve(lambda e: e.tensor_scalar(out=ax.t[:], in0=xx.t[:], scalar1=-1.0, scalar2=None, op0=ALU.mult), reads=[xx.r], writes=[ax.r])
            S.dve(lambda e: e.tensor_tensor(out=ax.t[:], in0=ax.t[:], in1=xx.t[:], op=ALU.max), reads=[xx.r, ax.r], writes=[ax.r])
            S.act(lambda e: e.activation(out=uu.t[:], in_=ax.t[:], func=AF.Exp, scale=-1.0), reads=[ax.r], writes=[uu.r])
            S.dve(lambda e: e.tensor_scalar(out=ss_.t[:], in0=uu.t[:], scalar1=2.0, scalar2=None, op0=ALU.add), reads=[uu.r], writes=[ss_.r])
            S.dve(lambda e: e.reciprocal(out=ss_.t[:], in_=ss_.t[:]), reads=[ss_.r], writes=[ss_.r])
            S.dve(lambda e: e.tensor_tensor(out=ss_.t[:], in0=ss_.t[:], in1=uu.t[:], op=ALU.mult), reads=[ss_.r, uu.r], writes=[ss_.r])
            S.dve(lambda e: e.tensor_tensor(out=s2.t[:], in0=ss_.t[:], in1=ss_.t[:], op=ALU.mult), reads=[ss_.r], writes=[s2.r])
            S.dve(lambda e: e.tensor_scalar(out=pl.t[:], in0=s2.t[:], scalar1=1.0 / 13, scalar2=1.0 / 11, op0=ALU.mult, op1=ALU.add), reads=[s2.r], writes=[pl.r])
            for cc in (1.0 / 9, 1.0 / 7, 1.0 / 5, 1.0 / 3, 1.0):
                S.dve(lambda e: e.tensor_tensor(out=pl.t[:], in0=pl.t[:], in1=s2.t[:], op=ALU.mult), reads=[pl.r, s2.r], writes=[pl.r])
                S.dve(lambda e, cc=cc: e.tensor_scalar(out=pl.t[:], in0=pl.t[:], scalar1=cc, scalar2=None, op0=ALU.add), reads=[pl.r], writes=[pl.r])
            S.dve(lambda e: e.tensor_tensor(out=pl.t[:], in0=pl.t[:], in1=ss_.t[:], op=ALU.mult), reads=[pl.r, ss_.r], writes=[pl.r])
            S.dve(lambda e: e.tensor_scalar(out=ax.t[:], in0=xx.t[:], scalar1=0.0, scalar2=None, op0=ALU.min), reads=[xx.r], writes=[ax.r])
            S.dve(lambda e: e.scalar_tensor_tensor(out=lg.t[:], in0=pl.t[:], scalar=-2.0, in1=ax.t[:], op0=ALU.mult, op1=ALU.add), reads=[pl.r, ax.r], writes=[lg.r])
            lg4 = lg.t[:].rearrange("p (l a h) -> p l a h", l=DEPTH, a=2)
            lgc3 = lgcol.t[:].rearrange("p (l h) -> p l h", l=DEPTH)
            S.dve(lambda e: e.tensor_copy(out=lgc3[0:64], in_=lg4[0:64, :, 0, :]), reads=[lg.r], writes=[lgcol.r])
            S.dve(lambda e: e.tensor_copy(out=lgc3[64:128], in_=lg4[64:128, :, 1, :]), reads=[lg.r], writes=[lgcol.r])
            pr = sb(st0, "lpr", [128, DEPTH * 2 * 64], F32)
            sm = sb(st0, "lsm", [128, DEPTH * 2], F32)
            lp5 = lp.t[:].rearrange("p (l a b d) -> p l a b d", l=DEPTH, a=2, b=2)
            pr4 = pr.t[:].rearrange("p (l a d) -> p l a d", l=DEPTH, a=2)
            S.dve(lambda e: e.tensor_tensor(out=pr4, in0=lp5[:, :, :, 0, :], in1=lp5[:, :, :, 1, :], op=ALU.mult), reads=[lp.r], writes=[pr.r])
            S.dve(lambda e: e.tensor_reduce(out=sm.t[:], in_=pr.t[:].rearrange("p (g d) -> p g d", d=64), axis=AX.X, op=ALU.add), reads=[pr.r], writes=[sm.r])
            S.act(lambda e: e.activation(out=sm.t[:], in_=sm.t[:], func=AF.Exp), reads=[sm.r], writes=[sm.r])
            sm3 = sm.t[:].rearrange("p (l a) -> p l a", a=2)
            S.dve(lambda e: e.tensor_tensor(out=lam.t[:], in0=sm3[:, :, 0], in1=sm3[:, :, 1], op=ALU.subtract), reads=[sm.r], writes=[lam.r])
            for l in range(DEPTH):
                li = 0.8 - 0.6 * math.exp(-0.3 * l)
                S.dve(lambda e, l=l, li=li: e.tensor_scalar(out=lam.t[:, l:l + 1], in0=lam.t[:, l:l + 1], scalar1=li, scalar2=None, op0=ALU.add), reads=[lam.r], writes=[lam.r])
            S.dve(lambda e: e.tensor_scalar(out=neglam.t[:], in0=lam.t[:], scalar1=-1.0, scalar2=None, op0=ALU.mult), reads=[lam.r], writes=[neglam.r])
            S.barrier()

        def rsqrt_act(out_ap, in_ap, scale, rbuf, wbuf):
            S.act(lambda e: e.activation(out=out_ap, in_=in_ap, func=AF.Ln, scale=scale, bias=epsc.t[0:out_ap.shape[0], 0:1]), reads=[rbuf.r, epsc.r], writes=[wbuf.r])
            S.act(lambda e: e.activation(out=out_ap, in_=out_ap, func=AF.Exp, scale=-0.5), reads=[wbuf.r], writes=[wbuf.r])

        class WGroups:
            def __init__(self, t, gn):
                self.t = t
                self.gn = gn
                self.res = {}

            def r(self, k, n0):
                return self.res[(k // 4, n0 // self.gn)]

        def load_w(st, name, src, K, N, key, gn=512, korder=False):
            kc = K // 128
            gn = min(gn, N, 2048)
            uid[0] += 1
            t = st.enter_context(nc.sbuf_tensor(f"sb{uid[0]}_{name}", [128, kc, N], BF16))
            w = WGroups(t, gn)
            srcv = src.rearrange("(c p) n -> p c n", p=128)
            chain = [Res(name + "_chA"), Res(name + "_chB")]
            groups = [(c0, n0) for n0 in range(0, N, gn) for c0 in range(0, kc, 4)]
            if korder:
                groups = [(c0, n0) for c0 in range(0, kc, 4) for n0 in range(0, N, gn)]
            for i, (c0, n0) in enumerate(groups):
                c1 = min(kc, c0 + 4)
                n1 = min(N, n0 + gn)
                rr_ = Res(f"{name}_{c0}_{n0}")
                w.res[(c0 // 4, n0 // gn)] = rr_
                S.dma(key + "AB"[i % 2], lambda e, c0=c0, c1=c1, n0=n0, n1=n1: e.dma_start(out=t[:, c0:c1, n0:n1], in_=srcv[:, c0:c1, n0:n1]),
                      writes=[rr_, chain[i % 2]], eng="pool", cost=8.0)
            return w

        for l in range(DEPTH):
            lam_init = 0.8 - 0.6 * math.exp(-0.3 * l)
            x_src = x_in if l == 0 else xres
            x_dst = y_out if l == DEPTH - 1 else xres

            S.dma("c_lay", lambda e, l=l: e.dma_start(out=ln1b.t[:], in_=ln1_in[l:l + 1, :].to_broadcast([128, D])), writes=[ln1b.r])
            S.dma("c_lay", lambda e, l=l: e.dma_start(out=ln2b.t[:], in_=ln2_in[l:l + 1, :].to_broadcast([128, D])), writes=[ln2b.r])
            S.dma("c_lay", lambda e, l=l: e.dma_start(out=qnw.t[:], in_=qn_in[l:l + 1, :].to_broadcast([128, 64])), writes=[qnw.r])
            S.dma("c_lay", lambda e, l=l: e.dma_start(out=knw.t[:], in_=kn_in[l:l + 1, :].to_broadcast([128, 64])), writes=[knw.r])
            S.dma("c_lay", lambda e, l=l: e.dma_start(out=gnw.t[:], in_=gn_in[l:l + 1, :].to_broadcast([128, 64])), writes=[gnw.r])
            S.dma("c_lay", lambda e, l=l: e.dma_start(out=subw.t[:], in_=sub_in[l:l + 1, :].rearrange("o e -> e o"), allow_slow_non_contiguous=True), writes=[subw.r])
            S.barrier()
            S.dve(lambda e: e.tensor_scalar(out=qnw.t[:], in0=qnw.t[:], scalar1=0.125, scalar2=None, op0=ALU.mult), reads=[qnw.r], writes=[qnw.r])
            S.dve(lambda e, li=lam_init: e.tensor_scalar(out=subw.t[:], in0=subw.t[:], scalar1=1.0 - li, scalar2=None, op0=ALU.mult), reads=[subw.r], writes=[subw.r])
            lgl = lg.t[:, l * 16:(l + 1) * 16]
            S.act(lambda e, lgl=lgl: e.activation(out=qdec.t[:, :, 0], in_=lgl[:, 0:8], func=AF.Exp, scale=idx_p1), reads=[lg.r, cst.r], writes=[qdec.r])
            S.act(lambda e, lgl=lgl: e.activation(out=qdec.t[:, :, 1], in_=lgl[:, 8:16], func=AF.Exp, scale=idx_128m), reads=[lg.r, cst.r], writes=[qdec.r])
            S.act(lambda e, lgl=lgl: e.activation(out=kdec.t[:, :, 0], in_=lgl[:, 0:8], func=AF.Exp, scale=idx_127m), reads=[lg.r, cst.r], writes=[kdec.r])
            S.act(lambda e, lgl=lgl: e.activation(out=kdec.t[:, :, 1], in_=lgl[:, 8:16], func=AF.Exp, scale=idx_p), reads=[lg.r, cst.r], writes=[kdec.r])
            S.dve(lambda e: e.tensor_scalar(out=kdec.t[:], in0=kdec.t[:], scalar1=0.125, scalar2=None, op0=ALU.mult), reads=[kdec.r], writes=[kdec.r])
            S.act(lambda e, l=l: e.activation(out=cdec.t[:], in_=lgcol.t[:, l * 8:(l + 1) * 8], func=AF.Exp, scale=128.0), reads=[lgcol.r], writes=[cdec.r])
            with ExitStack() as stc:
                tmpa = sb(stc, "dt_a", [128, 128], F32)
                tmpb = sb(stc, "dt_b", [128, 128], F32)
                for h in range(8):
                    dst = (DTe if h % 2 == 0 else DTo)
                    dsl = dst.t[:, h // 2, :]
                    S.act(lambda e, h=h: e.activation(out=tmpa.t[:], in_=relF, func=AF.Exp, scale=lgl[:, h:h + 1]), reads=[cst.r, lg.r], writes=[tmpa.r])
                    S.act(lambda e, h=h: e.activation(out=tmpb.t[:], in_=relB, func=AF.Exp, scale=lgl[:, 8 + h:9 + h]), reads=[cst.r, lg.r], writes=[tmpb.r])
                    S.dve(lambda e: e.tensor_tensor(out=tmpa.t[:], in0=tmpa.t[:], in1=mskF, op=ALU.mult), reads=[tmpa.r, cst.r], writes=[tmpa.r])
                    S.dve(lambda e: e.tensor_tensor(out=tmpb.t[:], in0=tmpb.t[:], in1=mskB, op=ALU.mult), reads=[tmpb.r, cst.r], writes=[tmpb.r])
                    S.dve(lambda e, dsl=dsl: e.tensor_tensor(out=dsl, in0=tmpa.t[:], in1=tmpb.t[:], op=ALU.add), reads=[tmpa.r, tmpb.r], writes=[dst.r])
                for r_ in range(NR):
                    S.act(lambda e, r_=r_, l=l: e.activation(out=coef.t[:, r_, :], in_=lgcol.t[:, l * 8:(l + 1) * 8], func=AF.Exp, scale=rkt.t[:, r_:r_ + 1]), reads=[lgcol.r, rkt.r], writes=[coef.r])
                    S.dve(lambda e, r_=r_: e.tensor_scalar(out=coef.t[:, r_, :], in0=coef.t[:, r_, :], scalar1=rkt.t[:, 4 + r_:5 + r_], scalar2=None, op0=ALU.mult), reads=[coef.r, rkt.r], writes=[coef.r])
                S.barrier()

            with ExitStack() as st:
                w_in = load_w(st, "w_in", w_in_in[l], D, INW, "w_in")
                P2 = range(2)
                xt = [sb(st, f"xt{i}", [128, D], F32) for i in P2]
                post = [sb(st, f"post{i}", [128, 80], F32) for i in P2]
                junk = sb(st, "junk", [128, D], BF16)
                ssx = [sb(st, f"ssx{i}", [128, 1], F32) for i in P2]
                hn = [sb(st, f"hn{i}", [128, D], BF16) for i in P2]
                hnT = [sb(st, f"hnT{i}", [128, 8, 128], BF16) for i in P2]
                qraw = [[sb(st, f"qraw{w}{i}", [128, 512], F32) for i in P2] for w in P2]
                qsq = [sb(st, f"qsq{w}", [128, 512], F32) for w in P2]
                ssg = [[sb(st, f"ssg{w}{i}", [128, 8], F32) for i in P2] for w in P2]
                qu = [sb(st, f"qu{w}", [128, 512], F32) for w in P2]
                w16 = [sb(st, f"w16{w}", [128, 8, 16], F32) for w in P2]
                rt = [[sb(st, f"rt{w}{i}", [128, 8, 8], F32) for i in range(4)] for w in P2]
                qb = [[sb(st, f"qb{w}{i}", [128, 512], BF16) for i in P2] for w in P2]
                rqf = [sb(st, f"rqf{w}", [128, 512], F32) for w in P2]
                rta = [sb(st, f"rta{w}", [128, 8, 32], F32) for w in P2]
                rtb = [sb(st, f"rtb{w}", [128, 8, 32], F32) for w in P2]
                rqb = [[sb(st, f"rqb{w}{i}", [128, 512], BF16) for i in P2] for w in P2]
                qd = [sb(st, f"qd{i}", [128, 1024], BF16) for i in P2]
                esg = [sb(st, f"esg{i}", [128, 512], F32) for i in P2]
                stg_qT = [sb(st, f"sg_qT{i}", [128, 4, 512], BF16) for i in P2]
                stg_kT = [sb(st, f"sg_kT{i}", [128, 4, 512], BF16) for i in P2]
                stg_rqT = [sb(st, f"sg_rqT{i}", [128, 4, 512], BF16) for i in P2]
                stg_rkT = [sb(st, f"sg_rkT{i}", [128, 4, 512], BF16) for i in P2]
                stg_qdT = [sb(st, f"sg_qdT{i}", [128, 8, 512], BF16) for i in P2]
                tv = [sb(st, f"tv{i}", [128, 512], BF16) for i in P2]
                trv = [sb(st, f"trv{i}", [128, 512], BF16) for i in P2]
                tgs = [sb(st, f"tgs{i}", [128, 512], BF16) for i in P2]
                tkd = [sb(st, f"tkd{i}", [128, 1024], BF16) for i in P2]
                pT = psb(st, "pT", [128, 8, 128], BF16)
                pT2 = [psb(st, f"pT2_{i}", [128, 8, 128], BF16) for i in P2]
                pj = [psb(st, f"pj{i}", [128, 512], F32) for i in range(5)]
                pjn = [0]

                def proj(cb, hnT_):
                    b = pj[pjn[0] % 5]
                    pjn[0] += 1
                    for k in range(8):
                        S.pe(lambda e, k=k, b=b, cb=cb: e.matmul(b.t[:], lhsT=hnT_.t[:, k, :], rhs=w_in.t[:, k, cb * 512:(cb + 1) * 512], start=(k == 0), stop=(k == 7)),
                             reads=[hnT_.r, w_in.r(k, cb * 512)], writes=[b.r])
                    return b

                def transpose_to(src, nblk, dst_stage, dst_ap_fn, pbuf):
                    for c in range(nblk):
                        S.pe(lambda e, c=c: e.transpose(out=pbuf.t[:, c, :], in_=src.t[:, c * 128:(c + 1) * 128], identity=ident.t[:]),
                             reads=[src.r, ident.r], writes=[pbuf.r])
                    S.act(lambda e: e.activation(out=dst_ap_fn(), in_=pbuf.t[:, 0:nblk, :], func=AF.Copy), reads=[pbuf.r], writes=[dst_stage.r])

                for t in range(NT):
                    sti = t // 4
                    j = t % 4
                    sl = sti % 2
                    par = t % 2
                    tok0 = t * 128
                    is_s = tok0 < TS
                    tl = tok0 if is_s else tok0 - TS
                    xb = xt[par]
                    pb = post[par]
                    hn_, hnT_, ssx_ = hn[par], hnT[par], ssx[par]
                    S.dma(f"a_x{par}", lambda e, xb=xb, tok0=tok0: e.dma_start(out=xb.t[:], in_=x_src[tok0:tok0 + 128, :]), writes=[xb.r])
                    S.dma(f"a_pos{par}", lambda e, pb=pb, tok0=tok0: e.dma_start(out=pb.t[:], in_=pos_in[tok0:tok0 + 128, :]), writes=[pb.r])
                    S.act(lambda e, xb=xb, ssx_=ssx_: e.activation(out=junk.t[:], in_=xb.t[:], func=AF.Square, accum_out=ssx_.t[:]), reads=[xb.r], writes=[junk.r, ssx_.r])
                    rsqrt_act(ssx_.t[:], ssx_.t[:], 1.0 / D, ssx_, ssx_)
                    S.dve(lambda e, xb=xb, hn_=hn_, ssx_=ssx_: e.scalar_tensor_tensor(out=hn_.t[:], in0=xb.t[:], scalar=ssx_.t[:, 0:1], in1=ln1b.t[:], op0=ALU.mult, op1=ALU.mult),
                          reads=[xb.r, ssx_.r, ln1b.r], writes=[hn_.r])
                    for c in range(8):
                        S.pe(lambda e, c=c, hn_=hn_: e.transpose(out=pT.t[:, c, :], in_=hn_.t[:, c * 128:(c + 1) * 128], identity=ident.t[:]), reads=[hn_.r, ident.r], writes=[pT.r])
                    S.act(lambda e, hnT_=hnT_: e.activation(out=hnT_.t[:], in_=pT.t[:], func=AF.Copy), reads=[pT.r], writes=[hnT_.r], cost=1.1)
                    cosd = pb.t[:, 0:8].unsqueeze(1).to_broadcast([128, 8, 8])
                    sind = pb.t[:, 8:16].unsqueeze(1).to_broadcast([128, 8, 8])
                    cosr = pb.t[:, 16:48].unsqueeze(1).to_broadcast([128, 8, 32])
                    sinr = pb.t[:, 48:80].unsqueeze(1).to_broadcast([128, 8, 32])
                    for which in range(2):
                        b = proj(which, hnT_)
                        nw = qnw if which == 0 else knw
                        stg = (stg_qT if which == 0 else stg_kT)[sl]
                        qraw_, qsq_, ssg_, qu_, w16_, rt_, qb_ = qraw[which][par], qsq[which], ssg[which][par], qu[which], w16[which], rt[which], qb[which][par]
                        S.act(lambda e, b=b, qraw_=qraw_: e.activation(out=qraw_.t[:], in_=b.t[:], func=AF.Copy), reads=[b.r], writes=[qraw_.r])
                        S.act(lambda e, b=b, qsq_=qsq_: e.activation(out=qsq_.t[:], in_=b.t[:], func=AF.Square), reads=[b.r], writes=[qsq_.r])
                        S.dve(lambda e, qsq_=qsq_, ssg_=ssg_: e.tensor_reduce(out=ssg_.t[:], in_=qsq_.t[:].rearrange("p (g d) -> p g d", d=64), axis=AX.X, op=ALU.add), reads=[qsq_.r], writes=[ssg_.r])
                        rsqrt_act(ssg_.t[:], ssg_.t[:], 1.0 / 64, ssg_, ssg_)
                        qr3 = qraw_.t[:].rearrange("p (g d) -> p g d", d=64)
                        qu3 = qu_.t[:].rearrange("p (g d) -> p g d", d=64)
                        qb3 = qb_.t[:].rearrange("p (g d) -> p g d", d=64)
                        S.dve(lambda e, qr3=qr3, qu3=qu3, ssg_=ssg_: e.tensor_tensor(out=qu3, in0=qr3, in1=ssg_.t[:].unsqueeze(2).to_broadcast([128, 8, 64]), op=ALU.mult), reads=[qraw_.r, ssg_.r], writes=[qu_.r])
                        S.dve(lambda e, qu3=qu3, qb3=qb3, nw=nw: e.tensor_tensor(out=qb3, in0=qu3, in1=nw.t[:].unsqueeze(1).to_broadcast([128, 8, 64]), op=ALU.mult), reads=[qu_.r, nw.r], writes=[qb_.r])
                        S.dve(lambda e, qu3=qu3, nw=nw, w16_=w16_: e.tensor_tensor(out=w16_.t[:], in0=qu3[:, :, 0:16], in1=nw.t[:, 0:16].unsqueeze(1).to_broadcast([128, 8, 16]), op=ALU.mult), reads=[qu_.r, nw.r], writes=[w16_.r], cost=0.15)
                        x1 = w16_.t[:, :, 0:8]
                        x2 = w16_.t[:, :, 8:16]
                        S.dve(lambda e, x1=x1, cosd=cosd, rt_=rt_: e.tensor_tensor(out=rt_[0].t[:], in0=x1, in1=cosd, op=ALU.mult), reads=[w16_.r, pb.r], writes=[rt_[0].r], cost=0.12)
                        S.dve(lambda e, x2=x2, sind=sind, rt_=rt_: e.tensor_tensor(out=rt_[1].t[:], in0=x2, in1=sind, op=ALU.mult), reads=[w16_.r, pb.r], writes=[rt_[1].r], cost=0.12)
                        S.dve(lambda e, x1=x1, sind=sind, rt_=rt_: e.tensor_tensor(out=rt_[2].t[:], in0=x1, in1=sind, op=ALU.mult), reads=[w16_.r, pb.r], writes=[rt_[2].r], cost=0.12)
                        S.dve(lambda e, x2=x2, cosd=cosd, rt_=rt_: e.tensor_tensor(out=rt_[3].t[:], in0=x2, in1=cosd, op=ALU.mult), reads=[w16_.r, pb.r], writes=[rt_[3].r], cost=0.12)
                        S.dve(lambda e, qb3=qb3, rt_=rt_: e.tensor_tensor(out=qb3[:, :, 0:8], in0=rt_[0].t[:], in1=rt_[1].t[:], op=ALU.subtract), reads=[rt_[0].r, rt_[1].r, qb_.r], writes=[qb_.r], cost=0.12)
                        S.dve(lambda e, qb3=qb3, rt_=rt_: e.tensor_tensor(out=qb3[:, :, 8:16], in0=rt_[2].t[:], in1=rt_[3].t[:], op=ALU.add), reads=[rt_[2].r, rt_[3].r, qb_.r], writes=[qb_.r], cost=0.12)
                        transpose_to(qb_, 4, stg, lambda stg=stg, j=j: stg.t[:, :, j * 128:(j + 1) * 128], pT2[0])
                    b = proj(2, hnT_)
                    tv_ = tv[par]
                    S.act(lambda e, b=b, tv_=tv_: e.activation(out=tv_.t[:], in_=b.t[:], func=AF.Copy), reads=[b.r], writes=[tv_.r])
                    if is_s:
                        S.dma(f"s_v{par}", lambda e, tv_=tv_, tl=tl: e.dma_start(out=dv_s[tl:tl + 128, :], in_=tv_.t[:]), reads=[tv_.r])
                    else:
                        S.dma(f"s_v{par}", lambda e, tv_=tv_, tl=tl: e.dma_start(out=dv_l[tl // TPS][tl % TPS:tl % TPS + 128, :], in_=tv_.t[:]), reads=[tv_.r])
                    for which in range(2):
                        b = proj(3 + which, hnT_)
                        rqf_, rta_, rtb_, rqb_ = rqf[which], rta[which], rtb[which], rqb[which][par]
                        b3 = b.t[:].rearrange("p (g d) -> p g d", d=64)
                        rq3 = rqf_.t[:].rearrange("p (g d) -> p g d", d=64)
                        S.dve(lambda e, b3=b3, cosr=cosr, rta_=rta_: e.tensor_tensor(out=rta_.t[:], in0=b3[:, :, 0:32], in1=cosr, op=ALU.mult), reads=[b.r, pb.r], writes=[rta_.r])
                        S.dve(lambda e, b3=b3, sinr=sinr, rtb_=rtb_: e.tensor_tensor(out=rtb_.t[:], in0=b3[:, :, 32:64], in1=sinr, op=ALU.mult), reads=[b.r, pb.r], writes=[rtb_.r])
                        S.dve(lambda e, rq3=rq3, rta_=rta_, rtb_=rtb_: e.tensor_tensor(out=rq3[:, :, 0:32], in0=rta_.t[:], in1=rtb_.t[:], op=ALU.subtract), reads=[rta_.r, rtb_.r], writes=[rqf_.r])
                        S.dve(lambda e, b3=b3, sinr=sinr, rta_=rta_: e.tensor_tensor(out=rta_.t[:], in0=b3[:, :, 0:32], in1=sinr, op=ALU.mult), reads=[b.r, pb.r], writes=[rta_.r])
                        S.dve(lambda e, b3=b3, cosr=cosr, rtb_=rtb_: e.tensor_tensor(out=rtb_.t[:], in0=b3[:, :, 32:64], in1=cosr, op=ALU.mult), reads=[b.r, pb.r], writes=[rtb_.r])
                        S.dve(lambda e, rq3=rq3, rta_=rta_, rtb_=rtb_: e.tensor_tensor(out=rq3[:, :, 32:64], in0=rta_.t[:], in1=rtb_.t[:], op=ALU.add), reads=[rta_.r, rtb_.r, rqf_.r], writes=[rqf_.r])
                        S.act(lambda e, rqb_=rqb_, rqf_=rqf_: e.activation(out=rqb_.t[:], in_=rqf_.t[:], func=AF.Copy), reads=[rqf_.r], writes=[rqb_.r], cost=0.6)
                        dec = qdec if which == 0 else kdec
                        rq4 = rqf_.t[:].rearrange("p (g d) -> p g d", d=64).unsqueeze(2).to_broadcast([128, 8, 2, 64])
                        dc4 = dec.t[:].unsqueeze(3).to_broadcast([128, 8, 2, 64])
                        if which == 0:
                            qd_ = qd[par]
                            S.pool(lambda e, rq4=rq4, dc4=dc4, qd_=qd_: e.tensor_tensor(out=qd_.t[:].rearrange("p (g a d) -> p g a d", g=8, a=2), in0=rq4, in1=dc4, op=ALU.mult),
                                  reads=[rqf_.r, dec.r], writes=[qd_.r], cost=2.5)
                            transpose_to(rqb_, 4, stg_rqT[sl], lambda sl=sl, j=j: stg_rqT[sl].t[:, :, j * 128:(j + 1) * 128], pT2[1])
                            transpose_to(qd_, 8, stg_qdT[sl], lambda sl=sl, j=j: stg_qdT[sl].t[:, :, j * 128:(j + 1) * 128], pT2[0])
                        else:
                            tkd_ = tkd[par]
                            S.pool(lambda e, rq4=rq4, dc4=dc4, tkd_=tkd_: e.tensor_tensor(out=tkd_.t[:].rearrange("p (g a d) -> p g a d", g=8, a=2), in0=rq4, in1=dc4, op=ALU.mult),
                                  reads=[rqf_.r, dec.r], writes=[tkd_.r], cost=2.5)
                            S.dma(f"s_kd{par}", lambda e, tkd_=tkd_, tok0=tok0: e.dma_start(out=rkd[tok0:tok0 + 128, :], in_=tkd_.t[:]), reads=[tkd_.r])
                            transpose_to(rqb_, 4, stg_rkT[sl], lambda sl=sl, j=j: stg_rkT[sl].t[:, :, j * 128:(j + 1) * 128], pT2[1])
                    b = proj(5, hnT_)
                    trv_ = trv[par]
                    S.act(lambda e, b=b, trv_=trv_: e.activation(out=trv_.t[:], in_=b.t[:], func=AF.Copy), reads=[b.r], writes=[trv_.r])
                    S.dma(f"s_rv{par}", lambda e, trv_=trv_, tok0=tok0: e.dma_start(out=rv[tok0:tok0 + 128, :], in_=trv_.t[:]), reads=[trv_.r])
                    b = proj(6, hnT_)
                    esg_, tgs_ = esg[par], tgs[par]
                    S.act(lambda e, b=b, esg_=esg_: e.activation(out=esg_.t[:], in_=b.t[:], func=AF.Exp, scale=-1.0), reads=[b.r], writes=[esg_.r])
                    S.act(lambda e, esg_=esg_: e.activation(out=esg_.t[:], in_=esg_.t[:], func=AF.Ln, bias=ones_f.t[:, 0:1]), reads=[esg_.r, ones_f.r], writes=[esg_.r], cost=0.6)
                    S.act(lambda e, esg_=esg_: e.activation(out=esg_.t[:], in_=esg_.t[:], func=AF.Exp, scale=-1.0), reads=[esg_.r], writes=[esg_.r], cost=0.6)
                    S.dve(lambda e, b=b, esg_=esg_, tgs_=tgs_: e.tensor_tensor(out=tgs_.t[:], in0=b.t[:], in1=esg_.t[:], op=ALU.mult), reads=[b.r, esg_.r], writes=[tgs_.r])
                    S.dma(f"s_gs{par}", lambda e, tgs_=tgs_, tok0=tok0: e.dma_start(out=rgs[tok0:tok0 + 128, :], in_=tgs_.t[:]), reads=[tgs_.r])
                    if j == 3:
                        c0 = sti * 512
                        cl = c0 if is_s else c0 - TS
                        S.dma(f"s_qT{sl}", lambda e, sl=sl, c0=c0: e.dma_start(out=dqT.rearrange("(h f) t -> f h t", f=128)[:, :, c0:c0 + 512], in_=stg_qT[sl].t[:]), reads=[stg_qT[sl].r])
                        if is_s:
                            S.dma(f"s_kT{sl}", lambda e, sl=sl, cl=cl: e.dma_start(out=dkT_s.rearrange("(h f) t -> f h t", f=128)[:, :, cl:cl + 512], in_=stg_kT[sl].t[:]), reads=[stg_kT[sl].r])
                        else:
                            for a in range(NSPL):
                                S.dma(f"s_kT{sl}", lambda e, sl=sl, cl=cl, a=a: e.dma_start(out=dkT_l[a].rearrange("(h f) t -> f h t", f=128)[:, :, cl:cl + 512], in_=stg_kT[sl].t[:, a * HPS:(a + 1) * HPS, :]), reads=[stg_kT[sl].r])
                        S.dma(f"s_rqT{sl}", lambda e, sl=sl, c0=c0: e.dma_start(out=rqT.rearrange("(h f) t -> f h t", f=128)[:, :, c0:c0 + 512], in_=stg_rqT[sl].t[:]), reads=[stg_rqT[sl].r])
                        S.dma(f"s_rkT{sl}", lambda e, sl=sl, c0=c0: e.dma_start(out=rkT.rearrange("(h f) t -> f h t", f=128)[:, :, c0:c0 + 512], in_=stg_rkT[sl].t[:]), reads=[stg_rkT[sl].r])
                        S.dma(f"s_qdT{sl}", lambda e, sl=sl, c0=c0: e.dma_start(out=rqdT.rearrange("(h f) t -> f h t", f=128)[:, :, c0:c0 + 512], in_=stg_qdT[sl].t[:]), reads=[stg_qdT[sl].r])
                S.barrier()

            RG = [[0, 1, 2, 3], [4, 5, 6, 7]]
            Rkg = Res("dkT_g")
            Rvg = Res("dv_g")
            for a in range(NSPL):
                S.dma("cc_k", lambda e, a=a: e.collective_compute("AllGather", ALU.bypass, replica_groups=RG, ins=[dkT_l[a]], outs=[dkT_g[a]]), writes=[Rkg], eng="pool", inc=1, cost=60.0)
                S.dma("cc_v", lambda e, a=a: e.collective_compute("AllGather", ALU.bypass, replica_groups=RG, ins=[dv_l[a]], outs=[dv_g[a]]), writes=[Rvg], eng="pool", inc=1, cost=60.0)

            with ExitStack() as st:
                SKMAX = max(TS, SKP)
                kTh = [sb(st, f"kTh{i}", [128, SKMAX], BF16) for i in range(2)]
                vh = [sb(st, f"vh{i}", [128, SKMAX // 128, 128], BF16) for i in range(2)]
                qA = [sb(st, f"qA{i}", [128, max(TS, TP)], BF16) for i in range(2)]
                qB = [sb(st, f"qB{i}", [128, max(TS, TP)], BF16) for i in range(2)]
                for i in range(2):
                    S.pool(lambda e, i=i: e.memset(qA[i].t[64:128, :], 0.0), writes=[qA[i].r])
                    S.pool(lambda e, i=i: e.memset(qB[i].t[0:64, :], 0.0), writes=[qB[i].r])
                NPX = 6
                pexp2 = [sb(st, f"pexp{i}", [128, 2, 512], BF16) for i in range(NPX)]
                s01 = [[sb(st, f"s01_{c}{i}", [128, 512], BF16) for i in range(2)] for c in range(2)]
                s23 = [[sb(st, f"s23_{c}{i}", [128, 512], BF16) for i in range(2)] for c in range(2)]
                s4 = [[sb(st, f"s4_{c}{i}", [128, 512], BF16) for i in range(2)] for c in range(2)]
                s8 = [[sb(st, f"s8_{c}{i}", [128, 512], BF16) for i in range(2)] for c in range(2)]
                gcount = 0
                r0 = sb(st, "r0", [128, 512], F32)
                r1 = sb(st, "r1", [128, 512], F32)
                a0 = sb(st, "a0", [128, 512], F32)
                a1 = sb(st, "a1", [128, 512], F32)
                osq = sb(st, "osq", [128, 512], F32)
                rsn = sb(st, "rsn", [128, 512], F32)
                aout = [sb(st, f"aout{i}", [128, 512], BF16) for i in range(2)]
                sbk2 = [psb(st, f"sbk{i}", [128, 2, 512], F32) for i in range(2)]
                O = [psb(st, f"Oacc{i}", [128, 512], F32) for i in range(2)]
                L = [psb(st, f"Lacc{i}", [128, 512], F32) for i in range(2)]
                heads = [(job, h) for job in range(2) for h in range(4)]

                def load_head(hi):
                    job, h = heads[hi]
                    hb = hi % 2
                    kb_, vb_ = kTh[hb], vh[hb]
                    Tq = TS if job == 0 else TP
                    qoff = 0 if job == 0 else TS
                    if job == 0:
                        S.dma(f"b_k{hb}", lambda e, kb_=kb_, h=h: e.dma_start(out=kb_.t[:, 0:TS], in_=dkT_s[h * 128:(h + 1) * 128, :]), writes=[kb_.r])
                        S.dma(f"b_v{hb}", lambda e, vb_=vb_, h=h: e.dma_start(out=vb_.t[:, 0:TS // 128, :], in_=dv_s[:, h * 128:(h + 1) * 128].rearrange("(k p) e -> p k e", p=128)), writes=[vb_.r])
                    else:
                        ha = h // HPS
                        hl = h % HPS
                        S.dma(f"b_k{hb}", lambda e, kb_=kb_, ha=ha, hl=hl: e.dma_start(out=kb_.t[:, 0:SKP].rearrange("p (r t) -> p r t", r=NR), in_=dkT_g[ha].rearrange("(r f) t -> f r t", f=HPS * 128)[hl * 128:(hl + 1) * 128, :, :]), reads=[Rkg], writes=[kb_.r])
                        for a in range(NSPL):
                            for r_ in range(NR):
                                S.dma(f"b_v{hb}", lambda e, vb_=vb_, h=h, a=a, r_=r_: e.dma_start(
                                    out=vb_.t[:, (r_ * TP + a * TPS) // 128:(r_ * TP + (a + 1) * TPS) // 128, :],
                                    in_=dv_g[a][r_ * TPS:(r_ + 1) * TPS, h * 128:(h + 1) * 128].rearrange("(k p) e -> p k e", p=128)), reads=[Rvg], writes=[vb_.r])
                    S.dma(f"b_q{hb}", lambda e, hb=hb, h=h, qoff=qoff, Tq=Tq: e.dma_start(out=qA[hb].t[0:64, 0:Tq], in_=dqT[h * 128:h * 128 + 64, qoff:qoff + Tq]), writes=[qA[hb].r])
                    S.dma(f"b_r{hb}", lambda e, hb=hb, h=h, qoff=qoff, Tq=Tq: e.dma_start(out=qB[hb].t[64:128, 0:Tq], in_=dqT[h * 128 + 64:h * 128 + 128, qoff:qoff + Tq]), writes=[qB[hb].r])

                units = []
                for hi, (job, h) in enumerate(heads):
                    Tq = TS if job == 0 else TP
                    Sk = TS if job == 0 else SKP
                    for qc in range(Tq // 512):
                        for kb in range(Sk // 128):
                            units.append((hi, qc, kb, Sk // 128))

                def emit_qk(u):
                    hi, qc, kb, nkb = units[u]
                    hb = hi % 2
                    kb_ = kTh[hb]
                    sbuf_ = sbk2[u % 2]
                    for c in range(2):
                        qb_ = (qA if c == 0 else qB)[hb]
                        S.pe(lambda e, sbuf_=sbuf_, c=c, kb=kb, qc=qc, kb_=kb_, qb_=qb_: e.matmul(sbuf_.t[:, c, :], lhsT=kb_.t[:, kb * 128:(kb + 1) * 128],
                                                                                             rhs=qb_.t[:, qc * 512:(qc + 1) * 512], start=True, stop=True),
                             reads=[kb_.r, qb_.r], writes=[sbuf_.r])

                ocount = 0
                load_head(0)
                emit_qk(0)
                emit_qk(1)
                for u in range(len(units)):
                    hi, qc, kb, nkb = units[u]
                    job, h = heads[hi]
                    hb = hi % 2
                    vb_ = vh[hb]
                    qoff = 0 if job == 0 else TS
                    if qc == 0 and kb == 0 and hi + 1 < len(heads):
                        load_head(hi + 1)
                    pe2 = pexp2[u % NPX]
                    s2_ = sbk2[u % 2]
                    S.act(lambda e, pe2=pe2, s2_=s2_: e.activation(out=pe2.t[:], in_=s2_.t[:], func=AF.Exp), reads=[s2_.r], writes=[pe2.r], cost=1.05)
                    if u + 2 < len(units):
                        if units[u + 2][0] != hi and units[u + 2][1] == 0 and units[u + 2][2] == 0 and units[u + 2][0] + 1 < len(heads):
                            pass
                        emit_qk(u + 2)
                    gp = gcount % 2
                    for c in range(2):
                        pe_ = pexp2[u % NPX]
                        S.pe(lambda e, pe_=pe_, c=c, kb=kb, vb_=vb_, nkb=nkb: e.matmul(O[c].t[:], lhsT=vb_.t[:, kb, :], rhs=pe_.t[:, c, :], start=(kb == 0), stop=(kb == nkb - 1)),
                             reads=[vb_.r, pe_.r], writes=[O[c].r])
                        if kb % 2 == 1:
                            pp_ = pexp2[(u - 1) % NPX]
                            dst = (s01 if kb % 4 == 1 else s23)[c][gp]
                            S.dve(lambda e, dst=dst, pp_=pp_, pe_=pe_, c=c: e.tensor_tensor(out=dst.t[:], in0=pp_.t[:, c, :], in1=pe_.t[:, c, :], op=ALU.add), reads=[pp_.r, pe_.r], writes=[dst.r], cost=0.3)
                        if kb % 4 == 3:
                            a_, b_, d_ = s01[c][gp], s23[c][gp], s4[c][gp]
                            S.dve(lambda e, a_=a_, b_=b_, d_=d_: e.tensor_tensor(out=d_.t[:], in0=a_.t[:], in1=b_.t[:], op=ALU.add), reads=[a_.r, b_.r], writes=[d_.r], cost=0.3)
                        if kb % 8 == 7:
                            a_, b_, d_ = s4[c][gp ^ 1], s4[c][gp], s8[c][(gcount // 2) % 2]
                            S.dve(lambda e, a_=a_, b_=b_, d_=d_: e.tensor_tensor(out=d_.t[:], in0=a_.t[:], in1=b_.t[:], op=ALU.add), reads=[a_.r, b_.r], writes=[d_.r], cost=0.3)
                            S.pe(lambda e, d_=d_, c=c, kb=kb, nkb=nkb: e.matmul(L[c].t[:], lhsT=ones_b.t[:], rhs=d_.t[:], start=(kb == 7), stop=(kb == nkb - 1)),
                                 reads=[ones_b.r, d_.r], writes=[L[c].r])
                    if kb % 4 == 3:
                        gcount += 1
                    if kb != nkb - 1:
                        continue
                    S.act(lambda e: e.activation(out=a0.t[:], in_=O[0].t[:], func=AF.Copy), reads=[O[0].r], writes=[a0.r])
                    S.dve(lambda e: e.tensor_copy(out=a1.t[:], in_=O[1].t[:]), reads=[O[1].r], writes=[a1.r])
                    S.dve(lambda e: e.reciprocal(out=r0.t[:], in_=L[0].t[:]), reads=[L[0].r], writes=[r0.r])
                    S.dve(lambda e: e.reciprocal(out=r1.t[:], in_=L[1].t[:]), reads=[L[1].r], writes=[r1.r])
                    S.dve(lambda e: e.tensor_tensor(out=a0.t[:], in0=a0.t[:], in1=r0.t[:], op=ALU.mult), reads=[a0.r, r0.r], writes=[a0.r])
                    S.dve(lambda e: e.tensor_tensor(out=a1.t[:], in0=a1.t[:], in1=r1.t[:], op=ALU.mult), reads=[a1.r, r1.r], writes=[a1.r])
                    S.dve(lambda e, l=l: e.scalar_tensor_tensor(out=a0.t[:], in0=a1.t[:], scalar=neglam.t[:, l:l + 1], in1=a0.t[:], op0=ALU.mult, op1=ALU.add),
                          reads=[a1.r, a0.r, neglam.r], writes=[a0.r])
                    S.act(lambda e: e.activation(out=osq.t[:], in_=a0.t[:], func=AF.Square), reads=[a0.r], writes=[osq.r])
                    sB = L[0]
                    S.pe(lambda e, sB=sB: e.matmul(sB.t[:], lhsT=ones_f.t[:], rhs=osq.t[:], start=True, stop=True), reads=[ones_f.r, osq.r], writes=[sB.r])
                    rsqrt_act(rsn.t[:], sB.t[:], 1.0 / 128, sB, rsn)
                    ao = aout[ocount % 2]
                    S.dve(lambda e, ao=ao: e.scalar_tensor_tensor(out=ao.t[:], in0=a0.t[:], scalar=subw.t[:, 0:1], in1=rsn.t[:], op0=ALU.mult, op1=ALU.mult),
                          reads=[a0.r, subw.r, rsn.r], writes=[ao.r])
                    tcol = qoff + qc * 512
                    S.dma(f"b_o{ocount % 2}", lambda e, ao=ao, h=h, tcol=tcol: e.dma_start(out=aT[h * 128:(h + 1) * 128, tcol:tcol + 512], in_=ao.t[:]), reads=[ao.r], eng="act")
                    ocount += 1
                S.barrier()

            with ExitStack() as st:
                SallJ = [sb(st, "Sall0", [128, TS // 128, 512], BF16), sb(st, "Sall1", [128, TP // 128, 512], BF16)]
                sttJ = [sb(st, f"stt{i}", [128, 512], F32) for i in range(2)]
                sttmpJ = [sb(st, f"sttmp{i}", [128, 512], F32) for i in range(2)]
                tg = sb(st, "tg", [128, NR, 512], F32)
                kdl = [sb(st, f"kdl{i}", [128, 4, 1024], BF16) for i in range(4)]
                rvl = [sb(st, f"rvl{i}", [128, 4, 512], BF16) for i in range(4)]
                okT = [sb(st, f"okT{i}", [128, 4, 512], BF16) for i in range(2)]
                oqT = [sb(st, f"oqT{i}", [128, 4, 512], BF16) for i in range(2)]
                oqd = [sb(st, f"oqd{i}", [128, 8, 512], BF16) for i in range(2)]
                orv = [sb(st, f"orv{i}", [128, 4, 512], BF16) for i in range(2)]
                ogs = [sb(st, f"ogs{i}", [128, 4, 512], BF16) for i in range(2)]
                PT = [sb(st, f"PT{i}", [128, 8, 128], BF16) for i in range(2)]
                rsq = sb(st, "rsq", [128, 512], F32)
                rss = sb(st, "rss", [128, 8], F32)
                rn = sb(st, "rn", [128, 512], F32)
                rr = sb(st, "rr", [128, 512], BF16)
                rTst = [sb(st, f"rTst{i}", [128, 4, 512], BF16) for i in range(2)]
                pkvJ = [psb(st, f"pkv{i}", [128, 512], F32) for i in range(2)]
                psc = [psb(st, f"psc{i}", [128, 4, 128], F32) for i in range(2)]
                po = [psb(st, f"pro{i}", [128, 512], F32) for i in range(2)]
                ptr = psb(st, "ptr", [128, 8, 128], BF16)
                Rtg = Res("st_g")
                ldn = [0]

                def sweep(job, use_init):
                    stt, sttmp, Sall = sttJ[job], sttmpJ[job], SallJ[job]
                    Tq = TS if job == 0 else TP
                    qoff = 0 if job == 0 else TS
                    n = Tq // 128
                    nsc = n // 4
                    if use_init:
                        S.dma("r_tg", lambda e: e.dma_start(out=tg.t[:], in_=st_g.rearrange("(r p) c -> p r c", p=128)), reads=[Rtg], writes=[tg.r])
                        for r_ in range(NR):
                            cb = coef.t[:, r_, :].unsqueeze(2).to_broadcast([128, 8, 64])
                            tg3 = tg.t[:, r_, :].rearrange("p (h e) -> p h e", e=64)
                            if r_ == 0:
                                S.dve(lambda e, cb=cb, tg3=tg3: e.tensor_tensor(out=stt.t[:].rearrange("p (h e) -> p h e", e=64), in0=tg3, in1=cb, op=ALU.mult), reads=[tg.r, coef.r], writes=[stt.r])
                            else:
                                S.dve(lambda e, cb=cb, tg3=tg3: e.tensor_tensor(out=sttmp.t[:].rearrange("p (h e) -> p h e", e=64), in0=tg3, in1=cb, op=ALU.mult), reads=[tg.r, coef.r], writes=[sttmp.r])
                                S.dve(lambda e: e.tensor_tensor(out=stt.t[:], in0=stt.t[:], in1=sttmp.t[:], op=ALU.add), reads=[stt.r, sttmp.r], writes=[stt.r])
                    else:
                        S.dve(lambda e: e.memset(stt.t[:], 0.0), writes=[stt.r])
                    cur = {}
                    for t in range(n):
                        tf = t
                        tb = n - 1 - t
                        bufs = {}
                        for nm, ti in (("f", tf), ("b", tb)):
                            sc = ti // 4
                            if (nm, sc) not in cur:
                                slot = job * 2 + (0 if nm == "f" else 1)
                                c0 = qoff + sc * 512
                                S.dma(f"r_kd{slot}", lambda e, slot=slot, c0=c0: e.dma_start(out=kdl[slot].t[:], in_=rkd[c0:c0 + 512, :].rearrange("(j p) c -> p j c", p=128)), writes=[kdl[slot].r])
                                S.dma(f"r_rv{slot}", lambda e, slot=slot, c0=c0: e.dma_start(out=rvl[slot].t[:], in_=rv[c0:c0 + 512, :].rearrange("(j p) c -> p j c", p=128)), writes=[rvl[slot].r])
                                cur = {k: v for k, v in cur.items() if k[0] != nm}
                                cur[(nm, sc)] = slot
                            bufs[nm] = (cur[(nm, sc)], ti % 4)
                        S.act(lambda e, tf=tf: e.activation(out=Sall.t[0:64, tf, :], in_=stt.t[0:64, :], func=AF.Copy), reads=[stt.r], writes=[Sall.r])
                        S.act(lambda e, tb=tb: e.activation(out=Sall.t[64:128, tb, :], in_=stt.t[64:128, :], func=AF.Copy), reads=[stt.r], writes=[Sall.r])
                        pk = pkvJ[job]
                        (sf, jf), (sb_, jb) = bufs["f"], bufs["b"]
                        for h in range(8):
                            kf = kdl[sf].t[:, jf, :].rearrange("p (g a d) -> p g a d", g=8, a=2)
                            kbk = kdl[sb_].t[:, jb, :].rearrange("p (g a d) -> p g a d", g=8, a=2)
                            S.pe(lambda e, pk=pk, h=h, kf=kf, sf=sf, jf=jf: e.matmul(pk.t[0:64, h * 64:(h + 1) * 64], lhsT=kf[:, h, 0, :], rhs=rvl[sf].t[:, jf, h * 64:(h + 1) * 64], start=True, stop=True),
                                 reads=[kdl[sf].r, rvl[sf].r], writes=[pk.r])
                            S.pe(lambda e, pk=pk, h=h, kbk=kbk, sb_=sb_, jb=jb: e.matmul(pk.t[64:128, h * 64:(h + 1) * 64], lhsT=kbk[:, h, 1, :], rhs=rvl[sb_].t[:, jb, h * 64:(h + 1) * 64], start=True, stop=True),
                                 reads=[kdl[sb_].r, rvl[sb_].r], writes=[pk.r])
                        S.dve(lambda e: e.tensor_tensor(out=sttmp.t[:].rearrange("p (h e) -> p h e", e=64), in0=stt.t[:].rearrange("p (h e) -> p h e", e=64),
                                                        in1=cdec.t[:].unsqueeze(2).to_broadcast([128, 8, 64]), op=ALU.mult), reads=[stt.r, cdec.r], writes=[sttmp.r])
                        S.dve(lambda e, pk=pk: e.tensor_tensor(out=stt.t[:], in0=pk.t[:], in1=sttmp.t[:], op=ALU.add), reads=[pk.r, sttmp.r], writes=[stt.r])

                def outputs(job):
                    Sall = SallJ[job]
                    Tq = TS if job == 0 else TP
                    qoff = 0 if job == 0 else TS
                    n = Tq // 128
                    for sc in range(n // 4):
                        sl = sc % 2
                        c0 = qoff + sc * 512
                        S.dma(f"o_kT{sl}", lambda e, sl=sl, c0=c0: e.dma_start(out=okT[sl].t[:], in_=rkT.rearrange("(b p) t -> p b t", p=128)[:, :, c0:c0 + 512]), writes=[okT[sl].r])
                        S.dma(f"o_qT{sl}", lambda e, sl=sl, c0=c0: e.dma_start(out=oqT[sl].t[:], in_=rqT.rearrange("(b p) t -> p b t", p=128)[:, :, c0:c0 + 512]), writes=[oqT[sl].r])
                        S.dma(f"o_qd{sl}", lambda e, sl=sl, c0=c0: e.dma_start(out=oqd[sl].t[:], in_=rqdT.rearrange("(b p) t -> p b t", p=128)[:, :, c0:c0 + 512]), writes=[oqd[sl].r])
                        S.dma(f"o_rv{sl}", lambda e, sl=sl, c0=c0: e.dma_start(out=orv[sl].t[:], in_=rv[c0:c0 + 512, :].rearrange("(j p) c -> p j c", p=128)), writes=[orv[sl].r])
                        S.dma(f"o_gs{sl}", lambda e, sl=sl, c0=c0: e.dma_start(out=ogs[sl].t[:], in_=rgs[c0:c0 + 512, :].rearrange("(j p) c -> p j c", p=128)), writes=[ogs[sl].r])
                        for j in range(4):
                            i = sc * 4 + j
                            ptb = PT[j % 2]
                            for h in range(8):
                                pb_ = psc[h % 2]
                                hp = (h % 2) * 64
                                S.pe(lambda e, pb_=pb_, h=h, hp=hp, sl=sl, j=j: e.matmul(pb_.t[:, h // 2, :], lhsT=okT[sl].t[hp:hp + 64, h // 2, j * 128:(j + 1) * 128],
                                                                                   rhs=oqT[sl].t[hp:hp + 64, h // 2, j * 128:(j + 1) * 128], start=True, stop=True),
                                     reads=[okT[sl].r, oqT[sl].r], writes=[pb_.r])
                            pt4 = ptb.t[:].rearrange("p (b a) n -> p b a n", a=2)
                            S.dve(lambda e, pt4=pt4: e.tensor_tensor(out=pt4[:, :, 0, :], in0=psc[0].t[:], in1=DTe.t[:], op=ALU.mult), reads=[psc[0].r, DTe.r], writes=[ptb.r])
                            S.dve(lambda e, pt4=pt4: e.tensor_tensor(out=pt4[:, :, 1, :], in0=psc[1].t[:], in1=DTo.t[:], op=ALU.mult), reads=[psc[1].r, DTo.r, ptb.r], writes=[ptb.r])
                            pob = po[j % 2]
                            for h in range(8):
                                S.pe(lambda e, pob=pob, h=h, ptb=ptb, sl=sl, j=j: e.matmul(pob.t[:, h * 64:(h + 1) * 64], lhsT=ptb.t[:, h, :], rhs=orv[sl].t[:, j, h * 64:(h + 1) * 64], start=True, stop=False),
                                     reads=[ptb.r, orv[sl].r], writes=[pob.r])
                                S.pe(lambda e, pob=pob, h=h, sl=sl, j=j, i=i: e.matmul(pob.t[:, h * 64:(h + 1) * 64], lhsT=oqd[sl].t[:, h, j * 128:(j + 1) * 128], rhs=Sall.t[:, i, h * 64:(h + 1) * 64], start=False, stop=True),
                                     reads=[oqd[sl].r, Sall.r], writes=[pob.r])
                            S.act(lambda e, pob=pob: e.activation(out=rsq.t[:], in_=pob.t[:], func=AF.Square), reads=[pob.r], writes=[rsq.r])
                            S.dve(lambda e: e.tensor_reduce(out=rss.t[:], in_=rsq.t[:].rearrange("p (g d) -> p g d", d=64), axis=AX.X, op=ALU.add), reads=[rsq.r], writes=[rss.r])
                            rsqrt_act(rss.t[:], rss.t[:], 1.0 / 64, rss, rss)
                            S.dve(lambda e, pob=pob: e.tensor_tensor(out=rn.t[:].rearrange("p (g d) -> p g d", d=64), in0=pob.t[:].rearrange("p (g d) -> p g d", d=64),
                                                                    in1=rss.t[:].unsqueeze(2).to_broadcast([128, 8, 64]), op=ALU.mult), reads=[pob.r, rss.r], writes=[rn.r])
                            S.dve(lambda e: e.tensor_tensor(out=rn.t[:].rearrange("p (g d) -> p g d", d=64), in0=rn.t[:].rearrange("p (g d) -> p g d", d=64),
                                                            in1=gnw.t[:].unsqueeze(1).to_broadcast([128, 8, 64]), op=ALU.mult), reads=[rn.r, gnw.r], writes=[rn.r])
                            S.dve(lambda e, sl=sl, j=j: e.tensor_tensor(out=rr.t[:], in0=rn.t[:], in1=ogs[sl].t[:, j, :], op=ALU.mult), reads=[rn.r, ogs[sl].r], writes=[rr.r])
                            for c in range(4):
                                S.pe(lambda e, c=c: e.transpose(out=ptr.t[:, c, :], in_=rr.t[:, c * 128:(c + 1) * 128], identity=ident.t[:]), reads=[rr.r, ident.r], writes=[ptr.r])
                            S.act(lambda e, sl=sl, j=j: e.activation(out=rTst[sl].t[:, :, j * 128:(j + 1) * 128], in_=ptr.t[:, 0:4, :], func=AF.Copy), reads=[ptr.r], writes=[rTst[sl].r])
                        S.dma(f"o_rT{sl}", lambda e, sl=sl, c0=c0: e.dma_start(out=rT.rearrange("(b p) t -> p b t", p=128)[:, :, c0:c0 + 512], in_=rTst[sl].t[:]), reads=[rTst[sl].r])

                sweep(1, False)
                Rstl = Res("st_l")
                S.dma("r_stl", lambda e: e.dma_start(out=st_l, in_=sttJ[1].t[:]), reads=[sttJ[1].r], writes=[Rstl])
                S.dma("cc_s", lambda e: e.collective_compute("AllGather", ALU.bypass, replica_groups=RG, ins=[st_l], outs=[st_g]), reads=[Rstl], writes=[Rtg], eng="pool", inc=1, cost=250.0)
                sweep(0, False)
                outputs(0)
                sweep(1, True)
                outputs(1)
                S.barrier()

            with ExitStack() as st:
                w_out = load_w(st, "w_out", w_out_in[l], D, D, "w_out")
                w1 = load_w(st, "w1", w1_in[l], D, DFF, "w1")
                arT = [sb(st, f"arT{i}", [128, 8, 512], BF16) for i in range(2)]
                xs_ = [sb(st, f"xs{i}", [128, 4, D], F32) for i in range(2)]
                junk = sb(st, "junk2", [128, D], BF16)
                ss2 = sb(st, "ss2", [128, 1], F32)
                h2l = [sb(st, f"h2_{i}", [128, D], BF16) for i in range(2)]
                h2T = [sb(st, f"h2T{i}", [128, 8, 512], BF16) for i in range(1)] * 2
                rl = [sb(st, f"rl{i}", [128, 512], F32) for i in range(2)]
                uTs = [sb(st, f"uTs{i}", [128, 32, 512], BF16) for i in range(1)] * 2
                pw = [psb(st, f"pw{i}", [128, 512], F32) for i in range(2)]
                ph = psb(st, "ph", [128, 8, 128], BF16)
                pu = [psb(st, f"pu{i}", [128, 512], F32) for i in range(4)]
                for s in range(NST):
                    sl = s % 2
                    c0 = s * 512
                    S.dma(f"c_a{sl}", lambda e, sl=sl, c0=c0: e.dma_start(out=arT[sl].t[:, 0:4, :], in_=aT.rearrange("(b p) t -> p b t", p=128)[:, :, c0:c0 + 512]), writes=[arT[sl].r])
                    S.dma(f"c_r{sl}", lambda e, sl=sl, c0=c0: e.dma_start(out=arT[sl].t[:, 4:8, :], in_=rT.rearrange("(b p) t -> p b t", p=128)[:, :, c0:c0 + 512]), writes=[arT[sl].r])
                    S.dma(f"c_x{sl}", lambda e, sl=sl, c0=c0: e.dma_start(out=xs_[sl].t[:], in_=x_src[c0:c0 + 512, :].rearrange("(j p) c -> p j c", p=128)), writes=[xs_[sl].r])
                    for j in range(4):
                        for cb in range(2):
                            pb_ = pw[cb]
                            for k in range(8):
                                S.pe(lambda e, pb_=pb_, k=k, cb=cb, sl=sl, j=j: e.matmul(pb_.t[:], lhsT=arT[sl].t[:, k, j * 128:(j + 1) * 128], rhs=w_out.t[:, k, cb * 512:(cb + 1) * 512], start=(k == 0), stop=(k == 7)),
                                     reads=[arT[sl].r, w_out.r(k, cb * 512)], writes=[pb_.r])
                            S.dve(lambda e, pb_=pb_, cb=cb, sl=sl, j=j: e.tensor_tensor(out=xs_[sl].t[:, j, cb * 512:(cb + 1) * 512], in0=pb_.t[:], in1=xs_[sl].t[:, j, cb * 512:(cb + 1) * 512], op=ALU.add),
                                  reads=[pb_.r, xs_[sl].r], writes=[xs_[sl].r])
                        S.act(lambda e, sl=sl, j=j: e.activation(out=junk.t[:], in_=xs_[sl].t[:, j, :], func=AF.Square, accum_out=ss2.t[:]), reads=[xs_[sl].r], writes=[junk.r, ss2.r])
                        rsqrt_act(ss2.t[:], ss2.t[:], 1.0 / D, ss2, ss2)
                        h2 = h2l[j % 2]
                        S.dve(lambda e, sl=sl, j=j, h2=h2: e.scalar_tensor_tensor(out=h2.t[:], in0=xs_[sl].t[:, j, :], scalar=ss2.t[:, 0:1], in1=ln2b.t[:], op0=ALU.mult, op1=ALU.mult),
                              reads=[xs_[sl].r, ss2.r, ln2b.r], writes=[h2.r])
                        for c in range(8):
                            S.pe(lambda e, c=c, h2=h2: e.transpose(out=ph.t[:, c, :], in_=h2.t[:, c * 128:(c + 1) * 128], identity=ident.t[:]), reads=[h2.r, ident.r], writes=[ph.r])
                        S.act(lambda e, sl=sl, j=j: e.activation(out=h2T[sl].t[:, :, j * 128:(j + 1) * 128], in_=ph.t[:], func=AF.Copy), reads=[ph.r], writes=[h2T[sl].r])
                    S.dma(f"c_xo{sl}", lambda e, sl=sl, c0=c0: e.dma_start(out=xres[c0:c0 + 512, :].rearrange("(j p) c -> p j c", p=128), in_=xs_[sl].t[:]), reads=[xs_[sl].r], eng="act")
                    for fc in range(32):
                        pb_ = pu[fc % 4]
                        for k in range(8):
                            S.pe(lambda e, pb_=pb_, k=k, fc=fc, sl=sl: e.matmul(pb_.t[:], lhsT=w1.t[:, k, fc * 128:(fc + 1) * 128], rhs=h2T[sl].t[:, k, :], start=(k == 0), stop=(k == 7)),
                                 reads=[w1.r(k, fc * 128), h2T[sl].r], writes=[pb_.r])
                        rb = rl[fc % 2]
                        S.act(lambda e, pb_=pb_, rb=rb: e.activation(out=rb.t[:], in_=pb_.t[:], func=AF.Relu), reads=[pb_.r], writes=[rb.r])
                        S.dve(lambda e, rb=rb, fc=fc, sl=sl: e.tensor_tensor(out=uTs[sl].t[:, fc, :], in0=rb.t[:], in1=rb.t[:], op=ALU.mult), reads=[rb.r], writes=[uTs[sl].r])
                    S.dma(f"c_u{sl}", lambda e, sl=sl, c0=c0: e.dma_start(out=uT.rearrange("(c p) t -> p c t", p=128)[:, :, c0:c0 + 512], in_=uTs[sl].t[:]), reads=[uTs[sl].r], eng="act")
                S.barrier()

            with ExitStack() as st:
                w2 = load_w(st, "w2", w2_in[l], DFF, D, "w2", gn=1024, korder=True)
                wg = load_w(st, "wg", wg_in[l], D, D, "wg")
                wp = load_w(st, "wp", wp_in[l], PLE, D, "wp")
                uTl = [sb(st, f"uTl{i}", [128, 32, 512], BF16) for i in range(2)]
                xsj = [sb(st, f"xc{i}", [128, D], F32) for i in range(4)]
                pl_ = [sb(st, f"pl{i}", [128, 4, PLE], F32) for i in range(1)] * 2
                x2bl = [sb(st, f"x2b{i}", [128, D], BF16) for i in range(2)]
                x2Tl = [sb(st, f"x2T{i}", [128, 8, 128], BF16) for i in range(2)]
                pbfl = [sb(st, f"pbf{i}", [128, PLE], BF16) for i in range(2)]
                ppTl = [sb(st, f"ppT{i}", [128, 2, 128], BF16) for i in range(2)]
                sg = [sb(st, f"sg{i}", [128, 512], F32) for i in range(2)]
                pm = [psb(st, f"pm{i}", [128, 512], F32) for i in range(2)]
                pg = [psb(st, f"pg{i}", [128, 512], F32) for i in range(2)]
                pq = [psb(st, f"pq{i}", [128, 512], F32) for i in range(2)]
                px = psb(st, "px", [128, 8, 128], BF16)
                pp2 = psb(st, "pp2", [128, 8, 128], BF16)
                for s in range(NST):
                    sl = s % 2
                    c0 = s * 512
                    S.dma(f"d_u{sl}", lambda e, sl=sl, c0=c0: e.dma_start(out=uTl[sl].t[:], in_=uT.rearrange("(c p) t -> p c t", p=128)[:, :, c0:c0 + 512]), writes=[uTl[sl].r])
                    for j in range(4):
                        S.dma(f"d_x{j}", lambda e, j=j, c0=c0: e.dma_start(out=xsj[j].t[:], in_=xres[c0 + j * 128:c0 + (j + 1) * 128, :]), writes=[xsj[j].r])
                    S.dma("d_p", lambda e, sl=sl, c0=c0, l=l: e.dma_start(out=pl_[sl].t[:], in_=p_in[l, c0:c0 + 512, :].rearrange("(j p) c -> p j c", p=128)), writes=[pl_[sl].r])
                    for j in range(4):
                        for cb in range(2):
                            pb_ = pm[cb]
                            for fc in range(32):
                                S.pe(lambda e, pb_=pb_, fc=fc, cb=cb, sl=sl, j=j: e.matmul(pb_.t[:], lhsT=uTl[sl].t[:, fc, j * 128:(j + 1) * 128], rhs=w2.t[:, fc, cb * 512:(cb + 1) * 512], start=(fc == 0), stop=(fc == 31)),
                                     reads=[uTl[sl].r, w2.r(fc, cb * 512)], writes=[pb_.r])
                            S.dve(lambda e, pb_=pb_, cb=cb, j=j: e.tensor_tensor(out=xsj[j].t[:, cb * 512:(cb + 1) * 512], in0=pb_.t[:], in1=xsj[j].t[:, cb * 512:(cb + 1) * 512], op=ALU.add),
                                  reads=[pb_.r, xsj[j].r], writes=[xsj[j].r])
                        x2b, x2T, pbf, ppT = x2bl[j % 2], x2Tl[j % 2], pbfl[j % 2], ppTl[j % 2]
                        S.act(lambda e, j=j, x2b=x2b: e.activation(out=x2b.t[:], in_=xsj[j].t[:], func=AF.Copy), reads=[xsj[j].r], writes=[x2b.r], cost=1.1)
                        for c in range(8):
                            S.pe(lambda e, c=c, x2b=x2b: e.transpose(out=px.t[:, c, :], in_=x2b.t[:, c * 128:(c + 1) * 128], identity=ident.t[:]), reads=[x2b.r, ident.r], writes=[px.r])
                        S.act(lambda e, x2T=x2T: e.activation(out=x2T.t[:], in_=px.t[:], func=AF.Copy), reads=[px.r], writes=[x2T.r], cost=1.1)
                        S.pool(lambda e, sl=sl, j=j, pbf=pbf: e.tensor_copy(out=pbf.t[:], in_=pl_[sl].t[:, j, :]), reads=[pl_[sl].r], writes=[pbf.r])
                        for c in range(2):
                            S.pe(lambda e, c=c, pbf=pbf: e.transpose(out=pp2.t[:, c, :], in_=pbf.t[:, c * 128:(c + 1) * 128], identity=ident.t[:]), reads=[pbf.r, ident.r], writes=[pp2.r])
                        S.act(lambda e, ppT=ppT: e.activation(out=ppT.t[:], in_=pp2.t[:, 0:2, :], func=AF.Copy), reads=[pp2.r], writes=[ppT.r])
                        for cb in range(2):
                            g_ = pg[cb]
                            q_ = pq[cb]
                            for k in range(8):
                                S.pe(lambda e, g_=g_, k=k, cb=cb: e.matmul(g_.t[:], lhsT=x2T.t[:, k, :], rhs=wg.t[:, k, cb * 512:(cb + 1) * 512], start=(k == 0), stop=(k == 7)),
                                     reads=[x2T.r, wg.r(k, cb * 512)], writes=[g_.r])
                            for k in range(2):
                                S.pe(lambda e, q_=q_, k=k, cb=cb: e.matmul(q_.t[:], lhsT=ppT.t[:, k, :], rhs=wp.t[:, k, cb * 512:(cb + 1) * 512], start=(k == 0), stop=(k == 1)),
                                     reads=[ppT.r, wp.r(k, cb * 512)], writes=[q_.r])
                            sgb = sg[cb]
                            S.act(lambda e, g_=g_, sgb=sgb: e.activation(out=sgb.t[:], in_=g_.t[:], func=AF.Exp, scale=-1.0), reads=[g_.r], writes=[sgb.r])
                            S.dve(lambda e, sgb=sgb: e.tensor_scalar(out=sgb.t[:], in0=sgb.t[:], scalar1=1.0, scalar2=None, op0=ALU.add), reads=[sgb.r], writes=[sgb.r])
                            S.dve(lambda e, sgb=sgb: e.reciprocal(out=sgb.t[:], in_=sgb.t[:]), reads=[sgb.r], writes=[sgb.r])
                            S.dve(lambda e, sgb=sgb, q_=q_: e.tensor_tensor(out=sgb.t[:], in0=q_.t[:], in1=sgb.t[:], op=ALU.mult), reads=[q_.r, sgb.r], writes=[sgb.r])
                            S.dve(lambda e, sgb=sgb, cb=cb, j=j: e.tensor_tensor(out=xsj[j].t[:, cb * 512:(cb + 1) * 512], in0=sgb.t[:], in1=xsj[j].t[:, cb * 512:(cb + 1) * 512], op=ALU.add),
                                  reads=[sgb.r, xsj[j].r], writes=[xsj[j].r])
                        S.dma(f"d_xo{j}", lambda e, j=j, c0=c0: e.dma_start(out=x_dst[c0 + j * 128:c0 + (j + 1) * 128, :], in_=xsj[j].t[:]), reads=[xsj[j].r], eng="act")
                S.barrier()

        if DEBUG:
            for nm, src in (("rgs", rgs), ("rv", rv), ("dv_s", dv_s), ("dqT", dqT), ("aT", aT), ("rT", rT), ("xres", xres), ("uT", uT), ("rqT", rqT), ("rkd", rkd), ("rqdT", rqdT), ("rkT", rkT), ("dkT_s", dkT_s)):
                dbg = dram("dbg_" + nm, list(src.shape), src.dtype, "ExternalOutput")
                S.dma("dbg", lambda e, dbg=dbg, src=src: e.dma_start(out=dbg, in_=src))
        S.emit(top)
        nc._n_ops = len(S.ops)
        nc._n_sem = S.nsem
    return nc


ROPE_THETA = 500000.0
RET_THETA = 10000.0


def host_tables(TS, TP, rank):
    posv = np.concatenate([np.arange(TS), rank * TP + np.arange(TP)]).astype(np.float32)
    inv_d = (np.float32(1.0) / (np.float32(ROPE_THETA) ** (np.arange(0, 16, 2, dtype=np.float32) / np.float32(16)))).astype(np.float32)
    inv_r = (np.float32(1.0) / (np.float32(RET_THETA) ** (np.arange(0, 64, 2, dtype=np.float32) / np.float32(64)))).astype(np.float32)
    ang_d = (posv[:, None] * inv_d[None, :]).astype(np.float32)
    ang_r = (posv[:, None] * inv_r[None, :]).astype(np.float32)
    pos = np.concatenate([np.cos(ang_d), np.sin(ang_d), np.cos(ang_r), np.sin(ang_r)], axis=1).astype(np.float32)
    m = np.arange(128)[:, None].astype(np.float32)
    n = np.arange(128)[None, :].astype(np.float32)
    relF = np.maximum(n - m, 0.0)
    mskF = (n >= m).astype(np.float32) * 0.125
    relB = np.maximum(m - n, 0.0)
    mskB = (m > n).astype(np.float32) * 0.125
    p = np.arange(128, dtype=np.float32)[:, None]
    cst = np.concatenate([relF, mskF, relB, mskB, p + 1, 128 - p, 127 - p, p], axis=1).astype(np.float32)
    rkt = np.zeros((128, 8), np.float32)
    for r in range(NR):
        if r < rank:
            rkt[0:64, r] = TP * (rank - 1 - r)
            rkt[0:64, 4 + r] = 1.0
        if r > rank:
            rkt[64:128, r] = TP * (r - rank - 1)
            rkt[64:128, 4 + r] = 1.0
    return pos, cst, rkt


_NC_CACHE = {}


def run_cores(inputs, TS, TP, DEPTH):
    key = (TS, TP, DEPTH)
    if key not in _NC_CACHE:
        _NC_CACHE[key] = build(TS, TP, DEPTH)
    nc = _NC_CACHE[key]
    f = lambda a: np.ascontiguousarray(np.asarray(a, dtype=np.float32))
    xp = f(inputs["x_prompt"])
    xs = f(inputs["x_sample"])
    pp = f(inputs["p_prompt"])
    ps = f(inputs["p_sample"])
    shared = {
        "ln1_w": f(inputs["ln1_w"]), "w_in": f(inputs["w_in"]), "diff_q_norm": f(inputs["diff_q_norm"]),
        "diff_k_norm": f(inputs["diff_k_norm"]), "diff_lambda": f(inputs["diff_lambda"]).reshape(1, DEPTH * 256),
        "diff_subln": f(inputs["diff_subln"]), "ret_decay_logit": f(inputs["ret_decay_logit"]).reshape(1, DEPTH * 16),
        "ret_gn": f(inputs["ret_gn"]), "w_out": f(inputs["w_out"]), "ln2_w": f(inputs["ln2_w"]),
        "w_mlp1": f(inputs["w_mlp1"]), "w_mlp2": f(inputs["w_mlp2"]), "w_ple_gate": f(inputs["w_ple_gate"]),
        "w_ple_proj": f(inputs["w_ple_proj"]),
    }
    in_maps = []
    for c in range(8):
        g, r = c // NR, c % NR
        pos, cst, rkt = host_tables(TS, TP, r)
        m = dict(shared)
        m["x"] = np.ascontiguousarray(np.concatenate([xs[c], xp[g, r * TP:(r + 1) * TP]], axis=0))
        m["p"] = np.ascontiguousarray(np.concatenate([ps[:, c], pp[:, g, r * TP:(r + 1) * TP]], axis=1))
        m["pos"] = pos
        m["cst"] = cst
        m["rkt"] = rkt
        in_maps.append(m)
    res = run_bass_kernel_spmd(nc, in_maps, core_ids=list(range(8)))
    y_s = np.stack([np.asarray(res.results[c]["y"][:TS]) for c in range(8)], axis=0)
    y_p = np.stack([np.concatenate([np.asarray(res.results[g * NR + r]["y"][TS:]) for r in range(NR)], axis=0) for g in range(2)], axis=0)
    return y_p.astype(np.float32), y_s.astype(np.float32)


def kernel(**inputs):
    return run_cores(inputs, 4096, 2048, 4)
```
